# Optimizing a Trainium2 kernel written in Bass

```python
import math
import jax, jax.numpy as jnp
from jax import lax
import numpy as np

D_MODEL = 2048
BATCH = 16
SEQ = 2048
DEPTH = 2

N_META = 16
BLOCK = 128
D_FF = 5632
EPS = 1e-6
S5_GROUP = 16
S5_STATE = 64
S5_GROUPS = 32
S5_WIDTH = S5_GROUPS * S5_GROUP
MLA_HEADS = 8
MLA_Q_RANK = 512
MLA_KV_RANK = 256
MLA_NOPE = 128
MLA_ROPE = 64
MLA_QK = MLA_NOPE + MLA_ROPE
MLA_V = 128
ROPE_THETA = 10000.0
EVEN_IN = S5_WIDTH + MLA_Q_RANK + MLA_KV_RANK + MLA_ROPE
EVEN_MIX = S5_WIDTH + MLA_HEADS * MLA_V
SB_HEADS = 8
SB_DIM = 128
SB_W = SB_HEADS * SB_DIM
GDN_HEADS = 8
GDN_DK = 128
GDN_DV = 128
GDN_CONV = 4
GDN_QKV = GDN_HEADS * (2 * GDN_DK + GDN_DV)
ODD_IN = 3 * SB_W + GDN_QKV + 2 * GDN_HEADS + GDN_HEADS * GDN_DV
ODD_MIX = SB_W + GDN_HEADS * GDN_DV

kernel_name = "hybrid_s5_mla_stickbreak_gdn_macaron"


def rmsnorm(x, g):
    xf = x.astype(jnp.float32)
    y = xf * lax.rsqrt(jnp.mean(xf * xf, axis=-1, keepdims=True) + EPS)
    return (y * g.astype(jnp.float32)).astype(x.dtype)


def l2norm(x):
    xf = x.astype(jnp.float32)
    return xf * lax.rsqrt(jnp.sum(xf * xf, axis=-1, keepdims=True) + EPS)


def swiglu(x, w_gate, w_up, w_down):
    return (jax.nn.silu(x @ w_gate) * (x @ w_up)) @ w_down


def query_blocks(length):
    return [(0, N_META)] + [(s, min(s + BLOCK, length)) for s in range(N_META, length, BLOCK)]


def apply_rope(x, pos):
    half = x.shape[-1] // 2
    inv = ROPE_THETA ** (-jnp.arange(half, dtype=jnp.float32) / half)
    ang = pos.astype(jnp.float32)[:, None] * inv[None, :]
    cos = jnp.cos(ang)[None, :, None, :]
    sin = jnp.sin(ang)[None, :, None, :]
    x1 = x[..., :half].astype(jnp.float32)
    x2 = x[..., half:].astype(jnp.float32)
    return jnp.concatenate([x1 * cos - x2 * sin, x1 * sin + x2 * cos], axis=-1).astype(x.dtype)


def s5_mixer(u, log_dt, a_re, a_im, b_re, b_im, c_re, c_im, d_skip, w_glu):
    bsz, L, _ = u.shape
    f32 = jnp.float32
    dt = jnp.exp(log_dt.astype(f32))[:, None]
    ar, ai = a_re.astype(f32), a_im.astype(f32)
    mag = jnp.exp(dt * ar)
    lam_r, lam_i = mag * jnp.cos(dt * ai), mag * jnp.sin(dt * ai)
    den = ar * ar + ai * ai
    coef_r = ((lam_r - 1.0) * ar + lam_i * ai) / den
    coef_i = (lam_i * ar - (lam_r - 1.0) * ai) / den
    br, bi = b_re.astype(f32), b_im.astype(f32)
    bbar_r = coef_r[..., None] * br - coef_i[..., None] * bi
    bbar_i = coef_r[..., None] * bi + coef_i[..., None] * br
    ut = jnp.swapaxes(u.astype(f32).reshape(bsz, L, S5_GROUPS, S5_GROUP), 0, 1)
    xr = jnp.einsum('lbgc,gpc->lbgp', ut, bbar_r)
    xi = jnp.einsum('lbgc,gpc->lbgp', ut, bbar_i)
    lr_t = jnp.broadcast_to(lam_r, (L, 1, S5_GROUPS, S5_STATE))
    li_t = jnp.broadcast_to(lam_i, (L, 1, S5_GROUPS, S5_STATE))

    def combine(e1, e2):
        a1r, a1i, b1r, b1i = e1
        a2r, a2i, b2r, b2i = e2
        return (a2r * a1r - a2i * a1i, a2r * a1i + a2i * a1r,
                a2r * b1r - a2i * b1i + b2r, a2r * b1i + a2i * b1r + b2i)

    _, _, sr, si = lax.associative_scan(combine, (lr_t, li_t, xr, xi), axis=0)
    y = (jnp.einsum('lbgp,gcp->lbgc', sr, c_re.astype(f32))
         - jnp.einsum('lbgp,gcp->lbgc', si, c_im.astype(f32))
         + d_skip.astype(f32).reshape(S5_GROUPS, S5_GROUP) * ut)
    y = jnp.swapaxes(y, 0, 1).reshape(bsz, L, S5_WIDTH)
    y = jax.nn.gelu(y).astype(u.dtype)
    return y * jax.nn.sigmoid(y @ w_glu)


def mla(c_q, c_kv, k_rope, g_cq, g_ckv, w_uq, w_ukv, g_qn, g_kn):
    bsz, L, _ = c_q.shape
    pos = jnp.arange(L)
    q = (rmsnorm(c_q, g_cq) @ w_uq).reshape(bsz, L, MLA_HEADS, MLA_QK)
    kv = (rmsnorm(c_kv, g_ckv) @ w_ukv).reshape(bsz, L, MLA_HEADS, MLA_NOPE + MLA_V)
    k_nope, v = kv[..., :MLA_NOPE], kv[..., MLA_NOPE:]
    k = jnp.concatenate([k_nope, jnp.broadcast_to(k_rope[:, :, None, :], (bsz, L, MLA_HEADS, MLA_ROPE))], axis=-1)
    q = rmsnorm(q, g_qn)
    k = rmsnorm(k, g_kn)
    q = jnp.concatenate([q[..., :MLA_NOPE], apply_rope(q[..., MLA_NOPE:], pos)], axis=-1)
    k = jnp.concatenate([k[..., :MLA_NOPE], apply_rope(k[..., MLA_NOPE:], pos)], axis=-1)
    scale = MLA_QK ** -0.5
    outs = []
    for s, e in query_blocks(L):
        sc = jnp.einsum('bqhd,bkhd->bhqk', q[:, s:e], k[:, :e]).astype(jnp.float32) * scale
        causal = jnp.arange(s, e)[:, None] >= jnp.arange(e)[None, :]
        p = jax.nn.softmax(jnp.where(causal, sc, -jnp.inf), axis=-1)
        outs.append(jnp.einsum('bhqk,bkhd->bqhd', p.astype(v.dtype), v[:, :e]))
    return jnp.concatenate(outs, axis=1).reshape(bsz, L, MLA_HEADS * MLA_V)


def stick_breaking(q, k, v):
    L = q.shape[1]
    scale = SB_DIM ** -0.5
    outs = []
    for s, e in query_blocks(L):
        z = jnp.einsum('bqhd,bkhd->bhqk', q[:, s:e], k[:, :e]).astype(jnp.float32) * scale
        strict = jnp.arange(s, e)[:, None] > jnp.arange(e)[None, :]
        log_keep = jnp.where(strict, jax.nn.log_sigmoid(-z), 0.0)
        tail = lax.cumsum(log_keep, axis=3, reverse=True) - log_keep
        w = jnp.where(strict, jnp.exp(jax.nn.log_sigmoid(z) + tail), 0.0)
        outs.append(jnp.einsum('bhqk,bkhd->bqhd', w.astype(v.dtype), v[:, :e]))
    return jnp.concatenate(outs, axis=1)


def causal_depthwise_conv(x, w):
    return lax.conv_general_dilated(x, w.astype(x.dtype)[:, None, :], window_strides=(1,),
                                    padding=[(GDN_CONV - 1, 0)],
                                    dimension_numbers=('NWC', 'WIO', 'NWC'),
                                    feature_group_count=x.shape[-1])


def gdn_chunk_scan(q, k, v, g, beta, state):
    c = q.shape[3]
    G = jnp.cumsum(g, axis=-1)
    lower = jnp.tril(jnp.ones((c, c), dtype=bool))
    strict = jnp.tril(jnp.ones((c, c), dtype=bool), -1)
    decay = jnp.exp(jnp.where(lower, G[..., :, None] - G[..., None, :], -jnp.inf))
    kb = k * beta[..., None]
    m = jnp.where(strict, jnp.einsum('bhnid,bhnjd->bhnij', kb, k) * decay, 0.0)
    eye = jnp.eye(c, dtype=q.dtype)
    t_inv = lax.linalg.triangular_solve(eye + m, jnp.broadcast_to(eye, m.shape), left_side=True,
                                        lower=True, unit_diagonal=True)
    w = t_inv @ (kb * jnp.exp(G)[..., None])
    u = t_inv @ (v * beta[..., None])
    attn = jnp.einsum('bhnid,bhnjd->bhnij', q, k) * decay
    q_dec = q * jnp.exp(G)[..., None]
    k_dec = k * jnp.exp(G[..., -1:] - G)[..., None]
    g_last = jnp.exp(G[..., -1])

    def step(s, xs):
        w_c, u_c, a_c, qd_c, kd_c, gl_c = xs
        v_new = u_c - w_c @ s
        o = qd_c @ s + a_c @ v_new
        s = s * gl_c[..., None, None] + jnp.swapaxes(kd_c, -1, -2) @ v_new
        return s, o

    xs = tuple(jnp.moveaxis(t, 2, 0) for t in (w, u, attn, q_dec, k_dec, g_last))
    state, o = lax.scan(step, state, xs)
    return jnp.moveaxis(o, 0, 2), state


def gated_deltanet(qkv_in, a_in, b_in, z, conv_w, a_log, dt_bias, g_out):
    bsz, L, _ = qkv_in.shape
    H = GDN_HEADS
    f32 = jnp.float32
    qkv = jax.nn.silu(causal_depthwise_conv(qkv_in, conv_w))
    q, k, v = jnp.split(qkv, [H * GDN_DK, 2 * H * GDN_DK], axis=-1)
    q = l2norm(q.reshape(bsz, L, H, GDN_DK)) * (GDN_DK ** -0.5)
    k = l2norm(k.reshape(bsz, L, H, GDN_DK))
    v = v.reshape(bsz, L, H, GDN_DV).astype(f32)
    beta = jax.nn.sigmoid(b_in.astype(f32))
    g = -jnp.exp(a_log.astype(f32)) * jax.nn.softplus(a_in.astype(f32) + dt_bias.astype(f32))

    def split_chunks(t):
        t = jnp.swapaxes(t, 1, 2)
        meta = t[:, :, :N_META][:, :, None]
        real = t[:, :, N_META:]
        real = real.reshape(real.shape[:2] + (real.shape[2] // BLOCK, BLOCK) + real.shape[3:])
        return meta, real

    parts = [split_chunks(t) for t in (q, k, v, g, beta)]
    s0 = jnp.zeros((bsz, H, GDN_DK, GDN_DV), f32)
    o_meta, s1 = gdn_chunk_scan(*[p[0] for p in parts], s0)
    o_real, _ = gdn_chunk_scan(*[p[1] for p in parts], s1)
    o = jnp.concatenate([o_meta.reshape(bsz, H, N_META, GDN_DV),
                         o_real.reshape(bsz, H, L - N_META, GDN_DV)], axis=2)
    o = jnp.swapaxes(o, 1, 2)
    o = rmsnorm(o, g_out) * jax.nn.silu(z.astype(f32).reshape(bsz, L, H, GDN_DV))
    return o.reshape(bsz, L, H * GDN_DV).astype(qkv_in.dtype)


def even_mixer(h, w_in, log_dt, a_re, a_im, b_re, b_im, c_re, c_im, d_skip, w_glu,
               g_cq, g_ckv, w_uq, w_ukv, g_qn, g_kn, w_out):
    p = h @ w_in
    u, c_q, c_kv, k_rope = jnp.split(
        p, [S5_WIDTH, S5_WIDTH + MLA_Q_RANK, S5_WIDTH + MLA_Q_RANK + MLA_KV_RANK], axis=-1)
    y_a = s5_mixer(u, log_dt, a_re, a_im, b_re, b_im, c_re, c_im, d_skip, w_glu)
    y_b = mla(c_q, c_kv, k_rope, g_cq, g_ckv, w_uq, w_ukv, g_qn, g_kn)
    return jnp.concatenate([y_a, y_b], axis=-1) @ w_out


def odd_mixer(h, w_in, conv_w, a_log, dt_bias, g_out, w_out):
    bsz, L, _ = h.shape
    p = h @ w_in
    o1 = 3 * SB_W + GDN_QKV
    sq, sk, sv, gqkv, ga, gb, gz = jnp.split(
        p, [SB_W, 2 * SB_W, 3 * SB_W, o1, o1 + GDN_HEADS, o1 + 2 * GDN_HEADS], axis=-1)
    shp = (bsz, L, SB_HEADS, SB_DIM)
    y_c = stick_breaking(sq.reshape(shp), sk.reshape(shp), sv.reshape(shp)).reshape(bsz, L, SB_W)
    y_d = gated_deltanet(gqkv, ga, gb, gz, conv_w, a_log, dt_bias, g_out)
    return jnp.concatenate([y_c, y_d], axis=-1) @ w_out


def setup_inputs(seed: int = 0) -> dict:
    key = jax.random.key(seed)
    k = jax.random.split(key, 34)
    f32 = jnp.float32
    ne, no = (DEPTH + 1) // 2, DEPTH // 2
    G, P, C = S5_GROUPS, S5_STATE, S5_GROUP

    def nrm(i, shape, std):
        return std * jax.random.normal(k[i], shape, f32)

    def gain(i, shape):
        return 1.0 + 0.02 * jax.random.normal(k[i], shape, f32)

    dt_gdn = jnp.exp(jax.random.uniform(k[30], (no, GDN_HEADS), f32, math.log(1e-3), math.log(1e-1)))
    return {
        "x": nrm(0, (BATCH, SEQ, D_MODEL), 1.0),
        "meta_tokens": nrm(1, (N_META, D_MODEL), 1.0),
        "norm_ffn1": gain(2, (DEPTH, D_MODEL)),
        "w1_gate": nrm(3, (DEPTH, D_MODEL, D_FF), D_MODEL ** -0.5),
        "w1_up": nrm(4, (DEPTH, D_MODEL, D_FF), D_MODEL ** -0.5),
        "w1_down": nrm(5, (DEPTH, D_FF, D_MODEL), D_FF ** -0.5),
        "norm_mix": gain(10, (DEPTH, D_MODEL)),
        "norm_ffn2": gain(6, (DEPTH, D_MODEL)),
        "w2_gate": nrm(7, (DEPTH, D_MODEL, D_FF), D_MODEL ** -0.5),
        "w2_up": nrm(8, (DEPTH, D_MODEL, D_FF), D_MODEL ** -0.5),
        "w2_down": nrm(9, (DEPTH, D_FF, D_MODEL), D_FF ** -0.5),
        "ev_w_in": nrm(11, (ne, D_MODEL, EVEN_IN), D_MODEL ** -0.5),
        "s5_log_dt": jax.random.uniform(k[12], (ne, G), f32, math.log(1e-3), math.log(1e-1)),
        "s5_a_re": -0.5 + 0.01 * jax.random.uniform(k[13], (ne, G, P), f32, -1.0, 1.0),
        "s5_a_im": jnp.broadcast_to(jnp.pi * jnp.arange(P, dtype=f32), (ne, G, P)),
        "s5_b_re": nrm(14, (ne, G, P, C), (2 * C) ** -0.5),
        "s5_b_im": nrm(15, (ne, G, P, C), (2 * C) ** -0.5),
        "s5_c_re": nrm(16, (ne, G, C, P), P ** -0.5),
        "s5_c_im": nrm(17, (ne, G, C, P), P ** -0.5),
        "s5_d": nrm(18, (ne, S5_WIDTH), 1.0),
        "s5_w_glu": nrm(19, (ne, S5_WIDTH, S5_WIDTH), S5_WIDTH ** -0.5),
        "mla_g_cq": gain(20, (ne, MLA_Q_RANK)),
        "mla_g_ckv": gain(21, (ne, MLA_KV_RANK)),
        "mla_w_uq": nrm(22, (ne, MLA_Q_RANK, MLA_HEADS * MLA_QK), MLA_Q_RANK ** -0.5),
        "mla_w_ukv": nrm(23, (ne, MLA_KV_RANK, MLA_HEADS * (MLA_NOPE + MLA_V)), MLA_KV_RANK ** -0.5),
        "mla_g_q": gain(24, (ne, MLA_QK)),
        "mla_g_k": gain(25, (ne, MLA_QK)),
        "ev_w_out": nrm(26, (ne, EVEN_MIX, D_MODEL), EVEN_MIX ** -0.5),
        "od_w_in": nrm(27, (no, D_MODEL, ODD_IN), D_MODEL ** -0.5),
        "gdn_conv": nrm(28, (no, GDN_CONV, GDN_QKV), GDN_CONV ** -0.5),
        "gdn_a_log": jnp.log(jax.random.uniform(k[29], (no, GDN_HEADS), f32, 1.0, 16.0)),
        "gdn_dt_bias": dt_gdn + jnp.log(-jnp.expm1(-dt_gdn)),
        "gdn_g_out": gain(31, (no, GDN_DV)),
        "od_w_out": nrm(32, (no, ODD_MIX, D_MODEL), ODD_MIX ** -0.5),
    }


def reference(x, meta_tokens, norm_ffn1, w1_gate, w1_up, w1_down, norm_mix, norm_ffn2,
              w2_gate, w2_up, w2_down, ev_w_in, s5_log_dt, s5_a_re, s5_a_im, s5_b_re, s5_b_im,
              s5_c_re, s5_c_im, s5_d, s5_w_glu, mla_g_cq, mla_g_ckv, mla_w_uq, mla_w_ukv,
              mla_g_q, mla_g_k, ev_w_out, od_w_in, gdn_conv, gdn_a_log, gdn_dt_bias, gdn_g_out,
              od_w_out):
    bsz = x.shape[0]
    meta = jnp.broadcast_to(meta_tokens[None].astype(x.dtype), (bsz, N_META, D_MODEL))
    h = jnp.concatenate([meta, x], axis=1)
    for l in range(DEPTH):
        i = l // 2
        h = h + 0.5 * swiglu(rmsnorm(h, norm_ffn1[l]), w1_gate[l], w1_up[l], w1_down[l])
        hn = rmsnorm(h, norm_mix[l])
        if l % 2 == 0:
            mix = even_mixer(hn, ev_w_in[i], s5_log_dt[i], s5_a_re[i], s5_a_im[i], s5_b_re[i],
                             s5_b_im[i], s5_c_re[i], s5_c_im[i], s5_d[i], s5_w_glu[i],
                             mla_g_cq[i], mla_g_ckv[i], mla_w_uq[i], mla_w_ukv[i],
                             mla_g_q[i], mla_g_k[i], ev_w_out[i])
        else:
            mix = odd_mixer(hn, od_w_in[i], gdn_conv[i], gdn_a_log[i], gdn_dt_bias[i],
                            gdn_g_out[i], od_w_out[i])
        h = h + mix
        h = h + 0.5 * swiglu(rmsnorm(h, norm_ffn2[l]), w2_gate[l], w2_up[l], w2_down[l])
    return h[:, N_META:]
```

```python
import math
import numpy as np
import concourse.bass as bass
import concourse.mybir as mybir
from concourse.bass_utils import run_bass_kernel_spmd

F32 = mybir.dt.float32
AF = mybir.ActivationFunctionType
ALU = mybir.AluOpType
AX = mybir.AxisListType

D = 2048
DC = 16
DFF = 5632
FC = 44
N_META = 16
EPS = 1e-6
ARENA_WORDS = 52000


class Tok:
    __slots__ = ("w", "r", "name")

    def __init__(self, name=""):
        self.w = None
        self.r = {}
        self.name = name


class Eng:
    def __init__(self, P, eng, name, is_pe=False):
        self.P = P
        self.eng = eng
        self.name = name
        self.sem = P.nc.alloc_semaphore("s_" + name)
        self.key = "E" + name
        self.cnt = 0
        self.seen = {}
        self.is_pe = is_pe
        self.nwait = 0

    def wait_ev(self, ev):
        if ev is None:
            return
        key, sem, val = ev
        if self.is_pe and key == self.key:
            return
        if key == self.key and val <= self.cnt_done_known():
            pass
        if self.seen.get(key, 0) >= val:
            return
        self.eng.wait_ge(sem, val)
        self.nwait += 1
        self.seen[key] = val

    def cnt_done_known(self):
        return self.seen.get(self.key, 0)

    def deps(self, outs, ins):
        for t in ins:
            self.wait_ev(t.w)
        for t in outs:
            self.wait_ev(t.w)
            for k, (sem, val) in t.r.items():
                self.wait_ev((k, sem, val))

    def op(self, fn, outs=(), ins=()):
        self.deps(outs, ins)
        inst = fn()
        self.cnt += 1
        inst.then_inc(self.sem, 1)
        ev = (self.key, self.sem, self.cnt)
        for t in ins:
            t.r[self.key] = (self.sem, self.cnt)
        for t in outs:
            t.w = ev
            t.r = {}
        return inst


class DmaQ:
    def __init__(self, P, eng, name, nsem):
        self.P = P
        self.eng = eng
        self.name = name
        self.sems = [P.nc.alloc_semaphore("d_%s%d" % (name, i)) for i in range(nsem)]
        self.vals = [0] * nsem
        self.i = 0
        self.seen = {}
        self.nwait = 0

    def wait_ev(self, ev):
        if ev is None:
            return
        key, sem, val = ev
        if self.seen.get(key, 0) >= val:
            return
        self.eng.wait_ge(sem, val)
        self.nwait += 1
        self.seen[key] = val

    def dma(self, out, in_, outs=(), ins=(), **kw):
        for t in ins:
            self.wait_ev(t.w)
        for t in outs:
            self.wait_ev(t.w)
            for k, (sem, val) in t.r.items():
                self.wait_ev((k, sem, val))
        i = self.i
        self.i = (self.i + 1) % len(self.sems)
        key = "D%s%d" % (self.name, i)
        if self.vals[i] > 0:
            self.wait_ev((key, self.sems[i], self.vals[i]))
        inst = self.eng.dma_start(out=out, in_=in_, **kw)
        self.vals[i] += 16
        inst.then_inc(self.sems[i], 16)
        ev = (key, self.sems[i], self.vals[i])
        for t in ins:
            t.r[key] = (self.sems[i], self.vals[i])
        for t in outs:
            t.w = ev
            t.r = {}
        return ev

    def drain(self):
        for i, s in enumerate(self.sems):
            if self.vals[i] > 0:
                self.wait_ev(("D%s%d" % (self.name, i), s, self.vals[i]))


class Prog:
    def __init__(self):
        self.nc = bass.Bass("TRN2", target_bir_lowering=False)
        nc = self.nc
        self.pe = Eng(self, nc.tensor, "pe", is_pe=True)
        self.act = Eng(self, nc.scalar, "act")
        self.dve = Eng(self, nc.vector, "dve")
        self.pool = Eng(self, nc.gpsimd, "pool")
        self.q_w = DmaQ(self, nc.sync, "w", 12)
        self.q_a = DmaQ(self, nc.gpsimd, "a", 8)
        self._n = 0
        self.AW = ARENA_WORDS
        self.arena = nc.alloc_sbuf_tensor("arena", [128, self.AW], F32)
        self.off = 0
        self.psb = [(nc.alloc_psum_tensor("psb%d" % i, [128, 512], F32), Tok("psb%d" % i)) for i in range(8)]

    def sb(self, shape, name=None):
        n = 1
        for d in shape[1:]:
            n *= d
        assert self.off + n <= self.AW, ("SBUF arena overflow", name, self.off, n)
        ap = self.arena[0:shape[0], self.off:self.off + n]
        self.off += n
        if len(shape) == 3:
            ap = ap.rearrange("p (a b) -> p a b", a=shape[1])
        elif len(shape) == 4:
            ap = ap.rearrange("p (a b c) -> p a b c", a=shape[1], b=shape[2])
        return ap

    def mark(self):
        return self.off

    def release(self, m):
        self.barrier()
        self.off = m

    def barrier(self):
        evs = []
        for e in (self.pe, self.act, self.dve, self.pool):
            if e.cnt > 0:
                evs.append((e.key, e.sem, e.cnt))
        for q in (self.q_w, self.q_a):
            for i, sm in enumerate(q.sems):
                if q.vals[i] > 0:
                    evs.append(("D%s%d" % (q.name, i), sm, q.vals[i]))
        for e in (self.pe, self.act, self.dve, self.pool, self.q_w, self.q_a):
            for ev in evs:
                e.wait_ev(ev)

    def dram(self, name, shape, kind="Internal", dtype=F32):
        return self.nc.dram_tensor(name, list(shape), dtype, kind=kind).ap()


class Ring:
    def __init__(self, bufs):
        self.bufs = bufs
        self.i = 0

    def next(self):
        b = self.bufs[self.i]
        self.i = (self.i + 1) % len(self.bufs)
        return b


class Kern:
    LAYERS = None

    def __init__(self, NS, NT, depth=2, do=("ffn1", "mix", "ffn2"), seq_real=None):
        self.NS = NS
        self.NT = NT
        self.LP = NT * 128
        self.NTOK = NS * self.LP
        self.depth = depth
        self.do = do
        self.L = seq_real if seq_real is not None else (self.LP - 112)
        self.P = Prog()
        self.build()

    def build(self):
        P = self.P
        nc = P.nc
        NS, NT, LP, NTOK = self.NS, self.NT, self.LP, self.NTOK
        SEQ = self.L - N_META
        dep = self.depth
        self.xin = P.dram("xin", [NS * LP, D], "ExternalInput")
        self.ident_d = P.dram("ident", [128, 128], "ExternalInput")
        self.w = {}
        def ext(name, shape):
            self.w[name] = P.dram(name, shape, "ExternalInput")
        ext("norm_ffn1", [dep, D]); ext("norm_mix", [dep, D]); ext("norm_ffn2", [dep, D])
        for nm in ("w1_gate", "w1_up", "w2_gate", "w2_up"):
            ext(nm, [dep, D, DFF])
        for nm in ("w1_down", "w2_down"):
            ext(nm, [dep, DFF, D])
        ne, no = (dep + 1) // 2, dep // 2
        if "mix" in self.do:
            ext("ev_w_in", [ne, D, 1344]); ext("s5_log_dt", [ne, 32]); ext("s5_a_re", [ne, 32, 64]); ext("s5_a_im", [ne, 32, 64])
            ext("s5_b_re", [ne, 32, 64, 16]); ext("s5_b_im", [ne, 32, 64, 16]); ext("s5_c_re", [ne, 32, 16, 64]); ext("s5_c_im", [ne, 32, 16, 64])
            ext("s5_d", [ne, 512]); ext("s5_w_glu", [ne, 512, 512]); ext("mla_g_cq", [ne, 512]); ext("mla_g_ckv", [ne, 256])
            ext("mla_w_uq", [ne, 512, 1536]); ext("mla_w_ukv", [ne, 256, 2048]); ext("mla_g_q", [ne, 192]); ext("mla_g_k", [ne, 192])
            ext("ev_w_out", [ne, 1536, D])
            if no > 0:
                ext("od_w_in", [no, D, 7184]); ext("gdn_conv", [no, 4, 3072]); ext("gdn_a_log", [no, 8]); ext("gdn_dt_bias", [no, 8])
                ext("gdn_g_out", [no, 128]); ext("od_w_out", [no, 2048, D])
            self.c_rope_cos = P.dram("rope_cos", [LP, 32], "ExternalInput")
            self.c_rope_sin = P.dram("rope_sin", [LP, 32], "ExternalInput")
            self.c_maskT = P.dram("maskT", [128, 128], "ExternalInput")
            self.uT = P.dram("uT", [4, 128, NTOK])
            self.QT = P.dram("QT", [8, 192, NTOK])
            self.KT = P.dram("KT", [8, 192, NTOK])
            self.Vd = P.dram("Vd", [NTOK, 1024])
            self.yE = P.dram("yE", [12, 128, NTOK])
            if no > 0:
                self.c_maskS = P.dram("maskS", [128, 128], "ExternalInput")
                self.c_maskL = P.dram("maskL", [128, 128], "ExternalInput")
                self.c_sel = P.dram("sel", [8, 1024], "ExternalInput")
                self.sbQ = P.dram("sbQ", [8, 128, NTOK]); self.sbK = P.dram("sbK", [8, 128, NTOK]); self.sbV = P.dram("sbV", [NTOK, 1024])
                self.gT = P.dram("gT", [24, 128, NTOK]); self.gzT = P.dram("gzT", [8, 128, NTOK]); self.gabT = P.dram("gabT", [16, NTOK])
                self.yO = P.dram("yO", [16, 128, NTOK])
        self.out = P.dram("out", [NS * SEQ, D], "ExternalOutput")
        self.hT = P.dram("hT", [DC, 128, NTOK])
        self.ident = P.sb([128, 128], "ident"); self.t_ident = Tok("ident")
        P.q_a.dma(self.ident[:], self.ident_d[:, :], outs=[self.t_ident])
        self.ones = P.sb([128, 128], "ones"); self.t_ones = Tok("ones")
        P.dve.op(lambda: nc.vector.memset(self.ones[:], 1.0), outs=[self.t_ones])
        self.epsb = P.sb([128, 1], "epsb"); self.t_eps = Tok("eps")
        P.dve.op(lambda: nc.vector.memset(self.epsb[:], EPS), outs=[self.t_eps])
        self.gains = {}
        for nm in ("norm_ffn1", "norm_mix", "norm_ffn2"):
            g = P.sb([128, dep, DC], nm); t = Tok(nm)
            for l in range(dep):
                P.q_a.dma(g[:, l, :], self.w[nm][l].rearrange("(c p) -> p c", p=128), outs=[t],
                          allow_slow_non_contiguous=True)
            self.gains[nm] = (g, t)

        self.stage_in()
        for l in (self.LAYERS if self.LAYERS is not None else range(dep)):
            if "ffn1" in self.do:
                self.ffn(l, "1")
            if "mix" in self.do:
                if l % 2 == 0:
                    self.even_mixer(l)
                else:
                    self.odd_mixer(l)
            if "ffn2" in self.do:
                self.ffn(l, "2")
        self.stage_out()
        P.q_a.drain()
        P.q_w.drain()

    def psring(self, idx):
        return Ring([self.P.psb[i] for i in idx])

    def stage_in(self):
        P = self.P; nc = P.nc
        m = P.mark()
        NB = self.NTOK // 128
        xin_r = Ring([(P.sb([128, D], "xin"), Tok()) for _ in range(2)])
        ps_r = self.psring([0, 1])
        ht_r = Ring([(P.sb([128, DC, 128], "htin"), Tok()) for _ in range(2)])
        for b in range(NB):
            xt, xtok = xin_r.next()
            P.q_a.dma(xt[:], self.xin[b * 128:(b + 1) * 128, :], outs=[xtok])
            ht, htok = ht_r.next()
            for g in range(4):
                ps, ptok = ps_r.next()
                for j in range(4):
                    c = g * 4 + j
                    P.pe.op(lambda: nc.tensor.transpose(ps[:, j * 128:(j + 1) * 128],
                                                        xt[:, c * 128:(c + 1) * 128], self.ident[:]),
                            outs=[ptok], ins=[xtok, self.t_ident])
                P.act.op(lambda: nc.scalar.copy(out=ht[:, g * 4:(g + 1) * 4, :],
                                                in_=ps[:].rearrange("p (c t) -> p c t", c=4)),
                         outs=[htok], ins=[ptok])
            P.q_a.dma(self.hT[:, :, b * 128:(b + 1) * 128].rearrange("c p t -> p c t"), ht[:], ins=[htok],
                      outs=[self.tok_h(b * 128, 128)])
        P.release(m)

    def tok_h(self, t0, n):
        if not hasattr(self, "_htoks"):
            self._htoks = {}
        key = t0 // 128
        if key not in self._htoks:
            self._htoks[key] = Tok("h%d" % key)
        return self._htoks[key]

    def toks_h(self, t0, n):
        return [self.tok_h(t, 128) for t in range(t0, t0 + n, 128)]

    def stage_out(self):
        P = self.P; nc = P.nc
        m = P.mark()
        SEQ = self.L - N_META
        ht_r = Ring([(P.sb([128, DC, 128], "htout"), Tok()) for _ in range(2)])
        ps_r = self.psring([0, 1])
        o_r = Ring([(P.sb([128, D], "osb"), Tok()) for _ in range(2)])
        for s in range(self.NS):
            for j in range(self.NT):
                t0 = j * 128
                lo = max(t0, N_META); hi = min(t0 + 128, self.L)
                if hi <= lo:
                    continue
                b = s * self.NT + j
                ht, htok = ht_r.next()
                P.q_a.dma(ht[:], self.hT[:, :, b * 128:(b + 1) * 128].rearrange("c p t -> p c t"), outs=[htok],
                          ins=[self.tok_h(b * 128, 128)])
                ot, otok = o_r.next()
                for g in range(4):
                    ps, ptok = ps_r.next()
                    for jj in range(4):
                        c = g * 4 + jj
                        P.pe.op(lambda: nc.tensor.transpose(ps[:, jj * 128:(jj + 1) * 128], ht[:, c, :], self.ident[:]),
                                outs=[ptok], ins=[htok, self.t_ident])
                    P.act.op(lambda: nc.scalar.copy(out=ot[:, g * 512:(g + 1) * 512], in_=ps[:]),
                             outs=[otok], ins=[ptok])
                r0 = s * SEQ + lo - N_META
                P.q_a.dma(self.out[r0:r0 + (hi - lo), :], ot[lo - t0:hi - t0, :], ins=[otok])
        P.release(m)

    def rms_scale(self, src, stok, T, gain_cols, gtok, ps_sum, pstok, sq, sqtok, rstd, rtok, nchunks=DC, dim=D):
        P = self.P; nc = P.nc
        for c in range(nchunks):
            P.act.op(lambda: nc.scalar.activation(out=sq[:, :T], in_=src[:, c, :T], func=AF.Square),
                     outs=[sqtok], ins=[stok])
            P.pe.op(lambda: nc.tensor.matmul(ps_sum[:, :T], self.ones[:], sq[:, :T], start=(c == 0), stop=(c == nchunks - 1)),
                    outs=[pstok], ins=[sqtok, self.t_ones])
        P.act.op(lambda: nc.scalar.activation(out=rstd[:, :T], in_=ps_sum[:, :T], func=AF.Sqrt,
                                              bias=self.epsb[:], scale=1.0 / dim),
                 outs=[rtok], ins=[pstok, self.t_eps])
        P.dve.op(lambda: nc.vector.reciprocal(out=rstd[:, :T], in_=rstd[:, :T]), outs=[rtok], ins=[rtok])
        for c in range(nchunks):
            P.dve.op(lambda: nc.vector.scalar_tensor_tensor(out=src[:, c, :T], in0=src[:, c, :T],
                                                            scalar=gain_cols(c), in1=rstd[:, :T],
                                                            op0=ALU.mult, op1=ALU.mult),
                     outs=[stok], ins=[stok, rtok, gtok])

    def ffn(self, l, which):
        P = self.P; nc = P.nc
        NTOK = self.NTOK
        TT = 512
        HF = FC // 2
        wg = self.w["w%s_gate" % which][l]
        wu = self.w["w%s_up" % which][l]
        wd = self.w["w%s_down" % which][l]
        gain, gtok = self.gains["norm_ffn%s" % which]
        m = P.mark()
        xt, xtok = P.sb([128, DC, TT], "ffx"), Tok("ffx")
        hf = P.sb([128, FC, TT], "ffh")
        hftoks = [Tok("ffh%d" % f) for f in range(FC)]
        wg_r = Ring([(P.sb([128, DC, 128], "wg"), Tok()) for _ in range(2)])
        wu_r = Ring([(P.sb([128, DC, 128], "wu"), Tok()) for _ in range(2)])
        wd_r = Ring([(P.sb([128, HF, 128], "wd"), Tok()) for _ in range(3)])
        psg_r = self.psring([0, 1]); psu_r = self.psring([2, 3]); psd_r = self.psring([4, 5])
        pss, psstok = P.psb[6]
        rstd, rtok = P.sb([128, TT], "ffrstd"), Tok()
        sg_r = Ring([(P.sb([128, TT], "ffsg"), Tok()) for _ in range(2)])
        hres_r = Ring([(P.sb([128, TT], "ffres"), Tok()) for _ in range(2)])
        wgv = wg.rearrange("(c p) f -> p c f", p=128)
        wuv = wu.rearrange("(c p) f -> p c f", p=128)
        wdv = wd.rearrange("(c p) d -> p c d", p=128)
        for t0 in range(0, NTOK, TT):
            T = min(TT, NTOK - t0)
            htoks = self.toks_h(t0, T)
            P.q_a.dma(xt[:, :, :T], self.hT[:, :, t0:t0 + T].rearrange("c p t -> p c t"), outs=[xtok], ins=htoks)
            sq, sqtok = sg_r.next()
            self.rms_scale(xt, xtok, T, lambda c: gain[:, l, c:c + 1], gtok, pss, psstok, sq, sqtok, rstd, rtok)
            def load_gu(f):
                g_, gt_ = wg_r.next(); u_, ut_ = wu_r.next()
                P.q_w.dma(g_[:], wgv[:, :, f * 128:(f + 1) * 128], outs=[gt_])
                P.q_w.dma(u_[:], wuv[:, :, f * 128:(f + 1) * 128], outs=[ut_])
                return g_, gt_, u_, ut_
            nxt = load_gu(0)
            for f in range(FC):
                g_, gt_, u_, ut_ = nxt
                if f + 1 < FC:
                    nxt = load_gu(f + 1)
                psg, psgt = psg_r.next(); psu, psut = psu_r.next()
                for c in range(DC):
                    P.pe.op(lambda: nc.tensor.matmul(psg[:, :T], g_[:, c, :], xt[:, c, :T], start=(c == 0), stop=(c == DC - 1)),
                            outs=[psgt], ins=[gt_, xtok])
                for c in range(DC):
                    P.pe.op(lambda: nc.tensor.matmul(psu[:, :T], u_[:, c, :], xt[:, c, :T], start=(c == 0), stop=(c == DC - 1)),
                            outs=[psut], ins=[ut_, xtok])
                sg, sgt = sg_r.next()
                P.act.op(lambda: nc.scalar.activation(out=sg[:, :T], in_=psg[:, :T], func=AF.Silu), outs=[sgt], ins=[psgt])
                P.dve.op(lambda: nc.vector.tensor_tensor(out=hf[:, f, :T], in0=psu[:, :T], in1=sg[:, :T], op=ALU.mult),
                         outs=[hftoks[f]], ins=[psut, sgt])
            def load_d(i):
                dc, hh = divmod(i, 2)
                d_, dt_ = wd_r.next()
                P.q_w.dma(d_[:], wdv[:, hh * HF:(hh + 1) * HF, dc * 128:(dc + 1) * 128], outs=[dt_])
                return d_, dt_
            pend = [load_d(0), load_d(1)]
            for dc in range(DC):
                hres, hrt = hres_r.next()
                P.q_a.dma(hres[:, :T], self.hT[dc, :, t0:t0 + T], outs=[hrt], ins=htoks)
                psd, psdt = psd_r.next()
                for hh in range(2):
                    d_, dt_ = pend.pop(0)
                    i_next = dc * 2 + hh + 2
                    if i_next < 2 * DC:
                        pend.append(load_d(i_next))
                    for ff in range(HF):
                        f = hh * HF + ff
                        P.pe.op(lambda: nc.tensor.matmul(psd[:, :T], d_[:, ff, :], hf[:, f, :T], start=(f == 0), stop=(f == FC - 1)),
                                outs=[psdt], ins=[dt_, hftoks[f]])
                P.dve.op(lambda: nc.vector.scalar_tensor_tensor(out=hres[:, :T], in0=psd[:, :T], scalar=0.5, in1=hres[:, :T],
                                                                op0=ALU.mult, op1=ALU.add),
                         outs=[hrt], ins=[psdt, hrt])
                P.q_a.dma(self.hT[dc, :, t0:t0 + T], hres[:, :T], ins=[hrt], outs=htoks)
        P.release(m)


    def bc_rows(self, dram_ap_row_tensor, offset, n, parts=128):
        return bass.AP(dram_ap_row_tensor.tensor, offset, [[0, parts], [1, n]])

    def even_mixer(self, l):
        import os
        st = os.environ.get("DBG_STAGES", "inproj,attn,s5,out").split(",")
        if "inproj" in st:
            self.even_inproj(l)
        if "attn" in st:
            self.even_attn(l)
        if "s5" in st:
            self.even_s5(l)
        if "out" in st:
            self.even_out(l)

    def headnorm_rope(self, buf, btok, g_rep, gtok, cos, sin, cstok, tmp, ttok, ss, sstok, t4, t4tok):
        P = self.P; nc = P.nc
        P.dve.op(lambda: nc.vector.tensor_tensor(out=tmp[:], in0=buf[:], in1=buf[:], op=ALU.mult), outs=[ttok], ins=[btok])
        P.dve.op(lambda: nc.vector.tensor_reduce(out=ss[:], in_=tmp[:], axis=AX.X, op=ALU.add), outs=[sstok], ins=[ttok])
        P.act.op(lambda: nc.scalar.activation(out=ss[:], in_=ss[:], func=AF.Sqrt, bias=self.epsb[:], scale=1.0 / 192),
                 outs=[sstok], ins=[sstok, self.t_eps])
        P.dve.op(lambda: nc.vector.reciprocal(out=ss[:], in_=ss[:]), outs=[sstok], ins=[sstok])
        P.dve.op(lambda: nc.vector.tensor_tensor(out=buf[:], in0=buf[:], in1=ss[:].unsqueeze(2).to_broadcast([128, 8, 192]), op=ALU.mult),
                 outs=[btok], ins=[btok, sstok])
        P.dve.op(lambda: nc.vector.tensor_tensor(out=buf[:], in0=buf[:], in1=g_rep[:].unsqueeze(1).to_broadcast([128, 8, 192]), op=ALU.mult),
                 outs=[btok], ins=[btok, gtok])
        x1 = buf[:, :, 128:160]; x2 = buf[:, :, 160:192]
        cb = cos.unsqueeze(1).to_broadcast([128, 8, 32]); sb_ = sin.unsqueeze(1).to_broadcast([128, 8, 32])
        P.dve.op(lambda: nc.vector.tensor_tensor(out=t4[:, 0], in0=x1, in1=cb, op=ALU.mult), outs=[t4tok], ins=[btok, cstok])
        P.dve.op(lambda: nc.vector.tensor_tensor(out=t4[:, 1], in0=x2, in1=sb_, op=ALU.mult), outs=[t4tok], ins=[btok, cstok])
        P.dve.op(lambda: nc.vector.tensor_tensor(out=t4[:, 2], in0=x1, in1=sb_, op=ALU.mult), outs=[t4tok], ins=[btok, cstok])
        P.dve.op(lambda: nc.vector.tensor_tensor(out=t4[:, 3], in0=x2, in1=cb, op=ALU.mult), outs=[t4tok], ins=[btok, cstok])
        P.dve.op(lambda: nc.vector.tensor_tensor(out=x1, in0=t4[:, 0], in1=t4[:, 1], op=ALU.subtract), outs=[btok], ins=[t4tok])
        P.dve.op(lambda: nc.vector.tensor_tensor(out=x2, in0=t4[:, 2], in1=t4[:, 3], op=ALU.add), outs=[btok], ins=[t4tok])

    def even_inproj(self, l):
        P = self.P; nc = P.nc; i = l // 2
        NS, LP, NT = self.NS, self.LP, self.NT
        m = P.mark()
        TT = 512
        Wv = self.w["ev_w_in"][i].rearrange("(c p) f -> p c f", p=128)
        gain, gtok = self.gains["norm_mix"]
        wuq = P.sb([128, 4, 1536], "wuq"); t_wuq = Tok()
        P.q_w.dma(wuq[:], self.w["mla_w_uq"][i].rearrange("(c p) f -> p c f", p=128), outs=[t_wuq])
        wukv = P.sb([128, 2, 2048], "wukv"); t_wukv = Tok()
        P.q_w.dma(wukv[:], self.w["mla_w_ukv"][i].rearrange("(c p) f -> p c f", p=128), outs=[t_wukv])
        wkr = P.sb([128, DC, 64], "wkr"); t_wkr = Tok()
        P.q_w.dma(wkr[:], Wv[:, :, 1280:1344], outs=[t_wkr])
        gcq = P.sb([128, 4], "gcq"); gckv = P.sb([128, 2], "gckv"); t_g = Tok()
        P.q_a.dma(gcq[:], self.w["mla_g_cq"][i].rearrange("(c p) -> p c", p=128), outs=[t_g], allow_slow_non_contiguous=True)
        P.q_a.dma(gckv[:], self.w["mla_g_ckv"][i].rearrange("(c p) -> p c", p=128), outs=[t_g], allow_slow_non_contiguous=True)
        gq = P.sb([128, 192], "gq"); gk = P.sb([128, 192], "gk"); t_gq = Tok()
        P.q_a.dma(gq[:], self.bc_rows(self.w["mla_g_q"], i * 192, 192), outs=[t_gq])
        P.q_a.dma(gk[:], self.bc_rows(self.w["mla_g_k"], i * 192, 192), outs=[t_gq])
        cosT = P.sb([128, NT, 32], "cos"); sinT = P.sb([128, NT, 32], "sin"); t_cs = Tok()
        P.q_a.dma(cosT[:], self.c_rope_cos.rearrange("(j p) f -> p j f", p=128), outs=[t_cs])
        P.q_a.dma(sinT[:], self.c_rope_sin.rearrange("(j p) f -> p j f", p=128), outs=[t_cs])
        xt, xtok = P.sb([128, DC, TT], "evx"), Tok()
        cq, cqtok = P.sb([128, 4, TT], "cq"), Tok()
        ckv, ckvtok = P.sb([128, 2, TT], "ckv"), Tok()
        win_r = Ring([(P.sb([128, DC, 128], "win"), Tok()) for _ in range(2)])
        ub_r = Ring([(P.sb([128, TT], "ub"), Tok()) for _ in range(2)])
        rstd, rtok = P.sb([128, TT], "rstd"), Tok()
        sq, sqtok = P.sb([128, TT], "sq"), Tok()
        q_sb, qtok = P.sb([128, 8, 192], "q_sb"), Tok()
        k_sb, ktok = P.sb([128, 8, 192], "k_sb"), Tok()
        v_sb, vtok = P.sb([128, 8, 128], "v_sb"), Tok()
        tmp, ttok = P.sb([128, 8, 192], "tmp"), Tok()
        ss, sstok = P.sb([128, 8], "ss"), Tok()
        t4, t4tok = P.sb([128, 4, 8, 32], "t4"), Tok()
        qTn, qTntok = P.sb([128, 8, 128], "qTn"), Tok()
        qTr, qTrtok = P.sb([64, 8, 128], "qTr"), Tok()
        kTn, kTntok = P.sb([128, 8, 128], "kTn"), Tok()
        kTr, kTrtok = P.sb([64, 8, 128], "kTr"), Tok()
        ps01 = self.psring([0, 1])
        pss, psstok = P.psb[7]
        for s in range(NS):
            for t0 in range(0, LP, TT):
                T = min(TT, LP - t0)
                tok0 = s * LP + t0
                htoks = self.toks_h(tok0, T)
                P.q_a.dma(xt[:, :, :T], self.hT[:, :, tok0:tok0 + T].rearrange("c p t -> p c t"), outs=[xtok], ins=htoks)
                self.rms_scale(xt, xtok, T, lambda c: gain[:, l, c:c + 1], gtok, pss, psstok, sq, sqtok, rstd, rtok)
                def load_w(j):
                    w_, wt_ = win_r.next()
                    P.q_w.dma(w_[:], Wv[:, :, j * 128:(j + 1) * 128], outs=[wt_])
                    return w_, wt_
                nxt = load_w(0)
                for j in range(10):
                    w_, wt_ = nxt
                    if j + 1 < 10:
                        nxt = load_w(j + 1)
                    ps, ptok = ps01.next()
                    for c in range(DC):
                        P.pe.op(lambda: nc.tensor.matmul(ps[:, :T], w_[:, c, :], xt[:, c, :T], start=(c == 0), stop=(c == DC - 1)),
                                outs=[ptok], ins=[wt_, xtok])
                    if j < 4:
                        ub, ubtok = ub_r.next()
                        P.act.op(lambda: nc.scalar.copy(out=ub[:, :T], in_=ps[:, :T]), outs=[ubtok], ins=[ptok])
                        P.q_a.dma(self.uT[j, :, tok0:tok0 + T], ub[:, :T], ins=[ubtok])
                    elif j < 8:
                        P.act.op(lambda: nc.scalar.copy(out=cq[:, j - 4, :T], in_=ps[:, :T]), outs=[cqtok], ins=[ptok])
                    else:
                        P.act.op(lambda: nc.scalar.copy(out=ckv[:, j - 8, :T], in_=ps[:, :T]), outs=[ckvtok], ins=[ptok])
                self.rms_scale(cq, cqtok, T, lambda c: gcq[:, c:c + 1], t_g, pss, psstok, sq, sqtok, rstd, rtok, nchunks=4, dim=512)
                self.rms_scale(ckv, ckvtok, T, lambda c: gckv[:, c:c + 1], t_g, pss, psstok, sq, sqtok, rstd, rtok, nchunks=2, dim=256)
                import os
                LV = int(os.environ.get("DBG_INPROJ", "9"))
                for tb in range(T // 128 if LV >= 2 else 0):
                    tsl = slice(tb * 128, (tb + 1) * 128)
                    jblk = (t0 // 128) + tb
                    tokb = tok0 + tb * 128
                    qflat = q_sb[:].rearrange("p h d -> p (h d)")
                    SUB = os.environ.get("DBG_SUB", "q,kv,kr").split(",")
                    for n in range(3 if "q" in SUB else 0):
                        ps, ptok = P.psb[2 + n]
                        for c in range(4):
                            P.pe.op(lambda: nc.tensor.matmul(ps[:, :], cq[:, c, tsl], wuq[:, c, n * 512:(n + 1) * 512], start=(c == 0), stop=(c == 3)),
                                    outs=[ptok], ins=[cqtok, t_wuq])
                        P.act.op(lambda: nc.scalar.copy(out=qflat[:, n * 512:(n + 1) * 512], in_=ps[:, :]), outs=[qtok], ins=[ptok])
                    for n4 in range(4 if "kv" in SUB else 0):
                        ps, ptok = P.psb[5 + (n4 % 2)]
                        for c in range(2):
                            P.pe.op(lambda: nc.tensor.matmul(ps[:, :], ckv[:, c, tsl], wukv[:, c, n4 * 512:(n4 + 1) * 512], start=(c == 0), stop=(c == 1)),
                                    outs=[ptok], ins=[ckvtok, t_wukv])
                        psv = ps[:, :].rearrange("p (h d) -> p h d", h=2)
                        P.act.op(lambda: nc.scalar.copy(out=k_sb[:, 2 * n4:2 * n4 + 2, 0:128], in_=psv[:, :, 0:128]), outs=[ktok], ins=[ptok])
                        P.dve.op(lambda: nc.vector.tensor_copy(out=v_sb[:, 2 * n4:2 * n4 + 2, :], in_=psv[:, :, 128:256]), outs=[vtok], ins=[ptok, ktok])
                    ps, ptok = ps01.next()
                    for c in range(DC if "kr" in SUB else 0):
                        P.pe.op(lambda: nc.tensor.matmul(ps[:, 0:64], xt[:, c, tsl], wkr[:, c, :], start=(c == 0), stop=(c == DC - 1)),
                                outs=[ptok], ins=[xtok, t_wkr])
                    if "kr" in SUB:
                      P.act.op(lambda: nc.scalar.copy(out=k_sb[:, :, 128:192], in_=ps[:, 0:64].unsqueeze(1).to_broadcast([128, 8, 64])),
                             outs=[ktok], ins=[ptok])
                    if LV >= 3:
                        P.q_a.dma(self.Vd[tokb:tokb + 128, :], v_sb[:].rearrange("p h d -> p (h d)"), ins=[vtok])
                    if LV < 4:
                        continue
                    self.headnorm_rope(q_sb, qtok, gq, t_gq, cosT[:, jblk, :], sinT[:, jblk, :], t_cs, tmp, ttok, ss, sstok, t4, t4tok)
                    self.headnorm_rope(k_sb, ktok, gk, t_gq, cosT[:, jblk, :], sinT[:, jblk, :], t_cs, tmp, ttok, ss, sstok, t4, t4tok)
                    if LV < 5:
                        continue
                    for (src, stok_, dn, dntok, dr, drtok, dst) in ((q_sb, qtok, qTn, qTntok, qTr, qTrtok, self.QT),
                                                                  (k_sb, ktok, kTn, kTntok, kTr, kTrtok, self.KT)):
                        for hg in range(2):
                            ps, ptok = ps01.next()
                            for hl in range(4):
                                h = hg * 4 + hl
                                P.pe.op(lambda: nc.tensor.transpose(ps[:, hl * 128:(hl + 1) * 128], src[:, h, 0:128], self.ident[:]),
                                        outs=[ptok], ins=[stok_, self.t_ident])
                            P.act.op(lambda: nc.scalar.copy(out=dn[:, hg * 4:(hg + 1) * 4, :], in_=ps[:, :].rearrange("p (h t) -> p h t", h=4)),
                                     outs=[dntok], ins=[ptok])
                            ps, ptok = ps01.next()
                            for hl in range(4):
                                h = hg * 4 + hl
                                P.pe.op(lambda: nc.tensor.transpose(ps[0:64, hl * 128:(hl + 1) * 128], src[:, h, 128:192], self.ident[:]),
                                        outs=[ptok], ins=[stok_, self.t_ident])
                            P.dve.op(lambda: nc.vector.tensor_copy(out=dr[:, hg * 4:(hg + 1) * 4, :], in_=ps[0:64, :].rearrange("p (h t) -> p h t", h=4)),
                                     outs=[drtok], ins=[ptok])
                        P.q_a.dma(dst[:, 0:128, tokb:tokb + 128].rearrange("h p t -> p h t"), dn[:], ins=[dntok])
                        P.q_a.dma(dst[:, 128:192, tokb:tokb + 128].rearrange("h p t -> p h t"), dr[:], ins=[drtok])
        P.release(m)

    def even_attn(self, l):
        P = self.P; nc = P.nc
        NS, LP, NT = self.NS, self.LP, self.NT
        m = P.mark()
        maskT = P.sb([128, 128], "maskT"); t_mask = Tok()
        P.q_a.dma(maskT[:], self.c_maskT[:, :], outs=[t_mask])
        sets = Ring([dict(qn=P.sb([128, LP]), qr=P.sb([64, LP]), kn=P.sb([128, LP]), kr=P.sb([64, LP]), v=P.sb([128, NT, 128]), tok=Tok())
                     for _ in range(2)])
        pt_r = Ring([(P.sb([128, 512], "pt"), Tok()) for _ in range(3)])
        rden, rdtok = P.sb([128, 512], "rden"), Tok()
        yo_r = Ring([(P.sb([128, 512], "yo"), Tok()) for _ in range(2)])
        ps_s = self.psring([0, 1, 2])
        ps_o = self.psring([3, 4]); ps_d = self.psring([5, 6])
        scale = 192.0 ** -0.5
        for s in range(NS):
            for h in range(8):
                S = sets.next(); stok = S["tok"]
                c0s = s * LP
                P.q_a.dma(S["qn"][:], self.QT[h, 0:128, c0s:c0s + LP], outs=[stok])
                P.q_a.dma(S["qr"][:], self.QT[h, 128:192, c0s:c0s + LP], outs=[stok])
                P.q_a.dma(S["kn"][:], self.KT[h, 0:128, c0s:c0s + LP], outs=[stok])
                P.q_a.dma(S["kr"][:], self.KT[h, 128:192, c0s:c0s + LP], outs=[stok])
                P.q_a.dma(S["v"][:], self.Vd[c0s:c0s + LP, h * 128:(h + 1) * 128].rearrange("(b p) d -> p b d", p=128), outs=[stok])
                for q0 in range(0, LP, 512):
                    Tq = min(512, LP - q0)
                    nkb = (q0 + Tq) // 128
                    po, potok = ps_o.next(); pd, pdtok = ps_d.next()
                    for kb in range(nkb):
                        c0 = max(q0, kb * 128); off = c0 - q0; w = q0 + Tq - c0
                        ps, pstok = ps_s.next()
                        ksl = slice(kb * 128, (kb + 1) * 128)
                        P.pe.op(lambda: nc.tensor.matmul(ps[:, :w], S["kn"][:, ksl], S["qn"][:, c0:c0 + w], start=True, stop=False),
                                outs=[pstok], ins=[stok])
                        P.pe.op(lambda: nc.tensor.matmul(ps[:, :w], S["kr"][:, ksl], S["qr"][:, c0:c0 + w], start=False, stop=True),
                                outs=[pstok], ins=[stok])
                        pt, pttok = pt_r.next()
                        P.act.op(lambda: nc.scalar.activation(out=pt[:, :w], in_=ps[:, :w], func=AF.Exp, scale=scale), outs=[pttok], ins=[pstok])
                        if kb * 128 >= q0:
                            P.dve.op(lambda: nc.vector.tensor_tensor(out=pt[:, 0:128], in0=pt[:, 0:128], in1=maskT[:], op=ALU.mult),
                                     outs=[pttok], ins=[pttok, t_mask])
                        P.pe.op(lambda: nc.tensor.matmul(po[:, off:off + w], S["v"][:, kb, :], pt[:, :w], start=(kb == 0), stop=(kb == nkb - 1)),
                                outs=[potok], ins=[stok, pttok])
                        P.pe.op(lambda: nc.tensor.matmul(pd[:, off:off + w], self.ones[:], pt[:, :w], start=(kb == 0), stop=(kb == nkb - 1)),
                                outs=[pdtok], ins=[self.t_ones, pttok])
                    P.dve.op(lambda: nc.vector.reciprocal(out=rden[:, :Tq], in_=pd[:, :Tq]), outs=[rdtok], ins=[pdtok])
                    yo, yotok = yo_r.next()
                    P.dve.op(lambda: nc.vector.tensor_tensor(out=yo[:, :Tq], in0=po[:, :Tq], in1=rden[:, :Tq], op=ALU.mult),
                             outs=[yotok], ins=[potok, rdtok])
                    P.q_a.dma(self.yE[4 + h, :, c0s + q0:c0s + q0 + Tq], yo[:, :Tq], ins=[yotok])
        P.release(m)
    def even_s5(self, l):
        P = self.P; nc = P.nc; i = l // 2
        NS, LP, NT = self.NS, self.LP, self.NT
        m = P.mark()
        PI = math.pi
        dve = P.dve; act = P.act; pe = P.pe
        V = nc.vector
        tk = Tok("s5setup")
        def sbt(shape, name):
            return P.sb(shape, name)
        ps0, ps0tok = P.psb[0]
        raw = sbt([16, 3, 128], "s5raw")
        P.q_a.dma(raw[:, 0, :], self.w["s5_a_re"][i].rearrange("(gp gl) p -> gp (gl p)", gl=2), outs=[tk])
        P.q_a.dma(raw[:, 1, :], self.w["s5_a_im"][i].rearrange("(gp gl) p -> gp (gl p)", gl=2), outs=[tk])
        ldt = sbt([16, 2], "ldt")
        P.q_a.dma(ldt[:], self.w["s5_log_dt"][i].rearrange("(gp gl) -> gp gl", gl=2), outs=[tk])
        dve.op(lambda: V.tensor_copy(out=raw[:, 2, :].rearrange("g (a b) -> g a b", a=2), in_=ldt[:].unsqueeze(2).to_broadcast([16, 2, 64])),
               outs=[tk], ins=[tk])
        ar = sbt([128, 16], "ar"); ai = sbt([128, 16], "ai"); dt = sbt([128, 16], "dt")
        for k, dst in enumerate((ar, ai, dt)):
            pe.op(lambda: nc.tensor.transpose(ps0[:, k * 16:(k + 1) * 16], raw[:, k, :], self.ident[0:16, 0:16]), outs=[ps0tok], ins=[tk, self.t_ident])
        act.op(lambda: nc.scalar.copy(out=ar[:], in_=ps0[:, 0:16]), outs=[tk], ins=[ps0tok])
        act.op(lambda: nc.scalar.copy(out=ai[:], in_=ps0[:, 16:32]), outs=[tk], ins=[ps0tok])
        act.op(lambda: nc.scalar.activation(out=dt[:], in_=ps0[:, 32:48], func=AF.Exp), outs=[tk], ins=[ps0tok])
        def T16(name):
            return sbt([128, 16], name)
        def tt(out, a, b, op):
            dve.op(lambda: V.tensor_tensor(out=out, in0=a, in1=b, op=op), outs=[tk], ins=[tk])
        def ts(out, a, s1, op0, s2=None, op1=None):
            if op1 is None:
                dve.op(lambda: V.tensor_scalar(out, a, s1, None, op0), outs=[tk], ins=[tk])
            else:
                dve.op(lambda: V.tensor_scalar(out, a, s1, s2, op0, op1), outs=[tk], ins=[tk])
        mag = T16("mag"); ang = T16("ang"); sn = T16("sn"); cs = T16("cs"); tmpa = T16("tmpa"); tmpb = T16("tmpb")
        twopi = T16("twopi")
        dve.op(lambda: V.memset(twopi[:], 2 * PI), outs=[tk], ins=[tk])
        tt(tmpa[:], dt[:], ar[:], ALU.mult)
        act.op(lambda: nc.scalar.activation(out=mag[:], in_=tmpa[:], func=AF.Exp), outs=[tk], ins=[tk])
        tt(ang[:], dt[:], ai[:], ALU.mult)
        for (dst, shift) in ((sn, PI), (cs, PI + PI / 2)):
            ts(tmpa[:], ang[:], shift, ALU.add)
            for mult_ in (16.0, 8.0, 4.0, 2.0, 1.0):
                ts(tmpb[:], tmpa[:], mult_ * 2 * PI, ALU.is_ge, -mult_ * 2 * PI, ALU.mult)
                tt(tmpa[:], tmpa[:], tmpb[:], ALU.add)
            ts(tmpa[:], tmpa[:], -PI, ALU.add, 3.1415925, ALU.min)
            ts(tmpa[:], tmpa[:], -3.1415925, ALU.max)
            act.op(lambda: nc.scalar.activation(out=dst[:], in_=tmpa[:], func=AF.Sin), outs=[tk], ins=[tk])
        lr = T16("lr"); li = T16("li"); lr1 = T16("lr1"); den = T16("den"); cr = T16("cr"); ci = T16("ci")
        tt(lr[:], mag[:], cs[:], ALU.mult); tt(li[:], mag[:], sn[:], ALU.mult)
        ts(lr1[:], lr[:], -1.0, ALU.add)
        tt(den[:], ar[:], ar[:], ALU.mult); tt(tmpa[:], ai[:], ai[:], ALU.mult); tt(den[:], den[:], tmpa[:], ALU.add)
        dve.op(lambda: V.reciprocal(out=den[:], in_=den[:]), outs=[tk], ins=[tk])
        tt(cr[:], lr1[:], ar[:], ALU.mult); tt(tmpa[:], li[:], ai[:], ALU.mult); tt(cr[:], cr[:], tmpa[:], ALU.add); tt(cr[:], cr[:], den[:], ALU.mult)
        tt(ci[:], li[:], ar[:], ALU.mult); tt(tmpa[:], lr1[:], ai[:], ALU.mult); tt(ci[:], ci[:], tmpa[:], ALU.subtract); tt(ci[:], ci[:], den[:], ALU.mult)
        br = sbt([128, 16, 16], "br"); bi = sbt([128, 16, 16], "bi")
        P.q_a.dma(br[:], self.w["s5_b_re"][i].rearrange("(gp gl) p c -> (gl p) gp c", gl=2), outs=[tk], ins=[tk])
        P.q_a.dma(bi[:], self.w["s5_b_im"][i].rearrange("(gp gl) p c -> (gl p) gp c", gl=2), outs=[tk], ins=[tk])
        t3a = sbt([128, 16, 16], "t3a"); t3b = sbt([128, 16, 16], "t3b")
        Bblk = [sbt([128, 16, 32], "Bblk_r"), sbt([128, 16, 32], "Bblk_i")]
        crb = cr[:].unsqueeze(2).to_broadcast([128, 16, 16]); cib = ci[:].unsqueeze(2).to_broadcast([128, 16, 16])
        for k in range(2):
            dve.op(lambda: V.memset(Bblk[k][:], 0.0), outs=[tk], ins=[tk])
        for k, (x0, x1, op) in enumerate(((br, bi, ALU.subtract), (bi, br, ALU.add))):
            tt(t3a[:], x0[:], crb, ALU.mult); tt(t3b[:], x1[:], cib, ALU.mult); tt(t3a[:], t3a[:], t3b[:], op)
            dve.op(lambda: V.tensor_copy(out=Bblk[k][0:64, :, 0:16], in_=t3a[0:64]), outs=[tk], ins=[tk])
            dve.op(lambda: V.tensor_copy(out=Bblk[k][64:128, :, 16:32], in_=t3a[64:128]), outs=[tk], ins=[tk])
        WbT = [sbt([32, 16, 128], "WbT_r"), sbt([32, 16, 128], "WbT_i")]
        for k in range(2):
            for g4 in range(4):
                for gq in range(4):
                    gp = g4 * 4 + gq
                    pe.op(lambda: nc.tensor.transpose(ps0[0:32, gq * 128:(gq + 1) * 128], Bblk[k][:, gp, :], self.ident[:]), outs=[ps0tok], ins=[tk, self.t_ident])
                act.op(lambda: nc.scalar.copy(out=WbT[k][:, g4 * 4:(g4 + 1) * 4, :], in_=ps0[0:32, :].rearrange("p (g s) -> p g s", g=4)), outs=[tk], ins=[ps0tok])
        Cblk = [sbt([128, 16, 32], "Cblk_r"), sbt([128, 16, 32], "Cblk_i")]
        craw = sbt([128, 128], "craw")
        for k, nm in enumerate(("s5_c_re", "s5_c_im")):
            dve.op(lambda: V.memset(Cblk[k][:], 0.0), outs=[tk], ins=[tk])
            cv = self.w[nm][i].rearrange("g c p -> (g c) p")
            for q4 in range(4):
                P.q_a.dma(craw[:, 0:64], cv[q4 * 128:(q4 + 1) * 128, :], outs=[tk], ins=[tk])
                P.q_a.dma(craw[:, 64:128], cv[q4 * 128:(q4 + 1) * 128, :], outs=[tk], ins=[tk])
                pe.op(lambda: nc.tensor.transpose(ps0[:, 0:128], craw[:], self.ident[:]), outs=[ps0tok], ins=[tk, self.t_ident])
                psv = ps0[:, 0:128].rearrange("p (gq gl c) -> p gq gl c", gq=4, gl=2)
                sc = 1.0 if k == 0 else -1.0
                act.op(lambda: nc.scalar.mul(out=Cblk[k][0:64, q4 * 4:(q4 + 1) * 4, 0:16], in_=psv[0:64, :, 0, :], mul=sc), outs=[tk], ins=[ps0tok])
                act.op(lambda: nc.scalar.mul(out=Cblk[k][64:128, q4 * 4:(q4 + 1) * 4, 16:32], in_=psv[64:128, :, 1, :], mul=sc), outs=[tk], ins=[ps0tok])
        dsk = sbt([32, 16], "dsk")
        P.q_a.dma(dsk[:], self.w["s5_d"][i].rearrange("(gp r) -> r gp", r=32), outs=[tk], ins=[tk], allow_slow_non_contiguous=True)
        NK = max(1, (LP - 1).bit_length())
        wr = sbt([128, NK, 16], "wr"); wi = sbt([128, NK, 16], "wi")
        dve.op(lambda: V.tensor_copy(out=wr[:, 0, :], in_=cs[:]), outs=[tk], ins=[tk])
        ts(wi[:, 0, :], sn[:], -1.0, ALU.mult)
        for k in range(1, NK):
            tt(tmpa[:], wr[:, k - 1, :], wr[:, k - 1, :], ALU.mult); tt(tmpb[:], wi[:, k - 1, :], wi[:, k - 1, :], ALU.mult)
            tt(wr[:, k, :], tmpa[:], tmpb[:], ALU.subtract)
            tt(tmpa[:], wr[:, k - 1, :], wi[:, k - 1, :], ALU.mult)
            ts(wi[:, k, :], tmpa[:], 2.0, ALU.mult)
        Er = sbt([128, LP], "Er"); Ei = sbt([128, LP], "Ei"); tmpE = sbt([128, LP], "tmpE"); tE = Tok("E")
        u_r = Ring([(sbt([32, LP], "u_gp"), Tok()) for _ in range(2)])
        vr = sbt([128, LP], "vr"); vi = sbt([128, LP], "vi"); tv = Tok("v")
        zr = sbt([128, LP], "zr"); zi = sbt([128, LP], "zi"); tz = Tok("z")
        ta = sbt([128, 512], "ta"); tb_ = sbt([128, 512], "tb"); tta = Tok(); ttb = Tok()
        y_r = Ring([(sbt([32, LP], "y_gp"), Tok()) for _ in range(2)])
        ps_b = self.psring([1, 2, 3, 4]); ps_y = self.psring([5, 6])
        for gp in range(16):
            dve.op(lambda: V.memset(Er[:, 0:1], 1.0), outs=[tE], ins=[tk])
            dve.op(lambda: V.memset(Ei[:, 0:1], 0.0), outs=[tE], ins=[tk])
            n = 1; k = 0
            while n < LP:
                cnt = min(n, LP - n)
                wrk = wr[:, k, gp:gp + 1]; wik = wi[:, k, gp:gp + 1]
                dve.op(lambda: V.tensor_scalar(tmpE[:, 0:cnt], Ei[:, 0:cnt], wik, None, ALU.mult), outs=[tE], ins=[tE, tk])
                dve.op(lambda: V.scalar_tensor_tensor(out=Er[:, n:n + cnt], in0=Er[:, 0:cnt], scalar=wrk, in1=tmpE[:, 0:cnt], op0=ALU.mult, op1=ALU.subtract), outs=[tE], ins=[tE, tk])
                dve.op(lambda: V.tensor_scalar(tmpE[:, 0:cnt], Ei[:, 0:cnt], wrk, None, ALU.mult), outs=[tE], ins=[tE, tk])
                dve.op(lambda: V.scalar_tensor_tensor(out=Ei[:, n:n + cnt], in0=Er[:, 0:cnt], scalar=wik, in1=tmpE[:, 0:cnt], op0=ALU.mult, op1=ALU.add), outs=[tE], ins=[tE, tk])
                n += cnt; k += 1
            for s in range(NS):
                c0s = s * LP
                u, utok = u_r.next()
                P.q_a.dma(u[:], self.uT[gp // 4, (gp % 4) * 32:(gp % 4) * 32 + 32, c0s:c0s + LP], outs=[utok])
                for q0 in range(0, LP, 512):
                    Tq = min(512, LP - q0); sl = slice(q0, q0 + Tq)
                    pbr, pbrt = ps_b.next(); pbi, pbit = ps_b.next()
                    pe.op(lambda: nc.tensor.matmul(pbr[:, :Tq], WbT[0][:, gp, :], u[:, sl], start=True, stop=True), outs=[pbrt], ins=[tk, utok])
                    pe.op(lambda: nc.tensor.matmul(pbi[:, :Tq], WbT[1][:, gp, :], u[:, sl], start=True, stop=True), outs=[pbit], ins=[tk, utok])
                    dve.op(lambda: V.tensor_tensor(out=ta[:, :Tq], in0=pbr[:, :Tq], in1=Er[:, sl], op=ALU.mult), outs=[tta], ins=[pbrt, tE])
                    dve.op(lambda: V.tensor_tensor(out=tb_[:, :Tq], in0=pbi[:, :Tq], in1=Ei[:, sl], op=ALU.mult), outs=[ttb], ins=[pbit, tE])
                    dve.op(lambda: V.tensor_tensor(out=vr[:, sl], in0=ta[:, :Tq], in1=tb_[:, :Tq], op=ALU.subtract), outs=[tv], ins=[tta, ttb])
                    dve.op(lambda: V.tensor_tensor(out=ta[:, :Tq], in0=pbi[:, :Tq], in1=Er[:, sl], op=ALU.mult), outs=[tta], ins=[pbit, tE])
                    dve.op(lambda: V.tensor_tensor(out=tb_[:, :Tq], in0=pbr[:, :Tq], in1=Ei[:, sl], op=ALU.mult), outs=[ttb], ins=[pbrt, tE])
                    dve.op(lambda: V.tensor_tensor(out=vi[:, sl], in0=ta[:, :Tq], in1=tb_[:, :Tq], op=ALU.add), outs=[tv], ins=[tta, ttb])
                rho = mag[:, gp:gp + 1].to_broadcast([128, LP])
                dve.op(lambda: V.tensor_tensor_scan(out=zr[:], data0=rho, data1=vr[:], initial=0.0, op0=ALU.mult, op1=ALU.add), outs=[tz], ins=[tv, tk])
                dve.op(lambda: V.tensor_tensor_scan(out=zi[:], data0=rho, data1=vi[:], initial=0.0, op0=ALU.mult, op1=ALU.add), outs=[tz], ins=[tv, tk])
                dve.op(lambda: V.tensor_tensor(out=vr[:], in0=Er[:], in1=zr[:], op=ALU.mult), outs=[tv], ins=[tz, tE])
                dve.op(lambda: V.tensor_tensor(out=tmpE[:], in0=Ei[:], in1=zi[:], op=ALU.mult), outs=[tv], ins=[tz, tE])
                dve.op(lambda: V.tensor_tensor(out=vr[:], in0=vr[:], in1=tmpE[:], op=ALU.add), outs=[tv], ins=[tv])
                dve.op(lambda: V.tensor_tensor(out=vi[:], in0=Er[:], in1=zi[:], op=ALU.mult), outs=[tv], ins=[tz, tE])
                dve.op(lambda: V.tensor_tensor(out=tmpE[:], in0=Ei[:], in1=zr[:], op=ALU.mult), outs=[tv], ins=[tz, tE])
                dve.op(lambda: V.tensor_tensor(out=vi[:], in0=vi[:], in1=tmpE[:], op=ALU.subtract), outs=[tv], ins=[tv])
                y, ytok = y_r.next()
                for q0 in range(0, LP, 512):
                    Tq = min(512, LP - q0); sl = slice(q0, q0 + Tq)
                    py, pyt = ps_y.next()
                    pe.op(lambda: nc.tensor.matmul(py[0:32, :Tq], Cblk[0][:, gp, :], vr[:, sl], start=True, stop=False), outs=[pyt], ins=[tk, tv])
                    pe.op(lambda: nc.tensor.matmul(py[0:32, :Tq], Cblk[1][:, gp, :], vi[:, sl], start=False, stop=True), outs=[pyt], ins=[tk, tv])
                    dve.op(lambda: V.scalar_tensor_tensor(out=y[:, sl], in0=u[:, sl], scalar=dsk[:, gp:gp + 1], in1=py[0:32, :Tq], op0=ALU.mult, op1=ALU.add),
                           outs=[ytok], ins=[pyt, utok, tk])
                P.q_a.dma(self.yE[gp // 4, (gp % 4) * 32:(gp % 4) * 32 + 32, c0s:c0s + LP], y[:], ins=[ytok])
        P.release(m)

    def out_proj(self, Wd, NCH, ymix, ytoks_fn, T, tok0, htoks, w_r, psd_r, hres_r):
        P = self.P; nc = P.nc
        Wv = Wd.rearrange("(c p) d -> p c d", p=128)
        def load_d(dc):
            d_, dt_ = w_r.next()
            P.q_w.dma(d_[:, :NCH, :], Wv[:, :, dc * 128:(dc + 1) * 128], outs=[dt_])
            return d_, dt_
        nxt = load_d(0)
        for dc in range(DC):
            d_, dt_ = nxt
            if dc + 1 < DC:
                nxt = load_d(dc + 1)
            hres, hrt = hres_r.next()
            P.q_a.dma(hres[:, :T], self.hT[dc, :, tok0:tok0 + T], outs=[hrt], ins=htoks)
            psd, psdt = psd_r.next()
            for c in range(NCH):
                P.pe.op(lambda: nc.tensor.matmul(psd[:, :T], d_[:, c, :], ymix[:, c, :T], start=(c == 0), stop=(c == NCH - 1)),
                        outs=[psdt], ins=[dt_] + ytoks_fn(c))
            P.dve.op(lambda: nc.vector.tensor_tensor(out=hres[:, :T], in0=psd[:, :T], in1=hres[:, :T], op=ALU.add),
                     outs=[hrt], ins=[psdt, hrt])
            P.q_a.dma(self.hT[dc, :, tok0:tok0 + T], hres[:, :T], ins=[hrt], outs=htoks)

    def even_out(self, l):
        P = self.P; nc = P.nc; i = l // 2
        NS, LP, NT, NTOK = self.NS, self.LP, self.NT, self.NTOK
        m = P.mark()
        TT = 512
        V = nc.vector
        wglu = P.sb([128, 4, 512], "wglu"); t_wglu = Tok()
        P.q_w.dma(wglu[:], self.w["s5_w_glu"][i].rearrange("(c p) f -> p c f", p=128), outs=[t_wglu])
        ya, yatok = P.sb([128, 4, TT], "ya"), Tok()
        yg, ygtok = P.sb([128, 4, TT], "yg"), Tok()
        ymix = P.sb([128, 12, TT], "ymix"); ymtok = Tok(); ybtok = Tok()
        sg_r = Ring([(P.sb([128, TT], "sg"), Tok()) for _ in range(2)])
        w_r = Ring([(P.sb([128, 12, 128], "wo"), Tok()) for _ in range(2)])
        hres_r = Ring([(P.sb([128, TT], "hres"), Tok()) for _ in range(2)])
        psg_r = self.psring([0, 1]); psd_r = self.psring([2, 3])
        C0 = math.sqrt(2.0 / math.pi)
        for t0 in range(0, NTOK, TT):
            T = min(TT, NTOK - t0)
            htoks = self.toks_h(t0, T)
            P.q_a.dma(ya[:, :, :T], self.yE[0:4, :, t0:t0 + T].rearrange("c p t -> p c t"), outs=[yatok])
            P.q_a.dma(ymix[:, 4:12, :T], self.yE[4:12, :, t0:t0 + T].rearrange("c p t -> p c t"), outs=[ybtok])
            P.dve.op(lambda: V.tensor_tensor(out=yg[:, :, :T], in0=ya[:, :, :T], in1=ya[:, :, :T], op=ALU.mult), outs=[ygtok], ins=[yatok])
            P.dve.op(lambda: V.tensor_scalar(yg[:, :, :T], yg[:, :, :T], 0.044715, 1.0, ALU.mult, ALU.add), outs=[ygtok], ins=[ygtok])
            P.dve.op(lambda: V.tensor_tensor(out=yg[:, :, :T], in0=yg[:, :, :T], in1=ya[:, :, :T], op=ALU.mult), outs=[ygtok], ins=[ygtok, yatok])
            P.act.op(lambda: nc.scalar.activation(out=yg[:, :, :T], in_=yg[:, :, :T], func=AF.Tanh, scale=C0), outs=[ygtok], ins=[ygtok])
            P.dve.op(lambda: V.tensor_scalar(yg[:, :, :T], yg[:, :, :T], 1.0, 0.5, ALU.add, ALU.mult), outs=[ygtok], ins=[ygtok])
            P.dve.op(lambda: V.tensor_tensor(out=yg[:, :, :T], in0=yg[:, :, :T], in1=ya[:, :, :T], op=ALU.mult), outs=[ygtok], ins=[ygtok, yatok])
            for jc in range(4):
                ps, ptok = psg_r.next()
                for c in range(4):
                    P.pe.op(lambda: nc.tensor.matmul(ps[:, :T], wglu[:, c, jc * 128:(jc + 1) * 128], yg[:, c, :T], start=(c == 0), stop=(c == 3)),
                            outs=[ptok], ins=[t_wglu, ygtok])
                sg, sgt = sg_r.next()
                P.act.op(lambda: nc.scalar.activation(out=sg[:, :T], in_=ps[:, :T], func=AF.Sigmoid), outs=[sgt], ins=[ptok])
                P.dve.op(lambda: V.tensor_tensor(out=ymix[:, jc, :T], in0=yg[:, jc, :T], in1=sg[:, :T], op=ALU.mult), outs=[ymtok], ins=[ygtok, sgt])
            self.out_proj(self.w["ev_w_out"][i], 12, ymix, lambda c: [ymtok, ybtok], T, t0, htoks, w_r, psd_r, hres_r)
        P.release(m)


    def odd_mixer(self, l):
        import os
        st = os.environ.get("DBG_STAGES", "inproj,attn,gdn,out").split(",")
        if "inproj" in st:
            self.odd_inproj(l)
        if "attn" in st:
            self.odd_sb(l)
        if "gdn" in st:
            self.odd_gdn(l)
        if "out" in st:
            self.odd_out(l)

    def odd_inproj(self, l):
        P = self.P; nc = P.nc; i = l // 2
        NS, LP, NT = self.NS, self.LP, self.NT
        m = P.mark()
        TT = 512
        Wv = self.w["od_w_in"][i].rearrange("(c p) f -> p c f", p=128)
        gain, gtok = self.gains["norm_mix"]
        xt, xtok = P.sb([128, DC, TT], "odx"), Tok()
        win_r = Ring([(P.sb([128, DC, 128], "win"), Tok()) for _ in range(3)])
        wv_r = Ring([(P.sb([128, DC, 512], "wsv"), Tok()) for _ in range(1)])
        wab = P.sb([128, DC, 16], "wab"); t_wab = Tok()
        P.q_w.dma(wab[:], Wv[:, :, 6144:6160], outs=[t_wab])
        ob_r = Ring([(P.sb([128, TT], "ob"), Tok()) for _ in range(3)])
        rstd, rtok = P.sb([128, TT], "rstd"), Tok()
        sq, sqtok = P.sb([128, TT], "sq"), Tok()
        ps01 = self.psring([0, 1, 2]); psv_r = self.psring([3, 4])
        pss, psstok = P.psb[7]
        plan = []
        for h in range(8):
            plan.append((self.sbQ, h, h * 128))
        for h in range(8):
            plan.append((self.sbK, h, 1024 + h * 128))
        for c in range(24):
            plan.append((self.gT, c, 3072 + c * 128))
        for h in range(8):
            plan.append((self.gzT, h, 6160 + h * 128))
        for s in range(NS):
            for t0 in range(0, LP, TT):
                T = min(TT, LP - t0)
                tok0 = s * LP + t0
                htoks = self.toks_h(tok0, T)
                P.q_a.dma(xt[:, :, :T], self.hT[:, :, tok0:tok0 + T].rearrange("c p t -> p c t"), outs=[xtok], ins=htoks)
                self.rms_scale(xt, xtok, T, lambda c: gain[:, l, c:c + 1], gtok, pss, psstok, sq, sqtok, rstd, rtok)
                def load_w(j):
                    w_, wt_ = win_r.next()
                    c0 = plan[j][2]
                    P.q_w.dma(w_[:], Wv[:, :, c0:c0 + 128], outs=[wt_])
                    return w_, wt_
                pend = [load_w(0), load_w(1)]
                for j in range(len(plan)):
                    w_, wt_ = pend.pop(0)
                    if j + 2 < len(plan):
                        pend.append(load_w(j + 2))
                    ps, ptok = ps01.next()
                    for c in range(DC):
                        P.pe.op(lambda: nc.tensor.matmul(ps[:, :T], w_[:, c, :], xt[:, c, :T], start=(c == 0), stop=(c == DC - 1)),
                                outs=[ptok], ins=[wt_, xtok])
                    ob, obtok = ob_r.next()
                    if j % 2 == 0:
                        P.act.op(lambda: nc.scalar.copy(out=ob[:, :T], in_=ps[:, :T]), outs=[obtok], ins=[ptok])
                    else:
                        P.dve.op(lambda: nc.vector.tensor_copy(out=ob[:, :T], in_=ps[:, :T]), outs=[obtok], ins=[ptok])
                    P.q_a.dma(plan[j][0][plan[j][1], :, tok0:tok0 + T], ob[:, :T], ins=[obtok])
                ps, ptok = ps01.next()
                for c in range(DC):
                    P.pe.op(lambda: nc.tensor.matmul(ps[0:16, :T], wab[:, c, :], xt[:, c, :T], start=(c == 0), stop=(c == DC - 1)),
                            outs=[ptok], ins=[t_wab, xtok])
                ob, obtok = ob_r.next()
                P.act.op(lambda: nc.scalar.copy(out=ob[0:16, :T], in_=ps[0:16, :T]), outs=[obtok], ins=[ptok])
                P.q_a.dma(self.gabT[:, tok0:tok0 + T], ob[0:16, :T], ins=[obtok])
                for n in range(2):
                    wv, wvt = wv_r.next()
                    P.q_w.dma(wv[:], Wv[:, :, 2048 + n * 512:2048 + (n + 1) * 512], outs=[wvt])
                    for tb in range(T // 128):
                        ps, ptok = psv_r.next()
                        for c in range(DC):
                            P.pe.op(lambda: nc.tensor.matmul(ps[:, :], xt[:, c, tb * 128:(tb + 1) * 128], wv[:, c, :], start=(c == 0), stop=(c == DC - 1)),
                                    outs=[ptok], ins=[wvt, xtok])
                        ob, obtok = ob_r.next()
                        P.act.op(lambda: nc.scalar.copy(out=ob[:, :], in_=ps[:, :]), outs=[obtok], ins=[ptok])
                        P.q_a.dma(self.sbV[tok0 + tb * 128:tok0 + (tb + 1) * 128, n * 512:(n + 1) * 512], ob[:, :], ins=[obtok])
        P.release(m)

    def load_const(self, dram_ap, shape, name):
        t = self.P.sb(shape, name); tok = Tok(name)
        self.P.q_a.dma(t[:], dram_ap, outs=[tok])
        return t, tok

    def odd_sb(self, l):
        P = self.P; nc = P.nc
        NS, LP, NT = self.NS, self.LP, self.NT
        m = P.mark()
        V = nc.vector
        maskS, t_ms = self.load_const(self.c_maskS[:, :], [128, 128], "maskS")
        Uincl, t_ui = self.load_const(self.c_maskL[:, :], [128, 128], "Uincl")
        zeros = P.sb([128, 128], "zeros"); t_z = Tok()
        P.dve.op(lambda: V.memset(zeros[:], 0.0), outs=[t_z])
        sets = Ring([dict(q=P.sb([128, LP]), k=P.sb([128, LP]), nk=P.sb([128, LP]), v=P.sb([128, NT, 128]), tok=Tok()) for _ in range(2)])
        ez_r = Ring([(P.sb([128, 512], "ez"), Tok()) for _ in range(2)])
        sp_r = Ring([(P.sb([128, 512], "sp"), Tok()) for _ in range(2)])
        wt_r = Ring([(P.sb([128, 512], "wt"), Tok()) for _ in range(2)])
        A, Atok = P.sb([128, 512], "A"), Tok()
        yo_r = Ring([(P.sb([128, 512], "yo"), Tok()) for _ in range(2)])
        ps_z = self.psring([0, 1]); ps_e = self.psring([2, 3]); ps_o = self.psring([4, 5])
        scale = 128.0 ** -0.5
        for s in range(NS):
            for h in range(8):
                S = sets.next(); stok = S["tok"]
                c0s = s * LP
                P.q_a.dma(S["q"][:], self.sbQ[h, :, c0s:c0s + LP], outs=[stok])
                P.q_a.dma(S["k"][:], self.sbK[h, :, c0s:c0s + LP], outs=[stok])
                P.q_a.dma(S["v"][:], self.sbV[c0s:c0s + LP, h * 128:(h + 1) * 128].rearrange("(b p) d -> p b d", p=128), outs=[stok])
                P.act.op(lambda: nc.scalar.mul(out=S["q"][:], in_=S["q"][:], mul=scale), outs=[stok], ins=[stok])
                P.act.op(lambda: nc.scalar.mul(out=S["nk"][:], in_=S["k"][:], mul=-1.0), outs=[stok], ins=[stok])
                for q0 in range(0, LP, 512):
                    Tq = min(512, LP - q0)
                    nkb = (q0 + Tq) // 128
                    po, potok = ps_o.next()
                    P.pe.op(lambda: nc.tensor.matmul(po[:, :Tq], zeros[:], S["q"][:, q0:q0 + Tq], start=True, stop=False),
                            outs=[potok], ins=[t_z, stok])
                    P.dve.op(lambda: V.memset(A[:, :Tq], 0.0), outs=[Atok])
                    for kb in range(nkb - 1, -1, -1):
                        c0 = max(q0, kb * 128); off = c0 - q0; w = q0 + Tq - c0
                        diag = kb * 128 >= q0
                        ksl = slice(kb * 128, (kb + 1) * 128)
                        pz, pztok = ps_z.next()
                        P.pe.op(lambda: nc.tensor.matmul(pz[:, :w], S["k"][:, ksl], S["q"][:, c0:c0 + w], start=True, stop=True),
                                outs=[pztok], ins=[stok])
                        ez, eztok = ez_r.next()
                        P.act.op(lambda: nc.scalar.activation(out=ez[:, :w], in_=pz[:, :w], func=AF.Exp), outs=[eztok], ins=[pztok])
                        sp, sptok = sp_r.next()
                        P.act.op(lambda: nc.scalar.activation(out=sp[:, :w], in_=ez[:, :w], func=AF.Ln, bias=self.ones[:, 0:1]),
                                 outs=[sptok], ins=[eztok, self.t_ones])
                        if diag:
                            P.dve.op(lambda: V.tensor_tensor(out=sp[:, 0:128], in0=sp[:, 0:128], in1=maskS[:], op=ALU.mult),
                                     outs=[sptok], ins=[sptok, t_ms])
                        pe_, petok = ps_e.next()
                        P.pe.op(lambda: nc.tensor.matmul(pe_[:, :w], Uincl[:], sp[:, :w], start=True, stop=False), outs=[petok], ins=[t_ui, sptok])
                        if kb < nkb - 1:
                            P.pe.op(lambda: nc.tensor.matmul(pe_[:, :w], self.ones[:], A[:, off:off + w], start=False, stop=False),
                                    outs=[petok], ins=[self.t_ones, Atok])
                        P.pe.op(lambda: nc.tensor.matmul(pe_[:, :w], S["nk"][:, ksl], S["q"][:, c0:c0 + w], start=False, stop=True),
                                outs=[petok], ins=[stok])
                        wt, wttok = wt_r.next()
                        P.act.op(lambda: nc.scalar.activation(out=wt[:, :w], in_=pe_[:, :w], func=AF.Exp, scale=-1.0), outs=[wttok], ins=[petok])
                        if diag:
                            P.dve.op(lambda: V.tensor_tensor(out=wt[:, 0:128], in0=wt[:, 0:128], in1=maskS[:], op=ALU.mult),
                                     outs=[wttok], ins=[wttok, t_ms])
                        P.pe.op(lambda: nc.tensor.matmul(po[:, off:off + w], S["v"][:, kb, :], wt[:, :w], start=False, stop=(kb == 0)),
                                outs=[potok], ins=[stok, wttok])
                        if kb > 0:
                            P.dve.op(lambda: V.tensor_tensor(out=A[:, off:off + w], in0=A[:, off:off + w], in1=sp[:, :w], op=ALU.add),
                                     outs=[Atok], ins=[Atok, sptok])
                    yo, yotok = yo_r.next()
                    P.act.op(lambda: nc.scalar.copy(out=yo[:, :Tq], in_=po[:, :Tq]), outs=[yotok], ins=[potok])
                    P.q_a.dma(self.yO[h, :, c0s + q0:c0s + q0 + Tq], yo[:, :Tq], ins=[yotok])
        P.release(m)

    def odd_out(self, l):
        P = self.P; nc = P.nc; i = l // 2
        NTOK = self.NTOK
        m = P.mark()
        TT = 512
        ymix = P.sb([128, 16, TT], "ymix"); ytok = Tok()
        w_r = Ring([(P.sb([128, 16, 128], "wo"), Tok()) for _ in range(2)])
        hres_r = Ring([(P.sb([128, TT], "hres"), Tok()) for _ in range(2)])
        psd_r = self.psring([2, 3])
        for t0 in range(0, NTOK, TT):
            T = min(TT, NTOK - t0)
            htoks = self.toks_h(t0, T)
            P.q_a.dma(ymix[:, :, :T], self.yO[:, :, t0:t0 + T].rearrange("c p t -> p c t"), outs=[ytok])
            self.out_proj(self.w["od_w_out"][i], 16, ymix, lambda c: [ytok], T, t0, htoks, w_r, psd_r, hres_r)
        P.release(m)
    def odd_gdn(self, l):
        P = self.P; nc = P.nc; i = l // 2
        NS, LP, NT = self.NS, self.LP, self.NT
        m = P.mark()
        V = nc.vector
        dve = P.dve; act = P.act; pe = P.pe
        maskI, t_mi = self.load_const(self.c_maskT[:, :], [128, 128], "maskI")
        maskSt, t_mst = self.load_const(self.c_maskS[:, :], [128, 128], "maskSt")
        sel, t_sel = self.load_const(self.c_sel[:, :], [8, 1024], "sel")
        tk = Tok("gdnsetup")
        cw = P.sb([128, 24, 4], "cw")
        for j in range(4):
            P.q_a.dma(cw[:, :, j], self.w["gdn_conv"][i, j].rearrange("(c p) -> p c", p=128), outs=[tk], allow_slow_non_contiguous=True)
        nA = P.sb([8, 1], "nA"); dtb = P.sb([8, 1], "dtb"); gout = P.sb([128, 1], "gout")
        P.q_a.dma(nA[:], self.w["gdn_a_log"][i].rearrange("(p o) -> p o", o=1), outs=[tk])
        P.q_a.dma(dtb[:], self.w["gdn_dt_bias"][i].rearrange("(p o) -> p o", o=1), outs=[tk])
        P.q_a.dma(gout[:], self.w["gdn_g_out"][i].rearrange("(p o) -> p o", o=1), outs=[tk])
        act.op(lambda: nc.scalar.activation(out=nA[:], in_=nA[:], func=AF.Exp), outs=[tk], ins=[tk])
        dve.op(lambda: V.tensor_scalar(nA[:], nA[:], -1.0, None, ALU.mult), outs=[tk], ins=[tk])
        def big(name):
            return P.sb([128, LP], name)
        g_ = big("g"); be = big("beta"); Gc = big("Gc"); tg = Tok("g")
        cols = P.sb([128, NT, 16], "cols"); tcols = Tok("cols")
        qT = big("qT"); kT = big("kT"); vT = big("vT"); qdT = big("qdT"); cv = big("cv")
        tq = Tok("q"); tkk = Tok("k"); tv = Tok("v"); tqd = Tok("qd"); tcv = Tok("cv")
        Grow = big("Grow"); eG = big("eG"); Brow = big("Brow"); tGr = Tok("Grow")
        k_tok = P.sb([128, NT, 128], "k_tok"); v_tok = P.sb([128, NT, 128], "v_tok"); tkv = Tok("kvtok")
        u_tok = P.sb([128, NT, 128], "u_tok"); wT = P.sb([128, NT, 128], "wT"); attnT = P.sb([128, NT, 128], "attnT"); kd_tok = P.sb([128, NT, 128], "kd_tok")
        tpre = [Tok("pre%d" % c) for c in range(NT)]
        zT = big("zT"); tz = Tok("z")
        Mp = [P.sb([128, 128], "Mp%d" % k) for k in range(7)]; Np = [P.sb([128, 128], "Np%d" % k) for k in range(6)]
        tM = [Tok() for _ in range(7)]; tN = [Tok() for _ in range(6)]
        Y_r = Ring([(P.sb([128, 128], "Y"), Tok()) for _ in range(2)])
        dT = P.sb([128, 128], "dT"); tdT = Tok(); dI = P.sb([128, 128], "dI"); tdI = Tok(); aa = P.sb([128, 128], "aa"); taa = Tok()
        Xu = P.sb([128, 128], "Xu"); Xw = P.sb([128, 128], "Xw"); tX = Tok()
        sc = P.sb([128, 4], "sc"); tsc = Tok()
        Sst = P.sb([128, 128], "S"); tS = Tok("S")
        vn = P.sb([128, 128], "vn"); tvn = Tok()
        rs = P.sb([128, 512], "rs"); trs = Tok()
        b0, tb0 = P.psb[0]; b1, tb1 = P.psb[1]; b2, tb2 = P.psb[2]; b3, tb3 = P.psb[3]
        b4, tb4 = P.psb[4]; b5, tb5 = P.psb[5]; b6, tb6 = P.psb[6]; b7, tb7 = P.psb[7]
        chunks = [(q0, min(512, LP - q0)) for q0 in range(0, LP, 512)]
        for s in range(NS):
            c0s = s * LP
            P.q_a.dma(g_[0:8, :], self.gabT[0:8, c0s:c0s + LP], outs=[tg])
            P.q_a.dma(be[0:8, :], self.gabT[8:16, c0s:c0s + LP], outs=[tg])
            dve.op(lambda: V.tensor_scalar(g_[0:8, :], g_[0:8, :], dtb[:, 0:1], None, ALU.add), outs=[tg], ins=[tg, tk])
            act.op(lambda: nc.scalar.activation(out=g_[0:8, :], in_=g_[0:8, :], func=AF.Exp), outs=[tg], ins=[tg])
            act.op(lambda: nc.scalar.activation(out=g_[0:8, :], in_=g_[0:8, :], func=AF.Ln, bias=self.ones[0:8, 0:1]), outs=[tg], ins=[tg, self.t_ones])
            dve.op(lambda: V.tensor_scalar(g_[0:8, :], g_[0:8, :], nA[:, 0:1], None, ALU.mult), outs=[tg], ins=[tg, tk])
            act.op(lambda: nc.scalar.activation(out=be[0:8, :], in_=be[0:8, :], func=AF.Sigmoid), outs=[tg], ins=[tg])
            for c in range(NT):
                csl = slice(c * 128, (c + 1) * 128)
                dve.op(lambda: V.tensor_tensor_scan(out=Gc[0:8, csl], data0=self.ones[0:8, 0:128], data1=g_[0:8, csl], initial=0.0,
                                                    op0=ALU.mult, op1=ALU.add), outs=[tg], ins=[tg, self.t_ones])
            for c in range(NT):
                csl = slice(c * 128, (c + 1) * 128)
                pe.op(lambda: nc.tensor.transpose(b0[:, c * 16:c * 16 + 8], Gc[0:8, csl], self.ident[0:8, 0:8]), outs=[tb0], ins=[tg, self.t_ident])
                pe.op(lambda: nc.tensor.transpose(b0[:, c * 16 + 8:c * 16 + 16], be[0:8, csl], self.ident[0:8, 0:8]), outs=[tb0], ins=[tg, self.t_ident])
            act.op(lambda: nc.scalar.copy(out=cols[:], in_=b0[:, 0:NT * 16].rearrange("p (c k) -> p c k", k=16)), outs=[tcols], ins=[tb0])
            for h in range(8):
                for part, (x, tx) in enumerate(((qT, tq), (kT, tkk), (vT, tv))):
                    ch = part * 8 + h
                    P.q_a.dma(x[:], self.gT[ch, :, c0s:c0s + LP], outs=[tx])
                    dve.op(lambda: V.tensor_scalar(cv[:], x[:], cw[:, ch, 3:4], None, ALU.mult), outs=[tcv], ins=[tx, tk])
                    for sh in (1, 2, 3):
                        dve.op(lambda: V.scalar_tensor_tensor(out=cv[:, sh:LP], in0=x[:, 0:LP - sh], scalar=cw[:, ch, 3 - sh:4 - sh], in1=cv[:, sh:LP],
                                                              op0=ALU.mult, op1=ALU.add), outs=[tcv], ins=[tcv, tx, tk])
                    act.op(lambda: nc.scalar.activation(out=x[:], in_=cv[:], func=AF.Silu), outs=[tx], ins=[tcv])
                P.q_a.dma(zT[:], self.gzT[h, :, c0s:c0s + LP], outs=[tz])
                for (x, tx, extra) in ((qT, tq, 128.0 ** -0.5), (kT, tkk, 1.0)):
                    act.op(lambda: nc.scalar.activation(out=cv[:], in_=x[:], func=AF.Square), outs=[tcv], ins=[tx])
                    for (q0, Tq) in chunks:
                        sl = slice(q0, q0 + Tq)
                        pe.op(lambda: nc.tensor.matmul(b0[:, :Tq], self.ones[:], cv[:, sl], start=True, stop=True), outs=[tb0], ins=[tcv, self.t_ones])
                        act.op(lambda: nc.scalar.activation(out=rs[:, :Tq], in_=b0[:, :Tq], func=AF.Sqrt, bias=self.epsb[:], scale=1.0), outs=[trs], ins=[tb0, self.t_eps])
                        dve.op(lambda: V.reciprocal(out=rs[:, :Tq], in_=rs[:, :Tq]), outs=[trs], ins=[trs])
                        dve.op(lambda: V.scalar_tensor_tensor(out=x[:, sl], in0=x[:, sl], scalar=extra, in1=rs[:, :Tq], op0=ALU.mult, op1=ALU.mult),
                               outs=[tx], ins=[tx, trs])
                for (q0, Tq) in chunks:
                    sl = slice(q0, q0 + Tq)
                    pe.op(lambda: nc.tensor.matmul(b0[:, :Tq], sel[:, h * 128:(h + 1) * 128], Gc[0:8, sl], start=True, stop=True), outs=[tb0], ins=[t_sel, tg])
                    act.op(lambda: nc.scalar.copy(out=Grow[:, sl], in_=b0[:, :Tq]), outs=[tGr], ins=[tb0])
                    act.op(lambda: nc.scalar.activation(out=eG[:, sl], in_=b0[:, :Tq], func=AF.Exp), outs=[tGr], ins=[tb0])
                    pe.op(lambda: nc.tensor.matmul(b0[:, :Tq], sel[:, h * 128:(h + 1) * 128], be[0:8, sl], start=True, stop=True), outs=[tb0], ins=[t_sel, tg])
                    act.op(lambda: nc.scalar.copy(out=Brow[:, sl], in_=b0[:, :Tq]), outs=[tGr], ins=[tb0])
                dve.op(lambda: V.tensor_tensor(out=qdT[:], in0=qT[:], in1=eG[:], op=ALU.mult), outs=[tqd], ins=[tq, tGr])
                for c in range(NT):
                    csl = slice(c * 128, (c + 1) * 128)
                    cend = c * 128 + 127
                    Gcol = cols[:, c, h:h + 1]; bcol = cols[:, c, 8 + h:9 + h]
                    pe.op(lambda: nc.tensor.transpose(b2[:, 0:128], kT[:, csl], self.ident[:]), outs=[tb2], ins=[tkk, self.t_ident])
                    pe.op(lambda: nc.tensor.transpose(b2[:, 128:256], vT[:, csl], self.ident[:]), outs=[tb2], ins=[tv, self.t_ident])
                    act.op(lambda: nc.scalar.copy(out=k_tok[:, c, :], in_=b2[:, 0:128]), outs=[tkv], ins=[tb2])
                    act.op(lambda: nc.scalar.copy(out=v_tok[:, c, :], in_=b2[:, 128:256]), outs=[tkv], ins=[tb2])
                    dve.op(lambda: V.tensor_scalar(dT[:], Grow[:, csl], Gcol, 0.0, ALU.subtract, ALU.min), outs=[tdT], ins=[tGr, tcols])
                    act.op(lambda: nc.scalar.activation(out=dT[:], in_=dT[:], func=AF.Exp), outs=[tdT], ins=[tdT])
                    pe.op(lambda: nc.tensor.matmul(b1[:, 0:128], kT[:, csl], kT[:, csl], start=True, stop=True), outs=[tb1], ins=[tkk])
                    pe.op(lambda: nc.tensor.matmul(b1[:, 128:256], kT[:, csl], qT[:, csl], start=True, stop=True), outs=[tb1], ins=[tkk, tq])
                    dve.op(lambda: V.tensor_tensor(out=dI[:], in0=dT[:], in1=maskI[:], op=ALU.mult), outs=[tdI], ins=[tdT, t_mi])
                    dve.op(lambda: V.tensor_tensor(out=attnT[:, c, :], in0=b1[:, 128:256], in1=dI[:], op=ALU.mult), outs=[tpre[c]], ins=[tb1, tdI])
                    dve.op(lambda: V.tensor_tensor(out=aa[:], in0=dT[:], in1=maskSt[:], op=ALU.mult), outs=[taa], ins=[tdT, t_mst])
                    dve.op(lambda: V.tensor_tensor(out=aa[:], in0=aa[:], in1=Brow[:, csl], op=ALU.mult), outs=[taa], ins=[taa, tGr])
                    dve.op(lambda: V.tensor_tensor(out=Np[0][:], in0=b1[:, 0:128], in1=aa[:], op=ALU.mult), outs=[tN[0]], ins=[tb1, taa])
                    pe.op(lambda: nc.tensor.transpose(b2[:, 256:384], Np[0][:], self.ident[:]), outs=[tb2], ins=[tN[0], self.t_ident])
                    act.op(lambda: nc.scalar.copy(out=Mp[0][:], in_=b2[:, 256:384]), outs=[tM[0]], ins=[tb2])
                    for k in range(1, 7):
                        pe.op(lambda: nc.tensor.matmul(b3[:, 0:128], Np[k - 1][:], Mp[k - 1][:], start=True, stop=True), outs=[tb3], ins=[tN[k - 1], tM[k - 1]])
                        act.op(lambda: nc.scalar.copy(out=Mp[k][:], in_=b3[:, 0:128]), outs=[tM[k]], ins=[tb3])
                        if k < 6:
                            pe.op(lambda: nc.tensor.matmul(b4[:, 0:128], Mp[k - 1][:], Np[k - 1][:], start=True, stop=True), outs=[tb4], ins=[tN[k - 1], tM[k - 1]])
                            dve.op(lambda: V.tensor_copy(out=Np[k][:], in_=b4[:, 0:128]), outs=[tN[k]], ins=[tb4])
                    Y, tY = Y_r.next()
                    dve.op(lambda: V.tensor_tensor(out=Y[:], in0=self.ident[:], in1=Np[0][:], op=ALU.subtract), outs=[tY], ins=[tN[0], self.t_ident])
                    for k in range(1, 7):
                        pe.op(lambda: nc.tensor.matmul(b5[:, 0:128], Mp[k][:], Y[:], start=True, stop=True), outs=[tb5], ins=[tM[k], tY])
                        Y2, tY2 = Y_r.next()
                        dve.op(lambda: V.tensor_tensor(out=Y2[:], in0=b5[:, 0:128], in1=Y[:], op=ALU.add), outs=[tY2], ins=[tb5, tY])
                        Y, tY = Y2, tY2
                    act.op(lambda: nc.scalar.activation(out=sc[:, 0:1], in_=Gcol, func=AF.Exp), outs=[tsc], ins=[tcols])
                    act.op(lambda: nc.scalar.activation(out=sc[:, 1:2], in_=Gcol, func=AF.Exp, scale=-1.0, bias=Grow[:, cend:cend + 1]), outs=[tsc], ins=[tcols, tGr])
                    dve.op(lambda: V.tensor_scalar(Xu[:], v_tok[:, c, :], bcol, None, ALU.mult), outs=[tX], ins=[tkv, tcols])
                    dve.op(lambda: V.tensor_scalar(Xw[:], k_tok[:, c, :], bcol, sc[:, 0:1], ALU.mult, ALU.mult), outs=[tX], ins=[tkv, tcols, tsc])
                    dve.op(lambda: V.tensor_scalar(kd_tok[:, c, :], k_tok[:, c, :], sc[:, 1:2], None, ALU.mult), outs=[tpre[c]], ins=[tkv, tsc])
                    pe.op(lambda: nc.tensor.matmul(b5[:, 128:256], Y[:], Xu[:], start=True, stop=True), outs=[tb5], ins=[tY, tX])
                    pe.op(lambda: nc.tensor.matmul(b5[:, 256:384], Xw[:], Y[:], start=True, stop=True), outs=[tb5], ins=[tY, tX])
                    act.op(lambda: nc.scalar.copy(out=u_tok[:, c, :], in_=b5[:, 128:256]), outs=[tpre[c]], ins=[tb5])
                    act.op(lambda: nc.scalar.copy(out=wT[:, c, :], in_=b5[:, 256:384]), outs=[tpre[c]], ins=[tb5])
                dve.op(lambda: V.memset(Sst[:], 0.0), outs=[tS])
                for c in range(NT):
                    csl = slice(c * 128, (c + 1) * 128)
                    cend = c * 128 + 127
                    pe.op(lambda: nc.tensor.matmul(b6[:, 0:128], wT[:, c, :], Sst[:], start=True, stop=True), outs=[tb6], ins=[tpre[c], tS])
                    dve.op(lambda: V.tensor_tensor(out=vn[:], in0=u_tok[:, c, :], in1=b6[:, 0:128], op=ALU.subtract), outs=[tvn], ins=[tpre[c], tb6])
                    pe.op(lambda: nc.tensor.matmul(b7[:, 0:128], Sst[:], qdT[:, csl], start=True, stop=False), outs=[tb7], ins=[tS, tqd])
                    pe.op(lambda: nc.tensor.matmul(b7[:, 0:128], vn[:], attnT[:, c, :], start=False, stop=True), outs=[tb7], ins=[tvn, tpre[c]])
                    act.op(lambda: nc.scalar.copy(out=cv[:, csl], in_=b7[:, 0:128]), outs=[tcv], ins=[tb7])
                    pe.op(lambda: nc.tensor.matmul(b6[:, 128:256], kd_tok[:, c, :], vn[:], start=True, stop=True), outs=[tb6], ins=[tpre[c], tvn])
                    dve.op(lambda: V.scalar_tensor_tensor(out=Sst[:], in0=Sst[:], scalar=eG[:, cend:cend + 1], in1=b6[:, 128:256], op0=ALU.mult, op1=ALU.add),
                           outs=[tS], ins=[tS, tb6, tGr])
                oT = cv
                act.op(lambda: nc.scalar.activation(out=zT[:], in_=zT[:], func=AF.Silu), outs=[tz], ins=[tz])
                for (q0, Tq) in chunks:
                    sl = slice(q0, q0 + Tq)
                    act.op(lambda: nc.scalar.activation(out=qdT[:, sl], in_=oT[:, sl], func=AF.Square), outs=[tqd], ins=[tcv])
                    pe.op(lambda: nc.tensor.matmul(b0[:, :Tq], self.ones[:], qdT[:, sl], start=True, stop=True), outs=[tb0], ins=[tqd, self.t_ones])
                    act.op(lambda: nc.scalar.activation(out=rs[:, :Tq], in_=b0[:, :Tq], func=AF.Sqrt, bias=self.epsb[:], scale=1.0 / 128), outs=[trs], ins=[tb0, self.t_eps])
                    dve.op(lambda: V.reciprocal(out=rs[:, :Tq], in_=rs[:, :Tq]), outs=[trs], ins=[trs])
                    dve.op(lambda: V.scalar_tensor_tensor(out=oT[:, sl], in0=oT[:, sl], scalar=gout[:, 0:1], in1=rs[:, :Tq], op0=ALU.mult, op1=ALU.mult),
                           outs=[tcv], ins=[tcv, trs, tk])
                    dve.op(lambda: V.tensor_tensor(out=oT[:, sl], in0=oT[:, sl], in1=zT[:, sl], op=ALU.mult), outs=[tcv], ins=[tcv, tz])
                P.q_a.dma(self.yO[8 + h, :, c0s:c0s + LP], oT[:], ins=[tcv])
        P.release(m)


def host_consts(LP=None, mix=True):
    c = {"ident": np.eye(128, dtype=np.float32)}
    if mix and LP is not None:
        half = 32
        inv = (10000.0 ** (-np.arange(half, dtype=np.float32) / half)).astype(np.float32)
        ang = np.arange(LP, dtype=np.float32)[:, None] * inv[None, :]
        c["rope_cos"] = np.cos(ang).astype(np.float32)
        c["rope_sin"] = np.sin(ang).astype(np.float32)
        k = np.arange(128)[:, None]; q = np.arange(128)[None, :]
        c["maskT"] = (q >= k).astype(np.float32)
        c["maskS"] = (q > k).astype(np.float32)
        c["maskL"] = (q <= k).astype(np.float32)
        sel = np.zeros((8, 8, 128), np.float32)
        for h in range(8):
            sel[h, h, :] = 1.0
        c["sel"] = sel.reshape(8, 1024)
    return c


def make_xin(x, meta, LP):
    NS, SEQ, _ = x.shape
    xin = np.zeros((NS, LP, D), np.float32)
    xin[:, :N_META] = meta[None]
    xin[:, N_META:N_META + SEQ] = x
    return xin.reshape(NS * LP, D)


_WNAMES = ["norm_ffn1", "norm_mix", "norm_ffn2", "w1_gate", "w1_up", "w2_gate", "w2_up", "w1_down", "w2_down"]


def kernel(**inputs):
    x = np.asarray(inputs["x"])
    B, SEQ, _ = x.shape
    ncores = 8
    NS = B // ncores
    L = SEQ + N_META
    NT = (L + 127) // 128
    K = Kern(NS, NT, depth=2, seq_real=L)
    consts = host_consts(NT * 128)
    in_maps = []
    for c in range(ncores):
        m = {"xin": make_xin(x[c * NS:(c + 1) * NS], np.asarray(inputs["meta_tokens"]), NT * 128)}
        m.update(consts)
        for nm in K.w:
            m[nm] = np.ascontiguousarray(np.asarray(inputs[nm]))
        in_maps.append(m)
    res = run_bass_kernel_spmd(K.P.nc, in_maps, core_ids=list(range(ncores)))
    outs = [r["out"].reshape(NS, SEQ, D) for r in res.results]
    return np.concatenate(outs, axis=0).astype(np.float32)
```

```python
import math
import numpy as np
import concourse.bass as bass
import concourse.mybir as mybir
from concourse.bass_utils import run_bass_kernel_spmd

F32 = mybir.dt.float32
BF16 = mybir.dt.bfloat16
AF = mybir.ActivationFunctionType
ALU = mybir.AluOpType
AX = mybir.AxisListType

D = 2048
DC = 16
DFF = 5632
FC = 44
N_META = 16
EPS = 1e-6
ARENA_WORDS = 53200


class Tok:
    __slots__ = ("w", "r", "name")

    def __init__(self, name=""):
        self.w = None
        self.r = {}
        self.name = name


class Eng:
    def __init__(self, P, eng, name, is_pe=False):
        self.P = P
        self.eng = eng
        self.name = name
        self.sem = P.nc.alloc_semaphore("s_" + name)
        self.key = "E" + name
        self.cnt = 0
        self.seen = {}
        self.is_pe = is_pe
        self.nwait = 0

    def wait_ev(self, ev):
        if ev is None:
            return
        key, sem, val = ev
        if self.is_pe and key == self.key:
            return
        if key == self.key and val <= self.cnt_done_known():
            pass
        if self.seen.get(key, 0) >= val:
            return
        self.eng.wait_ge(sem, val)
        self.nwait += 1
        self.seen[key] = val

    def cnt_done_known(self):
        return self.seen.get(self.key, 0)

    def deps(self, outs, ins):
        for t in ins:
            self.wait_ev(t.w)
        for t in outs:
            self.wait_ev(t.w)
            for k, (sem, val) in t.r.items():
                self.wait_ev((k, sem, val))

    def op(self, fn, outs=(), ins=()):
        self.deps(outs, ins)
        inst = fn()
        self.cnt += 1
        inst.then_inc(self.sem, 1)
        ev = (self.key, self.sem, self.cnt)
        for t in ins:
            t.r[self.key] = (self.sem, self.cnt)
        for t in outs:
            t.w = ev
            t.r = {}
        return inst


class DmaQ:
    def __init__(self, P, eng, name, nsem):
        self.P = P
        self.eng = eng
        self.name = name
        self.sems = [P.nc.alloc_semaphore("d_%s%d" % (name, i)) for i in range(nsem)]
        self.vals = [0] * nsem
        self.i = 0
        self.seen = {}
        self.nwait = 0

    def wait_ev(self, ev):
        if ev is None:
            return
        key, sem, val = ev
        if self.seen.get(key, 0) >= val:
            return
        self.eng.wait_ge(sem, val)
        self.nwait += 1
        self.seen[key] = val

    def dma(self, out, in_, outs=(), ins=(), **kw):
        for t in ins:
            self.wait_ev(t.w)
        for t in outs:
            self.wait_ev(t.w)
            for k, (sem, val) in t.r.items():
                self.wait_ev((k, sem, val))
        i = self.i
        self.i = (self.i + 1) % len(self.sems)
        key = "D%s%d" % (self.name, i)
        if self.vals[i] > 0:
            self.wait_ev((key, self.sems[i], self.vals[i]))
        inst = self.eng.dma_start(out=out, in_=in_, **kw)
        self.vals[i] += 16
        inst.then_inc(self.sems[i], 16)
        ev = (key, self.sems[i], self.vals[i])
        for t in ins:
            t.r[key] = (self.sems[i], self.vals[i])
        for t in outs:
            t.w = ev
            t.r = {}
        return ev

    def drain(self):
        for i, s in enumerate(self.sems):
            if self.vals[i] > 0:
                self.wait_ev(("D%s%d" % (self.name, i), s, self.vals[i]))


class Prog:
    def __init__(self):
        self.nc = bass.Bass("TRN2", target_bir_lowering=False)
        nc = self.nc
        self.pe = Eng(self, nc.tensor, "pe", is_pe=True)
        self.act = Eng(self, nc.scalar, "act")
        self.dve = Eng(self, nc.vector, "dve")
        self.pool = Eng(self, nc.gpsimd, "pool")
        self.q_w = DmaQ(self, nc.sync, "w", 12)
        self.q_a = DmaQ(self, nc.gpsimd, "a", 8)
        self._n = 0
        self.AW = ARENA_WORDS
        self.arena = nc.alloc_sbuf_tensor("arena", [128, self.AW], F32)
        self.off = 0
        self.psb = [(nc.alloc_psum_tensor("psb%d" % i, [128, 512], F32), Tok("psb%d" % i)) for i in range(8)]

    def sb(self, shape, name=None):
        n = 1
        for d in shape[1:]:
            n *= d
        assert self.off + n <= self.AW, ("SBUF arena overflow", name, self.off, n)
        ap = self.arena[0:shape[0], self.off:self.off + n]
        self.off += n
        if len(shape) == 3:
            ap = ap.rearrange("p (a b) -> p a b", a=shape[1])
        elif len(shape) == 4:
            ap = ap.rearrange("p (a b c) -> p a b c", a=shape[1], b=shape[2])
        return ap

    def sbh(self, shape, name=None):
        n = 1
        for d in shape[1:]:
            n *= d
        import os
        if os.environ.get("DBG_NOBF"):
            return self.sb(shape, name)
        assert n % 2 == 0
        w = n // 2
        assert self.off + w <= self.AW, ("SBUF arena overflow", name, self.off, w)
        ap = self.arena[0:shape[0], self.off:self.off + w].bitcast(BF16)
        self.off += w
        if len(shape) == 3:
            ap = ap.rearrange("p (a b) -> p a b", a=shape[1])
        return ap

    def mark(self):
        return self.off

    def release(self, m):
        self.barrier()
        self.off = m

    def barrier(self):
        evs = []
        for e in (self.pe, self.act, self.dve, self.pool):
            if e.cnt > 0:
                evs.append((e.key, e.sem, e.cnt))
        for q in (self.q_w, self.q_a):
            for i, sm in enumerate(q.sems):
                if q.vals[i] > 0:
                    evs.append(("D%s%d" % (q.name, i), sm, q.vals[i]))
        for e in (self.pe, self.act, self.dve, self.pool, self.q_w, self.q_a):
            for ev in evs:
                e.wait_ev(ev)

    def dram(self, name, shape, kind="Internal", dtype=F32):
        return self.nc.dram_tensor(name, list(shape), dtype, kind=kind).ap()


class Ring:
    def __init__(self, bufs):
        self.bufs = bufs
        self.i = 0

    def next(self):
        b = self.bufs[self.i]
        self.i = (self.i + 1) % len(self.bufs)
        return b


class Kern:
    LAYERS = None

    def __init__(self, NS, NT, depth=2, do=("ffn1", "mix", "ffn2"), seq_real=None):
        self.NS = NS
        self.NT = NT
        self.LP = NT * 128
        self.NTOK = NS * self.LP
        self.depth = depth
        self.do = do
        self.L = seq_real if seq_real is not None else (self.LP - 112)
        self.P = Prog()
        self.build()

    def build(self):
        P = self.P
        nc = P.nc
        NS, NT, LP, NTOK = self.NS, self.NT, self.LP, self.NTOK
        SEQ = self.L - N_META
        dep = self.depth
        self.xin = P.dram("xin", [NS * LP, D], "ExternalInput")
        self.ident_d = P.dram("ident", [128, 128], "ExternalInput")
        self.w = {}
        def ext(name, shape):
            self.w[name] = P.dram(name, shape, "ExternalInput")
        ext("norm_ffn1", [dep, D]); ext("norm_mix", [dep, D]); ext("norm_ffn2", [dep, D])
        for nm in ("w1_gate", "w1_up", "w2_gate", "w2_up"):
            ext(nm, [dep, D, DFF])
        for nm in ("w1_down", "w2_down"):
            ext(nm, [dep, DFF, D])
        ne, no = (dep + 1) // 2, dep // 2
        if "mix" in self.do:
            ext("ev_w_in", [ne, D, 1344]); ext("s5_log_dt", [ne, 32]); ext("s5_a_re", [ne, 32, 64]); ext("s5_a_im", [ne, 32, 64])
            ext("s5_b_re", [ne, 32, 64, 16]); ext("s5_b_im", [ne, 32, 64, 16]); ext("s5_c_re", [ne, 32, 16, 64]); ext("s5_c_im", [ne, 32, 16, 64])
            ext("s5_d", [ne, 512]); ext("s5_w_glu", [ne, 512, 512]); ext("mla_g_cq", [ne, 512]); ext("mla_g_ckv", [ne, 256])
            ext("mla_w_uq", [ne, 512, 1536]); ext("mla_w_ukv", [ne, 256, 2048]); ext("mla_g_q", [ne, 192]); ext("mla_g_k", [ne, 192])
            ext("ev_w_out", [ne, 1536, D])
            if no > 0:
                ext("od_w_in", [no, D, 7184]); ext("gdn_conv", [no, 4, 3072]); ext("gdn_a_log", [no, 8]); ext("gdn_dt_bias", [no, 8])
                ext("gdn_g_out", [no, 128]); ext("od_w_out", [no, 2048, D])
            self.c_rope_cos = P.dram("rope_cos", [LP, 32], "ExternalInput")
            self.c_rope_sin = P.dram("rope_sin", [LP, 32], "ExternalInput")
            self.c_maskT = P.dram("maskT", [128, 128], "ExternalInput")
            self.uT = P.dram("uT", [4, 128, NTOK])
            self.QT = P.dram("QT", [8, 192, NTOK])
            self.KT = P.dram("KT", [8, 192, NTOK])
            self.Vd = P.dram("Vd", [NTOK, 1024])
            self.yE = P.dram("yE", [12, 128, NTOK])
            if no > 0:
                self.c_maskS = P.dram("maskS", [128, 128], "ExternalInput")
                self.c_maskL = P.dram("maskL", [128, 128], "ExternalInput")
                self.c_sel = P.dram("sel", [8, 1024], "ExternalInput")
                self.sbQ = P.dram("sbQ", [8, 128, NTOK]); self.sbK = P.dram("sbK", [8, 128, NTOK]); self.sbV = P.dram("sbV", [NTOK, 1024])
                self.gT = P.dram("gT", [24, 128, NTOK]); self.gzT = P.dram("gzT", [8, 128, NTOK]); self.gabT = P.dram("gabT", [16, NTOK])
                self.yO = P.dram("yO", [16, 128, NTOK])
        self.out = P.dram("out", [NS * SEQ, D], "ExternalOutput")
        self.hT = P.dram("hT", [DC, 128, NTOK])
        self.ident = P.sb([128, 128], "ident"); self.t_ident = Tok("ident")
        P.q_a.dma(self.ident[:], self.ident_d[:, :], outs=[self.t_ident])
        self.ones = P.sb([128, 128], "ones"); self.t_ones = Tok("ones")
        P.dve.op(lambda: nc.vector.memset(self.ones[:], 1.0), outs=[self.t_ones])
        self.epsb = P.sb([128, 1], "epsb"); self.t_eps = Tok("eps")
        P.dve.op(lambda: nc.vector.memset(self.epsb[:], EPS), outs=[self.t_eps])
        self.gains = {}
        for nm in ("norm_ffn1", "norm_mix", "norm_ffn2"):
            g = P.sb([128, dep, DC], nm); t = Tok(nm)
            for l in range(dep):
                P.q_a.dma(g[:, l, :], self.w[nm][l].rearrange("(c p) -> p c", p=128), outs=[t],
                          allow_slow_non_contiguous=True)
            self.gains[nm] = (g, t)

        self.stage_in()
        for l in (self.LAYERS if self.LAYERS is not None else range(dep)):
            if "ffn1" in self.do:
                self.ffn(l, "1")
            if "mix" in self.do:
                if l % 2 == 0:
                    self.even_mixer(l)
                else:
                    self.odd_mixer(l)
            if "ffn2" in self.do:
                self.ffn(l, "2")
        self.stage_out()
        P.q_a.drain()
        P.q_w.drain()

    def psring(self, idx):
        return Ring([self.P.psb[i] for i in idx])

    def stage_in(self):
        P = self.P; nc = P.nc
        m = P.mark()
        NB = self.NTOK // 128
        xin_r = Ring([(P.sb([128, D], "xin"), Tok()) for _ in range(2)])
        ps_r = self.psring([0, 1])
        ht_r = Ring([(P.sb([128, DC, 128], "htin"), Tok()) for _ in range(2)])
        for b in range(NB):
            xt, xtok = xin_r.next()
            P.q_a.dma(xt[:], self.xin[b * 128:(b + 1) * 128, :], outs=[xtok])
            ht, htok = ht_r.next()
            for g in range(4):
                ps, ptok = ps_r.next()
                for j in range(4):
                    c = g * 4 + j
                    P.pe.op(lambda: nc.tensor.transpose(ps[:, j * 128:(j + 1) * 128],
                                                        xt[:, c * 128:(c + 1) * 128], self.ident[:]),
                            outs=[ptok], ins=[xtok, self.t_ident])
                P.act.op(lambda: nc.scalar.copy(out=ht[:, g * 4:(g + 1) * 4, :],
                                                in_=ps[:].rearrange("p (c t) -> p c t", c=4)),
                         outs=[htok], ins=[ptok])
            P.q_a.dma(self.hT[:, :, b * 128:(b + 1) * 128].rearrange("c p t -> p c t"), ht[:], ins=[htok],
                      outs=[self.tok_h(b * 128, 128)])
        P.release(m)

    def tok_h(self, t0, n):
        if not hasattr(self, "_htoks"):
            self._htoks = {}
        key = t0 // 128
        if key not in self._htoks:
            self._htoks[key] = Tok("h%d" % key)
        return self._htoks[key]

    def toks_h(self, t0, n):
        return [self.tok_h(t, 128) for t in range(t0, t0 + n, 128)]

    def stage_out(self):
        P = self.P; nc = P.nc
        m = P.mark()
        SEQ = self.L - N_META
        ht_r = Ring([(P.sb([128, DC, 128], "htout"), Tok()) for _ in range(2)])
        ps_r = self.psring([0, 1])
        o_r = Ring([(P.sb([128, D], "osb"), Tok()) for _ in range(2)])
        for s in range(self.NS):
            for j in range(self.NT):
                t0 = j * 128
                lo = max(t0, N_META); hi = min(t0 + 128, self.L)
                if hi <= lo:
                    continue
                b = s * self.NT + j
                ht, htok = ht_r.next()
                P.q_a.dma(ht[:], self.hT[:, :, b * 128:(b + 1) * 128].rearrange("c p t -> p c t"), outs=[htok],
                          ins=[self.tok_h(b * 128, 128)])
                ot, otok = o_r.next()
                for g in range(4):
                    ps, ptok = ps_r.next()
                    for jj in range(4):
                        c = g * 4 + jj
                        P.pe.op(lambda: nc.tensor.transpose(ps[:, jj * 128:(jj + 1) * 128], ht[:, c, :], self.ident[:]),
                                outs=[ptok], ins=[htok, self.t_ident])
                    P.act.op(lambda: nc.scalar.copy(out=ot[:, g * 512:(g + 1) * 512], in_=ps[:]),
                             outs=[otok], ins=[ptok])
                r0 = s * SEQ + lo - N_META
                P.q_a.dma(self.out[r0:r0 + (hi - lo), :], ot[lo - t0:hi - t0, :], ins=[otok])
        P.release(m)

    def rms_scale(self, src, stok, T, gain_cols, gtok, ps_sum, pstok, sq, sqtok, rstd, rtok, nchunks=DC, dim=D):
        P = self.P; nc = P.nc
        for c in range(nchunks):
            P.act.op(lambda: nc.scalar.activation(out=sq[:, :T], in_=src[:, c, :T], func=AF.Square),
                     outs=[sqtok], ins=[stok])
            P.pe.op(lambda: nc.tensor.matmul(ps_sum[:, :T], self.ones[:], sq[:, :T], start=(c == 0), stop=(c == nchunks - 1)),
                    outs=[pstok], ins=[sqtok, self.t_ones])
        P.act.op(lambda: nc.scalar.activation(out=rstd[:, :T], in_=ps_sum[:, :T], func=AF.Sqrt,
                                              bias=self.epsb[:], scale=1.0 / dim),
                 outs=[rtok], ins=[pstok, self.t_eps])
        P.dve.op(lambda: nc.vector.reciprocal(out=rstd[:, :T], in_=rstd[:, :T]), outs=[rtok], ins=[rtok])
        for c in range(nchunks):
            P.dve.op(lambda: nc.vector.scalar_tensor_tensor(out=src[:, c, :T], in0=src[:, c, :T],
                                                            scalar=gain_cols(c), in1=rstd[:, :T],
                                                            op0=ALU.mult, op1=ALU.mult),
                     outs=[stok], ins=[stok, rtok, gtok])

    def ffn(self, l, which):
        P = self.P; nc = P.nc
        NTOK = self.NTOK
        TT = 768; XS = 256
        HF = FC // 2
        wg = self.w["w%s_gate" % which][l]
        wu = self.w["w%s_up" % which][l]
        wd = self.w["w%s_down" % which][l]
        gain, gtok = self.gains["norm_ffn%s" % which]
        m = P.mark()
        xs, xstok = P.sb([128, DC, XS], "ffxs"), Tok("ffxs")
        xb, xbtok = P.sbh([128, DC, TT], "ffxb"), Tok("ffxb")
        hf = P.sbh([128, FC, TT], "ffh")
        hftoks = [Tok("ffh%d" % f) for f in range(FC)]
        wst_r = Ring([(P.sb([128, DC, 128], "wst"), Tok()) for _ in range(3)])
        wgb_r = Ring([(P.sbh([128, DC, 128], "wgb"), Tok()) for _ in range(2)])
        wub_r = Ring([(P.sbh([128, DC, 128], "wub"), Tok()) for _ in range(2)])
        wdst_r = Ring([(P.sb([128, HF, 128], "wdst"), Tok()) for _ in range(2)])
        wdb_r = Ring([(P.sbh([128, HF, 128], "wdb"), Tok()) for _ in range(4)])
        rstd, rtok = P.sb([128, XS], "ffrstd"), Tok()
        sq_r = Ring([(P.sb([128, XS], "ffsq"), Tok()) for _ in range(2)])
        sg_r = Ring([(P.sb([128, TT // 2], "ffsg"), Tok()) for _ in range(2)])
        hres_r = Ring([(P.sb([128, TT], "ffres"), Tok()) for _ in range(2)])
        psg = [P.psb[0], P.psb[1]]; psu = [P.psb[2], P.psb[3]]; psd = [P.psb[4], P.psb[5]]
        pss, psstok = P.psb[6]
        wgv = wg.rearrange("(c p) f -> p c f", p=128)
        wuv = wu.rearrange("(c p) f -> p c f", p=128)
        wdv = wd.rearrange("(c p) d -> p c d", p=128)
        with nc.allow_low_precision("bf16 matmul operands, fp32 accumulation"):
            for t0 in range(0, NTOK, TT):
                T = min(TT, NTOK - t0)
                HW = T // 2
                halves = [(0, HW), (HW, T - HW)]
                htoks = self.toks_h(t0, T)
                for p0 in range(0, T, XS):
                    P.q_a.dma(xs[:, :, :], self.hT[:, :, t0 + p0:t0 + p0 + XS].rearrange("c p t -> p c t"), outs=[xstok], ins=htoks)
                    for c in range(DC):
                        sq, sqtok = sq_r.next()
                        P.act.op(lambda: nc.scalar.activation(out=sq[:, :], in_=xs[:, c, :], func=AF.Square), outs=[sqtok], ins=[xstok])
                        P.pe.op(lambda: nc.tensor.matmul(pss[:, :XS], self.ones[:], sq[:, :], start=(c == 0), stop=(c == DC - 1)),
                                outs=[psstok], ins=[sqtok, self.t_ones])
                    P.act.op(lambda: nc.scalar.activation(out=rstd[:, :], in_=pss[:, :XS], func=AF.Sqrt, bias=self.epsb[:], scale=1.0 / D),
                             outs=[rtok], ins=[psstok, self.t_eps])
                    P.dve.op(lambda: nc.vector.reciprocal(out=rstd[:, :], in_=rstd[:, :]), outs=[rtok], ins=[rtok])
                    for c in range(DC):
                        P.dve.op(lambda: nc.vector.scalar_tensor_tensor(out=xb[:, c, p0:p0 + XS], in0=xs[:, c, :], scalar=gain[:, l, c:c + 1],
                                                                        in1=rstd[:, :], op0=ALU.mult, op1=ALU.mult),
                                 outs=[xbtok], ins=[xstok, rtok, gtok])
                def load_gu(f):
                    gs, gst = wst_r.next()
                    P.q_w.dma(gs[:], wgv[:, :, f * 128:(f + 1) * 128], outs=[gst])
                    gb, gbt = wgb_r.next()
                    P.dve.op(lambda: nc.vector.tensor_copy(out=gb[:], in_=gs[:]), outs=[gbt], ins=[gst])
                    us, ust = wst_r.next()
                    P.q_w.dma(us[:], wuv[:, :, f * 128:(f + 1) * 128], outs=[ust])
                    ub, ubt = wub_r.next()
                    P.dve.op(lambda: nc.vector.tensor_copy(out=ub[:], in_=us[:]), outs=[ubt], ins=[ust])
                    return gb, gbt, ub, ubt
                nxt = load_gu(0)
                for f in range(FC):
                    g_, gt_, u_, ut_ = nxt
                    if f + 1 < FC:
                        nxt = load_gu(f + 1)
                    for hi, (h0, hw) in enumerate(halves):
                        ps, pt = psg[hi]
                        for c in range(DC):
                            P.pe.op(lambda: nc.tensor.matmul(ps[:, :hw], g_[:, c, :], xb[:, c, h0:h0 + hw], start=(c == 0), stop=(c == DC - 1)),
                                    outs=[pt], ins=[gt_, xbtok])
                    for hi, (h0, hw) in enumerate(halves):
                        ps, pt = psu[hi]
                        for c in range(DC):
                            P.pe.op(lambda: nc.tensor.matmul(ps[:, :hw], u_[:, c, :], xb[:, c, h0:h0 + hw], start=(c == 0), stop=(c == DC - 1)),
                                    outs=[pt], ins=[ut_, xbtok])
                    for hi, (h0, hw) in enumerate(halves):
                        sg, sgt = sg_r.next()
                        P.act.op(lambda: nc.scalar.activation(out=sg[:, :hw], in_=psg[hi][0][:, :hw], func=AF.Silu), outs=[sgt], ins=[psg[hi][1]])
                        P.dve.op(lambda: nc.vector.tensor_tensor(out=hf[:, f, h0:h0 + hw], in0=psu[hi][0][:, :hw], in1=sg[:, :hw], op=ALU.mult),
                                 outs=[hftoks[f]], ins=[psu[hi][1], sgt])
                def load_d(i):
                    dc, hh = divmod(i, 2)
                    ds_, dst_ = wdst_r.next()
                    P.q_w.dma(ds_[:], wdv[:, hh * HF:(hh + 1) * HF, dc * 128:(dc + 1) * 128], outs=[dst_])
                    db, dbt = wdb_r.next()
                    P.act.op(lambda: nc.scalar.copy(out=db[:], in_=ds_[:]), outs=[dbt], ins=[dst_])
                    return db, dbt
                pend = [load_d(0), load_d(1)]
                for dc in range(DC):
                    hres, hrt = hres_r.next()
                    P.q_a.dma(hres[:, :T], self.hT[dc, :, t0:t0 + T], outs=[hrt], ins=htoks)
                    slabs = []
                    for hh in range(2):
                        slabs.append(pend.pop(0))
                        i_next = dc * 2 + hh + 2
                        if i_next < 2 * DC:
                            pend.append(load_d(i_next))
                    for hi, (h0, hw) in enumerate(halves):
                        ps, pt = psd[hi]
                        for hh in range(2):
                            d_, dt_ = slabs[hh]
                            for ff in range(HF):
                                f = hh * HF + ff
                                P.pe.op(lambda: nc.tensor.matmul(ps[:, :hw], d_[:, ff, :], hf[:, f, h0:h0 + hw], start=(f == 0), stop=(f == FC - 1)),
                                        outs=[pt], ins=[dt_, hftoks[f]])
                        P.dve.op(lambda: nc.vector.scalar_tensor_tensor(out=hres[:, h0:h0 + hw], in0=ps[:, :hw], scalar=0.5, in1=hres[:, h0:h0 + hw],
                                                                        op0=ALU.mult, op1=ALU.add),
                                 outs=[hrt], ins=[pt, hrt])
                    P.q_a.dma(self.hT[dc, :, t0:t0 + T], hres[:, :T], ins=[hrt], outs=htoks)
        P.release(m)

    def ffn_fp32(self, l, which):
        P = self.P; nc = P.nc
        NTOK = self.NTOK
        TT = 512
        HF = FC // 2
        wg = self.w["w%s_gate" % which][l]
        wu = self.w["w%s_up" % which][l]
        wd = self.w["w%s_down" % which][l]
        gain, gtok = self.gains["norm_ffn%s" % which]
        m = P.mark()
        xt, xtok = P.sb([128, DC, TT], "ffx"), Tok("ffx")
        hf = P.sb([128, FC, TT], "ffh")
        hftoks = [Tok("ffh%d" % f) for f in range(FC)]
        wg_r = Ring([(P.sb([128, DC, 128], "wg"), Tok()) for _ in range(2)])
        wu_r = Ring([(P.sb([128, DC, 128], "wu"), Tok()) for _ in range(2)])
        wd_r = Ring([(P.sb([128, HF, 128], "wd"), Tok()) for _ in range(3)])
        psg_r = self.psring([0, 1]); psu_r = self.psring([2, 3]); psd_r = self.psring([4, 5])
        pss, psstok = P.psb[6]
        rstd, rtok = P.sb([128, TT], "ffrstd"), Tok()
        sg_r = Ring([(P.sb([128, TT], "ffsg"), Tok()) for _ in range(2)])
        hres_r = Ring([(P.sb([128, TT], "ffres"), Tok()) for _ in range(2)])
        wgv = wg.rearrange("(c p) f -> p c f", p=128)
        wuv = wu.rearrange("(c p) f -> p c f", p=128)
        wdv = wd.rearrange("(c p) d -> p c d", p=128)
        for t0 in range(0, NTOK, TT):
            T = min(TT, NTOK - t0)
            htoks = self.toks_h(t0, T)
            P.q_a.dma(xt[:, :, :T], self.hT[:, :, t0:t0 + T].rearrange("c p t -> p c t"), outs=[xtok], ins=htoks)
            sq, sqtok = sg_r.next()
            self.rms_scale(xt, xtok, T, lambda c: gain[:, l, c:c + 1], gtok, pss, psstok, sq, sqtok, rstd, rtok)
            def load_gu(f):
                g_, gt_ = wg_r.next(); u_, ut_ = wu_r.next()
                P.q_w.dma(g_[:], wgv[:, :, f * 128:(f + 1) * 128], outs=[gt_])
                P.q_w.dma(u_[:], wuv[:, :, f * 128:(f + 1) * 128], outs=[ut_])
                return g_, gt_, u_, ut_
            nxt = load_gu(0)
            for f in range(FC):
                g_, gt_, u_, ut_ = nxt
                if f + 1 < FC:
                    nxt = load_gu(f + 1)
                psg, psgt = psg_r.next(); psu, psut = psu_r.next()
                for c in range(DC):
                    P.pe.op(lambda: nc.tensor.matmul(psg[:, :T], g_[:, c, :], xt[:, c, :T], start=(c == 0), stop=(c == DC - 1)),
                            outs=[psgt], ins=[gt_, xtok])
                for c in range(DC):
                    P.pe.op(lambda: nc.tensor.matmul(psu[:, :T], u_[:, c, :], xt[:, c, :T], start=(c == 0), stop=(c == DC - 1)),
                            outs=[psut], ins=[ut_, xtok])
                sg, sgt = sg_r.next()
                P.act.op(lambda: nc.scalar.activation(out=sg[:, :T], in_=psg[:, :T], func=AF.Silu), outs=[sgt], ins=[psgt])
                P.dve.op(lambda: nc.vector.tensor_tensor(out=hf[:, f, :T], in0=psu[:, :T], in1=sg[:, :T], op=ALU.mult),
                         outs=[hftoks[f]], ins=[psut, sgt])
            def load_d(i):
                dc, hh = divmod(i, 2)
                d_, dt_ = wd_r.next()
                P.q_w.dma(d_[:], wdv[:, hh * HF:(hh + 1) * HF, dc * 128:(dc + 1) * 128], outs=[dt_])
                return d_, dt_
            pend = [load_d(0), load_d(1)]
            for dc in range(DC):
                hres, hrt = hres_r.next()
                P.q_a.dma(hres[:, :T], self.hT[dc, :, t0:t0 + T], outs=[hrt], ins=htoks)
                psd, psdt = psd_r.next()
                for hh in range(2):
                    d_, dt_ = pend.pop(0)
                    i_next = dc * 2 + hh + 2
                    if i_next < 2 * DC:
                        pend.append(load_d(i_next))
                    for ff in range(HF):
                        f = hh * HF + ff
                        P.pe.op(lambda: nc.tensor.matmul(psd[:, :T], d_[:, ff, :], hf[:, f, :T], start=(f == 0), stop=(f == FC - 1)),
                                outs=[psdt], ins=[dt_, hftoks[f]])
                P.dve.op(lambda: nc.vector.scalar_tensor_tensor(out=hres[:, :T], in0=psd[:, :T], scalar=0.5, in1=hres[:, :T],
                                                                op0=ALU.mult, op1=ALU.add),
                         outs=[hrt], ins=[psdt, hrt])
                P.q_a.dma(self.hT[dc, :, t0:t0 + T], hres[:, :T], ins=[hrt], outs=htoks)
        P.release(m)


    def bc_rows(self, dram_ap_row_tensor, offset, n, parts=128):
        return bass.AP(dram_ap_row_tensor.tensor, offset, [[0, parts], [1, n]])

    def even_mixer(self, l):
        import os
        st = os.environ.get("DBG_STAGES", "inproj,attn,s5,out").split(",")
        if "inproj" in st:
            self.even_inproj(l)
        if "attn" in st:
            self.even_attn(l)
        if "s5" in st:
            self.even_s5(l)
        if "out" in st:
            self.even_out(l)

    def headnorm_rope(self, buf, btok, g_rep, gtok, cos, sin, cstok, tmp, ttok, ss, sstok, t4, t4tok):
        P = self.P; nc = P.nc
        P.dve.op(lambda: nc.vector.tensor_tensor(out=tmp[:], in0=buf[:], in1=buf[:], op=ALU.mult), outs=[ttok], ins=[btok])
        P.dve.op(lambda: nc.vector.tensor_reduce(out=ss[:], in_=tmp[:], axis=AX.X, op=ALU.add), outs=[sstok], ins=[ttok])
        P.act.op(lambda: nc.scalar.activation(out=ss[:], in_=ss[:], func=AF.Sqrt, bias=self.epsb[:], scale=1.0 / 192),
                 outs=[sstok], ins=[sstok, self.t_eps])
        P.dve.op(lambda: nc.vector.reciprocal(out=ss[:], in_=ss[:]), outs=[sstok], ins=[sstok])
        P.dve.op(lambda: nc.vector.tensor_tensor(out=buf[:], in0=buf[:], in1=ss[:].unsqueeze(2).to_broadcast([128, 8, 192]), op=ALU.mult),
                 outs=[btok], ins=[btok, sstok])
        P.dve.op(lambda: nc.vector.tensor_tensor(out=buf[:], in0=buf[:], in1=g_rep[:].unsqueeze(1).to_broadcast([128, 8, 192]), op=ALU.mult),
                 outs=[btok], ins=[btok, gtok])
        x1 = buf[:, :, 128:160]; x2 = buf[:, :, 160:192]
        cb = cos.unsqueeze(1).to_broadcast([128, 8, 32]); sb_ = sin.unsqueeze(1).to_broadcast([128, 8, 32])
        P.dve.op(lambda: nc.vector.tensor_tensor(out=t4[:, 0], in0=x1, in1=cb, op=ALU.mult), outs=[t4tok], ins=[btok, cstok])
        P.dve.op(lambda: nc.vector.tensor_tensor(out=t4[:, 1], in0=x2, in1=sb_, op=ALU.mult), outs=[t4tok], ins=[btok, cstok])
        P.dve.op(lambda: nc.vector.tensor_tensor(out=t4[:, 2], in0=x1, in1=sb_, op=ALU.mult), outs=[t4tok], ins=[btok, cstok])
        P.dve.op(lambda: nc.vector.tensor_tensor(out=t4[:, 3], in0=x2, in1=cb, op=ALU.mult), outs=[t4tok], ins=[btok, cstok])
        P.dve.op(lambda: nc.vector.tensor_tensor(out=x1, in0=t4[:, 0], in1=t4[:, 1], op=ALU.subtract), outs=[btok], ins=[t4tok])
        P.dve.op(lambda: nc.vector.tensor_tensor(out=x2, in0=t4[:, 2], in1=t4[:, 3], op=ALU.add), outs=[btok], ins=[t4tok])

    def even_inproj(self, l):
        P = self.P; nc = P.nc; i = l // 2
        NS, LP, NT = self.NS, self.LP, self.NT
        m = P.mark()
        TT = 512
        Wv = self.w["ev_w_in"][i].rearrange("(c p) f -> p c f", p=128)
        gain, gtok = self.gains["norm_mix"]
        wuq = P.sb([128, 4, 1536], "wuq"); t_wuq = Tok()
        P.q_w.dma(wuq[:], self.w["mla_w_uq"][i].rearrange("(c p) f -> p c f", p=128), outs=[t_wuq])
        wukv = P.sb([128, 2, 2048], "wukv"); t_wukv = Tok()
        P.q_w.dma(wukv[:], self.w["mla_w_ukv"][i].rearrange("(c p) f -> p c f", p=128), outs=[t_wukv])
        wkr = P.sb([128, DC, 64], "wkr"); t_wkr = Tok()
        P.q_w.dma(wkr[:], Wv[:, :, 1280:1344], outs=[t_wkr])
        gcq = P.sb([128, 4], "gcq"); gckv = P.sb([128, 2], "gckv"); t_g = Tok()
        P.q_a.dma(gcq[:], self.w["mla_g_cq"][i].rearrange("(c p) -> p c", p=128), outs=[t_g], allow_slow_non_contiguous=True)
        P.q_a.dma(gckv[:], self.w["mla_g_ckv"][i].rearrange("(c p) -> p c", p=128), outs=[t_g], allow_slow_non_contiguous=True)
        gq = P.sb([128, 192], "gq"); gk = P.sb([128, 192], "gk"); t_gq = Tok()
        P.q_a.dma(gq[:], self.bc_rows(self.w["mla_g_q"], i * 192, 192), outs=[t_gq])
        P.q_a.dma(gk[:], self.bc_rows(self.w["mla_g_k"], i * 192, 192), outs=[t_gq])
        cosT = P.sb([128, NT, 32], "cos"); sinT = P.sb([128, NT, 32], "sin"); t_cs = Tok()
        P.q_a.dma(cosT[:], self.c_rope_cos.rearrange("(j p) f -> p j f", p=128), outs=[t_cs])
        P.q_a.dma(sinT[:], self.c_rope_sin.rearrange("(j p) f -> p j f", p=128), outs=[t_cs])
        xt, xtok = P.sb([128, DC, TT], "evx"), Tok()
        cq, cqtok = P.sb([128, 4, TT], "cq"), Tok()
        ckv, ckvtok = P.sb([128, 2, TT], "ckv"), Tok()
        win_r = Ring([(P.sb([128, DC, 128], "win"), Tok()) for _ in range(2)])
        ub_r = Ring([(P.sb([128, TT], "ub"), Tok()) for _ in range(2)])
        rstd, rtok = P.sb([128, TT], "rstd"), Tok()
        sq, sqtok = P.sb([128, TT], "sq"), Tok()
        q_sb, qtok = P.sb([128, 8, 192], "q_sb"), Tok()
        k_sb, ktok = P.sb([128, 8, 192], "k_sb"), Tok()
        v_sb, vtok = P.sb([128, 8, 128], "v_sb"), Tok()
        tmp, ttok = P.sb([128, 8, 192], "tmp"), Tok()
        ss, sstok = P.sb([128, 8], "ss"), Tok()
        t4, t4tok = P.sb([128, 4, 8, 32], "t4"), Tok()
        qTn, qTntok = P.sb([128, 8, 128], "qTn"), Tok()
        qTr, qTrtok = P.sb([64, 8, 128], "qTr"), Tok()
        kTn, kTntok = P.sb([128, 8, 128], "kTn"), Tok()
        kTr, kTrtok = P.sb([64, 8, 128], "kTr"), Tok()
        ps01 = self.psring([0, 1])
        pss, psstok = P.psb[7]
        for s in range(NS):
            for t0 in range(0, LP, TT):
                T = min(TT, LP - t0)
                tok0 = s * LP + t0
                htoks = self.toks_h(tok0, T)
                P.q_a.dma(xt[:, :, :T], self.hT[:, :, tok0:tok0 + T].rearrange("c p t -> p c t"), outs=[xtok], ins=htoks)
                self.rms_scale(xt, xtok, T, lambda c: gain[:, l, c:c + 1], gtok, pss, psstok, sq, sqtok, rstd, rtok)
                def load_w(j):
                    w_, wt_ = win_r.next()
                    P.q_w.dma(w_[:], Wv[:, :, j * 128:(j + 1) * 128], outs=[wt_])
                    return w_, wt_
                nxt = load_w(0)
                for j in range(10):
                    w_, wt_ = nxt
                    if j + 1 < 10:
                        nxt = load_w(j + 1)
                    ps, ptok = ps01.next()
                    for c in range(DC):
                        P.pe.op(lambda: nc.tensor.matmul(ps[:, :T], w_[:, c, :], xt[:, c, :T], start=(c == 0), stop=(c == DC - 1)),
                                outs=[ptok], ins=[wt_, xtok])
                    if j < 4:
                        ub, ubtok = ub_r.next()
                        P.act.op(lambda: nc.scalar.copy(out=ub[:, :T], in_=ps[:, :T]), outs=[ubtok], ins=[ptok])
                        P.q_a.dma(self.uT[j, :, tok0:tok0 + T], ub[:, :T], ins=[ubtok])
                    elif j < 8:
                        P.act.op(lambda: nc.scalar.copy(out=cq[:, j - 4, :T], in_=ps[:, :T]), outs=[cqtok], ins=[ptok])
                    else:
                        P.act.op(lambda: nc.scalar.copy(out=ckv[:, j - 8, :T], in_=ps[:, :T]), outs=[ckvtok], ins=[ptok])
                self.rms_scale(cq, cqtok, T, lambda c: gcq[:, c:c + 1], t_g, pss, psstok, sq, sqtok, rstd, rtok, nchunks=4, dim=512)
                self.rms_scale(ckv, ckvtok, T, lambda c: gckv[:, c:c + 1], t_g, pss, psstok, sq, sqtok, rstd, rtok, nchunks=2, dim=256)
                import os
                LV = int(os.environ.get("DBG_INPROJ", "9"))
                for tb in range(T // 128 if LV >= 2 else 0):
                    tsl = slice(tb * 128, (tb + 1) * 128)
                    jblk = (t0 // 128) + tb
                    tokb = tok0 + tb * 128
                    qflat = q_sb[:].rearrange("p h d -> p (h d)")
                    SUB = os.environ.get("DBG_SUB", "q,kv,kr").split(",")
                    for n in range(3 if "q" in SUB else 0):
                        ps, ptok = P.psb[2 + n]
                        for c in range(4):
                            P.pe.op(lambda: nc.tensor.matmul(ps[:, :], cq[:, c, tsl], wuq[:, c, n * 512:(n + 1) * 512], start=(c == 0), stop=(c == 3)),
                                    outs=[ptok], ins=[cqtok, t_wuq])
                        P.act.op(lambda: nc.scalar.copy(out=qflat[:, n * 512:(n + 1) * 512], in_=ps[:, :]), outs=[qtok], ins=[ptok])
                    for n4 in range(4 if "kv" in SUB else 0):
                        ps, ptok = P.psb[5 + (n4 % 2)]
                        for c in range(2):
                            P.pe.op(lambda: nc.tensor.matmul(ps[:, :], ckv[:, c, tsl], wukv[:, c, n4 * 512:(n4 + 1) * 512], start=(c == 0), stop=(c == 1)),
                                    outs=[ptok], ins=[ckvtok, t_wukv])
                        psv = ps[:, :].rearrange("p (h d) -> p h d", h=2)
                        P.act.op(lambda: nc.scalar.copy(out=k_sb[:, 2 * n4:2 * n4 + 2, 0:128], in_=psv[:, :, 0:128]), outs=[ktok], ins=[ptok])
                        P.dve.op(lambda: nc.vector.tensor_copy(out=v_sb[:, 2 * n4:2 * n4 + 2, :], in_=psv[:, :, 128:256]), outs=[vtok], ins=[ptok, ktok])
                    ps, ptok = ps01.next()
                    for c in range(DC if "kr" in SUB else 0):
                        P.pe.op(lambda: nc.tensor.matmul(ps[:, 0:64], xt[:, c, tsl], wkr[:, c, :], start=(c == 0), stop=(c == DC - 1)),
                                outs=[ptok], ins=[xtok, t_wkr])
                    if "kr" in SUB:
                      P.act.op(lambda: nc.scalar.copy(out=k_sb[:, :, 128:192], in_=ps[:, 0:64].unsqueeze(1).to_broadcast([128, 8, 64])),
                             outs=[ktok], ins=[ptok])
                    if LV >= 3:
                        P.q_a.dma(self.Vd[tokb:tokb + 128, :], v_sb[:].rearrange("p h d -> p (h d)"), ins=[vtok])
                    if LV < 4:
                        continue
                    self.headnorm_rope(q_sb, qtok, gq, t_gq, cosT[:, jblk, :], sinT[:, jblk, :], t_cs, tmp, ttok, ss, sstok, t4, t4tok)
                    self.headnorm_rope(k_sb, ktok, gk, t_gq, cosT[:, jblk, :], sinT[:, jblk, :], t_cs, tmp, ttok, ss, sstok, t4, t4tok)
                    if LV < 5:
                        continue
                    for (src, stok_, dn, dntok, dr, drtok, dst) in ((q_sb, qtok, qTn, qTntok, qTr, qTrtok, self.QT),
                                                                  (k_sb, ktok, kTn, kTntok, kTr, kTrtok, self.KT)):
                        for hg in range(2):
                            ps, ptok = ps01.next()
                            for hl in range(4):
                                h = hg * 4 + hl
                                P.pe.op(lambda: nc.tensor.transpose(ps[:, hl * 128:(hl + 1) * 128], src[:, h, 0:128], self.ident[:]),
                                        outs=[ptok], ins=[stok_, self.t_ident])
                            P.act.op(lambda: nc.scalar.copy(out=dn[:, hg * 4:(hg + 1) * 4, :], in_=ps[:, :].rearrange("p (h t) -> p h t", h=4)),
                                     outs=[dntok], ins=[ptok])
                            ps, ptok = ps01.next()
                            for hl in range(4):
                                h = hg * 4 + hl
                                P.pe.op(lambda: nc.tensor.transpose(ps[0:64, hl * 128:(hl + 1) * 128], src[:, h, 128:192], self.ident[:]),
                                        outs=[ptok], ins=[stok_, self.t_ident])
                            P.dve.op(lambda: nc.vector.tensor_copy(out=dr[:, hg * 4:(hg + 1) * 4, :], in_=ps[0:64, :].rearrange("p (h t) -> p h t", h=4)),
                                     outs=[drtok], ins=[ptok])
                        P.q_a.dma(dst[:, 0:128, tokb:tokb + 128].rearrange("h p t -> p h t"), dn[:], ins=[dntok])
                        P.q_a.dma(dst[:, 128:192, tokb:tokb + 128].rearrange("h p t -> p h t"), dr[:], ins=[drtok])
        P.release(m)

    def even_attn(self, l):
        P = self.P; nc = P.nc
        NS, LP, NT = self.NS, self.LP, self.NT
        m = P.mark()
        maskT = P.sb([128, 128], "maskT"); t_mask = Tok()
        P.q_a.dma(maskT[:], self.c_maskT[:, :], outs=[t_mask])
        sets = Ring([dict(qn=P.sb([128, LP]), qr=P.sb([64, LP]), kn=P.sb([128, LP]), kr=P.sb([64, LP]), v=P.sb([128, NT, 128]), tok=Tok())
                     for _ in range(2)])
        pt_r = Ring([(P.sb([128, 512], "pt"), Tok()) for _ in range(3)])
        rden, rdtok = P.sb([128, 512], "rden"), Tok()
        yo_r = Ring([(P.sb([128, 512], "yo"), Tok()) for _ in range(2)])
        ps_s = self.psring([0, 1, 2])
        ps_o = self.psring([3, 4]); ps_d = self.psring([5, 6])
        scale = 192.0 ** -0.5
        for s in range(NS):
            for h in range(8):
                S = sets.next(); stok = S["tok"]
                c0s = s * LP
                P.q_a.dma(S["qn"][:], self.QT[h, 0:128, c0s:c0s + LP], outs=[stok])
                P.q_a.dma(S["qr"][:], self.QT[h, 128:192, c0s:c0s + LP], outs=[stok])
                P.q_a.dma(S["kn"][:], self.KT[h, 0:128, c0s:c0s + LP], outs=[stok])
                P.q_a.dma(S["kr"][:], self.KT[h, 128:192, c0s:c0s + LP], outs=[stok])
                P.q_a.dma(S["v"][:], self.Vd[c0s:c0s + LP, h * 128:(h + 1) * 128].rearrange("(b p) d -> p b d", p=128), outs=[stok])
                for q0 in range(0, LP, 512):
                    Tq = min(512, LP - q0)
                    nkb = (q0 + Tq) // 128
                    po, potok = ps_o.next(); pd, pdtok = ps_d.next()
                    for kb in range(nkb):
                        c0 = max(q0, kb * 128); off = c0 - q0; w = q0 + Tq - c0
                        ps, pstok = ps_s.next()
                        ksl = slice(kb * 128, (kb + 1) * 128)
                        P.pe.op(lambda: nc.tensor.matmul(ps[:, :w], S["kn"][:, ksl], S["qn"][:, c0:c0 + w], start=True, stop=False),
                                outs=[pstok], ins=[stok])
                        P.pe.op(lambda: nc.tensor.matmul(ps[:, :w], S["kr"][:, ksl], S["qr"][:, c0:c0 + w], start=False, stop=True),
                                outs=[pstok], ins=[stok])
                        pt, pttok = pt_r.next()
                        P.act.op(lambda: nc.scalar.activation(out=pt[:, :w], in_=ps[:, :w], func=AF.Exp, scale=scale), outs=[pttok], ins=[pstok])
                        if kb * 128 >= q0:
                            P.dve.op(lambda: nc.vector.tensor_tensor(out=pt[:, 0:128], in0=pt[:, 0:128], in1=maskT[:], op=ALU.mult),
                                     outs=[pttok], ins=[pttok, t_mask])
                        P.pe.op(lambda: nc.tensor.matmul(po[:, off:off + w], S["v"][:, kb, :], pt[:, :w], start=(kb == 0), stop=(kb == nkb - 1)),
                                outs=[potok], ins=[stok, pttok])
                        P.pe.op(lambda: nc.tensor.matmul(pd[:, off:off + w], self.ones[:], pt[:, :w], start=(kb == 0), stop=(kb == nkb - 1)),
                                outs=[pdtok], ins=[self.t_ones, pttok])
                    P.dve.op(lambda: nc.vector.reciprocal(out=rden[:, :Tq], in_=pd[:, :Tq]), outs=[rdtok], ins=[pdtok])
                    yo, yotok = yo_r.next()
                    P.dve.op(lambda: nc.vector.tensor_tensor(out=yo[:, :Tq], in0=po[:, :Tq], in1=rden[:, :Tq], op=ALU.mult),
                             outs=[yotok], ins=[potok, rdtok])
                    P.q_a.dma(self.yE[4 + h, :, c0s + q0:c0s + q0 + Tq], yo[:, :Tq], ins=[yotok])
        P.release(m)
    def even_s5(self, l):
        P = self.P; nc = P.nc; i = l // 2
        NS, LP, NT = self.NS, self.LP, self.NT
        m = P.mark()
        PI = math.pi
        dve = P.dve; act = P.act; pe = P.pe
        V = nc.vector
        tk = Tok("s5setup")
        def sbt(shape, name):
            return P.sb(shape, name)
        ps0, ps0tok = P.psb[0]
        raw = sbt([16, 3, 128], "s5raw")
        P.q_a.dma(raw[:, 0, :], self.w["s5_a_re"][i].rearrange("(gp gl) p -> gp (gl p)", gl=2), outs=[tk])
        P.q_a.dma(raw[:, 1, :], self.w["s5_a_im"][i].rearrange("(gp gl) p -> gp (gl p)", gl=2), outs=[tk])
        ldt = sbt([16, 2], "ldt")
        P.q_a.dma(ldt[:], self.w["s5_log_dt"][i].rearrange("(gp gl) -> gp gl", gl=2), outs=[tk])
        dve.op(lambda: V.tensor_copy(out=raw[:, 2, :].rearrange("g (a b) -> g a b", a=2), in_=ldt[:].unsqueeze(2).to_broadcast([16, 2, 64])),
               outs=[tk], ins=[tk])
        ar = sbt([128, 16], "ar"); ai = sbt([128, 16], "ai"); dt = sbt([128, 16], "dt")
        for k, dst in enumerate((ar, ai, dt)):
            pe.op(lambda: nc.tensor.transpose(ps0[:, k * 16:(k + 1) * 16], raw[:, k, :], self.ident[0:16, 0:16]), outs=[ps0tok], ins=[tk, self.t_ident])
        act.op(lambda: nc.scalar.copy(out=ar[:], in_=ps0[:, 0:16]), outs=[tk], ins=[ps0tok])
        act.op(lambda: nc.scalar.copy(out=ai[:], in_=ps0[:, 16:32]), outs=[tk], ins=[ps0tok])
        act.op(lambda: nc.scalar.activation(out=dt[:], in_=ps0[:, 32:48], func=AF.Exp), outs=[tk], ins=[ps0tok])
        def T16(name):
            return sbt([128, 16], name)
        def tt(out, a, b, op):
            dve.op(lambda: V.tensor_tensor(out=out, in0=a, in1=b, op=op), outs=[tk], ins=[tk])
        def ts(out, a, s1, op0, s2=None, op1=None):
            if op1 is None:
                dve.op(lambda: V.tensor_scalar(out, a, s1, None, op0), outs=[tk], ins=[tk])
            else:
                dve.op(lambda: V.tensor_scalar(out, a, s1, s2, op0, op1), outs=[tk], ins=[tk])
        mag = T16("mag"); ang = T16("ang"); sn = T16("sn"); cs = T16("cs"); tmpa = T16("tmpa"); tmpb = T16("tmpb")
        twopi = T16("twopi")
        dve.op(lambda: V.memset(twopi[:], 2 * PI), outs=[tk], ins=[tk])
        tt(tmpa[:], dt[:], ar[:], ALU.mult)
        act.op(lambda: nc.scalar.activation(out=mag[:], in_=tmpa[:], func=AF.Exp), outs=[tk], ins=[tk])
        tt(ang[:], dt[:], ai[:], ALU.mult)
        for (dst, shift) in ((sn, PI), (cs, PI + PI / 2)):
            ts(tmpa[:], ang[:], shift, ALU.add)
            for mult_ in (16.0, 8.0, 4.0, 2.0, 1.0):
                ts(tmpb[:], tmpa[:], mult_ * 2 * PI, ALU.is_ge, -mult_ * 2 * PI, ALU.mult)
                tt(tmpa[:], tmpa[:], tmpb[:], ALU.add)
            ts(tmpa[:], tmpa[:], -PI, ALU.add, 3.1415925, ALU.min)
            ts(tmpa[:], tmpa[:], -3.1415925, ALU.max)
            act.op(lambda: nc.scalar.activation(out=dst[:], in_=tmpa[:], func=AF.Sin), outs=[tk], ins=[tk])
        lr = T16("lr"); li = T16("li"); lr1 = T16("lr1"); den = T16("den"); cr = T16("cr"); ci = T16("ci")
        tt(lr[:], mag[:], cs[:], ALU.mult); tt(li[:], mag[:], sn[:], ALU.mult)
        ts(lr1[:], lr[:], -1.0, ALU.add)
        tt(den[:], ar[:], ar[:], ALU.mult); tt(tmpa[:], ai[:], ai[:], ALU.mult); tt(den[:], den[:], tmpa[:], ALU.add)
        dve.op(lambda: V.reciprocal(out=den[:], in_=den[:]), outs=[tk], ins=[tk])
        tt(cr[:], lr1[:], ar[:], ALU.mult); tt(tmpa[:], li[:], ai[:], ALU.mult); tt(cr[:], cr[:], tmpa[:], ALU.add); tt(cr[:], cr[:], den[:], ALU.mult)
        tt(ci[:], li[:], ar[:], ALU.mult); tt(tmpa[:], lr1[:], ai[:], ALU.mult); tt(ci[:], ci[:], tmpa[:], ALU.subtract); tt(ci[:], ci[:], den[:], ALU.mult)
        br = sbt([128, 16, 16], "br"); bi = sbt([128, 16, 16], "bi")
        P.q_a.dma(br[:], self.w["s5_b_re"][i].rearrange("(gp gl) p c -> (gl p) gp c", gl=2), outs=[tk], ins=[tk])
        P.q_a.dma(bi[:], self.w["s5_b_im"][i].rearrange("(gp gl) p c -> (gl p) gp c", gl=2), outs=[tk], ins=[tk])
        t3a = sbt([128, 16, 16], "t3a"); t3b = sbt([128, 16, 16], "t3b")
        Bblk = [sbt([128, 16, 32], "Bblk_r"), sbt([128, 16, 32], "Bblk_i")]
        crb = cr[:].unsqueeze(2).to_broadcast([128, 16, 16]); cib = ci[:].unsqueeze(2).to_broadcast([128, 16, 16])
        for k in range(2):
            dve.op(lambda: V.memset(Bblk[k][:], 0.0), outs=[tk], ins=[tk])
        for k, (x0, x1, op) in enumerate(((br, bi, ALU.subtract), (bi, br, ALU.add))):
            tt(t3a[:], x0[:], crb, ALU.mult); tt(t3b[:], x1[:], cib, ALU.mult); tt(t3a[:], t3a[:], t3b[:], op)
            dve.op(lambda: V.tensor_copy(out=Bblk[k][0:64, :, 0:16], in_=t3a[0:64]), outs=[tk], ins=[tk])
            dve.op(lambda: V.tensor_copy(out=Bblk[k][64:128, :, 16:32], in_=t3a[64:128]), outs=[tk], ins=[tk])
        WbT = [sbt([32, 16, 128], "WbT_r"), sbt([32, 16, 128], "WbT_i")]
        for k in range(2):
            for g4 in range(4):
                for gq in range(4):
                    gp = g4 * 4 + gq
                    pe.op(lambda: nc.tensor.transpose(ps0[0:32, gq * 128:(gq + 1) * 128], Bblk[k][:, gp, :], self.ident[:]), outs=[ps0tok], ins=[tk, self.t_ident])
                act.op(lambda: nc.scalar.copy(out=WbT[k][:, g4 * 4:(g4 + 1) * 4, :], in_=ps0[0:32, :].rearrange("p (g s) -> p g s", g=4)), outs=[tk], ins=[ps0tok])
        Cblk = [sbt([128, 16, 32], "Cblk_r"), sbt([128, 16, 32], "Cblk_i")]
        craw = sbt([128, 128], "craw")
        for k, nm in enumerate(("s5_c_re", "s5_c_im")):
            dve.op(lambda: V.memset(Cblk[k][:], 0.0), outs=[tk], ins=[tk])
            cv = self.w[nm][i].rearrange("g c p -> (g c) p")
            for q4 in range(4):
                P.q_a.dma(craw[:, 0:64], cv[q4 * 128:(q4 + 1) * 128, :], outs=[tk], ins=[tk])
                P.q_a.dma(craw[:, 64:128], cv[q4 * 128:(q4 + 1) * 128, :], outs=[tk], ins=[tk])
                pe.op(lambda: nc.tensor.transpose(ps0[:, 0:128], craw[:], self.ident[:]), outs=[ps0tok], ins=[tk, self.t_ident])
                psv = ps0[:, 0:128].rearrange("p (gq gl c) -> p gq gl c", gq=4, gl=2)
                sc = 1.0 if k == 0 else -1.0
                act.op(lambda: nc.scalar.mul(out=Cblk[k][0:64, q4 * 4:(q4 + 1) * 4, 0:16], in_=psv[0:64, :, 0, :], mul=sc), outs=[tk], ins=[ps0tok])
                act.op(lambda: nc.scalar.mul(out=Cblk[k][64:128, q4 * 4:(q4 + 1) * 4, 16:32], in_=psv[64:128, :, 1, :], mul=sc), outs=[tk], ins=[ps0tok])
        dsk = sbt([32, 16], "dsk")
        P.q_a.dma(dsk[:], self.w["s5_d"][i].rearrange("(gp r) -> r gp", r=32), outs=[tk], ins=[tk], allow_slow_non_contiguous=True)
        NK = max(1, (LP - 1).bit_length())
        wr = sbt([128, NK, 16], "wr"); wi = sbt([128, NK, 16], "wi")
        dve.op(lambda: V.tensor_copy(out=wr[:, 0, :], in_=cs[:]), outs=[tk], ins=[tk])
        ts(wi[:, 0, :], sn[:], -1.0, ALU.mult)
        for k in range(1, NK):
            tt(tmpa[:], wr[:, k - 1, :], wr[:, k - 1, :], ALU.mult); tt(tmpb[:], wi[:, k - 1, :], wi[:, k - 1, :], ALU.mult)
            tt(wr[:, k, :], tmpa[:], tmpb[:], ALU.subtract)
            tt(tmpa[:], wr[:, k - 1, :], wi[:, k - 1, :], ALU.mult)
            ts(wi[:, k, :], tmpa[:], 2.0, ALU.mult)
        Er = sbt([128, LP], "Er"); Ei = sbt([128, LP], "Ei"); tmpE = sbt([128, LP], "tmpE"); tE = Tok("E")
        u_r = Ring([(sbt([32, LP], "u_gp"), Tok()) for _ in range(2)])
        vr = sbt([128, LP], "vr"); vi = sbt([128, LP], "vi"); tv = Tok("v")
        zr = sbt([128, LP], "zr"); zi = sbt([128, LP], "zi"); tz = Tok("z")
        ta = sbt([128, 512], "ta"); tb_ = sbt([128, 512], "tb"); tta = Tok(); ttb = Tok()
        y_r = Ring([(sbt([32, LP], "y_gp"), Tok()) for _ in range(2)])
        ps_b = self.psring([1, 2, 3, 4]); ps_y = self.psring([5, 6])
        for gp in range(16):
            dve.op(lambda: V.memset(Er[:, 0:1], 1.0), outs=[tE], ins=[tk])
            dve.op(lambda: V.memset(Ei[:, 0:1], 0.0), outs=[tE], ins=[tk])
            n = 1; k = 0
            while n < LP:
                cnt = min(n, LP - n)
                wrk = wr[:, k, gp:gp + 1]; wik = wi[:, k, gp:gp + 1]
                dve.op(lambda: V.tensor_scalar(tmpE[:, 0:cnt], Ei[:, 0:cnt], wik, None, ALU.mult), outs=[tE], ins=[tE, tk])
                dve.op(lambda: V.scalar_tensor_tensor(out=Er[:, n:n + cnt], in0=Er[:, 0:cnt], scalar=wrk, in1=tmpE[:, 0:cnt], op0=ALU.mult, op1=ALU.subtract), outs=[tE], ins=[tE, tk])
                dve.op(lambda: V.tensor_scalar(tmpE[:, 0:cnt], Ei[:, 0:cnt], wrk, None, ALU.mult), outs=[tE], ins=[tE, tk])
                dve.op(lambda: V.scalar_tensor_tensor(out=Ei[:, n:n + cnt], in0=Er[:, 0:cnt], scalar=wik, in1=tmpE[:, 0:cnt], op0=ALU.mult, op1=ALU.add), outs=[tE], ins=[tE, tk])
                n += cnt; k += 1
            for s in range(NS):
                c0s = s * LP
                u, utok = u_r.next()
                P.q_a.dma(u[:], self.uT[gp // 4, (gp % 4) * 32:(gp % 4) * 32 + 32, c0s:c0s + LP], outs=[utok])
                for q0 in range(0, LP, 512):
                    Tq = min(512, LP - q0); sl = slice(q0, q0 + Tq)
                    pbr, pbrt = ps_b.next(); pbi, pbit = ps_b.next()
                    pe.op(lambda: nc.tensor.matmul(pbr[:, :Tq], WbT[0][:, gp, :], u[:, sl], start=True, stop=True), outs=[pbrt], ins=[tk, utok])
                    pe.op(lambda: nc.tensor.matmul(pbi[:, :Tq], WbT[1][:, gp, :], u[:, sl], start=True, stop=True), outs=[pbit], ins=[tk, utok])
                    dve.op(lambda: V.tensor_tensor(out=ta[:, :Tq], in0=pbr[:, :Tq], in1=Er[:, sl], op=ALU.mult), outs=[tta], ins=[pbrt, tE])
                    dve.op(lambda: V.tensor_tensor(out=tb_[:, :Tq], in0=pbi[:, :Tq], in1=Ei[:, sl], op=ALU.mult), outs=[ttb], ins=[pbit, tE])
                    dve.op(lambda: V.tensor_tensor(out=vr[:, sl], in0=ta[:, :Tq], in1=tb_[:, :Tq], op=ALU.subtract), outs=[tv], ins=[tta, ttb])
                    dve.op(lambda: V.tensor_tensor(out=ta[:, :Tq], in0=pbi[:, :Tq], in1=Er[:, sl], op=ALU.mult), outs=[tta], ins=[pbit, tE])
                    dve.op(lambda: V.tensor_tensor(out=tb_[:, :Tq], in0=pbr[:, :Tq], in1=Ei[:, sl], op=ALU.mult), outs=[ttb], ins=[pbrt, tE])
                    dve.op(lambda: V.tensor_tensor(out=vi[:, sl], in0=ta[:, :Tq], in1=tb_[:, :Tq], op=ALU.add), outs=[tv], ins=[tta, ttb])
                rho = mag[:, gp:gp + 1].to_broadcast([128, LP])
                dve.op(lambda: V.tensor_tensor_scan(out=zr[:], data0=rho, data1=vr[:], initial=0.0, op0=ALU.mult, op1=ALU.add), outs=[tz], ins=[tv, tk])
                dve.op(lambda: V.tensor_tensor_scan(out=zi[:], data0=rho, data1=vi[:], initial=0.0, op0=ALU.mult, op1=ALU.add), outs=[tz], ins=[tv, tk])
                dve.op(lambda: V.tensor_tensor(out=vr[:], in0=Er[:], in1=zr[:], op=ALU.mult), outs=[tv], ins=[tz, tE])
                dve.op(lambda: V.tensor_tensor(out=tmpE[:], in0=Ei[:], in1=zi[:], op=ALU.mult), outs=[tv], ins=[tz, tE])
                dve.op(lambda: V.tensor_tensor(out=vr[:], in0=vr[:], in1=tmpE[:], op=ALU.add), outs=[tv], ins=[tv])
                dve.op(lambda: V.tensor_tensor(out=vi[:], in0=Er[:], in1=zi[:], op=ALU.mult), outs=[tv], ins=[tz, tE])
                dve.op(lambda: V.tensor_tensor(out=tmpE[:], in0=Ei[:], in1=zr[:], op=ALU.mult), outs=[tv], ins=[tz, tE])
                dve.op(lambda: V.tensor_tensor(out=vi[:], in0=vi[:], in1=tmpE[:], op=ALU.subtract), outs=[tv], ins=[tv])
                y, ytok = y_r.next()
                for q0 in range(0, LP, 512):
                    Tq = min(512, LP - q0); sl = slice(q0, q0 + Tq)
                    py, pyt = ps_y.next()
                    pe.op(lambda: nc.tensor.matmul(py[0:32, :Tq], Cblk[0][:, gp, :], vr[:, sl], start=True, stop=False), outs=[pyt], ins=[tk, tv])
                    pe.op(lambda: nc.tensor.matmul(py[0:32, :Tq], Cblk[1][:, gp, :], vi[:, sl], start=False, stop=True), outs=[pyt], ins=[tk, tv])
                    dve.op(lambda: V.scalar_tensor_tensor(out=y[:, sl], in0=u[:, sl], scalar=dsk[:, gp:gp + 1], in1=py[0:32, :Tq], op0=ALU.mult, op1=ALU.add),
                           outs=[ytok], ins=[pyt, utok, tk])
                P.q_a.dma(self.yE[gp // 4, (gp % 4) * 32:(gp % 4) * 32 + 32, c0s:c0s + LP], y[:], ins=[ytok])
        P.release(m)

    def out_proj(self, Wd, NCH, ymix, ytoks_fn, T, tok0, htoks, w_r, psd_r, hres_r):
        P = self.P; nc = P.nc
        Wv = Wd.rearrange("(c p) d -> p c d", p=128)
        def load_d(dc):
            d_, dt_ = w_r.next()
            P.q_w.dma(d_[:, :NCH, :], Wv[:, :, dc * 128:(dc + 1) * 128], outs=[dt_])
            return d_, dt_
        nxt = load_d(0)
        for dc in range(DC):
            d_, dt_ = nxt
            if dc + 1 < DC:
                nxt = load_d(dc + 1)
            hres, hrt = hres_r.next()
            P.q_a.dma(hres[:, :T], self.hT[dc, :, tok0:tok0 + T], outs=[hrt], ins=htoks)
            psd, psdt = psd_r.next()
            for c in range(NCH):
                P.pe.op(lambda: nc.tensor.matmul(psd[:, :T], d_[:, c, :], ymix[:, c, :T], start=(c == 0), stop=(c == NCH - 1)),
                        outs=[psdt], ins=[dt_] + ytoks_fn(c))
            P.dve.op(lambda: nc.vector.tensor_tensor(out=hres[:, :T], in0=psd[:, :T], in1=hres[:, :T], op=ALU.add),
                     outs=[hrt], ins=[psdt, hrt])
            P.q_a.dma(self.hT[dc, :, tok0:tok0 + T], hres[:, :T], ins=[hrt], outs=htoks)

    def even_out(self, l):
        P = self.P; nc = P.nc; i = l // 2
        NS, LP, NT, NTOK = self.NS, self.LP, self.NT, self.NTOK
        m = P.mark()
        TT = 512
        V = nc.vector
        wglu = P.sb([128, 4, 512], "wglu"); t_wglu = Tok()
        P.q_w.dma(wglu[:], self.w["s5_w_glu"][i].rearrange("(c p) f -> p c f", p=128), outs=[t_wglu])
        ya, yatok = P.sb([128, 4, TT], "ya"), Tok()
        yg, ygtok = P.sb([128, 4, TT], "yg"), Tok()
        ymix = P.sb([128, 12, TT], "ymix"); ymtok = Tok(); ybtok = Tok()
        sg_r = Ring([(P.sb([128, TT], "sg"), Tok()) for _ in range(2)])
        w_r = Ring([(P.sb([128, 12, 128], "wo"), Tok()) for _ in range(2)])
        hres_r = Ring([(P.sb([128, TT], "hres"), Tok()) for _ in range(2)])
        psg_r = self.psring([0, 1]); psd_r = self.psring([2, 3])
        C0 = math.sqrt(2.0 / math.pi)
        for t0 in range(0, NTOK, TT):
            T = min(TT, NTOK - t0)
            htoks = self.toks_h(t0, T)
            P.q_a.dma(ya[:, :, :T], self.yE[0:4, :, t0:t0 + T].rearrange("c p t -> p c t"), outs=[yatok])
            P.q_a.dma(ymix[:, 4:12, :T], self.yE[4:12, :, t0:t0 + T].rearrange("c p t -> p c t"), outs=[ybtok])
            P.dve.op(lambda: V.tensor_tensor(out=yg[:, :, :T], in0=ya[:, :, :T], in1=ya[:, :, :T], op=ALU.mult), outs=[ygtok], ins=[yatok])
            P.dve.op(lambda: V.tensor_scalar(yg[:, :, :T], yg[:, :, :T], 0.044715, 1.0, ALU.mult, ALU.add), outs=[ygtok], ins=[ygtok])
            P.dve.op(lambda: V.tensor_tensor(out=yg[:, :, :T], in0=yg[:, :, :T], in1=ya[:, :, :T], op=ALU.mult), outs=[ygtok], ins=[ygtok, yatok])
            P.act.op(lambda: nc.scalar.activation(out=yg[:, :, :T], in_=yg[:, :, :T], func=AF.Tanh, scale=C0), outs=[ygtok], ins=[ygtok])
            P.dve.op(lambda: V.tensor_scalar(yg[:, :, :T], yg[:, :, :T], 1.0, 0.5, ALU.add, ALU.mult), outs=[ygtok], ins=[ygtok])
            P.dve.op(lambda: V.tensor_tensor(out=yg[:, :, :T], in0=yg[:, :, :T], in1=ya[:, :, :T], op=ALU.mult), outs=[ygtok], ins=[ygtok, yatok])
            for jc in range(4):
                ps, ptok = psg_r.next()
                for c in range(4):
                    P.pe.op(lambda: nc.tensor.matmul(ps[:, :T], wglu[:, c, jc * 128:(jc + 1) * 128], yg[:, c, :T], start=(c == 0), stop=(c == 3)),
                            outs=[ptok], ins=[t_wglu, ygtok])
                sg, sgt = sg_r.next()
                P.act.op(lambda: nc.scalar.activation(out=sg[:, :T], in_=ps[:, :T], func=AF.Sigmoid), outs=[sgt], ins=[ptok])
                P.dve.op(lambda: V.tensor_tensor(out=ymix[:, jc, :T], in0=yg[:, jc, :T], in1=sg[:, :T], op=ALU.mult), outs=[ymtok], ins=[ygtok, sgt])
            self.out_proj(self.w["ev_w_out"][i], 12, ymix, lambda c: [ymtok, ybtok], T, t0, htoks, w_r, psd_r, hres_r)
        P.release(m)


    def odd_mixer(self, l):
        import os
        st = os.environ.get("DBG_STAGES", "inproj,attn,gdn,out").split(",")
        if "inproj" in st:
            self.odd_inproj(l)
        if "attn" in st:
            self.odd_sb(l)
        if "gdn" in st:
            self.odd_gdn(l)
        if "out" in st:
            self.odd_out(l)

    def odd_inproj(self, l):
        P = self.P; nc = P.nc; i = l // 2
        NS, LP, NT = self.NS, self.LP, self.NT
        m = P.mark()
        TT = 512
        Wv = self.w["od_w_in"][i].rearrange("(c p) f -> p c f", p=128)
        gain, gtok = self.gains["norm_mix"]
        xt, xtok = P.sb([128, DC, TT], "odx"), Tok()
        win_r = Ring([(P.sb([128, DC, 128], "win"), Tok()) for _ in range(3)])
        wv_r = Ring([(P.sb([128, DC, 512], "wsv"), Tok()) for _ in range(1)])
        wab = P.sb([128, DC, 16], "wab"); t_wab = Tok()
        P.q_w.dma(wab[:], Wv[:, :, 6144:6160], outs=[t_wab])
        ob_r = Ring([(P.sb([128, TT], "ob"), Tok()) for _ in range(3)])
        rstd, rtok = P.sb([128, TT], "rstd"), Tok()
        sq, sqtok = P.sb([128, TT], "sq"), Tok()
        ps01 = self.psring([0, 1, 2]); psv_r = self.psring([3, 4])
        pss, psstok = P.psb[7]
        plan = []
        for h in range(8):
            plan.append((self.sbQ, h, h * 128))
        for h in range(8):
            plan.append((self.sbK, h, 1024 + h * 128))
        for c in range(24):
            plan.append((self.gT, c, 3072 + c * 128))
        for h in range(8):
            plan.append((self.gzT, h, 6160 + h * 128))
        for s in range(NS):
            for t0 in range(0, LP, TT):
                T = min(TT, LP - t0)
                tok0 = s * LP + t0
                htoks = self.toks_h(tok0, T)
                P.q_a.dma(xt[:, :, :T], self.hT[:, :, tok0:tok0 + T].rearrange("c p t -> p c t"), outs=[xtok], ins=htoks)
                self.rms_scale(xt, xtok, T, lambda c: gain[:, l, c:c + 1], gtok, pss, psstok, sq, sqtok, rstd, rtok)
                def load_w(j):
                    w_, wt_ = win_r.next()
                    c0 = plan[j][2]
                    P.q_w.dma(w_[:], Wv[:, :, c0:c0 + 128], outs=[wt_])
                    return w_, wt_
                pend = [load_w(0), load_w(1)]
                for j in range(len(plan)):
                    w_, wt_ = pend.pop(0)
                    if j + 2 < len(plan):
                        pend.append(load_w(j + 2))
                    ps, ptok = ps01.next()
                    for c in range(DC):
                        P.pe.op(lambda: nc.tensor.matmul(ps[:, :T], w_[:, c, :], xt[:, c, :T], start=(c == 0), stop=(c == DC - 1)),
                                outs=[ptok], ins=[wt_, xtok])
                    ob, obtok = ob_r.next()
                    if j % 2 == 0:
                        P.act.op(lambda: nc.scalar.copy(out=ob[:, :T], in_=ps[:, :T]), outs=[obtok], ins=[ptok])
                    else:
                        P.dve.op(lambda: nc.vector.tensor_copy(out=ob[:, :T], in_=ps[:, :T]), outs=[obtok], ins=[ptok])
                    P.q_a.dma(plan[j][0][plan[j][1], :, tok0:tok0 + T], ob[:, :T], ins=[obtok])
                ps, ptok = ps01.next()
                for c in range(DC):
                    P.pe.op(lambda: nc.tensor.matmul(ps[0:16, :T], wab[:, c, :], xt[:, c, :T], start=(c == 0), stop=(c == DC - 1)),
                            outs=[ptok], ins=[t_wab, xtok])
                ob, obtok = ob_r.next()
                P.act.op(lambda: nc.scalar.copy(out=ob[0:16, :T], in_=ps[0:16, :T]), outs=[obtok], ins=[ptok])
                P.q_a.dma(self.gabT[:, tok0:tok0 + T], ob[0:16, :T], ins=[obtok])
                for n in range(2):
                    wv, wvt = wv_r.next()
                    P.q_w.dma(wv[:], Wv[:, :, 2048 + n * 512:2048 + (n + 1) * 512], outs=[wvt])
                    for tb in range(T // 128):
                        ps, ptok = psv_r.next()
                        for c in range(DC):
                            P.pe.op(lambda: nc.tensor.matmul(ps[:, :], xt[:, c, tb * 128:(tb + 1) * 128], wv[:, c, :], start=(c == 0), stop=(c == DC - 1)),
                                    outs=[ptok], ins=[wvt, xtok])
                        ob, obtok = ob_r.next()
                        P.act.op(lambda: nc.scalar.copy(out=ob[:, :], in_=ps[:, :]), outs=[obtok], ins=[ptok])
                        P.q_a.dma(self.sbV[tok0 + tb * 128:tok0 + (tb + 1) * 128, n * 512:(n + 1) * 512], ob[:, :], ins=[obtok])
        P.release(m)

    def load_const(self, dram_ap, shape, name):
        t = self.P.sb(shape, name); tok = Tok(name)
        self.P.q_a.dma(t[:], dram_ap, outs=[tok])
        return t, tok

    def odd_sb(self, l):
        P = self.P; nc = P.nc
        NS, LP, NT = self.NS, self.LP, self.NT
        m = P.mark()
        V = nc.vector
        maskS, t_ms = self.load_const(self.c_maskS[:, :], [128, 128], "maskS")
        Uincl, t_ui = self.load_const(self.c_maskL[:, :], [128, 128], "Uincl")
        zeros = P.sb([128, 128], "zeros"); t_z = Tok()
        P.dve.op(lambda: V.memset(zeros[:], 0.0), outs=[t_z])
        sets = Ring([dict(q=P.sb([128, LP]), k=P.sb([128, LP]), nk=P.sb([128, LP]), v=P.sb([128, NT, 128]), tok=Tok()) for _ in range(2)])
        ez_r = Ring([(P.sb([128, 512], "ez"), Tok()) for _ in range(2)])
        sp_r = Ring([(P.sb([128, 512], "sp"), Tok()) for _ in range(2)])
        wt_r = Ring([(P.sb([128, 512], "wt"), Tok()) for _ in range(2)])
        A, Atok = P.sb([128, 512], "A"), Tok()
        yo_r = Ring([(P.sb([128, 512], "yo"), Tok()) for _ in range(2)])
        ps_z = self.psring([0, 1]); ps_e = self.psring([2, 3]); ps_o = self.psring([4, 5])
        scale = 128.0 ** -0.5
        for s in range(NS):
            for h in range(8):
                S = sets.next(); stok = S["tok"]
                c0s = s * LP
                P.q_a.dma(S["q"][:], self.sbQ[h, :, c0s:c0s + LP], outs=[stok])
                P.q_a.dma(S["k"][:], self.sbK[h, :, c0s:c0s + LP], outs=[stok])
                P.q_a.dma(S["v"][:], self.sbV[c0s:c0s + LP, h * 128:(h + 1) * 128].rearrange("(b p) d -> p b d", p=128), outs=[stok])
                P.act.op(lambda: nc.scalar.mul(out=S["q"][:], in_=S["q"][:], mul=scale), outs=[stok], ins=[stok])
                P.act.op(lambda: nc.scalar.mul(out=S["nk"][:], in_=S["k"][:], mul=-1.0), outs=[stok], ins=[stok])
                for q0 in range(0, LP, 512):
                    Tq = min(512, LP - q0)
                    nkb = (q0 + Tq) // 128
                    po, potok = ps_o.next()
                    P.pe.op(lambda: nc.tensor.matmul(po[:, :Tq], zeros[:], S["q"][:, q0:q0 + Tq], start=True, stop=False),
                            outs=[potok], ins=[t_z, stok])
                    P.dve.op(lambda: V.memset(A[:, :Tq], 0.0), outs=[Atok])
                    for kb in range(nkb - 1, -1, -1):
                        c0 = max(q0, kb * 128); off = c0 - q0; w = q0 + Tq - c0
                        diag = kb * 128 >= q0
                        ksl = slice(kb * 128, (kb + 1) * 128)
                        pz, pztok = ps_z.next()
                        P.pe.op(lambda: nc.tensor.matmul(pz[:, :w], S["k"][:, ksl], S["q"][:, c0:c0 + w], start=True, stop=True),
                                outs=[pztok], ins=[stok])
                        ez, eztok = ez_r.next()
                        P.act.op(lambda: nc.scalar.activation(out=ez[:, :w], in_=pz[:, :w], func=AF.Exp), outs=[eztok], ins=[pztok])
                        sp, sptok = sp_r.next()
                        P.act.op(lambda: nc.scalar.activation(out=sp[:, :w], in_=ez[:, :w], func=AF.Ln, bias=self.ones[:, 0:1]),
                                 outs=[sptok], ins=[eztok, self.t_ones])
                        if diag:
                            P.dve.op(lambda: V.tensor_tensor(out=sp[:, 0:128], in0=sp[:, 0:128], in1=maskS[:], op=ALU.mult),
                                     outs=[sptok], ins=[sptok, t_ms])
                        pe_, petok = ps_e.next()
                        P.pe.op(lambda: nc.tensor.matmul(pe_[:, :w], Uincl[:], sp[:, :w], start=True, stop=False), outs=[petok], ins=[t_ui, sptok])
                        if kb < nkb - 1:
                            P.pe.op(lambda: nc.tensor.matmul(pe_[:, :w], self.ones[:], A[:, off:off + w], start=False, stop=False),
                                    outs=[petok], ins=[self.t_ones, Atok])
                        P.pe.op(lambda: nc.tensor.matmul(pe_[:, :w], S["nk"][:, ksl], S["q"][:, c0:c0 + w], start=False, stop=True),
                                outs=[petok], ins=[stok])
                        wt, wttok = wt_r.next()
                        P.act.op(lambda: nc.scalar.activation(out=wt[:, :w], in_=pe_[:, :w], func=AF.Exp, scale=-1.0), outs=[wttok], ins=[petok])
                        if diag:
                            P.dve.op(lambda: V.tensor_tensor(out=wt[:, 0:128], in0=wt[:, 0:128], in1=maskS[:], op=ALU.mult),
                                     outs=[wttok], ins=[wttok, t_ms])
                        P.pe.op(lambda: nc.tensor.matmul(po[:, off:off + w], S["v"][:, kb, :], wt[:, :w], start=False, stop=(kb == 0)),
                                outs=[potok], ins=[stok, wttok])
                        if kb > 0:
                            P.dve.op(lambda: V.tensor_tensor(out=A[:, off:off + w], in0=A[:, off:off + w], in1=sp[:, :w], op=ALU.add),
                                     outs=[Atok], ins=[Atok, sptok])
                    yo, yotok = yo_r.next()
                    P.act.op(lambda: nc.scalar.copy(out=yo[:, :Tq], in_=po[:, :Tq]), outs=[yotok], ins=[potok])
                    P.q_a.dma(self.yO[h, :, c0s + q0:c0s + q0 + Tq], yo[:, :Tq], ins=[yotok])
        P.release(m)

    def odd_out(self, l):
        P = self.P; nc = P.nc; i = l // 2
        NTOK = self.NTOK
        m = P.mark()
        TT = 512
        ymix = P.sb([128, 16, TT], "ymix"); ytok = Tok()
        w_r = Ring([(P.sb([128, 16, 128], "wo"), Tok()) for _ in range(2)])
        hres_r = Ring([(P.sb([128, TT], "hres"), Tok()) for _ in range(2)])
        psd_r = self.psring([2, 3])
        for t0 in range(0, NTOK, TT):
            T = min(TT, NTOK - t0)
            htoks = self.toks_h(t0, T)
            P.q_a.dma(ymix[:, :, :T], self.yO[:, :, t0:t0 + T].rearrange("c p t -> p c t"), outs=[ytok])
            self.out_proj(self.w["od_w_out"][i], 16, ymix, lambda c: [ytok], T, t0, htoks, w_r, psd_r, hres_r)
        P.release(m)
    def odd_gdn(self, l):
        P = self.P; nc = P.nc; i = l // 2
        NS, LP, NT = self.NS, self.LP, self.NT
        m = P.mark()
        V = nc.vector
        dve = P.dve; act = P.act; pe = P.pe
        maskI, t_mi = self.load_const(self.c_maskT[:, :], [128, 128], "maskI")
        maskSt, t_mst = self.load_const(self.c_maskS[:, :], [128, 128], "maskSt")
        sel, t_sel = self.load_const(self.c_sel[:, :], [8, 1024], "sel")
        tk = Tok("gdnsetup")
        cw = P.sb([128, 24, 4], "cw")
        for j in range(4):
            P.q_a.dma(cw[:, :, j], self.w["gdn_conv"][i, j].rearrange("(c p) -> p c", p=128), outs=[tk], allow_slow_non_contiguous=True)
        nA = P.sb([8, 1], "nA"); dtb = P.sb([8, 1], "dtb"); gout = P.sb([128, 1], "gout")
        P.q_a.dma(nA[:], self.w["gdn_a_log"][i].rearrange("(p o) -> p o", o=1), outs=[tk])
        P.q_a.dma(dtb[:], self.w["gdn_dt_bias"][i].rearrange("(p o) -> p o", o=1), outs=[tk])
        P.q_a.dma(gout[:], self.w["gdn_g_out"][i].rearrange("(p o) -> p o", o=1), outs=[tk])
        act.op(lambda: nc.scalar.activation(out=nA[:], in_=nA[:], func=AF.Exp), outs=[tk], ins=[tk])
        dve.op(lambda: V.tensor_scalar(nA[:], nA[:], -1.0, None, ALU.mult), outs=[tk], ins=[tk])
        def big(name):
            return P.sb([128, LP], name)
        g_ = big("g"); be = big("beta"); Gc = big("Gc"); tg = Tok("g")
        cols = P.sb([128, NT, 16], "cols"); tcols = Tok("cols")
        qT = big("qT"); kT = big("kT"); vT = big("vT"); qdT = big("qdT"); cv = big("cv")
        tq = Tok("q"); tkk = Tok("k"); tv = Tok("v"); tqd = Tok("qd"); tcv = Tok("cv")
        Grow = big("Grow"); eG = big("eG"); Brow = big("Brow"); tGr = Tok("Grow")
        k_tok = P.sb([128, NT, 128], "k_tok"); v_tok = P.sb([128, NT, 128], "v_tok"); tkv = Tok("kvtok")
        u_tok = P.sb([128, NT, 128], "u_tok"); wT = P.sb([128, NT, 128], "wT"); attnT = P.sb([128, NT, 128], "attnT"); kd_tok = P.sb([128, NT, 128], "kd_tok")
        tpre = [Tok("pre%d" % c) for c in range(NT)]
        zT = big("zT"); tz = Tok("z")
        Mp = [P.sb([128, 128], "Mp%d" % k) for k in range(7)]; Np = [P.sb([128, 128], "Np%d" % k) for k in range(6)]
        tM = [Tok() for _ in range(7)]; tN = [Tok() for _ in range(6)]
        Y_r = Ring([(P.sb([128, 128], "Y"), Tok()) for _ in range(2)])
        dT = P.sb([128, 128], "dT"); tdT = Tok(); dI = P.sb([128, 128], "dI"); tdI = Tok(); aa = P.sb([128, 128], "aa"); taa = Tok()
        Xu = P.sb([128, 128], "Xu"); Xw = P.sb([128, 128], "Xw"); tX = Tok()
        sc = P.sb([128, 4], "sc"); tsc = Tok()
        Sst = P.sb([128, 128], "S"); tS = Tok("S")
        vn = P.sb([128, 128], "vn"); tvn = Tok()
        rs = P.sb([128, 512], "rs"); trs = Tok()
        b0, tb0 = P.psb[0]; b1, tb1 = P.psb[1]; b2, tb2 = P.psb[2]; b3, tb3 = P.psb[3]
        b4, tb4 = P.psb[4]; b5, tb5 = P.psb[5]; b6, tb6 = P.psb[6]; b7, tb7 = P.psb[7]
        chunks = [(q0, min(512, LP - q0)) for q0 in range(0, LP, 512)]
        for s in range(NS):
            c0s = s * LP
            P.q_a.dma(g_[0:8, :], self.gabT[0:8, c0s:c0s + LP], outs=[tg])
            P.q_a.dma(be[0:8, :], self.gabT[8:16, c0s:c0s + LP], outs=[tg])
            dve.op(lambda: V.tensor_scalar(g_[0:8, :], g_[0:8, :], dtb[:, 0:1], None, ALU.add), outs=[tg], ins=[tg, tk])
            act.op(lambda: nc.scalar.activation(out=g_[0:8, :], in_=g_[0:8, :], func=AF.Exp), outs=[tg], ins=[tg])
            act.op(lambda: nc.scalar.activation(out=g_[0:8, :], in_=g_[0:8, :], func=AF.Ln, bias=self.ones[0:8, 0:1]), outs=[tg], ins=[tg, self.t_ones])
            dve.op(lambda: V.tensor_scalar(g_[0:8, :], g_[0:8, :], nA[:, 0:1], None, ALU.mult), outs=[tg], ins=[tg, tk])
            act.op(lambda: nc.scalar.activation(out=be[0:8, :], in_=be[0:8, :], func=AF.Sigmoid), outs=[tg], ins=[tg])
            for c in range(NT):
                csl = slice(c * 128, (c + 1) * 128)
                dve.op(lambda: V.tensor_tensor_scan(out=Gc[0:8, csl], data0=self.ones[0:8, 0:128], data1=g_[0:8, csl], initial=0.0,
                                                    op0=ALU.mult, op1=ALU.add), outs=[tg], ins=[tg, self.t_ones])
            for c in range(NT):
                csl = slice(c * 128, (c + 1) * 128)
                pe.op(lambda: nc.tensor.transpose(b0[:, c * 16:c * 16 + 8], Gc[0:8, csl], self.ident[0:8, 0:8]), outs=[tb0], ins=[tg, self.t_ident])
                pe.op(lambda: nc.tensor.transpose(b0[:, c * 16 + 8:c * 16 + 16], be[0:8, csl], self.ident[0:8, 0:8]), outs=[tb0], ins=[tg, self.t_ident])
            act.op(lambda: nc.scalar.copy(out=cols[:], in_=b0[:, 0:NT * 16].rearrange("p (c k) -> p c k", k=16)), outs=[tcols], ins=[tb0])
            for h in range(8):
                for part, (x, tx) in enumerate(((qT, tq), (kT, tkk), (vT, tv))):
                    ch = part * 8 + h
                    P.q_a.dma(x[:], self.gT[ch, :, c0s:c0s + LP], outs=[tx])
                    dve.op(lambda: V.tensor_scalar(cv[:], x[:], cw[:, ch, 3:4], None, ALU.mult), outs=[tcv], ins=[tx, tk])
                    for sh in (1, 2, 3):
                        dve.op(lambda: V.scalar_tensor_tensor(out=cv[:, sh:LP], in0=x[:, 0:LP - sh], scalar=cw[:, ch, 3 - sh:4 - sh], in1=cv[:, sh:LP],
                                                              op0=ALU.mult, op1=ALU.add), outs=[tcv], ins=[tcv, tx, tk])
                    act.op(lambda: nc.scalar.activation(out=x[:], in_=cv[:], func=AF.Silu), outs=[tx], ins=[tcv])
                P.q_a.dma(zT[:], self.gzT[h, :, c0s:c0s + LP], outs=[tz])
                for (x, tx, extra) in ((qT, tq, 128.0 ** -0.5), (kT, tkk, 1.0)):
                    act.op(lambda: nc.scalar.activation(out=cv[:], in_=x[:], func=AF.Square), outs=[tcv], ins=[tx])
                    for (q0, Tq) in chunks:
                        sl = slice(q0, q0 + Tq)
                        pe.op(lambda: nc.tensor.matmul(b0[:, :Tq], self.ones[:], cv[:, sl], start=True, stop=True), outs=[tb0], ins=[tcv, self.t_ones])
                        act.op(lambda: nc.scalar.activation(out=rs[:, :Tq], in_=b0[:, :Tq], func=AF.Sqrt, bias=self.epsb[:], scale=1.0), outs=[trs], ins=[tb0, self.t_eps])
                        dve.op(lambda: V.reciprocal(out=rs[:, :Tq], in_=rs[:, :Tq]), outs=[trs], ins=[trs])
                        dve.op(lambda: V.scalar_tensor_tensor(out=x[:, sl], in0=x[:, sl], scalar=extra, in1=rs[:, :Tq], op0=ALU.mult, op1=ALU.mult),
                               outs=[tx], ins=[tx, trs])
                for (q0, Tq) in chunks:
                    sl = slice(q0, q0 + Tq)
                    pe.op(lambda: nc.tensor.matmul(b0[:, :Tq], sel[:, h * 128:(h + 1) * 128], Gc[0:8, sl], start=True, stop=True), outs=[tb0], ins=[t_sel, tg])
                    act.op(lambda: nc.scalar.copy(out=Grow[:, sl], in_=b0[:, :Tq]), outs=[tGr], ins=[tb0])
                    act.op(lambda: nc.scalar.activation(out=eG[:, sl], in_=b0[:, :Tq], func=AF.Exp), outs=[tGr], ins=[tb0])
                    pe.op(lambda: nc.tensor.matmul(b0[:, :Tq], sel[:, h * 128:(h + 1) * 128], be[0:8, sl], start=True, stop=True), outs=[tb0], ins=[t_sel, tg])
                    act.op(lambda: nc.scalar.copy(out=Brow[:, sl], in_=b0[:, :Tq]), outs=[tGr], ins=[tb0])
                dve.op(lambda: V.tensor_tensor(out=qdT[:], in0=qT[:], in1=eG[:], op=ALU.mult), outs=[tqd], ins=[tq, tGr])
                for c in range(NT):
                    csl = slice(c * 128, (c + 1) * 128)
                    cend = c * 128 + 127
                    Gcol = cols[:, c, h:h + 1]; bcol = cols[:, c, 8 + h:9 + h]
                    pe.op(lambda: nc.tensor.transpose(b2[:, 0:128], kT[:, csl], self.ident[:]), outs=[tb2], ins=[tkk, self.t_ident])
                    pe.op(lambda: nc.tensor.transpose(b2[:, 128:256], vT[:, csl], self.ident[:]), outs=[tb2], ins=[tv, self.t_ident])
                    act.op(lambda: nc.scalar.copy(out=k_tok[:, c, :], in_=b2[:, 0:128]), outs=[tkv], ins=[tb2])
                    act.op(lambda: nc.scalar.copy(out=v_tok[:, c, :], in_=b2[:, 128:256]), outs=[tkv], ins=[tb2])
                    dve.op(lambda: V.tensor_scalar(dT[:], Grow[:, csl], Gcol, 0.0, ALU.subtract, ALU.min), outs=[tdT], ins=[tGr, tcols])
                    act.op(lambda: nc.scalar.activation(out=dT[:], in_=dT[:], func=AF.Exp), outs=[tdT], ins=[tdT])
                    pe.op(lambda: nc.tensor.matmul(b1[:, 0:128], kT[:, csl], kT[:, csl], start=True, stop=True), outs=[tb1], ins=[tkk])
                    pe.op(lambda: nc.tensor.matmul(b1[:, 128:256], kT[:, csl], qT[:, csl], start=True, stop=True), outs=[tb1], ins=[tkk, tq])
                    dve.op(lambda: V.tensor_tensor(out=dI[:], in0=dT[:], in1=maskI[:], op=ALU.mult), outs=[tdI], ins=[tdT, t_mi])
                    dve.op(lambda: V.tensor_tensor(out=attnT[:, c, :], in0=b1[:, 128:256], in1=dI[:], op=ALU.mult), outs=[tpre[c]], ins=[tb1, tdI])
                    dve.op(lambda: V.tensor_tensor(out=aa[:], in0=dT[:], in1=maskSt[:], op=ALU.mult), outs=[taa], ins=[tdT, t_mst])
                    dve.op(lambda: V.tensor_tensor(out=aa[:], in0=aa[:], in1=Brow[:, csl], op=ALU.mult), outs=[taa], ins=[taa, tGr])
                    dve.op(lambda: V.tensor_tensor(out=Np[0][:], in0=b1[:, 0:128], in1=aa[:], op=ALU.mult), outs=[tN[0]], ins=[tb1, taa])
                    pe.op(lambda: nc.tensor.transpose(b2[:, 256:384], Np[0][:], self.ident[:]), outs=[tb2], ins=[tN[0], self.t_ident])
                    act.op(lambda: nc.scalar.copy(out=Mp[0][:], in_=b2[:, 256:384]), outs=[tM[0]], ins=[tb2])
                    for k in range(1, 7):
                        pe.op(lambda: nc.tensor.matmul(b3[:, 0:128], Np[k - 1][:], Mp[k - 1][:], start=True, stop=True), outs=[tb3], ins=[tN[k - 1], tM[k - 1]])
                        act.op(lambda: nc.scalar.copy(out=Mp[k][:], in_=b3[:, 0:128]), outs=[tM[k]], ins=[tb3])
                        if k < 6:
                            pe.op(lambda: nc.tensor.matmul(b4[:, 0:128], Mp[k - 1][:], Np[k - 1][:], start=True, stop=True), outs=[tb4], ins=[tN[k - 1], tM[k - 1]])
                            dve.op(lambda: V.tensor_copy(out=Np[k][:], in_=b4[:, 0:128]), outs=[tN[k]], ins=[tb4])
                    Y, tY = Y_r.next()
                    dve.op(lambda: V.tensor_tensor(out=Y[:], in0=self.ident[:], in1=Np[0][:], op=ALU.subtract), outs=[tY], ins=[tN[0], self.t_ident])
                    for k in range(1, 7):
                        pe.op(lambda: nc.tensor.matmul(b5[:, 0:128], Mp[k][:], Y[:], start=True, stop=True), outs=[tb5], ins=[tM[k], tY])
                        Y2, tY2 = Y_r.next()
                        dve.op(lambda: V.tensor_tensor(out=Y2[:], in0=b5[:, 0:128], in1=Y[:], op=ALU.add), outs=[tY2], ins=[tb5, tY])
                        Y, tY = Y2, tY2
                    act.op(lambda: nc.scalar.activation(out=sc[:, 0:1], in_=Gcol, func=AF.Exp), outs=[tsc], ins=[tcols])
                    act.op(lambda: nc.scalar.activation(out=sc[:, 1:2], in_=Gcol, func=AF.Exp, scale=-1.0, bias=Grow[:, cend:cend + 1]), outs=[tsc], ins=[tcols, tGr])
                    dve.op(lambda: V.tensor_scalar(Xu[:], v_tok[:, c, :], bcol, None, ALU.mult), outs=[tX], ins=[tkv, tcols])
                    dve.op(lambda: V.tensor_scalar(Xw[:], k_tok[:, c, :], bcol, sc[:, 0:1], ALU.mult, ALU.mult), outs=[tX], ins=[tkv, tcols, tsc])
                    dve.op(lambda: V.tensor_scalar(kd_tok[:, c, :], k_tok[:, c, :], sc[:, 1:2], None, ALU.mult), outs=[tpre[c]], ins=[tkv, tsc])
                    pe.op(lambda: nc.tensor.matmul(b5[:, 128:256], Y[:], Xu[:], start=True, stop=True), outs=[tb5], ins=[tY, tX])
                    pe.op(lambda: nc.tensor.matmul(b5[:, 256:384], Xw[:], Y[:], start=True, stop=True), outs=[tb5], ins=[tY, tX])
                    act.op(lambda: nc.scalar.copy(out=u_tok[:, c, :], in_=b5[:, 128:256]), outs=[tpre[c]], ins=[tb5])
                    act.op(lambda: nc.scalar.copy(out=wT[:, c, :], in_=b5[:, 256:384]), outs=[tpre[c]], ins=[tb5])
                dve.op(lambda: V.memset(Sst[:], 0.0), outs=[tS])
                for c in range(NT):
                    csl = slice(c * 128, (c + 1) * 128)
                    cend = c * 128 + 127
                    pe.op(lambda: nc.tensor.matmul(b6[:, 0:128], wT[:, c, :], Sst[:], start=True, stop=True), outs=[tb6], ins=[tpre[c], tS])
                    dve.op(lambda: V.tensor_tensor(out=vn[:], in0=u_tok[:, c, :], in1=b6[:, 0:128], op=ALU.subtract), outs=[tvn], ins=[tpre[c], tb6])
                    pe.op(lambda: nc.tensor.matmul(b7[:, 0:128], Sst[:], qdT[:, csl], start=True, stop=False), outs=[tb7], ins=[tS, tqd])
                    pe.op(lambda: nc.tensor.matmul(b7[:, 0:128], vn[:], attnT[:, c, :], start=False, stop=True), outs=[tb7], ins=[tvn, tpre[c]])
                    act.op(lambda: nc.scalar.copy(out=cv[:, csl], in_=b7[:, 0:128]), outs=[tcv], ins=[tb7])
                    pe.op(lambda: nc.tensor.matmul(b6[:, 128:256], kd_tok[:, c, :], vn[:], start=True, stop=True), outs=[tb6], ins=[tpre[c], tvn])
                    dve.op(lambda: V.scalar_tensor_tensor(out=Sst[:], in0=Sst[:], scalar=eG[:, cend:cend + 1], in1=b6[:, 128:256], op0=ALU.mult, op1=ALU.add),
                           outs=[tS], ins=[tS, tb6, tGr])
                oT = cv
                act.op(lambda: nc.scalar.activation(out=zT[:], in_=zT[:], func=AF.Silu), outs=[tz], ins=[tz])
                for (q0, Tq) in chunks:
                    sl = slice(q0, q0 + Tq)
                    act.op(lambda: nc.scalar.activation(out=qdT[:, sl], in_=oT[:, sl], func=AF.Square), outs=[tqd], ins=[tcv])
                    pe.op(lambda: nc.tensor.matmul(b0[:, :Tq], self.ones[:], qdT[:, sl], start=True, stop=True), outs=[tb0], ins=[tqd, self.t_ones])
                    act.op(lambda: nc.scalar.activation(out=rs[:, :Tq], in_=b0[:, :Tq], func=AF.Sqrt, bias=self.epsb[:], scale=1.0 / 128), outs=[trs], ins=[tb0, self.t_eps])
                    dve.op(lambda: V.reciprocal(out=rs[:, :Tq], in_=rs[:, :Tq]), outs=[trs], ins=[trs])
                    dve.op(lambda: V.scalar_tensor_tensor(out=oT[:, sl], in0=oT[:, sl], scalar=gout[:, 0:1], in1=rs[:, :Tq], op0=ALU.mult, op1=ALU.mult),
                           outs=[tcv], ins=[tcv, trs, tk])
                    dve.op(lambda: V.tensor_tensor(out=oT[:, sl], in0=oT[:, sl], in1=zT[:, sl], op=ALU.mult), outs=[tcv], ins=[tcv, tz])
                P.q_a.dma(self.yO[8 + h, :, c0s:c0s + LP], oT[:], ins=[tcv])
        P.release(m)


def host_consts(LP=None, mix=True):
    c = {"ident": np.eye(128, dtype=np.float32)}
    if mix and LP is not None:
        half = 32
        inv = (10000.0 ** (-np.arange(half, dtype=np.float32) / half)).astype(np.float32)
        ang = np.arange(LP, dtype=np.float32)[:, None] * inv[None, :]
        c["rope_cos"] = np.cos(ang).astype(np.float32)
        c["rope_sin"] = np.sin(ang).astype(np.float32)
        k = np.arange(128)[:, None]; q = np.arange(128)[None, :]
        c["maskT"] = (q >= k).astype(np.float32)
        c["maskS"] = (q > k).astype(np.float32)
        c["maskL"] = (q <= k).astype(np.float32)
        sel = np.zeros((8, 8, 128), np.float32)
        for h in range(8):
            sel[h, h, :] = 1.0
        c["sel"] = sel.reshape(8, 1024)
    return c


def make_xin(x, meta, LP):
    NS, SEQ, _ = x.shape
    xin = np.zeros((NS, LP, D), np.float32)
    xin[:, :N_META] = meta[None]
    xin[:, N_META:N_META + SEQ] = x
    return xin.reshape(NS * LP, D)


_WNAMES = ["norm_ffn1", "norm_mix", "norm_ffn2", "w1_gate", "w1_up", "w2_gate", "w2_up", "w1_down", "w2_down"]


def kernel(**inputs):
    x = np.asarray(inputs["x"])
    B, SEQ, _ = x.shape
    ncores = 8
    NS = B // ncores
    L = SEQ + N_META
    NT = (L + 127) // 128
    K = Kern(NS, NT, depth=2, seq_real=L)
    consts = host_consts(NT * 128)
    in_maps = []
    for c in range(ncores):
        m = {"xin": make_xin(x[c * NS:(c + 1) * NS], np.asarray(inputs["meta_tokens"]), NT * 128)}
        m.update(consts)
        for nm in K.w:
            m[nm] = np.ascontiguousarray(np.asarray(inputs[nm]))
        in_maps.append(m)
    res = run_bass_kernel_spmd(K.P.nc, in_maps, core_ids=list(range(ncores)))
    outs = [r["out"].reshape(NS, SEQ, D) for r in res.results]
    return np.concatenate(outs, axis=0).astype(np.float32)
```

```python
import math
import numpy as np
import concourse.bass as bass
import concourse.mybir as mybir
from concourse.bass_utils import run_bass_kernel_spmd

F32 = mybir.dt.float32
BF16 = mybir.dt.bfloat16
AF = mybir.ActivationFunctionType
ALU = mybir.AluOpType
AX = mybir.AxisListType

D = 2048
DC = 16
DFF = 5632
FC = 44
N_META = 16
EPS = 1e-6
ARENA_WORDS = 53200


class Tok:
    __slots__ = ("w", "r", "name")

    def __init__(self, name=""):
        self.w = None
        self.r = {}
        self.name = name


class Eng:
    def __init__(self, P, eng, name, is_pe=False):
        self.P = P
        self.eng = eng
        self.name = name
        self.sem = P.nc.alloc_semaphore("s_" + name)
        self.key = "E" + name
        self.cnt = 0
        self.seen = {}
        self.is_pe = is_pe
        self.nwait = 0

    def wait_ev(self, ev):
        if ev is None:
            return
        key, sem, val = ev
        if self.is_pe and key == self.key:
            return
        if key == self.key and val <= self.cnt_done_known():
            pass
        if self.seen.get(key, 0) >= val:
            return
        self.eng.wait_ge(sem, val)
        self.nwait += 1
        self.seen[key] = val

    def cnt_done_known(self):
        return self.seen.get(self.key, 0)

    def deps(self, outs, ins):
        for t in ins:
            self.wait_ev(t.w)
        for t in outs:
            self.wait_ev(t.w)
            for k, (sem, val) in t.r.items():
                self.wait_ev((k, sem, val))

    def op(self, fn, outs=(), ins=()):
        self.deps(outs, ins)
        inst = fn()
        self.cnt += 1
        inst.then_inc(self.sem, 1)
        ev = (self.key, self.sem, self.cnt)
        for t in ins:
            t.r[self.key] = (self.sem, self.cnt)
        for t in outs:
            t.w = ev
            t.r = {}
        return inst


class DmaQ:
    def __init__(self, P, eng, name, nsem):
        self.P = P
        self.eng = eng
        self.name = name
        self.sems = [P.nc.alloc_semaphore("d_%s%d" % (name, i)) for i in range(nsem)]
        self.vals = [0] * nsem
        self.i = 0
        self.seen = {}
        self.nwait = 0

    def wait_ev(self, ev):
        if ev is None:
            return
        key, sem, val = ev
        if self.seen.get(key, 0) >= val:
            return
        self.eng.wait_ge(sem, val)
        self.nwait += 1
        self.seen[key] = val

    def dma(self, out, in_, outs=(), ins=(), **kw):
        for t in ins:
            self.wait_ev(t.w)
        for t in outs:
            self.wait_ev(t.w)
            for k, (sem, val) in t.r.items():
                self.wait_ev((k, sem, val))
        i = self.i
        self.i = (self.i + 1) % len(self.sems)
        key = "D%s%d" % (self.name, i)
        if self.vals[i] > 0:
            self.wait_ev((key, self.sems[i], self.vals[i]))
        inst = self.eng.dma_start(out=out, in_=in_, **kw)
        self.vals[i] += 16
        inst.then_inc(self.sems[i], 16)
        ev = (key, self.sems[i], self.vals[i])
        for t in ins:
            t.r[key] = (self.sems[i], self.vals[i])
        for t in outs:
            t.w = ev
            t.r = {}
        return ev

    def drain(self):
        for i, s in enumerate(self.sems):
            if self.vals[i] > 0:
                self.wait_ev(("D%s%d" % (self.name, i), s, self.vals[i]))


class Prog:
    def __init__(self):
        self.nc = bass.Bass("TRN2", target_bir_lowering=False)
        nc = self.nc
        self.pe = Eng(self, nc.tensor, "pe", is_pe=True)
        self.act = Eng(self, nc.scalar, "act")
        self.dve = Eng(self, nc.vector, "dve")
        self.pool = Eng(self, nc.gpsimd, "pool")
        self.q_w = DmaQ(self, nc.sync, "w", 12)
        self.q_a = DmaQ(self, nc.gpsimd, "a", 8)
        self._n = 0
        self.AW = ARENA_WORDS
        self.arena = nc.alloc_sbuf_tensor("arena", [128, self.AW], F32)
        self.off = 0
        self.psb = [(nc.alloc_psum_tensor("psb%d" % i, [128, 512], F32), Tok("psb%d" % i)) for i in range(8)]

    def sb(self, shape, name=None):
        n = 1
        for d in shape[1:]:
            n *= d
        assert self.off + n <= self.AW, ("SBUF arena overflow", name, self.off, n)
        ap = self.arena[0:shape[0], self.off:self.off + n]
        self.off += n
        if len(shape) == 3:
            ap = ap.rearrange("p (a b) -> p a b", a=shape[1])
        elif len(shape) == 4:
            ap = ap.rearrange("p (a b c) -> p a b c", a=shape[1], b=shape[2])
        return ap

    def sbh(self, shape, name=None):
        n = 1
        for d in shape[1:]:
            n *= d
        import os
        if os.environ.get("DBG_NOBF"):
            return self.sb(shape, name)
        assert n % 2 == 0
        w = n // 2
        assert self.off + w <= self.AW, ("SBUF arena overflow", name, self.off, w)
        ap = self.arena[0:shape[0], self.off:self.off + w].bitcast(BF16)
        self.off += w
        if len(shape) == 3:
            ap = ap.rearrange("p (a b) -> p a b", a=shape[1])
        return ap

    def mark(self):
        return self.off

    def release(self, m):
        self.barrier()
        self.off = m

    def barrier(self):
        evs = []
        for e in (self.pe, self.act, self.dve, self.pool):
            if e.cnt > 0:
                evs.append((e.key, e.sem, e.cnt))
        for q in (self.q_w, self.q_a):
            for i, sm in enumerate(q.sems):
                if q.vals[i] > 0:
                    evs.append(("D%s%d" % (q.name, i), sm, q.vals[i]))
        for e in (self.pe, self.act, self.dve, self.pool, self.q_w, self.q_a):
            for ev in evs:
                e.wait_ev(ev)

    def dram(self, name, shape, kind="Internal", dtype=F32):
        return self.nc.dram_tensor(name, list(shape), dtype, kind=kind).ap()


class Ring:
    def __init__(self, bufs):
        self.bufs = bufs
        self.i = 0

    def next(self):
        b = self.bufs[self.i]
        self.i = (self.i + 1) % len(self.bufs)
        return b


class Kern:
    LAYERS = None

    def __init__(self, NS, NT, depth=2, do=("ffn1", "mix", "ffn2"), seq_real=None):
        self.NS = NS
        self.NT = NT
        self.LP = NT * 128
        self.NTOK = NS * self.LP
        self.depth = depth
        self.do = do
        self.L = seq_real if seq_real is not None else (self.LP - 112)
        self.P = Prog()
        self.build()

    def build(self):
        P = self.P
        nc = P.nc
        NS, NT, LP, NTOK = self.NS, self.NT, self.LP, self.NTOK
        SEQ = self.L - N_META
        dep = self.depth
        self.xin = P.dram("xin", [NS * LP, D], "ExternalInput")
        self.ident_d = P.dram("ident", [128, 128], "ExternalInput")
        self.w = {}
        def ext(name, shape):
            self.w[name] = P.dram(name, shape, "ExternalInput")
        ext("norm_ffn1", [dep, D]); ext("norm_mix", [dep, D]); ext("norm_ffn2", [dep, D])
        for nm in ("w1_gate", "w1_up", "w2_gate", "w2_up"):
            ext(nm, [dep, D, DFF])
        for nm in ("w1_down", "w2_down"):
            ext(nm, [dep, DFF, D])
        ne, no = (dep + 1) // 2, dep // 2
        if "mix" in self.do:
            ext("ev_w_in", [ne, D, 1344]); ext("s5_log_dt", [ne, 32]); ext("s5_a_re", [ne, 32, 64]); ext("s5_a_im", [ne, 32, 64])
            ext("s5_b_re", [ne, 32, 64, 16]); ext("s5_b_im", [ne, 32, 64, 16]); ext("s5_c_re", [ne, 32, 16, 64]); ext("s5_c_im", [ne, 32, 16, 64])
            ext("s5_d", [ne, 512]); ext("s5_w_glu", [ne, 512, 512]); ext("mla_g_cq", [ne, 512]); ext("mla_g_ckv", [ne, 256])
            ext("mla_w_uq", [ne, 512, 1536]); ext("mla_w_ukv", [ne, 256, 2048]); ext("mla_g_q", [ne, 192]); ext("mla_g_k", [ne, 192])
            ext("ev_w_out", [ne, 1536, D])
            if no > 0:
                ext("od_w_in", [no, D, 7184]); ext("gdn_conv", [no, 4, 3072]); ext("gdn_a_log", [no, 8]); ext("gdn_dt_bias", [no, 8])
                ext("gdn_g_out", [no, 128]); ext("od_w_out", [no, 2048, D])
            self.c_rope_cos = P.dram("rope_cos", [LP, 32], "ExternalInput")
            self.c_rope_sin = P.dram("rope_sin", [LP, 32], "ExternalInput")
            self.c_maskT = P.dram("maskT", [128, 128], "ExternalInput")
            self.uT = P.dram("uT", [4, 128, NTOK])
            self.QT = P.dram("QT", [8, 192, NTOK])
            self.KT = P.dram("KT", [8, 192, NTOK])
            self.Vd = P.dram("Vd", [NTOK, 1024])
            self.yE = P.dram("yE", [12, 128, NTOK])
            if no > 0:
                self.c_maskS = P.dram("maskS", [128, 128], "ExternalInput")
                self.c_maskL = P.dram("maskL", [128, 128], "ExternalInput")
                self.c_sel = P.dram("sel", [8, 1024], "ExternalInput")
                self.sbQ = P.dram("sbQ", [8, 128, NTOK]); self.sbK = P.dram("sbK", [8, 128, NTOK]); self.sbV = P.dram("sbV", [NTOK, 1024])
                self.gT = P.dram("gT", [24, 128, NTOK]); self.gzT = P.dram("gzT", [8, 128, NTOK]); self.gabT = P.dram("gabT", [16, NTOK])
                self.yO = P.dram("yO", [16, 128, NTOK])
        self.out = P.dram("out", [NS * SEQ, D], "ExternalOutput")
        self.hT = P.dram("hT", [DC, 128, NTOK])
        self.ident = P.sb([128, 128], "ident"); self.t_ident = Tok("ident")
        P.q_a.dma(self.ident[:], self.ident_d[:, :], outs=[self.t_ident])
        self.ones = P.sb([128, 128], "ones"); self.t_ones = Tok("ones")
        P.dve.op(lambda: nc.vector.memset(self.ones[:], 1.0), outs=[self.t_ones])
        self.epsb = P.sb([128, 1], "epsb"); self.t_eps = Tok("eps")
        P.dve.op(lambda: nc.vector.memset(self.epsb[:], EPS), outs=[self.t_eps])
        self.gains = {}
        for nm in ("norm_ffn1", "norm_mix", "norm_ffn2"):
            g = P.sb([128, dep, DC], nm); t = Tok(nm)
            for l in range(dep):
                P.q_a.dma(g[:, l, :], self.w[nm][l].rearrange("(c p) -> p c", p=128), outs=[t],
                          allow_slow_non_contiguous=True)
            self.gains[nm] = (g, t)

        self.stage_in()
        for l in (self.LAYERS if self.LAYERS is not None else range(dep)):
            if "ffn1" in self.do:
                self.ffn(l, "1")
            if "mix" in self.do:
                if l % 2 == 0:
                    self.even_mixer(l)
                else:
                    self.odd_mixer(l)
            if "ffn2" in self.do:
                self.ffn(l, "2")
        self.stage_out()
        P.q_a.drain()
        P.q_w.drain()

    def psring(self, idx):
        return Ring([self.P.psb[i] for i in idx])

    def stage_in(self):
        P = self.P; nc = P.nc
        m = P.mark()
        NB = self.NTOK // 128
        xin_r = Ring([(P.sb([128, D], "xin"), Tok()) for _ in range(2)])
        ps_r = self.psring([0, 1])
        ht_r = Ring([(P.sb([128, DC, 128], "htin"), Tok()) for _ in range(2)])
        for b in range(NB):
            xt, xtok = xin_r.next()
            P.q_a.dma(xt[:], self.xin[b * 128:(b + 1) * 128, :], outs=[xtok])
            ht, htok = ht_r.next()
            for g in range(4):
                ps, ptok = ps_r.next()
                for j in range(4):
                    c = g * 4 + j
                    P.pe.op(lambda: nc.tensor.transpose(ps[:, j * 128:(j + 1) * 128],
                                                        xt[:, c * 128:(c + 1) * 128], self.ident[:]),
                            outs=[ptok], ins=[xtok, self.t_ident])
                P.act.op(lambda: nc.scalar.copy(out=ht[:, g * 4:(g + 1) * 4, :],
                                                in_=ps[:].rearrange("p (c t) -> p c t", c=4)),
                         outs=[htok], ins=[ptok])
            P.q_a.dma(self.hT[:, :, b * 128:(b + 1) * 128].rearrange("c p t -> p c t"), ht[:], ins=[htok],
                      outs=[self.tok_h(b * 128, 128)])
        P.release(m)

    def tok_h(self, t0, n):
        if not hasattr(self, "_htoks"):
            self._htoks = {}
        key = t0 // 128
        if key not in self._htoks:
            self._htoks[key] = Tok("h%d" % key)
        return self._htoks[key]

    def toks_h(self, t0, n):
        return [self.tok_h(t, 128) for t in range(t0, t0 + n, 128)]

    def stage_out(self):
        P = self.P; nc = P.nc
        m = P.mark()
        SEQ = self.L - N_META
        ht_r = Ring([(P.sb([128, DC, 128], "htout"), Tok()) for _ in range(2)])
        ps_r = self.psring([0, 1])
        o_r = Ring([(P.sb([128, D], "osb"), Tok()) for _ in range(2)])
        for s in range(self.NS):
            for j in range(self.NT):
                t0 = j * 128
                lo = max(t0, N_META); hi = min(t0 + 128, self.L)
                if hi <= lo:
                    continue
                b = s * self.NT + j
                ht, htok = ht_r.next()
                P.q_a.dma(ht[:], self.hT[:, :, b * 128:(b + 1) * 128].rearrange("c p t -> p c t"), outs=[htok],
                          ins=[self.tok_h(b * 128, 128)])
                ot, otok = o_r.next()
                for g in range(4):
                    ps, ptok = ps_r.next()
                    for jj in range(4):
                        c = g * 4 + jj
                        P.pe.op(lambda: nc.tensor.transpose(ps[:, jj * 128:(jj + 1) * 128], ht[:, c, :], self.ident[:]),
                                outs=[ptok], ins=[htok, self.t_ident])
                    P.act.op(lambda: nc.scalar.copy(out=ot[:, g * 512:(g + 1) * 512], in_=ps[:]),
                             outs=[otok], ins=[ptok])
                r0 = s * SEQ + lo - N_META
                P.q_a.dma(self.out[r0:r0 + (hi - lo), :], ot[lo - t0:hi - t0, :], ins=[otok])
        P.release(m)

    def rms_scale(self, src, stok, T, gain_cols, gtok, ps_sum, pstok, sq, sqtok, rstd, rtok, nchunks=DC, dim=D):
        P = self.P; nc = P.nc
        for c in range(nchunks):
            P.act.op(lambda: nc.scalar.activation(out=sq[:, :T], in_=src[:, c, :T], func=AF.Square),
                     outs=[sqtok], ins=[stok])
            P.pe.op(lambda: nc.tensor.matmul(ps_sum[:, :T], self.ones[:], sq[:, :T], start=(c == 0), stop=(c == nchunks - 1)),
                    outs=[pstok], ins=[sqtok, self.t_ones])
        P.act.op(lambda: nc.scalar.activation(out=rstd[:, :T], in_=ps_sum[:, :T], func=AF.Sqrt,
                                              bias=self.epsb[:], scale=1.0 / dim),
                 outs=[rtok], ins=[pstok, self.t_eps])
        P.dve.op(lambda: nc.vector.reciprocal(out=rstd[:, :T], in_=rstd[:, :T]), outs=[rtok], ins=[rtok])
        for c in range(nchunks):
            P.dve.op(lambda: nc.vector.scalar_tensor_tensor(out=src[:, c, :T], in0=src[:, c, :T],
                                                            scalar=gain_cols(c), in1=rstd[:, :T],
                                                            op0=ALU.mult, op1=ALU.mult),
                     outs=[stok], ins=[stok, rtok, gtok])

    def ffn(self, l, which):
        P = self.P; nc = P.nc
        NTOK = self.NTOK
        TT = 768; XS = 256
        HF = FC // 2
        wg = self.w["w%s_gate" % which][l]
        wu = self.w["w%s_up" % which][l]
        wd = self.w["w%s_down" % which][l]
        gain, gtok = self.gains["norm_ffn%s" % which]
        m = P.mark()
        xs, xstok = P.sb([128, DC, XS], "ffxs"), Tok("ffxs")
        xb, xbtok = P.sbh([128, DC, TT], "ffxb"), Tok("ffxb")
        hf = P.sbh([128, FC, TT], "ffh")
        hftoks = [Tok("ffh%d" % f) for f in range(FC)]
        wst_r = Ring([(P.sb([128, DC, 128], "wst"), Tok()) for _ in range(3)])
        wgb_r = Ring([(P.sbh([128, DC, 128], "wgb"), Tok()) for _ in range(2)])
        wub_r = Ring([(P.sbh([128, DC, 128], "wub"), Tok()) for _ in range(2)])
        wdst_r = Ring([(P.sb([128, HF, 128], "wdst"), Tok()) for _ in range(2)])
        wdb_r = Ring([(P.sbh([128, HF, 128], "wdb"), Tok()) for _ in range(4)])
        rstd, rtok = P.sb([128, XS], "ffrstd"), Tok()
        sq_r = Ring([(P.sb([128, XS], "ffsq"), Tok()) for _ in range(2)])
        sg_r = Ring([(P.sb([128, TT // 2], "ffsg"), Tok()) for _ in range(2)])
        hres_r = Ring([(P.sb([128, TT], "ffres"), Tok()) for _ in range(2)])
        psg = [P.psb[0], P.psb[1]]; psu = [P.psb[2], P.psb[3]]; psd = [P.psb[4], P.psb[5]]
        pss, psstok = P.psb[6]
        wgv = wg.rearrange("(c p) f -> p c f", p=128)
        wuv = wu.rearrange("(c p) f -> p c f", p=128)
        wdv = wd.rearrange("(c p) d -> p c d", p=128)
        with nc.allow_low_precision("bf16 matmul operands, fp32 accumulation"):
            for t0 in range(0, NTOK, TT):
                T = min(TT, NTOK - t0)
                HW = T // 2
                halves = [(0, HW), (HW, T - HW)]
                htoks = self.toks_h(t0, T)
                for p0 in range(0, T, XS):
                    P.q_a.dma(xs[:, :, :], self.hT[:, :, t0 + p0:t0 + p0 + XS].rearrange("c p t -> p c t"), outs=[xstok], ins=htoks)
                    for c in range(DC):
                        sq, sqtok = sq_r.next()
                        P.act.op(lambda: nc.scalar.activation(out=sq[:, :], in_=xs[:, c, :], func=AF.Square), outs=[sqtok], ins=[xstok])
                        P.pe.op(lambda: nc.tensor.matmul(pss[:, :XS], self.ones[:], sq[:, :], start=(c == 0), stop=(c == DC - 1)),
                                outs=[psstok], ins=[sqtok, self.t_ones])
                    P.act.op(lambda: nc.scalar.activation(out=rstd[:, :], in_=pss[:, :XS], func=AF.Sqrt, bias=self.epsb[:], scale=1.0 / D),
                             outs=[rtok], ins=[psstok, self.t_eps])
                    P.dve.op(lambda: nc.vector.reciprocal(out=rstd[:, :], in_=rstd[:, :]), outs=[rtok], ins=[rtok])
                    for c in range(DC):
                        P.dve.op(lambda: nc.vector.scalar_tensor_tensor(out=xb[:, c, p0:p0 + XS], in0=xs[:, c, :], scalar=gain[:, l, c:c + 1],
                                                                        in1=rstd[:, :], op0=ALU.mult, op1=ALU.mult),
                                 outs=[xbtok], ins=[xstok, rtok, gtok])
                def load_gu(f):
                    gs, gst = wst_r.next()
                    P.q_w.dma(gs[:], wgv[:, :, f * 128:(f + 1) * 128], outs=[gst])
                    gb, gbt = wgb_r.next()
                    P.dve.op(lambda: nc.vector.tensor_copy(out=gb[:], in_=gs[:]), outs=[gbt], ins=[gst])
                    us, ust = wst_r.next()
                    P.q_w.dma(us[:], wuv[:, :, f * 128:(f + 1) * 128], outs=[ust])
                    ub, ubt = wub_r.next()
                    P.dve.op(lambda: nc.vector.tensor_copy(out=ub[:], in_=us[:]), outs=[ubt], ins=[ust])
                    return gb, gbt, ub, ubt
                nxt = load_gu(0)
                for f in range(FC):
                    g_, gt_, u_, ut_ = nxt
                    if f + 1 < FC:
                        nxt = load_gu(f + 1)
                    for hi, (h0, hw) in enumerate(halves):
                        ps, pt = psg[hi]
                        for c in range(DC):
                            P.pe.op(lambda: nc.tensor.matmul(ps[:, :hw], g_[:, c, :], xb[:, c, h0:h0 + hw], start=(c == 0), stop=(c == DC - 1)),
                                    outs=[pt], ins=[gt_, xbtok])
                    for hi, (h0, hw) in enumerate(halves):
                        ps, pt = psu[hi]
                        for c in range(DC):
                            P.pe.op(lambda: nc.tensor.matmul(ps[:, :hw], u_[:, c, :], xb[:, c, h0:h0 + hw], start=(c == 0), stop=(c == DC - 1)),
                                    outs=[pt], ins=[ut_, xbtok])
                    for hi, (h0, hw) in enumerate(halves):
                        sg, sgt = sg_r.next()
                        P.act.op(lambda: nc.scalar.activation(out=sg[:, :hw], in_=psg[hi][0][:, :hw], func=AF.Silu), outs=[sgt], ins=[psg[hi][1]])
                        P.dve.op(lambda: nc.vector.tensor_tensor(out=hf[:, f, h0:h0 + hw], in0=psu[hi][0][:, :hw], in1=sg[:, :hw], op=ALU.mult),
                                 outs=[hftoks[f]], ins=[psu[hi][1], sgt])
                def load_d(i):
                    dc, hh = divmod(i, 2)
                    ds_, dst_ = wdst_r.next()
                    P.q_w.dma(ds_[:], wdv[:, hh * HF:(hh + 1) * HF, dc * 128:(dc + 1) * 128], outs=[dst_])
                    db, dbt = wdb_r.next()
                    P.act.op(lambda: nc.scalar.copy(out=db[:], in_=ds_[:]), outs=[dbt], ins=[dst_])
                    return db, dbt
                pend = [load_d(0), load_d(1)]
                for dc in range(DC):
                    hres, hrt = hres_r.next()
                    P.q_a.dma(hres[:, :T], self.hT[dc, :, t0:t0 + T], outs=[hrt], ins=htoks)
                    slabs = []
                    for hh in range(2):
                        slabs.append(pend.pop(0))
                        i_next = dc * 2 + hh + 2
                        if i_next < 2 * DC:
                            pend.append(load_d(i_next))
                    for hi, (h0, hw) in enumerate(halves):
                        ps, pt = psd[hi]
                        for hh in range(2):
                            d_, dt_ = slabs[hh]
                            for ff in range(HF):
                                f = hh * HF + ff
                                P.pe.op(lambda: nc.tensor.matmul(ps[:, :hw], d_[:, ff, :], hf[:, f, h0:h0 + hw], start=(f == 0), stop=(f == FC - 1)),
                                        outs=[pt], ins=[dt_, hftoks[f]])
                        P.dve.op(lambda: nc.vector.scalar_tensor_tensor(out=hres[:, h0:h0 + hw], in0=ps[:, :hw], scalar=0.5, in1=hres[:, h0:h0 + hw],
                                                                        op0=ALU.mult, op1=ALU.add),
                                 outs=[hrt], ins=[pt, hrt])
                    P.q_a.dma(self.hT[dc, :, t0:t0 + T], hres[:, :T], ins=[hrt], outs=htoks)
        P.release(m)

    def ffn_fp32(self, l, which):
        P = self.P; nc = P.nc
        NTOK = self.NTOK
        TT = 512
        HF = FC // 2
        wg = self.w["w%s_gate" % which][l]
        wu = self.w["w%s_up" % which][l]
        wd = self.w["w%s_down" % which][l]
        gain, gtok = self.gains["norm_ffn%s" % which]
        m = P.mark()
        xt, xtok = P.sb([128, DC, TT], "ffx"), Tok("ffx")
        hf = P.sb([128, FC, TT], "ffh")
        hftoks = [Tok("ffh%d" % f) for f in range(FC)]
        wg_r = Ring([(P.sb([128, DC, 128], "wg"), Tok()) for _ in range(2)])
        wu_r = Ring([(P.sb([128, DC, 128], "wu"), Tok()) for _ in range(2)])
        wd_r = Ring([(P.sb([128, HF, 128], "wd"), Tok()) for _ in range(3)])
        psg_r = self.psring([0, 1]); psu_r = self.psring([2, 3]); psd_r = self.psring([4, 5])
        pss, psstok = P.psb[6]
        rstd, rtok = P.sb([128, TT], "ffrstd"), Tok()
        sg_r = Ring([(P.sb([128, TT], "ffsg"), Tok()) for _ in range(2)])
        hres_r = Ring([(P.sb([128, TT], "ffres"), Tok()) for _ in range(2)])
        wgv = wg.rearrange("(c p) f -> p c f", p=128)
        wuv = wu.rearrange("(c p) f -> p c f", p=128)
        wdv = wd.rearrange("(c p) d -> p c d", p=128)
        for t0 in range(0, NTOK, TT):
            T = min(TT, NTOK - t0)
            htoks = self.toks_h(t0, T)
            P.q_a.dma(xt[:, :, :T], self.hT[:, :, t0:t0 + T].rearrange("c p t -> p c t"), outs=[xtok], ins=htoks)
            sq, sqtok = sg_r.next()
            self.rms_scale(xt, xtok, T, lambda c: gain[:, l, c:c + 1], gtok, pss, psstok, sq, sqtok, rstd, rtok)
            def load_gu(f):
                g_, gt_ = wg_r.next(); u_, ut_ = wu_r.next()
                P.q_w.dma(g_[:], wgv[:, :, f * 128:(f + 1) * 128], outs=[gt_])
                P.q_w.dma(u_[:], wuv[:, :, f * 128:(f + 1) * 128], outs=[ut_])
                return g_, gt_, u_, ut_
            nxt = load_gu(0)
            for f in range(FC):
                g_, gt_, u_, ut_ = nxt
                if f + 1 < FC:
                    nxt = load_gu(f + 1)
                psg, psgt = psg_r.next(); psu, psut = psu_r.next()
                for c in range(DC):
                    P.pe.op(lambda: nc.tensor.matmul(psg[:, :T], g_[:, c, :], xt[:, c, :T], start=(c == 0), stop=(c == DC - 1)),
                            outs=[psgt], ins=[gt_, xtok])
                for c in range(DC):
                    P.pe.op(lambda: nc.tensor.matmul(psu[:, :T], u_[:, c, :], xt[:, c, :T], start=(c == 0), stop=(c == DC - 1)),
                            outs=[psut], ins=[ut_, xtok])
                sg, sgt = sg_r.next()
                P.act.op(lambda: nc.scalar.activation(out=sg[:, :T], in_=psg[:, :T], func=AF.Silu), outs=[sgt], ins=[psgt])
                P.dve.op(lambda: nc.vector.tensor_tensor(out=hf[:, f, :T], in0=psu[:, :T], in1=sg[:, :T], op=ALU.mult),
                         outs=[hftoks[f]], ins=[psut, sgt])
            def load_d(i):
                dc, hh = divmod(i, 2)
                d_, dt_ = wd_r.next()
                P.q_w.dma(d_[:], wdv[:, hh * HF:(hh + 1) * HF, dc * 128:(dc + 1) * 128], outs=[dt_])
                return d_, dt_
            pend = [load_d(0), load_d(1)]
            for dc in range(DC):
                hres, hrt = hres_r.next()
                P.q_a.dma(hres[:, :T], self.hT[dc, :, t0:t0 + T], outs=[hrt], ins=htoks)
                psd, psdt = psd_r.next()
                for hh in range(2):
                    d_, dt_ = pend.pop(0)
                    i_next = dc * 2 + hh + 2
                    if i_next < 2 * DC:
                        pend.append(load_d(i_next))
                    for ff in range(HF):
                        f = hh * HF + ff
                        P.pe.op(lambda: nc.tensor.matmul(psd[:, :T], d_[:, ff, :], hf[:, f, :T], start=(f == 0), stop=(f == FC - 1)),
                                outs=[psdt], ins=[dt_, hftoks[f]])
                P.dve.op(lambda: nc.vector.scalar_tensor_tensor(out=hres[:, :T], in0=psd[:, :T], scalar=0.5, in1=hres[:, :T],
                                                                op0=ALU.mult, op1=ALU.add),
                         outs=[hrt], ins=[psdt, hrt])
                P.q_a.dma(self.hT[dc, :, t0:t0 + T], hres[:, :T], ins=[hrt], outs=htoks)
        P.release(m)


    def bc_rows(self, dram_ap_row_tensor, offset, n, parts=128):
        return bass.AP(dram_ap_row_tensor.tensor, offset, [[0, parts], [1, n]])

    def even_mixer(self, l):
        import os
        st = os.environ.get("DBG_STAGES", "inproj,attn,s5,out").split(",")
        if "inproj" in st:
            self.even_inproj(l)
        if "attn" in st:
            self.even_attn(l)
        if "s5" in st:
            self.even_s5(l)
        if "out" in st:
            self.even_out(l)

    def headnorm_rope(self, buf, btok, g_rep, gtok, cos, sin, cstok, tmp, ttok, ss, sstok, t4, t4tok):
        P = self.P; nc = P.nc
        P.dve.op(lambda: nc.vector.tensor_tensor(out=tmp[:], in0=buf[:], in1=buf[:], op=ALU.mult), outs=[ttok], ins=[btok])
        P.dve.op(lambda: nc.vector.tensor_reduce(out=ss[:], in_=tmp[:], axis=AX.X, op=ALU.add), outs=[sstok], ins=[ttok])
        P.act.op(lambda: nc.scalar.activation(out=ss[:], in_=ss[:], func=AF.Sqrt, bias=self.epsb[:], scale=1.0 / 192),
                 outs=[sstok], ins=[sstok, self.t_eps])
        P.dve.op(lambda: nc.vector.reciprocal(out=ss[:], in_=ss[:]), outs=[sstok], ins=[sstok])
        P.dve.op(lambda: nc.vector.tensor_tensor(out=buf[:], in0=buf[:], in1=ss[:].unsqueeze(2).to_broadcast([128, 8, 192]), op=ALU.mult),
                 outs=[btok], ins=[btok, sstok])
        P.dve.op(lambda: nc.vector.tensor_tensor(out=buf[:], in0=buf[:], in1=g_rep[:].unsqueeze(1).to_broadcast([128, 8, 192]), op=ALU.mult),
                 outs=[btok], ins=[btok, gtok])
        x1 = buf[:, :, 128:160]; x2 = buf[:, :, 160:192]
        cb = cos.unsqueeze(1).to_broadcast([128, 8, 32]); sb_ = sin.unsqueeze(1).to_broadcast([128, 8, 32])
        P.dve.op(lambda: nc.vector.tensor_tensor(out=t4[:, 0], in0=x1, in1=cb, op=ALU.mult), outs=[t4tok], ins=[btok, cstok])
        P.dve.op(lambda: nc.vector.tensor_tensor(out=t4[:, 1], in0=x2, in1=sb_, op=ALU.mult), outs=[t4tok], ins=[btok, cstok])
        P.dve.op(lambda: nc.vector.tensor_tensor(out=t4[:, 2], in0=x1, in1=sb_, op=ALU.mult), outs=[t4tok], ins=[btok, cstok])
        P.dve.op(lambda: nc.vector.tensor_tensor(out=t4[:, 3], in0=x2, in1=cb, op=ALU.mult), outs=[t4tok], ins=[btok, cstok])
        P.dve.op(lambda: nc.vector.tensor_tensor(out=x1, in0=t4[:, 0], in1=t4[:, 1], op=ALU.subtract), outs=[btok], ins=[t4tok])
        P.dve.op(lambda: nc.vector.tensor_tensor(out=x2, in0=t4[:, 2], in1=t4[:, 3], op=ALU.add), outs=[btok], ins=[t4tok])

    def even_inproj(self, l):
        P = self.P; nc = P.nc; i = l // 2
        NS, LP, NT = self.NS, self.LP, self.NT
        m = P.mark()
        TT = 512
        Wv = self.w["ev_w_in"][i].rearrange("(c p) f -> p c f", p=128)
        gain, gtok = self.gains["norm_mix"]
        wuq = P.sb([128, 4, 1536], "wuq"); t_wuq = Tok()
        P.q_w.dma(wuq[:], self.w["mla_w_uq"][i].rearrange("(c p) f -> p c f", p=128), outs=[t_wuq])
        wukv = P.sb([128, 2, 2048], "wukv"); t_wukv = Tok()
        P.q_w.dma(wukv[:], self.w["mla_w_ukv"][i].rearrange("(c p) f -> p c f", p=128), outs=[t_wukv])
        wkr = P.sb([128, DC, 64], "wkr"); t_wkr = Tok()
        P.q_w.dma(wkr[:], Wv[:, :, 1280:1344], outs=[t_wkr])
        gcq = P.sb([128, 4], "gcq"); gckv = P.sb([128, 2], "gckv"); t_g = Tok()
        P.q_a.dma(gcq[:], self.w["mla_g_cq"][i].rearrange("(c p) -> p c", p=128), outs=[t_g], allow_slow_non_contiguous=True)
        P.q_a.dma(gckv[:], self.w["mla_g_ckv"][i].rearrange("(c p) -> p c", p=128), outs=[t_g], allow_slow_non_contiguous=True)
        gq = P.sb([128, 192], "gq"); gk = P.sb([128, 192], "gk"); t_gq = Tok()
        P.q_a.dma(gq[:], self.bc_rows(self.w["mla_g_q"], i * 192, 192), outs=[t_gq])
        P.q_a.dma(gk[:], self.bc_rows(self.w["mla_g_k"], i * 192, 192), outs=[t_gq])
        cosT = P.sb([128, NT, 32], "cos"); sinT = P.sb([128, NT, 32], "sin"); t_cs = Tok()
        P.q_a.dma(cosT[:], self.c_rope_cos.rearrange("(j p) f -> p j f", p=128), outs=[t_cs])
        P.q_a.dma(sinT[:], self.c_rope_sin.rearrange("(j p) f -> p j f", p=128), outs=[t_cs])
        xt, xtok = P.sb([128, DC, TT], "evx"), Tok()
        cq, cqtok = P.sb([128, 4, TT], "cq"), Tok()
        ckv, ckvtok = P.sb([128, 2, TT], "ckv"), Tok()
        win_r = Ring([(P.sb([128, DC, 128], "win"), Tok()) for _ in range(2)])
        ub_r = Ring([(P.sb([128, TT], "ub"), Tok()) for _ in range(2)])
        rstd, rtok = P.sb([128, TT], "rstd"), Tok()
        sq, sqtok = P.sb([128, TT], "sq"), Tok()
        q_sb, qtok = P.sb([128, 8, 192], "q_sb"), Tok()
        k_sb, ktok = P.sb([128, 8, 192], "k_sb"), Tok()
        v_sb, vtok = P.sb([128, 8, 128], "v_sb"), Tok()
        tmp, ttok = P.sb([128, 8, 192], "tmp"), Tok()
        ss, sstok = P.sb([128, 8], "ss"), Tok()
        t4, t4tok = P.sb([128, 4, 8, 32], "t4"), Tok()
        qTn, qTntok = P.sb([128, 8, 128], "qTn"), Tok()
        qTr, qTrtok = P.sb([64, 8, 128], "qTr"), Tok()
        kTn, kTntok = P.sb([128, 8, 128], "kTn"), Tok()
        kTr, kTrtok = P.sb([64, 8, 128], "kTr"), Tok()
        ps01 = self.psring([0, 1])
        pss, psstok = P.psb[7]
        for s in range(NS):
            for t0 in range(0, LP, TT):
                T = min(TT, LP - t0)
                tok0 = s * LP + t0
                htoks = self.toks_h(tok0, T)
                P.q_a.dma(xt[:, :, :T], self.hT[:, :, tok0:tok0 + T].rearrange("c p t -> p c t"), outs=[xtok], ins=htoks)
                self.rms_scale(xt, xtok, T, lambda c: gain[:, l, c:c + 1], gtok, pss, psstok, sq, sqtok, rstd, rtok)
                def load_w(j):
                    w_, wt_ = win_r.next()
                    P.q_w.dma(w_[:], Wv[:, :, j * 128:(j + 1) * 128], outs=[wt_])
                    return w_, wt_
                nxt = load_w(0)
                for j in range(10):
                    w_, wt_ = nxt
                    if j + 1 < 10:
                        nxt = load_w(j + 1)
                    ps, ptok = ps01.next()
                    for c in range(DC):
                        P.pe.op(lambda: nc.tensor.matmul(ps[:, :T], w_[:, c, :], xt[:, c, :T], start=(c == 0), stop=(c == DC - 1)),
                                outs=[ptok], ins=[wt_, xtok])
                    if j < 4:
                        ub, ubtok = ub_r.next()
                        P.act.op(lambda: nc.scalar.copy(out=ub[:, :T], in_=ps[:, :T]), outs=[ubtok], ins=[ptok])
                        P.q_a.dma(self.uT[j, :, tok0:tok0 + T], ub[:, :T], ins=[ubtok])
                    elif j < 8:
                        P.act.op(lambda: nc.scalar.copy(out=cq[:, j - 4, :T], in_=ps[:, :T]), outs=[cqtok], ins=[ptok])
                    else:
                        P.act.op(lambda: nc.scalar.copy(out=ckv[:, j - 8, :T], in_=ps[:, :T]), outs=[ckvtok], ins=[ptok])
                self.rms_scale(cq, cqtok, T, lambda c: gcq[:, c:c + 1], t_g, pss, psstok, sq, sqtok, rstd, rtok, nchunks=4, dim=512)
                self.rms_scale(ckv, ckvtok, T, lambda c: gckv[:, c:c + 1], t_g, pss, psstok, sq, sqtok, rstd, rtok, nchunks=2, dim=256)
                import os
                LV = int(os.environ.get("DBG_INPROJ", "9"))
                for tb in range(T // 128 if LV >= 2 else 0):
                    tsl = slice(tb * 128, (tb + 1) * 128)
                    jblk = (t0 // 128) + tb
                    tokb = tok0 + tb * 128
                    qflat = q_sb[:].rearrange("p h d -> p (h d)")
                    SUB = os.environ.get("DBG_SUB", "q,kv,kr").split(",")
                    for n in range(3 if "q" in SUB else 0):
                        ps, ptok = P.psb[2 + n]
                        for c in range(4):
                            P.pe.op(lambda: nc.tensor.matmul(ps[:, :], cq[:, c, tsl], wuq[:, c, n * 512:(n + 1) * 512], start=(c == 0), stop=(c == 3)),
                                    outs=[ptok], ins=[cqtok, t_wuq])
                        P.act.op(lambda: nc.scalar.copy(out=qflat[:, n * 512:(n + 1) * 512], in_=ps[:, :]), outs=[qtok], ins=[ptok])
                    for n4 in range(4 if "kv" in SUB else 0):
                        ps, ptok = P.psb[5 + (n4 % 2)]
                        for c in range(2):
                            P.pe.op(lambda: nc.tensor.matmul(ps[:, :], ckv[:, c, tsl], wukv[:, c, n4 * 512:(n4 + 1) * 512], start=(c == 0), stop=(c == 1)),
                                    outs=[ptok], ins=[ckvtok, t_wukv])
                        psv = ps[:, :].rearrange("p (h d) -> p h d", h=2)
                        P.act.op(lambda: nc.scalar.copy(out=k_sb[:, 2 * n4:2 * n4 + 2, 0:128], in_=psv[:, :, 0:128]), outs=[ktok], ins=[ptok])
                        P.dve.op(lambda: nc.vector.tensor_copy(out=v_sb[:, 2 * n4:2 * n4 + 2, :], in_=psv[:, :, 128:256]), outs=[vtok], ins=[ptok, ktok])
                    ps, ptok = ps01.next()
                    for c in range(DC if "kr" in SUB else 0):
                        P.pe.op(lambda: nc.tensor.matmul(ps[:, 0:64], xt[:, c, tsl], wkr[:, c, :], start=(c == 0), stop=(c == DC - 1)),
                                outs=[ptok], ins=[xtok, t_wkr])
                    if "kr" in SUB:
                      P.act.op(lambda: nc.scalar.copy(out=k_sb[:, :, 128:192], in_=ps[:, 0:64].unsqueeze(1).to_broadcast([128, 8, 64])),
                             outs=[ktok], ins=[ptok])
                    if LV >= 3:
                        P.q_a.dma(self.Vd[tokb:tokb + 128, :], v_sb[:].rearrange("p h d -> p (h d)"), ins=[vtok])
                    if LV < 4:
                        continue
                    self.headnorm_rope(q_sb, qtok, gq, t_gq, cosT[:, jblk, :], sinT[:, jblk, :], t_cs, tmp, ttok, ss, sstok, t4, t4tok)
                    self.headnorm_rope(k_sb, ktok, gk, t_gq, cosT[:, jblk, :], sinT[:, jblk, :], t_cs, tmp, ttok, ss, sstok, t4, t4tok)
                    if LV < 5:
                        continue
                    for (src, stok_, dn, dntok, dr, drtok, dst) in ((q_sb, qtok, qTn, qTntok, qTr, qTrtok, self.QT),
                                                                  (k_sb, ktok, kTn, kTntok, kTr, kTrtok, self.KT)):
                        for hg in range(2):
                            ps, ptok = ps01.next()
                            for hl in range(4):
                                h = hg * 4 + hl
                                P.pe.op(lambda: nc.tensor.transpose(ps[:, hl * 128:(hl + 1) * 128], src[:, h, 0:128], self.ident[:]),
                                        outs=[ptok], ins=[stok_, self.t_ident])
                            P.act.op(lambda: nc.scalar.copy(out=dn[:, hg * 4:(hg + 1) * 4, :], in_=ps[:, :].rearrange("p (h t) -> p h t", h=4)),
                                     outs=[dntok], ins=[ptok])
                            ps, ptok = ps01.next()
                            for hl in range(4):
                                h = hg * 4 + hl
                                P.pe.op(lambda: nc.tensor.transpose(ps[0:64, hl * 128:(hl + 1) * 128], src[:, h, 128:192], self.ident[:]),
                                        outs=[ptok], ins=[stok_, self.t_ident])
                            P.dve.op(lambda: nc.vector.tensor_copy(out=dr[:, hg * 4:(hg + 1) * 4, :], in_=ps[0:64, :].rearrange("p (h t) -> p h t", h=4)),
                                     outs=[drtok], ins=[ptok])
                        P.q_a.dma(dst[:, 0:128, tokb:tokb + 128].rearrange("h p t -> p h t"), dn[:], ins=[dntok])
                        P.q_a.dma(dst[:, 128:192, tokb:tokb + 128].rearrange("h p t -> p h t"), dr[:], ins=[drtok])
        P.release(m)

    def even_attn(self, l):
        P = self.P; nc = P.nc
        NS, LP, NT = self.NS, self.LP, self.NT
        m = P.mark()
        maskT = P.sb([128, 128], "maskT"); t_mask = Tok()
        P.q_a.dma(maskT[:], self.c_maskT[:, :], outs=[t_mask])
        sets = Ring([dict(qn=P.sb([128, LP]), qr=P.sb([64, LP]), kn=P.sb([128, LP]), kr=P.sb([64, LP]), v=P.sb([128, NT, 128]), tok=Tok())
                     for _ in range(2)])
        pt_r = Ring([(P.sb([128, 512], "pt"), Tok()) for _ in range(3)])
        rden, rdtok = P.sb([128, 512], "rden"), Tok()
        yo_r = Ring([(P.sb([128, 512], "yo"), Tok()) for _ in range(2)])
        ps_s = self.psring([0, 1, 2])
        ps_o = self.psring([3, 4]); ps_d = self.psring([5, 6])
        scale = 192.0 ** -0.5
        for s in range(NS):
            for h in range(8):
                S = sets.next(); stok = S["tok"]
                c0s = s * LP
                P.q_a.dma(S["qn"][:], self.QT[h, 0:128, c0s:c0s + LP], outs=[stok])
                P.q_a.dma(S["qr"][:], self.QT[h, 128:192, c0s:c0s + LP], outs=[stok])
                P.q_a.dma(S["kn"][:], self.KT[h, 0:128, c0s:c0s + LP], outs=[stok])
                P.q_a.dma(S["kr"][:], self.KT[h, 128:192, c0s:c0s + LP], outs=[stok])
                P.q_a.dma(S["v"][:], self.Vd[c0s:c0s + LP, h * 128:(h + 1) * 128].rearrange("(b p) d -> p b d", p=128), outs=[stok])
                for q0 in range(0, LP, 512):
                    Tq = min(512, LP - q0)
                    nkb = (q0 + Tq) // 128
                    po, potok = ps_o.next(); pd, pdtok = ps_d.next()
                    for kb in range(nkb):
                        c0 = max(q0, kb * 128); off = c0 - q0; w = q0 + Tq - c0
                        ps, pstok = ps_s.next()
                        ksl = slice(kb * 128, (kb + 1) * 128)
                        P.pe.op(lambda: nc.tensor.matmul(ps[:, :w], S["kn"][:, ksl], S["qn"][:, c0:c0 + w], start=True, stop=False),
                                outs=[pstok], ins=[stok])
                        P.pe.op(lambda: nc.tensor.matmul(ps[:, :w], S["kr"][:, ksl], S["qr"][:, c0:c0 + w], start=False, stop=True),
                                outs=[pstok], ins=[stok])
                        pt, pttok = pt_r.next()
                        P.act.op(lambda: nc.scalar.activation(out=pt[:, :w], in_=ps[:, :w], func=AF.Exp, scale=scale), outs=[pttok], ins=[pstok])
                        if kb * 128 >= q0:
                            P.dve.op(lambda: nc.vector.tensor_tensor(out=pt[:, 0:128], in0=pt[:, 0:128], in1=maskT[:], op=ALU.mult),
                                     outs=[pttok], ins=[pttok, t_mask])
                        P.pe.op(lambda: nc.tensor.matmul(po[:, off:off + w], S["v"][:, kb, :], pt[:, :w], start=(kb == 0), stop=(kb == nkb - 1)),
                                outs=[potok], ins=[stok, pttok])
                        P.pe.op(lambda: nc.tensor.matmul(pd[:, off:off + w], self.ones[:], pt[:, :w], start=(kb == 0), stop=(kb == nkb - 1)),
                                outs=[pdtok], ins=[self.t_ones, pttok])
                    P.dve.op(lambda: nc.vector.reciprocal(out=rden[:, :Tq], in_=pd[:, :Tq]), outs=[rdtok], ins=[pdtok])
                    yo, yotok = yo_r.next()
                    P.dve.op(lambda: nc.vector.tensor_tensor(out=yo[:, :Tq], in0=po[:, :Tq], in1=rden[:, :Tq], op=ALU.mult),
                             outs=[yotok], ins=[potok, rdtok])
                    P.q_a.dma(self.yE[4 + h, :, c0s + q0:c0s + q0 + Tq], yo[:, :Tq], ins=[yotok])
        P.release(m)
    def even_s5(self, l):
        P = self.P; nc = P.nc; i = l // 2
        NS, LP, NT = self.NS, self.LP, self.NT
        m = P.mark()
        PI = math.pi
        dve = P.dve; act = P.act; pe = P.pe
        V = nc.vector
        tk = Tok("s5setup")
        def sbt(shape, name):
            return P.sb(shape, name)
        ps0, ps0tok = P.psb[0]
        raw = sbt([16, 3, 128], "s5raw")
        P.q_a.dma(raw[:, 0, :], self.w["s5_a_re"][i].rearrange("(gp gl) p -> gp (gl p)", gl=2), outs=[tk])
        P.q_a.dma(raw[:, 1, :], self.w["s5_a_im"][i].rearrange("(gp gl) p -> gp (gl p)", gl=2), outs=[tk])
        ldt = sbt([16, 2], "ldt")
        P.q_a.dma(ldt[:], self.w["s5_log_dt"][i].rearrange("(gp gl) -> gp gl", gl=2), outs=[tk])
        dve.op(lambda: V.tensor_copy(out=raw[:, 2, :].rearrange("g (a b) -> g a b", a=2), in_=ldt[:].unsqueeze(2).to_broadcast([16, 2, 64])),
               outs=[tk], ins=[tk])
        ar = sbt([128, 16], "ar"); ai = sbt([128, 16], "ai"); dt = sbt([128, 16], "dt")
        for k, dst in enumerate((ar, ai, dt)):
            pe.op(lambda: nc.tensor.transpose(ps0[:, k * 16:(k + 1) * 16], raw[:, k, :], self.ident[0:16, 0:16]), outs=[ps0tok], ins=[tk, self.t_ident])
        act.op(lambda: nc.scalar.copy(out=ar[:], in_=ps0[:, 0:16]), outs=[tk], ins=[ps0tok])
        act.op(lambda: nc.scalar.copy(out=ai[:], in_=ps0[:, 16:32]), outs=[tk], ins=[ps0tok])
        act.op(lambda: nc.scalar.activation(out=dt[:], in_=ps0[:, 32:48], func=AF.Exp), outs=[tk], ins=[ps0tok])
        def T16(name):
            return sbt([128, 16], name)
        def tt(out, a, b, op):
            dve.op(lambda: V.tensor_tensor(out=out, in0=a, in1=b, op=op), outs=[tk], ins=[tk])
        def ts(out, a, s1, op0, s2=None, op1=None):
            if op1 is None:
                dve.op(lambda: V.tensor_scalar(out, a, s1, None, op0), outs=[tk], ins=[tk])
            else:
                dve.op(lambda: V.tensor_scalar(out, a, s1, s2, op0, op1), outs=[tk], ins=[tk])
        mag = T16("mag"); ang = T16("ang"); sn = T16("sn"); cs = T16("cs"); tmpa = T16("tmpa"); tmpb = T16("tmpb")
        twopi = T16("twopi")
        dve.op(lambda: V.memset(twopi[:], 2 * PI), outs=[tk], ins=[tk])
        tt(tmpa[:], dt[:], ar[:], ALU.mult)
        act.op(lambda: nc.scalar.activation(out=mag[:], in_=tmpa[:], func=AF.Exp), outs=[tk], ins=[tk])
        tt(ang[:], dt[:], ai[:], ALU.mult)
        for (dst, shift) in ((sn, PI), (cs, PI + PI / 2)):
            ts(tmpa[:], ang[:], shift, ALU.add)
            for mult_ in (16.0, 8.0, 4.0, 2.0, 1.0):
                ts(tmpb[:], tmpa[:], mult_ * 2 * PI, ALU.is_ge, -mult_ * 2 * PI, ALU.mult)
                tt(tmpa[:], tmpa[:], tmpb[:], ALU.add)
            ts(tmpa[:], tmpa[:], -PI, ALU.add, 3.1415925, ALU.min)
            ts(tmpa[:], tmpa[:], -3.1415925, ALU.max)
            act.op(lambda: nc.scalar.activation(out=dst[:], in_=tmpa[:], func=AF.Sin), outs=[tk], ins=[tk])
        lr = T16("lr"); li = T16("li"); lr1 = T16("lr1"); den = T16("den"); cr = T16("cr"); ci = T16("ci")
        tt(lr[:], mag[:], cs[:], ALU.mult); tt(li[:], mag[:], sn[:], ALU.mult)
        ts(lr1[:], lr[:], -1.0, ALU.add)
        tt(den[:], ar[:], ar[:], ALU.mult); tt(tmpa[:], ai[:], ai[:], ALU.mult); tt(den[:], den[:], tmpa[:], ALU.add)
        dve.op(lambda: V.reciprocal(out=den[:], in_=den[:]), outs=[tk], ins=[tk])
        tt(cr[:], lr1[:], ar[:], ALU.mult); tt(tmpa[:], li[:], ai[:], ALU.mult); tt(cr[:], cr[:], tmpa[:], ALU.add); tt(cr[:], cr[:], den[:], ALU.mult)
        tt(ci[:], li[:], ar[:], ALU.mult); tt(tmpa[:], lr1[:], ai[:], ALU.mult); tt(ci[:], ci[:], tmpa[:], ALU.subtract); tt(ci[:], ci[:], den[:], ALU.mult)
        br = sbt([128, 16, 16], "br"); bi = sbt([128, 16, 16], "bi")
        P.q_a.dma(br[:], self.w["s5_b_re"][i].rearrange("(gp gl) p c -> (gl p) gp c", gl=2), outs=[tk], ins=[tk])
        P.q_a.dma(bi[:], self.w["s5_b_im"][i].rearrange("(gp gl) p c -> (gl p) gp c", gl=2), outs=[tk], ins=[tk])
        t3a = sbt([128, 16, 16], "t3a"); t3b = sbt([128, 16, 16], "t3b")
        Bblk = [sbt([128, 16, 32], "Bblk_r"), sbt([128, 16, 32], "Bblk_i")]
        crb = cr[:].unsqueeze(2).to_broadcast([128, 16, 16]); cib = ci[:].unsqueeze(2).to_broadcast([128, 16, 16])
        for k in range(2):
            dve.op(lambda: V.memset(Bblk[k][:], 0.0), outs=[tk], ins=[tk])
        for k, (x0, x1, op) in enumerate(((br, bi, ALU.subtract), (bi, br, ALU.add))):
            tt(t3a[:], x0[:], crb, ALU.mult); tt(t3b[:], x1[:], cib, ALU.mult); tt(t3a[:], t3a[:], t3b[:], op)
            dve.op(lambda: V.tensor_copy(out=Bblk[k][0:64, :, 0:16], in_=t3a[0:64]), outs=[tk], ins=[tk])
            dve.op(lambda: V.tensor_copy(out=Bblk[k][64:128, :, 16:32], in_=t3a[64:128]), outs=[tk], ins=[tk])
        WbT = [sbt([32, 16, 128], "WbT_r"), sbt([32, 16, 128], "WbT_i")]
        for k in range(2):
            for g4 in range(4):
                for gq in range(4):
                    gp = g4 * 4 + gq
                    pe.op(lambda: nc.tensor.transpose(ps0[0:32, gq * 128:(gq + 1) * 128], Bblk[k][:, gp, :], self.ident[:]), outs=[ps0tok], ins=[tk, self.t_ident])
                act.op(lambda: nc.scalar.copy(out=WbT[k][:, g4 * 4:(g4 + 1) * 4, :], in_=ps0[0:32, :].rearrange("p (g s) -> p g s", g=4)), outs=[tk], ins=[ps0tok])
        Cblk = [sbt([128, 16, 32], "Cblk_r"), sbt([128, 16, 32], "Cblk_i")]
        craw = sbt([128, 128], "craw")
        for k, nm in enumerate(("s5_c_re", "s5_c_im")):
            dve.op(lambda: V.memset(Cblk[k][:], 0.0), outs=[tk], ins=[tk])
            cv = self.w[nm][i].rearrange("g c p -> (g c) p")
            for q4 in range(4):
                P.q_a.dma(craw[:, 0:64], cv[q4 * 128:(q4 + 1) * 128, :], outs=[tk], ins=[tk])
                P.q_a.dma(craw[:, 64:128], cv[q4 * 128:(q4 + 1) * 128, :], outs=[tk], ins=[tk])
                pe.op(lambda: nc.tensor.transpose(ps0[:, 0:128], craw[:], self.ident[:]), outs=[ps0tok], ins=[tk, self.t_ident])
                psv = ps0[:, 0:128].rearrange("p (gq gl c) -> p gq gl c", gq=4, gl=2)
                sc = 1.0 if k == 0 else -1.0
                act.op(lambda: nc.scalar.mul(out=Cblk[k][0:64, q4 * 4:(q4 + 1) * 4, 0:16], in_=psv[0:64, :, 0, :], mul=sc), outs=[tk], ins=[ps0tok])
                act.op(lambda: nc.scalar.mul(out=Cblk[k][64:128, q4 * 4:(q4 + 1) * 4, 16:32], in_=psv[64:128, :, 1, :], mul=sc), outs=[tk], ins=[ps0tok])
        dsk = sbt([32, 16], "dsk")
        P.q_a.dma(dsk[:], self.w["s5_d"][i].rearrange("(gp r) -> r gp", r=32), outs=[tk], ins=[tk], allow_slow_non_contiguous=True)
        NK = max(1, (LP - 1).bit_length())
        wr = sbt([128, NK, 16], "wr"); wi = sbt([128, NK, 16], "wi")
        dve.op(lambda: V.tensor_copy(out=wr[:, 0, :], in_=cs[:]), outs=[tk], ins=[tk])
        ts(wi[:, 0, :], sn[:], -1.0, ALU.mult)
        for k in range(1, NK):
            tt(tmpa[:], wr[:, k - 1, :], wr[:, k - 1, :], ALU.mult); tt(tmpb[:], wi[:, k - 1, :], wi[:, k - 1, :], ALU.mult)
            tt(wr[:, k, :], tmpa[:], tmpb[:], ALU.subtract)
            tt(tmpa[:], wr[:, k - 1, :], wi[:, k - 1, :], ALU.mult)
            ts(wi[:, k, :], tmpa[:], 2.0, ALU.mult)
        Er = sbt([128, LP], "Er"); Ei = sbt([128, LP], "Ei"); tmpE = sbt([128, LP], "tmpE"); tE = Tok("E")
        u_r = Ring([(sbt([32, LP], "u_gp"), Tok()) for _ in range(2)])
        vr = sbt([128, LP], "vr"); vi = sbt([128, LP], "vi"); tv = Tok("v")
        zr = sbt([128, LP], "zr"); zi = sbt([128, LP], "zi"); tz = Tok("z")
        ta = sbt([128, 512], "ta"); tb_ = sbt([128, 512], "tb"); tta = Tok(); ttb = Tok()
        y_r = Ring([(sbt([32, LP], "y_gp"), Tok()) for _ in range(2)])
        ps_b = self.psring([1, 2, 3, 4]); ps_y = self.psring([5, 6])
        for gp in range(16):
            dve.op(lambda: V.memset(Er[:, 0:1], 1.0), outs=[tE], ins=[tk])
            dve.op(lambda: V.memset(Ei[:, 0:1], 0.0), outs=[tE], ins=[tk])
            n = 1; k = 0
            while n < LP:
                cnt = min(n, LP - n)
                wrk = wr[:, k, gp:gp + 1]; wik = wi[:, k, gp:gp + 1]
                dve.op(lambda: V.tensor_scalar(tmpE[:, 0:cnt], Ei[:, 0:cnt], wik, None, ALU.mult), outs=[tE], ins=[tE, tk])
                dve.op(lambda: V.scalar_tensor_tensor(out=Er[:, n:n + cnt], in0=Er[:, 0:cnt], scalar=wrk, in1=tmpE[:, 0:cnt], op0=ALU.mult, op1=ALU.subtract), outs=[tE], ins=[tE, tk])
                dve.op(lambda: V.tensor_scalar(tmpE[:, 0:cnt], Ei[:, 0:cnt], wrk, None, ALU.mult), outs=[tE], ins=[tE, tk])
                dve.op(lambda: V.scalar_tensor_tensor(out=Ei[:, n:n + cnt], in0=Er[:, 0:cnt], scalar=wik, in1=tmpE[:, 0:cnt], op0=ALU.mult, op1=ALU.add), outs=[tE], ins=[tE, tk])
                n += cnt; k += 1
            for s in range(NS):
                c0s = s * LP
                u, utok = u_r.next()
                P.q_a.dma(u[:], self.uT[gp // 4, (gp % 4) * 32:(gp % 4) * 32 + 32, c0s:c0s + LP], outs=[utok])
                for q0 in range(0, LP, 512):
                    Tq = min(512, LP - q0); sl = slice(q0, q0 + Tq)
                    pbr, pbrt = ps_b.next(); pbi, pbit = ps_b.next()
                    pe.op(lambda: nc.tensor.matmul(pbr[:, :Tq], WbT[0][:, gp, :], u[:, sl], start=True, stop=True), outs=[pbrt], ins=[tk, utok])
                    pe.op(lambda: nc.tensor.matmul(pbi[:, :Tq], WbT[1][:, gp, :], u[:, sl], start=True, stop=True), outs=[pbit], ins=[tk, utok])
                    dve.op(lambda: V.tensor_tensor(out=ta[:, :Tq], in0=pbr[:, :Tq], in1=Er[:, sl], op=ALU.mult), outs=[tta], ins=[pbrt, tE])
                    dve.op(lambda: V.tensor_tensor(out=tb_[:, :Tq], in0=pbi[:, :Tq], in1=Ei[:, sl], op=ALU.mult), outs=[ttb], ins=[pbit, tE])
                    dve.op(lambda: V.tensor_tensor(out=vr[:, sl], in0=ta[:, :Tq], in1=tb_[:, :Tq], op=ALU.subtract), outs=[tv], ins=[tta, ttb])
                    dve.op(lambda: V.tensor_tensor(out=ta[:, :Tq], in0=pbi[:, :Tq], in1=Er[:, sl], op=ALU.mult), outs=[tta], ins=[pbit, tE])
                    dve.op(lambda: V.tensor_tensor(out=tb_[:, :Tq], in0=pbr[:, :Tq], in1=Ei[:, sl], op=ALU.mult), outs=[ttb], ins=[pbrt, tE])
                    dve.op(lambda: V.tensor_tensor(out=vi[:, sl], in0=ta[:, :Tq], in1=tb_[:, :Tq], op=ALU.add), outs=[tv], ins=[tta, ttb])
                rho = mag[:, gp:gp + 1].to_broadcast([128, LP])
                dve.op(lambda: V.tensor_tensor_scan(out=zr[:], data0=rho, data1=vr[:], initial=0.0, op0=ALU.mult, op1=ALU.add), outs=[tz], ins=[tv, tk])
                dve.op(lambda: V.tensor_tensor_scan(out=zi[:], data0=rho, data1=vi[:], initial=0.0, op0=ALU.mult, op1=ALU.add), outs=[tz], ins=[tv, tk])
                dve.op(lambda: V.tensor_tensor(out=vr[:], in0=Er[:], in1=zr[:], op=ALU.mult), outs=[tv], ins=[tz, tE])
                dve.op(lambda: V.tensor_tensor(out=tmpE[:], in0=Ei[:], in1=zi[:], op=ALU.mult), outs=[tv], ins=[tz, tE])
                dve.op(lambda: V.tensor_tensor(out=vr[:], in0=vr[:], in1=tmpE[:], op=ALU.add), outs=[tv], ins=[tv])
                dve.op(lambda: V.tensor_tensor(out=vi[:], in0=Er[:], in1=zi[:], op=ALU.mult), outs=[tv], ins=[tz, tE])
                dve.op(lambda: V.tensor_tensor(out=tmpE[:], in0=Ei[:], in1=zr[:], op=ALU.mult), outs=[tv], ins=[tz, tE])
                dve.op(lambda: V.tensor_tensor(out=vi[:], in0=vi[:], in1=tmpE[:], op=ALU.subtract), outs=[tv], ins=[tv])
                y, ytok = y_r.next()
                for q0 in range(0, LP, 512):
                    Tq = min(512, LP - q0); sl = slice(q0, q0 + Tq)
                    py, pyt = ps_y.next()
                    pe.op(lambda: nc.tensor.matmul(py[0:32, :Tq], Cblk[0][:, gp, :], vr[:, sl], start=True, stop=False), outs=[pyt], ins=[tk, tv])
                    pe.op(lambda: nc.tensor.matmul(py[0:32, :Tq], Cblk[1][:, gp, :], vi[:, sl], start=False, stop=True), outs=[pyt], ins=[tk, tv])
                    dve.op(lambda: V.scalar_tensor_tensor(out=y[:, sl], in0=u[:, sl], scalar=dsk[:, gp:gp + 1], in1=py[0:32, :Tq], op0=ALU.mult, op1=ALU.add),
                           outs=[ytok], ins=[pyt, utok, tk])
                P.q_a.dma(self.yE[gp // 4, (gp % 4) * 32:(gp % 4) * 32 + 32, c0s:c0s + LP], y[:], ins=[ytok])
        P.release(m)

    def out_proj(self, Wd, NCH, ymix, ytoks, T, tok0, htoks, wst_r, w_r, psd_r, hres_r, ymb, ymbtok):
        P = self.P; nc = P.nc
        Wv = Wd.rearrange("(c p) d -> p c d", p=128)
        hc = NCH // 2
        P.dve.op(lambda: nc.vector.tensor_copy(out=ymb[:, 0:hc, :T], in_=ymix[:, 0:hc, :T]), outs=[ymbtok], ins=ytoks)
        P.act.op(lambda: nc.scalar.copy(out=ymb[:, hc:NCH, :T], in_=ymix[:, hc:NCH, :T]), outs=[ymbtok], ins=ytoks + [ymbtok])
        def load_d(dc):
            ds_, dst_ = wst_r.next()
            P.q_w.dma(ds_[:, :NCH, :], Wv[:, :, dc * 128:(dc + 1) * 128], outs=[dst_])
            d_, dt_ = w_r.next()
            if dc % 2 == 0:
                P.act.op(lambda: nc.scalar.copy(out=d_[:, :NCH, :], in_=ds_[:, :NCH, :]), outs=[dt_], ins=[dst_])
            else:
                P.dve.op(lambda: nc.vector.tensor_copy(out=d_[:, :NCH, :], in_=ds_[:, :NCH, :]), outs=[dt_], ins=[dst_])
            return d_, dt_
        with nc.allow_low_precision("bf16 matmul operands, fp32 accumulation"):
            nxt = load_d(0)
            for dc in range(DC):
                d_, dt_ = nxt
                if dc + 1 < DC:
                    nxt = load_d(dc + 1)
                hres, hrt = hres_r.next()
                P.q_a.dma(hres[:, :T], self.hT[dc, :, tok0:tok0 + T], outs=[hrt], ins=htoks)
                psd, psdt = psd_r.next()
                for c in range(NCH):
                    P.pe.op(lambda: nc.tensor.matmul(psd[:, :T], d_[:, c, :], ymb[:, c, :T], start=(c == 0), stop=(c == NCH - 1)),
                            outs=[psdt], ins=[dt_, ymbtok])
                P.dve.op(lambda: nc.vector.tensor_tensor(out=hres[:, :T], in0=psd[:, :T], in1=hres[:, :T], op=ALU.add),
                         outs=[hrt], ins=[psdt, hrt])
                P.q_a.dma(self.hT[dc, :, tok0:tok0 + T], hres[:, :T], ins=[hrt], outs=htoks)

    def even_out(self, l):
        P = self.P; nc = P.nc; i = l // 2
        NS, LP, NT, NTOK = self.NS, self.LP, self.NT, self.NTOK
        m = P.mark()
        TT = 512
        V = nc.vector
        wglu = P.sb([128, 4, 512], "wglu"); t_wglu = Tok()
        P.q_w.dma(wglu[:], self.w["s5_w_glu"][i].rearrange("(c p) f -> p c f", p=128), outs=[t_wglu])
        ya, yatok = P.sb([128, 4, TT], "ya"), Tok()
        yg, ygtok = P.sb([128, 4, TT], "yg"), Tok()
        ymix = P.sb([128, 12, TT], "ymix"); ymtok = Tok(); ybtok = Tok()
        sg_r = Ring([(P.sb([128, TT], "sg"), Tok()) for _ in range(2)])
        wst_r = Ring([(P.sb([128, 12, 128], "wost"), Tok()) for _ in range(2)])
        w_r = Ring([(P.sbh([128, 12, 128], "wo"), Tok()) for _ in range(2)])
        ymb = P.sbh([128, 12, TT], "ymb"); ymbtok = Tok()
        hres_r = Ring([(P.sb([128, TT], "hres"), Tok()) for _ in range(2)])
        psg_r = self.psring([0, 1]); psd_r = self.psring([2, 3])
        C0 = math.sqrt(2.0 / math.pi)
        for t0 in range(0, NTOK, TT):
            T = min(TT, NTOK - t0)
            htoks = self.toks_h(t0, T)
            P.q_a.dma(ya[:, :, :T], self.yE[0:4, :, t0:t0 + T].rearrange("c p t -> p c t"), outs=[yatok])
            P.q_a.dma(ymix[:, 4:12, :T], self.yE[4:12, :, t0:t0 + T].rearrange("c p t -> p c t"), outs=[ybtok])
            P.dve.op(lambda: V.tensor_tensor(out=yg[:, :, :T], in0=ya[:, :, :T], in1=ya[:, :, :T], op=ALU.mult), outs=[ygtok], ins=[yatok])
            P.dve.op(lambda: V.tensor_scalar(yg[:, :, :T], yg[:, :, :T], 0.044715, 1.0, ALU.mult, ALU.add), outs=[ygtok], ins=[ygtok])
            P.dve.op(lambda: V.tensor_tensor(out=yg[:, :, :T], in0=yg[:, :, :T], in1=ya[:, :, :T], op=ALU.mult), outs=[ygtok], ins=[ygtok, yatok])
            P.act.op(lambda: nc.scalar.activation(out=yg[:, :, :T], in_=yg[:, :, :T], func=AF.Tanh, scale=C0), outs=[ygtok], ins=[ygtok])
            P.dve.op(lambda: V.tensor_scalar(yg[:, :, :T], yg[:, :, :T], 1.0, 0.5, ALU.add, ALU.mult), outs=[ygtok], ins=[ygtok])
            P.dve.op(lambda: V.tensor_tensor(out=yg[:, :, :T], in0=yg[:, :, :T], in1=ya[:, :, :T], op=ALU.mult), outs=[ygtok], ins=[ygtok, yatok])
            for jc in range(4):
                ps, ptok = psg_r.next()
                for c in range(4):
                    P.pe.op(lambda: nc.tensor.matmul(ps[:, :T], wglu[:, c, jc * 128:(jc + 1) * 128], yg[:, c, :T], start=(c == 0), stop=(c == 3)),
                            outs=[ptok], ins=[t_wglu, ygtok])
                sg, sgt = sg_r.next()
                P.act.op(lambda: nc.scalar.activation(out=sg[:, :T], in_=ps[:, :T], func=AF.Sigmoid), outs=[sgt], ins=[ptok])
                P.dve.op(lambda: V.tensor_tensor(out=ymix[:, jc, :T], in0=yg[:, jc, :T], in1=sg[:, :T], op=ALU.mult), outs=[ymtok], ins=[ygtok, sgt])
            self.out_proj(self.w["ev_w_out"][i], 12, ymix, [ymtok, ybtok], T, t0, htoks, wst_r, w_r, psd_r, hres_r, ymb, ymbtok)
        P.release(m)


    def odd_mixer(self, l):
        import os
        st = os.environ.get("DBG_STAGES", "inproj,attn,gdn,out").split(",")
        if "inproj" in st:
            self.odd_inproj(l)
        if "attn" in st:
            self.odd_sb(l)
        if "gdn" in st:
            self.odd_gdn(l)
        if "out" in st:
            self.odd_out(l)

    def odd_inproj(self, l):
        P = self.P; nc = P.nc; i = l // 2
        NS, LP, NT = self.NS, self.LP, self.NT
        m = P.mark()
        TT = 512
        Wv = self.w["od_w_in"][i].rearrange("(c p) f -> p c f", p=128)
        gain, gtok = self.gains["norm_mix"]
        xt, xtok = P.sb([128, DC, TT], "odx"), Tok()
        xb, xbtok = P.sbh([128, DC, TT], "odxb"), Tok()
        wst_r = Ring([(P.sb([128, DC, 128], "wst"), Tok()) for _ in range(3)])
        win_r = Ring([(P.sbh([128, DC, 128], "win"), Tok()) for _ in range(3)])
        wvst, wvstt = P.sb([128, DC, 512], "wsvst"), Tok()
        wv, wvt = P.sbh([128, DC, 512], "wsv"), Tok()
        wab = P.sb([128, DC, 16], "wab"); t_wab = Tok()
        P.q_w.dma(wab[:], Wv[:, :, 6144:6160], outs=[t_wab])
        ob_r = Ring([(P.sb([128, TT], "ob"), Tok()) for _ in range(3)])
        rstd, rtok = P.sb([128, TT], "rstd"), Tok()
        sq, sqtok = P.sb([128, TT], "sq"), Tok()
        ps01 = self.psring([0, 1, 2]); psv_r = self.psring([3, 4])
        pss, psstok = P.psb[7]
        plan = []
        for h in range(8):
            plan.append((self.sbQ, h, h * 128))
        for h in range(8):
            plan.append((self.sbK, h, 1024 + h * 128))
        for c in range(24):
            plan.append((self.gT, c, 3072 + c * 128))
        for h in range(8):
            plan.append((self.gzT, h, 6160 + h * 128))
        with nc.allow_low_precision("bf16 matmul operands, fp32 accumulation"):
            for s in range(NS):
                for t0 in range(0, LP, TT):
                    T = min(TT, LP - t0)
                    tok0 = s * LP + t0
                    htoks = self.toks_h(tok0, T)
                    P.q_a.dma(xt[:, :, :T], self.hT[:, :, tok0:tok0 + T].rearrange("c p t -> p c t"), outs=[xtok], ins=htoks)
                    self.rms_scale(xt, xtok, T, lambda c: gain[:, l, c:c + 1], gtok, pss, psstok, sq, sqtok, rstd, rtok)
                    P.dve.op(lambda: nc.vector.tensor_copy(out=xb[:, 0:8, :T], in_=xt[:, 0:8, :T]), outs=[xbtok], ins=[xtok])
                    P.act.op(lambda: nc.scalar.copy(out=xb[:, 8:16, :T], in_=xt[:, 8:16, :T]), outs=[xbtok], ins=[xtok])
                    def load_w(j):
                        ws_, wst_ = wst_r.next()
                        c0 = plan[j][2]
                        P.q_w.dma(ws_[:], Wv[:, :, c0:c0 + 128], outs=[wst_])
                        w_, wt_ = win_r.next()
                        if j % 2 == 0:
                            P.dve.op(lambda: nc.vector.tensor_copy(out=w_[:], in_=ws_[:]), outs=[wt_], ins=[wst_])
                        else:
                            P.act.op(lambda: nc.scalar.copy(out=w_[:], in_=ws_[:]), outs=[wt_], ins=[wst_])
                        return w_, wt_
                    pend = [load_w(0), load_w(1)]
                    for j in range(len(plan)):
                        w_, wt_ = pend.pop(0)
                        if j + 2 < len(plan):
                            pend.append(load_w(j + 2))
                        ps, ptok = ps01.next()
                        for c in range(DC):
                            P.pe.op(lambda: nc.tensor.matmul(ps[:, :T], w_[:, c, :], xb[:, c, :T], start=(c == 0), stop=(c == DC - 1)),
                                    outs=[ptok], ins=[wt_, xbtok])
                        ob, obtok = ob_r.next()
                        if j % 2 == 0:
                            P.act.op(lambda: nc.scalar.copy(out=ob[:, :T], in_=ps[:, :T]), outs=[obtok], ins=[ptok])
                        else:
                            P.dve.op(lambda: nc.vector.tensor_copy(out=ob[:, :T], in_=ps[:, :T]), outs=[obtok], ins=[ptok])
                        P.q_a.dma(plan[j][0][plan[j][1], :, tok0:tok0 + T], ob[:, :T], ins=[obtok])
                    ps, ptok = ps01.next()
                    for c in range(DC):
                        P.pe.op(lambda: nc.tensor.matmul(ps[0:16, :T], wab[:, c, :], xt[:, c, :T], start=(c == 0), stop=(c == DC - 1)),
                                outs=[ptok], ins=[t_wab, xtok])
                    ob, obtok = ob_r.next()
                    P.act.op(lambda: nc.scalar.copy(out=ob[0:16, :T], in_=ps[0:16, :T]), outs=[obtok], ins=[ptok])
                    P.q_a.dma(self.gabT[:, tok0:tok0 + T], ob[0:16, :T], ins=[obtok])
                    for n in range(2):
                        P.q_w.dma(wvst[:], Wv[:, :, 2048 + n * 512:2048 + (n + 1) * 512], outs=[wvstt])
                        P.act.op(lambda: nc.scalar.copy(out=wv[:, 0:8, :], in_=wvst[:, 0:8, :]), outs=[wvt], ins=[wvstt])
                        P.dve.op(lambda: nc.vector.tensor_copy(out=wv[:, 8:16, :], in_=wvst[:, 8:16, :]), outs=[wvt], ins=[wvstt, wvt])
                        for tb in range(T // 128):
                            ps, ptok = psv_r.next()
                            for c in range(DC):
                                P.pe.op(lambda: nc.tensor.matmul(ps[:, :], xb[:, c, tb * 128:(tb + 1) * 128], wv[:, c, :], start=(c == 0), stop=(c == DC - 1)),
                                        outs=[ptok], ins=[wvt, xbtok])
                            ob, obtok = ob_r.next()
                            P.act.op(lambda: nc.scalar.copy(out=ob[:, :], in_=ps[:, :]), outs=[obtok], ins=[ptok])
                            P.q_a.dma(self.sbV[tok0 + tb * 128:tok0 + (tb + 1) * 128, n * 512:(n + 1) * 512], ob[:, :], ins=[obtok])
        P.release(m)

    def load_const(self, dram_ap, shape, name):
        t = self.P.sb(shape, name); tok = Tok(name)
        self.P.q_a.dma(t[:], dram_ap, outs=[tok])
        return t, tok

    def odd_sb(self, l):
        P = self.P; nc = P.nc
        NS, LP, NT = self.NS, self.LP, self.NT
        m = P.mark()
        V = nc.vector
        maskS, t_ms = self.load_const(self.c_maskS[:, :], [128, 128], "maskS")
        Uincl, t_ui = self.load_const(self.c_maskL[:, :], [128, 128], "Uincl")
        zeros = P.sb([128, 128], "zeros"); t_z = Tok()
        P.dve.op(lambda: V.memset(zeros[:], 0.0), outs=[t_z])
        sets = Ring([dict(q=P.sb([128, LP]), k=P.sb([128, LP]), nk=P.sb([128, LP]), v=P.sb([128, NT, 128]), tok=Tok()) for _ in range(2)])
        ez_r = Ring([(P.sb([128, 512], "ez"), Tok()) for _ in range(2)])
        sp_r = Ring([(P.sb([128, 512], "sp"), Tok()) for _ in range(2)])
        wt_r = Ring([(P.sb([128, 512], "wt"), Tok()) for _ in range(2)])
        A, Atok = P.sb([128, 512], "A"), Tok()
        yo_r = Ring([(P.sb([128, 512], "yo"), Tok()) for _ in range(2)])
        ps_z = self.psring([0, 1]); ps_e = self.psring([2, 3]); ps_o = self.psring([4, 5])
        scale = 128.0 ** -0.5
        for s in range(NS):
            for h in range(8):
                S = sets.next(); stok = S["tok"]
                c0s = s * LP
                P.q_a.dma(S["q"][:], self.sbQ[h, :, c0s:c0s + LP], outs=[stok])
                P.q_a.dma(S["k"][:], self.sbK[h, :, c0s:c0s + LP], outs=[stok])
                P.q_a.dma(S["v"][:], self.sbV[c0s:c0s + LP, h * 128:(h + 1) * 128].rearrange("(b p) d -> p b d", p=128), outs=[stok])
                P.act.op(lambda: nc.scalar.mul(out=S["q"][:], in_=S["q"][:], mul=scale), outs=[stok], ins=[stok])
                P.act.op(lambda: nc.scalar.mul(out=S["nk"][:], in_=S["k"][:], mul=-1.0), outs=[stok], ins=[stok])
                for q0 in range(0, LP, 512):
                    Tq = min(512, LP - q0)
                    nkb = (q0 + Tq) // 128
                    po, potok = ps_o.next()
                    P.pe.op(lambda: nc.tensor.matmul(po[:, :Tq], zeros[:], S["q"][:, q0:q0 + Tq], start=True, stop=False),
                            outs=[potok], ins=[t_z, stok])
                    P.dve.op(lambda: V.memset(A[:, :Tq], 0.0), outs=[Atok])
                    for kb in range(nkb - 1, -1, -1):
                        c0 = max(q0, kb * 128); off = c0 - q0; w = q0 + Tq - c0
                        diag = kb * 128 >= q0
                        ksl = slice(kb * 128, (kb + 1) * 128)
                        pz, pztok = ps_z.next()
                        P.pe.op(lambda: nc.tensor.matmul(pz[:, :w], S["k"][:, ksl], S["q"][:, c0:c0 + w], start=True, stop=True),
                                outs=[pztok], ins=[stok])
                        ez, eztok = ez_r.next()
                        P.act.op(lambda: nc.scalar.activation(out=ez[:, :w], in_=pz[:, :w], func=AF.Exp), outs=[eztok], ins=[pztok])
                        sp, sptok = sp_r.next()
                        P.act.op(lambda: nc.scalar.activation(out=sp[:, :w], in_=ez[:, :w], func=AF.Ln, bias=self.ones[:, 0:1]),
                                 outs=[sptok], ins=[eztok, self.t_ones])
                        if diag:
                            P.dve.op(lambda: V.tensor_tensor(out=sp[:, 0:128], in0=sp[:, 0:128], in1=maskS[:], op=ALU.mult),
                                     outs=[sptok], ins=[sptok, t_ms])
                        pe_, petok = ps_e.next()
                        P.pe.op(lambda: nc.tensor.matmul(pe_[:, :w], Uincl[:], sp[:, :w], start=True, stop=False), outs=[petok], ins=[t_ui, sptok])
                        if kb < nkb - 1:
                            P.pe.op(lambda: nc.tensor.matmul(pe_[:, :w], self.ones[:], A[:, off:off + w], start=False, stop=False),
                                    outs=[petok], ins=[self.t_ones, Atok])
                        P.pe.op(lambda: nc.tensor.matmul(pe_[:, :w], S["nk"][:, ksl], S["q"][:, c0:c0 + w], start=False, stop=True),
                                outs=[petok], ins=[stok])
                        wt, wttok = wt_r.next()
                        P.act.op(lambda: nc.scalar.activation(out=wt[:, :w], in_=pe_[:, :w], func=AF.Exp, scale=-1.0), outs=[wttok], ins=[petok])
                        if diag:
                            P.dve.op(lambda: V.tensor_tensor(out=wt[:, 0:128], in0=wt[:, 0:128], in1=maskS[:], op=ALU.mult),
                                     outs=[wttok], ins=[wttok, t_ms])
                        P.pe.op(lambda: nc.tensor.matmul(po[:, off:off + w], S["v"][:, kb, :], wt[:, :w], start=False, stop=(kb == 0)),
                                outs=[potok], ins=[stok, wttok])
                        if kb > 0:
                            P.dve.op(lambda: V.tensor_tensor(out=A[:, off:off + w], in0=A[:, off:off + w], in1=sp[:, :w], op=ALU.add),
                                     outs=[Atok], ins=[Atok, sptok])
                    yo, yotok = yo_r.next()
                    P.act.op(lambda: nc.scalar.copy(out=yo[:, :Tq], in_=po[:, :Tq]), outs=[yotok], ins=[potok])
                    P.q_a.dma(self.yO[h, :, c0s + q0:c0s + q0 + Tq], yo[:, :Tq], ins=[yotok])
        P.release(m)

    def odd_out(self, l):
        P = self.P; nc = P.nc; i = l // 2
        NTOK = self.NTOK
        m = P.mark()
        TT = 512
        ymix = P.sb([128, 16, TT], "ymix"); ytok = Tok()
        wst_r = Ring([(P.sb([128, 16, 128], "wost"), Tok()) for _ in range(2)])
        w_r = Ring([(P.sbh([128, 16, 128], "wo"), Tok()) for _ in range(2)])
        ymb = P.sbh([128, 16, TT], "ymb"); ymbtok = Tok()
        hres_r = Ring([(P.sb([128, TT], "hres"), Tok()) for _ in range(2)])
        psd_r = self.psring([2, 3])
        for t0 in range(0, NTOK, TT):
            T = min(TT, NTOK - t0)
            htoks = self.toks_h(t0, T)
            P.q_a.dma(ymix[:, :, :T], self.yO[:, :, t0:t0 + T].rearrange("c p t -> p c t"), outs=[ytok])
            self.out_proj(self.w["od_w_out"][i], 16, ymix, [ytok], T, t0, htoks, wst_r, w_r, psd_r, hres_r, ymb, ymbtok)
        P.release(m)
    def odd_gdn(self, l):
        P = self.P; nc = P.nc; i = l // 2
        NS, LP, NT = self.NS, self.LP, self.NT
        m = P.mark()
        V = nc.vector
        dve = P.dve; act = P.act; pe = P.pe
        maskI, t_mi = self.load_const(self.c_maskT[:, :], [128, 128], "maskI")
        maskSt, t_mst = self.load_const(self.c_maskS[:, :], [128, 128], "maskSt")
        sel, t_sel = self.load_const(self.c_sel[:, :], [8, 1024], "sel")
        tk = Tok("gdnsetup")
        cw = P.sb([128, 24, 4], "cw")
        for j in range(4):
            P.q_a.dma(cw[:, :, j], self.w["gdn_conv"][i, j].rearrange("(c p) -> p c", p=128), outs=[tk], allow_slow_non_contiguous=True)
        nA = P.sb([8, 1], "nA"); dtb = P.sb([8, 1], "dtb"); gout = P.sb([128, 1], "gout")
        P.q_a.dma(nA[:], self.w["gdn_a_log"][i].rearrange("(p o) -> p o", o=1), outs=[tk])
        P.q_a.dma(dtb[:], self.w["gdn_dt_bias"][i].rearrange("(p o) -> p o", o=1), outs=[tk])
        P.q_a.dma(gout[:], self.w["gdn_g_out"][i].rearrange("(p o) -> p o", o=1), outs=[tk])
        act.op(lambda: nc.scalar.activation(out=nA[:], in_=nA[:], func=AF.Exp), outs=[tk], ins=[tk])
        dve.op(lambda: V.tensor_scalar(nA[:], nA[:], -1.0, None, ALU.mult), outs=[tk], ins=[tk])
        def big(name):
            return P.sb([128, LP], name)
        g_ = big("g"); be = big("beta"); Gc = big("Gc"); tg = Tok("g")
        cols = P.sb([128, NT, 16], "cols"); tcols = Tok("cols")
        qT = big("qT"); kT = big("kT"); vT = big("vT"); qdT = big("qdT"); cv = big("cv")
        tq = Tok("q"); tkk = Tok("k"); tv = Tok("v"); tqd = Tok("qd"); tcv = Tok("cv")
        Grow = big("Grow"); eG = big("eG"); Brow = big("Brow"); tGr = Tok("Grow")
        k_tok = P.sb([128, NT, 128], "k_tok"); v_tok = P.sb([128, NT, 128], "v_tok"); tkv = [Tok("kvtok%d" % c) for c in range(NT)]
        u_tok = P.sb([128, NT, 128], "u_tok"); wT = P.sb([128, NT, 128], "wT"); attnT = P.sb([128, NT, 128], "attnT"); kd_tok = P.sb([128, NT, 128], "kd_tok")
        tpre = [Tok("pre%d" % c) for c in range(NT)]
        zT = big("zT"); tz = Tok("z")
        KI = 4
        slots = []
        for k_ in range(KI):
            R = {}
            for nm in ("dT", "dI", "aa", "Xu", "Xw"):
                R[nm] = (P.sb([128, 128], nm), Tok())
            R["sc"] = (P.sb([128, 4], "sc"), Tok())
            R["Mp"] = [P.sb([128, 128], "Mp") for _ in range(7)]; R["tM"] = [Tok() for _ in range(7)]
            R["Np"] = [P.sb([128, 128], "Np") for _ in range(6)]; R["tN"] = [Tok() for _ in range(6)]
            R["Y"] = [(P.sb([128, 128], "Y"), Tok()) for _ in range(2)]
            R["bA"] = P.psb[2 * k_]; R["bB"] = P.psb[2 * k_ + 1]
            slots.append(R)
        Sst = P.sb([128, 128], "S"); tS = Tok("S")
        vn = P.sb([128, 128], "vn"); tvn = Tok()
        rs = P.sb([128, 512], "rs"); trs = Tok()
        b0, tb0 = P.psb[0]; b1, tb1 = P.psb[1]; b2, tb2 = P.psb[2]; b3, tb3 = P.psb[3]
        b4, tb4 = P.psb[4]; b5, tb5 = P.psb[5]; b6, tb6 = P.psb[6]; b7, tb7 = P.psb[7]
        chunks = [(q0, min(512, LP - q0)) for q0 in range(0, LP, 512)]
        for s in range(NS):
            c0s = s * LP
            P.q_a.dma(g_[0:8, :], self.gabT[0:8, c0s:c0s + LP], outs=[tg])
            P.q_a.dma(be[0:8, :], self.gabT[8:16, c0s:c0s + LP], outs=[tg])
            dve.op(lambda: V.tensor_scalar(g_[0:8, :], g_[0:8, :], dtb[:, 0:1], None, ALU.add), outs=[tg], ins=[tg, tk])
            act.op(lambda: nc.scalar.activation(out=g_[0:8, :], in_=g_[0:8, :], func=AF.Exp), outs=[tg], ins=[tg])
            act.op(lambda: nc.scalar.activation(out=g_[0:8, :], in_=g_[0:8, :], func=AF.Ln, bias=self.ones[0:8, 0:1]), outs=[tg], ins=[tg, self.t_ones])
            dve.op(lambda: V.tensor_scalar(g_[0:8, :], g_[0:8, :], nA[:, 0:1], None, ALU.mult), outs=[tg], ins=[tg, tk])
            act.op(lambda: nc.scalar.activation(out=be[0:8, :], in_=be[0:8, :], func=AF.Sigmoid), outs=[tg], ins=[tg])
            for c in range(NT):
                csl = slice(c * 128, (c + 1) * 128)
                dve.op(lambda: V.tensor_tensor_scan(out=Gc[0:8, csl], data0=self.ones[0:8, 0:128], data1=g_[0:8, csl], initial=0.0,
                                                    op0=ALU.mult, op1=ALU.add), outs=[tg], ins=[tg, self.t_ones])
            for c in range(NT):
                csl = slice(c * 128, (c + 1) * 128)
                pe.op(lambda: nc.tensor.transpose(b0[:, c * 16:c * 16 + 8], Gc[0:8, csl], self.ident[0:8, 0:8]), outs=[tb0], ins=[tg, self.t_ident])
                pe.op(lambda: nc.tensor.transpose(b0[:, c * 16 + 8:c * 16 + 16], be[0:8, csl], self.ident[0:8, 0:8]), outs=[tb0], ins=[tg, self.t_ident])
            act.op(lambda: nc.scalar.copy(out=cols[:], in_=b0[:, 0:NT * 16].rearrange("p (c k) -> p c k", k=16)), outs=[tcols], ins=[tb0])
            for h in range(8):
                for part, (x, tx) in enumerate(((qT, tq), (kT, tkk), (vT, tv))):
                    ch = part * 8 + h
                    P.q_a.dma(x[:], self.gT[ch, :, c0s:c0s + LP], outs=[tx])
                    dve.op(lambda: V.tensor_scalar(cv[:], x[:], cw[:, ch, 3:4], None, ALU.mult), outs=[tcv], ins=[tx, tk])
                    for sh in (1, 2, 3):
                        dve.op(lambda: V.scalar_tensor_tensor(out=cv[:, sh:LP], in0=x[:, 0:LP - sh], scalar=cw[:, ch, 3 - sh:4 - sh], in1=cv[:, sh:LP],
                                                              op0=ALU.mult, op1=ALU.add), outs=[tcv], ins=[tcv, tx, tk])
                    act.op(lambda: nc.scalar.activation(out=x[:], in_=cv[:], func=AF.Silu), outs=[tx], ins=[tcv])
                P.q_a.dma(zT[:], self.gzT[h, :, c0s:c0s + LP], outs=[tz])
                for (x, tx, extra) in ((qT, tq, 128.0 ** -0.5), (kT, tkk, 1.0)):
                    act.op(lambda: nc.scalar.activation(out=cv[:], in_=x[:], func=AF.Square), outs=[tcv], ins=[tx])
                    for (q0, Tq) in chunks:
                        sl = slice(q0, q0 + Tq)
                        pe.op(lambda: nc.tensor.matmul(b0[:, :Tq], self.ones[:], cv[:, sl], start=True, stop=True), outs=[tb0], ins=[tcv, self.t_ones])
                        act.op(lambda: nc.scalar.activation(out=rs[:, :Tq], in_=b0[:, :Tq], func=AF.Sqrt, bias=self.epsb[:], scale=1.0), outs=[trs], ins=[tb0, self.t_eps])
                        dve.op(lambda: V.reciprocal(out=rs[:, :Tq], in_=rs[:, :Tq]), outs=[trs], ins=[trs])
                        dve.op(lambda: V.scalar_tensor_tensor(out=x[:, sl], in0=x[:, sl], scalar=extra, in1=rs[:, :Tq], op0=ALU.mult, op1=ALU.mult),
                               outs=[tx], ins=[tx, trs])
                for (q0, Tq) in chunks:
                    sl = slice(q0, q0 + Tq)
                    pe.op(lambda: nc.tensor.matmul(b0[:, :Tq], sel[:, h * 128:(h + 1) * 128], Gc[0:8, sl], start=True, stop=True), outs=[tb0], ins=[t_sel, tg])
                    act.op(lambda: nc.scalar.copy(out=Grow[:, sl], in_=b0[:, :Tq]), outs=[tGr], ins=[tb0])
                    act.op(lambda: nc.scalar.activation(out=eG[:, sl], in_=b0[:, :Tq], func=AF.Exp), outs=[tGr], ins=[tb0])
                    pe.op(lambda: nc.tensor.matmul(b0[:, :Tq], sel[:, h * 128:(h + 1) * 128], be[0:8, sl], start=True, stop=True), outs=[tb0], ins=[t_sel, tg])
                    act.op(lambda: nc.scalar.copy(out=Brow[:, sl], in_=b0[:, :Tq]), outs=[tGr], ins=[tb0])
                dve.op(lambda: V.tensor_tensor(out=qdT[:], in0=qT[:], in1=eG[:], op=ALU.mult), outs=[tqd], ins=[tq, tGr])
                def chunk_gen(c, R):
                    csl = slice(c * 128, (c + 1) * 128)
                    cend = c * 128 + 127
                    Gcol = cols[:, c, h:h + 1]; bcol = cols[:, c, 8 + h:9 + h]
                    bA, tA = R["bA"]; bB, tB = R["bB"]
                    dT, tdT = R["dT"]; dI, tdI = R["dI"]; aa, taa = R["aa"]; Xu, tXu = R["Xu"]; Xw, tXw = R["Xw"]; sc, tsc = R["sc"]
                    Np, tN = R["Np"], R["tN"]; Mp, tM = R["Mp"], R["tM"]
                    pe.op(lambda: nc.tensor.transpose(bB[:, 0:128], kT[:, csl], self.ident[:]), outs=[tB], ins=[tkk, self.t_ident])
                    pe.op(lambda: nc.tensor.transpose(bB[:, 128:256], vT[:, csl], self.ident[:]), outs=[tB], ins=[tv, self.t_ident])
                    dve.op(lambda: V.tensor_scalar(dT[:], Grow[:, csl], Gcol, 0.0, ALU.subtract, ALU.min), outs=[tdT], ins=[tGr, tcols])
                    pe.op(lambda: nc.tensor.matmul(bA[:, 0:128], kT[:, csl], kT[:, csl], start=True, stop=True), outs=[tA], ins=[tkk])
                    pe.op(lambda: nc.tensor.matmul(bA[:, 128:256], kT[:, csl], qT[:, csl], start=True, stop=True), outs=[tA], ins=[tkk, tq])
                    yield
                    act.op(lambda: nc.scalar.copy(out=k_tok[:, c, :], in_=bB[:, 0:128]), outs=[tkv[c]], ins=[tB])
                    act.op(lambda: nc.scalar.copy(out=v_tok[:, c, :], in_=bB[:, 128:256]), outs=[tkv[c]], ins=[tB])
                    act.op(lambda: nc.scalar.activation(out=dT[:], in_=dT[:], func=AF.Exp), outs=[tdT], ins=[tdT])
                    act.op(lambda: nc.scalar.activation(out=sc[:, 0:1], in_=Gcol, func=AF.Exp), outs=[tsc], ins=[tcols])
                    act.op(lambda: nc.scalar.activation(out=sc[:, 1:2], in_=Gcol, func=AF.Exp, scale=-1.0, bias=Grow[:, cend:cend + 1]), outs=[tsc], ins=[tcols, tGr])
                    yield
                    dve.op(lambda: V.tensor_tensor(out=dI[:], in0=dT[:], in1=maskI[:], op=ALU.mult), outs=[tdI], ins=[tdT, t_mi])
                    dve.op(lambda: V.tensor_tensor(out=attnT[:, c, :], in0=bA[:, 128:256], in1=dI[:], op=ALU.mult), outs=[tpre[c]], ins=[tA, tdI])
                    dve.op(lambda: V.tensor_tensor(out=aa[:], in0=dT[:], in1=maskSt[:], op=ALU.mult), outs=[taa], ins=[tdT, t_mst])
                    dve.op(lambda: V.tensor_tensor(out=aa[:], in0=aa[:], in1=Brow[:, csl], op=ALU.mult), outs=[taa], ins=[taa, tGr])
                    dve.op(lambda: V.tensor_tensor(out=Np[0][:], in0=bA[:, 0:128], in1=aa[:], op=ALU.mult), outs=[tN[0]], ins=[tA, taa])
                    yield
                    pe.op(lambda: nc.tensor.transpose(bB[:, 256:384], Np[0][:], self.ident[:]), outs=[tB], ins=[tN[0], self.t_ident])
                    Yi = 0
                    Y, tY = R["Y"][Yi]
                    dve.op(lambda: V.tensor_tensor(out=Y[:], in0=self.ident[:], in1=Np[0][:], op=ALU.subtract), outs=[tY], ins=[tN[0], self.t_ident])
                    dve.op(lambda: V.tensor_scalar(Xu[:], v_tok[:, c, :], bcol, None, ALU.mult), outs=[tXu], ins=[tkv[c], tcols])
                    dve.op(lambda: V.tensor_scalar(Xw[:], k_tok[:, c, :], bcol, sc[:, 0:1], ALU.mult, ALU.mult), outs=[tXw], ins=[tkv[c], tcols, tsc])
                    dve.op(lambda: V.tensor_scalar(kd_tok[:, c, :], k_tok[:, c, :], sc[:, 1:2], None, ALU.mult), outs=[tpre[c]], ins=[tkv[c], tsc])
                    yield
                    act.op(lambda: nc.scalar.copy(out=Mp[0][:], in_=bB[:, 256:384]), outs=[tM[0]], ins=[tB])
                    yield
                    for k in range(1, 7):
                        pe.op(lambda: nc.tensor.matmul(bB[:, 256:384], Np[k - 1][:], Mp[k - 1][:], start=True, stop=True), outs=[tB], ins=[tN[k - 1], tM[k - 1]])
                        if k < 6:
                            pe.op(lambda: nc.tensor.matmul(bA[:, 256:384], Mp[k - 1][:], Np[k - 1][:], start=True, stop=True), outs=[tA], ins=[tN[k - 1], tM[k - 1]])
                        yield
                        act.op(lambda: nc.scalar.copy(out=Mp[k][:], in_=bB[:, 256:384]), outs=[tM[k]], ins=[tB])
                        if k < 6:
                            dve.op(lambda: V.tensor_copy(out=Np[k][:], in_=bA[:, 256:384]), outs=[tN[k]], ins=[tA])
                        yield
                        pe.op(lambda: nc.tensor.matmul(bA[:, 384:512], Mp[k][:], Y[:], start=True, stop=True), outs=[tA], ins=[tM[k], tY])
                        yield
                        Yi ^= 1
                        Y2, tY2 = R["Y"][Yi]
                        dve.op(lambda: V.tensor_tensor(out=Y2[:], in0=bA[:, 384:512], in1=Y[:], op=ALU.add), outs=[tY2], ins=[tA, tY])
                        Y, tY = Y2, tY2
                    yield
                    pe.op(lambda: nc.tensor.matmul(bB[:, 384:512], Y[:], Xu[:], start=True, stop=True), outs=[tB], ins=[tY, tXu])
                    pe.op(lambda: nc.tensor.matmul(bB[:, 0:128], Xw[:], Y[:], start=True, stop=True), outs=[tB], ins=[tY, tXw])
                    yield
                    act.op(lambda: nc.scalar.copy(out=u_tok[:, c, :], in_=bB[:, 384:512]), outs=[tpre[c]], ins=[tB])
                    act.op(lambda: nc.scalar.copy(out=wT[:, c, :], in_=bB[:, 0:128]), outs=[tpre[c]], ins=[tB])
                for c0 in range(0, NT, KI):
                    gens = [chunk_gen(c, slots[c - c0]) for c in range(c0, min(NT, c0 + KI))]
                    while gens:
                        for g in list(gens):
                            try:
                                next(g)
                            except StopIteration:
                                gens.remove(g)
                dve.op(lambda: V.memset(Sst[:], 0.0), outs=[tS])
                for c in range(NT):
                    csl = slice(c * 128, (c + 1) * 128)
                    cend = c * 128 + 127
                    pe.op(lambda: nc.tensor.matmul(b6[:, 0:128], wT[:, c, :], Sst[:], start=True, stop=True), outs=[tb6], ins=[tpre[c], tS])
                    dve.op(lambda: V.tensor_tensor(out=vn[:], in0=u_tok[:, c, :], in1=b6[:, 0:128], op=ALU.subtract), outs=[tvn], ins=[tpre[c], tb6])
                    pe.op(lambda: nc.tensor.matmul(b7[:, 0:128], Sst[:], qdT[:, csl], start=True, stop=False), outs=[tb7], ins=[tS, tqd])
                    pe.op(lambda: nc.tensor.matmul(b7[:, 0:128], vn[:], attnT[:, c, :], start=False, stop=True), outs=[tb7], ins=[tvn, tpre[c]])
                    act.op(lambda: nc.scalar.copy(out=cv[:, csl], in_=b7[:, 0:128]), outs=[tcv], ins=[tb7])
                    pe.op(lambda: nc.tensor.matmul(b6[:, 128:256], kd_tok[:, c, :], vn[:], start=True, stop=True), outs=[tb6], ins=[tpre[c], tvn])
                    dve.op(lambda: V.scalar_tensor_tensor(out=Sst[:], in0=Sst[:], scalar=eG[:, cend:cend + 1], in1=b6[:, 128:256], op0=ALU.mult, op1=ALU.add),
                           outs=[tS], ins=[tS, tb6, tGr])
                oT = cv
                act.op(lambda: nc.scalar.activation(out=zT[:], in_=zT[:], func=AF.Silu), outs=[tz], ins=[tz])
                for (q0, Tq) in chunks:
                    sl = slice(q0, q0 + Tq)
                    act.op(lambda: nc.scalar.activation(out=qdT[:, sl], in_=oT[:, sl], func=AF.Square), outs=[tqd], ins=[tcv])
                    pe.op(lambda: nc.tensor.matmul(b0[:, :Tq], self.ones[:], qdT[:, sl], start=True, stop=True), outs=[tb0], ins=[tqd, self.t_ones])
                    act.op(lambda: nc.scalar.activation(out=rs[:, :Tq], in_=b0[:, :Tq], func=AF.Sqrt, bias=self.epsb[:], scale=1.0 / 128), outs=[trs], ins=[tb0, self.t_eps])
                    dve.op(lambda: V.reciprocal(out=rs[:, :Tq], in_=rs[:, :Tq]), outs=[trs], ins=[trs])
                    dve.op(lambda: V.scalar_tensor_tensor(out=oT[:, sl], in0=oT[:, sl], scalar=gout[:, 0:1], in1=rs[:, :Tq], op0=ALU.mult, op1=ALU.mult),
                           outs=[tcv], ins=[tcv, trs, tk])
                    dve.op(lambda: V.tensor_tensor(out=oT[:, sl], in0=oT[:, sl], in1=zT[:, sl], op=ALU.mult), outs=[tcv], ins=[tcv, tz])
                P.q_a.dma(self.yO[8 + h, :, c0s:c0s + LP], oT[:], ins=[tcv])
        P.release(m)


def host_consts(LP=None, mix=True):
    c = {"ident": np.eye(128, dtype=np.float32)}
    if mix and LP is not None:
        half = 32
        inv = (10000.0 ** (-np.arange(half, dtype=np.float32) / half)).astype(np.float32)
        ang = np.arange(LP, dtype=np.float32)[:, None] * inv[None, :]
        c["rope_cos"] = np.cos(ang).astype(np.float32)
        c["rope_sin"] = np.sin(ang).astype(np.float32)
        k = np.arange(128)[:, None]; q = np.arange(128)[None, :]
        c["maskT"] = (q >= k).astype(np.float32)
        c["maskS"] = (q > k).astype(np.float32)
        c["maskL"] = (q <= k).astype(np.float32)
        sel = np.zeros((8, 8, 128), np.float32)
        for h in range(8):
            sel[h, h, :] = 1.0
        c["sel"] = sel.reshape(8, 1024)
    return c


def make_xin(x, meta, LP):
    NS, SEQ, _ = x.shape
    xin = np.zeros((NS, LP, D), np.float32)
    xin[:, :N_META] = meta[None]
    xin[:, N_META:N_META + SEQ] = x
    return xin.reshape(NS * LP, D)


_WNAMES = ["norm_ffn1", "norm_mix", "norm_ffn2", "w1_gate", "w1_up", "w2_gate", "w2_up", "w1_down", "w2_down"]


def kernel(**inputs):
    x = np.asarray(inputs["x"])
    B, SEQ, _ = x.shape
    ncores = 8
    NS = B // ncores
    L = SEQ + N_META
    NT = (L + 127) // 128
    K = Kern(NS, NT, depth=2, seq_real=L)
    consts = host_consts(NT * 128)
    in_maps = []
    for c in range(ncores):
        m = {"xin": make_xin(x[c * NS:(c + 1) * NS], np.asarray(inputs["meta_tokens"]), NT * 128)}
        m.update(consts)
        for nm in K.w:
            m[nm] = np.ascontiguousarray(np.asarray(inputs[nm]))
        in_maps.append(m)
    res = run_bass_kernel_spmd(K.P.nc, in_maps, core_ids=list(range(ncores)))
    outs = [r["out"].reshape(NS, SEQ, D) for r in res.results]
    return np.concatenate(outs, axis=0).astype(np.float32)
```

```python
import math
import numpy as np
import concourse.bass as bass
import concourse.mybir as mybir
from concourse.bass_utils import run_bass_kernel_spmd

F32 = mybir.dt.float32
BF16 = mybir.dt.bfloat16
AF = mybir.ActivationFunctionType
ALU = mybir.AluOpType
AX = mybir.AxisListType

D = 2048
DC = 16
DFF = 5632
FC = 44
N_META = 16
EPS = 1e-6
ARENA_WORDS = 53200


class Tok:
    __slots__ = ("w", "r", "name")

    def __init__(self, name=""):
        self.w = None
        self.r = {}
        self.name = name


class Eng:
    def __init__(self, P, eng, name, is_pe=False):
        self.P = P
        self.eng = eng
        self.name = name
        self.sem = P.nc.alloc_semaphore("s_" + name)
        self.key = "E" + name
        self.cnt = 0
        self.seen = {}
        self.is_pe = is_pe
        self.nwait = 0

    def wait_ev(self, ev):
        if ev is None:
            return
        key, sem, val = ev
        if self.is_pe and key == self.key:
            return
        if key == self.key and val <= self.cnt_done_known():
            pass
        if self.seen.get(key, 0) >= val:
            return
        self.eng.wait_ge(sem, val)
        self.nwait += 1
        self.seen[key] = val

    def cnt_done_known(self):
        return self.seen.get(self.key, 0)

    def deps(self, outs, ins):
        for t in ins:
            self.wait_ev(t.w)
        for t in outs:
            self.wait_ev(t.w)
            for k, (sem, val) in t.r.items():
                self.wait_ev((k, sem, val))

    def op(self, fn, outs=(), ins=()):
        self.deps(outs, ins)
        inst = fn()
        self.cnt += 1
        inst.then_inc(self.sem, 1)
        ev = (self.key, self.sem, self.cnt)
        for t in ins:
            t.r[self.key] = (self.sem, self.cnt)
        for t in outs:
            t.w = ev
            t.r = {}
        return inst


class DmaQ:
    def __init__(self, P, eng, name, nsem):
        self.P = P
        self.eng = eng
        self.name = name
        self.sems = [P.nc.alloc_semaphore("d_%s%d" % (name, i)) for i in range(nsem)]
        self.vals = [0] * nsem
        self.i = 0
        self.seen = {}
        self.nwait = 0

    def wait_ev(self, ev):
        if ev is None:
            return
        key, sem, val = ev
        if self.seen.get(key, 0) >= val:
            return
        self.eng.wait_ge(sem, val)
        self.nwait += 1
        self.seen[key] = val

    def dma(self, out, in_, outs=(), ins=(), **kw):
        for t in ins:
            self.wait_ev(t.w)
        for t in outs:
            self.wait_ev(t.w)
            for k, (sem, val) in t.r.items():
                self.wait_ev((k, sem, val))
        i = self.i
        self.i = (self.i + 1) % len(self.sems)
        key = "D%s%d" % (self.name, i)
        if self.vals[i] > 0:
            self.wait_ev((key, self.sems[i], self.vals[i]))
        inst = self.eng.dma_start(out=out, in_=in_, **kw)
        self.vals[i] += 16
        inst.then_inc(self.sems[i], 16)
        ev = (key, self.sems[i], self.vals[i])
        for t in ins:
            t.r[key] = (self.sems[i], self.vals[i])
        for t in outs:
            t.w = ev
            t.r = {}
        return ev

    def drain(self):
        for i, s in enumerate(self.sems):
            if self.vals[i] > 0:
                self.wait_ev(("D%s%d" % (self.name, i), s, self.vals[i]))


class Prog:
    def __init__(self):
        self.nc = bass.Bass("TRN2", target_bir_lowering=False)
        nc = self.nc
        self.pe = Eng(self, nc.tensor, "pe", is_pe=True)
        self.act = Eng(self, nc.scalar, "act")
        self.dve = Eng(self, nc.vector, "dve")
        self.pool = Eng(self, nc.gpsimd, "pool")
        self.q_w = DmaQ(self, nc.sync, "w", 12)
        self.q_a = DmaQ(self, nc.gpsimd, "a", 8)
        self._n = 0
        self.AW = ARENA_WORDS
        self.arena = nc.alloc_sbuf_tensor("arena", [128, self.AW], F32)
        self.off = 0
        self.psb = [(nc.alloc_psum_tensor("psb%d" % i, [128, 512], F32), Tok("psb%d" % i)) for i in range(8)]

    def sb(self, shape, name=None):
        n = 1
        for d in shape[1:]:
            n *= d
        assert self.off + n <= self.AW, ("SBUF arena overflow", name, self.off, n)
        ap = self.arena[0:shape[0], self.off:self.off + n]
        self.off += n
        if len(shape) == 3:
            ap = ap.rearrange("p (a b) -> p a b", a=shape[1])
        elif len(shape) == 4:
            ap = ap.rearrange("p (a b c) -> p a b c", a=shape[1], b=shape[2])
        return ap

    def sbh(self, shape, name=None):
        n = 1
        for d in shape[1:]:
            n *= d
        import os
        if os.environ.get("DBG_NOBF"):
            return self.sb(shape, name)
        assert n % 2 == 0
        w = n // 2
        assert self.off + w <= self.AW, ("SBUF arena overflow", name, self.off, w)
        ap = self.arena[0:shape[0], self.off:self.off + w].bitcast(BF16)
        self.off += w
        if len(shape) == 3:
            ap = ap.rearrange("p (a b) -> p a b", a=shape[1])
        return ap

    def mark(self):
        return self.off

    def release(self, m):
        self.barrier()
        self.off = m

    def barrier(self):
        evs = []
        for e in (self.pe, self.act, self.dve, self.pool):
            if e.cnt > 0:
                evs.append((e.key, e.sem, e.cnt))
        for q in (self.q_w, self.q_a):
            for i, sm in enumerate(q.sems):
                if q.vals[i] > 0:
                    evs.append(("D%s%d" % (q.name, i), sm, q.vals[i]))
        for e in (self.pe, self.act, self.dve, self.pool, self.q_w, self.q_a):
            for ev in evs:
                e.wait_ev(ev)

    def dram(self, name, shape, kind="Internal", dtype=F32):
        return self.nc.dram_tensor(name, list(shape), dtype, kind=kind).ap()


class Ring:
    def __init__(self, bufs):
        self.bufs = bufs
        self.i = 0

    def next(self):
        b = self.bufs[self.i]
        self.i = (self.i + 1) % len(self.bufs)
        return b


class Kern:
    LAYERS = None

    def __init__(self, NS, NT, depth=2, do=("ffn1", "mix", "ffn2"), seq_real=None):
        self.NS = NS
        self.NT = NT
        self.LP = NT * 128
        self.NTOK = NS * self.LP
        self.depth = depth
        self.do = do
        self.L = seq_real if seq_real is not None else (self.LP - 112)
        self.P = Prog()
        self.build()

    def build(self):
        P = self.P
        nc = P.nc
        NS, NT, LP, NTOK = self.NS, self.NT, self.LP, self.NTOK
        SEQ = self.L - N_META
        dep = self.depth
        self.xin = P.dram("xin", [NS * LP, D], "ExternalInput")
        self.ident_d = P.dram("ident", [128, 128], "ExternalInput")
        self.w = {}
        def ext(name, shape):
            self.w[name] = P.dram(name, shape, "ExternalInput")
        ext("norm_ffn1", [dep, D]); ext("norm_mix", [dep, D]); ext("norm_ffn2", [dep, D])
        for nm in ("w1_gate", "w1_up", "w2_gate", "w2_up"):
            ext(nm, [dep, D, DFF])
        for nm in ("w1_down", "w2_down"):
            ext(nm, [dep, DFF, D])
        ne, no = (dep + 1) // 2, dep // 2
        if "mix" in self.do:
            ext("ev_w_in", [ne, D, 1344]); ext("s5_log_dt", [ne, 32]); ext("s5_a_re", [ne, 32, 64]); ext("s5_a_im", [ne, 32, 64])
            ext("s5_b_re", [ne, 32, 64, 16]); ext("s5_b_im", [ne, 32, 64, 16]); ext("s5_c_re", [ne, 32, 16, 64]); ext("s5_c_im", [ne, 32, 16, 64])
            ext("s5_d", [ne, 512]); ext("s5_w_glu", [ne, 512, 512]); ext("mla_g_cq", [ne, 512]); ext("mla_g_ckv", [ne, 256])
            ext("mla_w_uq", [ne, 512, 1536]); ext("mla_w_ukv", [ne, 256, 2048]); ext("mla_g_q", [ne, 192]); ext("mla_g_k", [ne, 192])
            ext("ev_w_out", [ne, 1536, D])
            if no > 0:
                ext("od_w_in", [no, D, 7184]); ext("gdn_conv", [no, 4, 3072]); ext("gdn_a_log", [no, 8]); ext("gdn_dt_bias", [no, 8])
                ext("gdn_g_out", [no, 128]); ext("od_w_out", [no, 2048, D])
            self.c_rope_cos = P.dram("rope_cos", [LP, 32], "ExternalInput")
            self.c_rope_sin = P.dram("rope_sin", [LP, 32], "ExternalInput")
            self.c_maskT = P.dram("maskT", [128, 128], "ExternalInput")
            self.uT = P.dram("uT", [4, 128, NTOK])
            self.QT = P.dram("QT", [8, 192, NTOK])
            self.KT = P.dram("KT", [8, 192, NTOK])
            self.Vd = P.dram("Vd", [NTOK, 1024])
            self.yE = P.dram("yE", [12, 128, NTOK])
            if no > 0:
                self.c_maskS = P.dram("maskS", [128, 128], "ExternalInput")
                self.c_maskL = P.dram("maskL", [128, 128], "ExternalInput")
                self.c_sel = P.dram("sel", [8, 1024], "ExternalInput")
                self.sbQ = P.dram("sbQ", [8, 128, NTOK]); self.sbK = P.dram("sbK", [8, 128, NTOK]); self.sbV = P.dram("sbV", [NTOK, 1024])
                self.gT = P.dram("gT", [24, 128, NTOK]); self.gzT = P.dram("gzT", [8, 128, NTOK]); self.gabT = P.dram("gabT", [16, NTOK])
                self.yO = P.dram("yO", [16, 128, NTOK])
        self.out = P.dram("out", [NS * SEQ, D], "ExternalOutput")
        self.hT = P.dram("hT", [DC, 128, NTOK])
        self.ident = P.sb([128, 128], "ident"); self.t_ident = Tok("ident")
        P.q_a.dma(self.ident[:], self.ident_d[:, :], outs=[self.t_ident])
        self.ones = P.sb([128, 128], "ones"); self.t_ones = Tok("ones")
        P.dve.op(lambda: nc.vector.memset(self.ones[:], 1.0), outs=[self.t_ones])
        self.epsb = P.sb([128, 1], "epsb"); self.t_eps = Tok("eps")
        P.dve.op(lambda: nc.vector.memset(self.epsb[:], EPS), outs=[self.t_eps])
        self.gains = {}
        for nm in ("norm_ffn1", "norm_mix", "norm_ffn2"):
            g = P.sb([128, dep, DC], nm); t = Tok(nm)
            for l in range(dep):
                P.q_a.dma(g[:, l, :], self.w[nm][l].rearrange("(c p) -> p c", p=128), outs=[t],
                          allow_slow_non_contiguous=True)
            self.gains[nm] = (g, t)

        self.stage_in()
        for l in (self.LAYERS if self.LAYERS is not None else range(dep)):
            if "ffn1" in self.do:
                self.ffn(l, "1")
            if "mix" in self.do:
                if l % 2 == 0:
                    self.even_mixer(l)
                else:
                    self.odd_mixer(l)
            if "ffn2" in self.do:
                self.ffn(l, "2")
        self.stage_out()
        P.q_a.drain()
        P.q_w.drain()

    def psring(self, idx):
        return Ring([self.P.psb[i] for i in idx])

    def stage_in(self):
        P = self.P; nc = P.nc
        m = P.mark()
        NB = self.NTOK // 128
        xin_r = Ring([(P.sb([128, D], "xin"), Tok()) for _ in range(2)])
        ps_r = self.psring([0, 1])
        ht_r = Ring([(P.sb([128, DC, 128], "htin"), Tok()) for _ in range(2)])
        for b in range(NB):
            xt, xtok = xin_r.next()
            P.q_a.dma(xt[:], self.xin[b * 128:(b + 1) * 128, :], outs=[xtok])
            ht, htok = ht_r.next()
            for g in range(4):
                ps, ptok = ps_r.next()
                for j in range(4):
                    c = g * 4 + j
                    P.pe.op(lambda: nc.tensor.transpose(ps[:, j * 128:(j + 1) * 128],
                                                        xt[:, c * 128:(c + 1) * 128], self.ident[:]),
                            outs=[ptok], ins=[xtok, self.t_ident])
                P.act.op(lambda: nc.scalar.copy(out=ht[:, g * 4:(g + 1) * 4, :],
                                                in_=ps[:].rearrange("p (c t) -> p c t", c=4)),
                         outs=[htok], ins=[ptok])
            P.q_a.dma(self.hT[:, :, b * 128:(b + 1) * 128].rearrange("c p t -> p c t"), ht[:], ins=[htok],
                      outs=[self.tok_h(b * 128, 128)])
        P.release(m)

    def tok_h(self, t0, n):
        if not hasattr(self, "_htoks"):
            self._htoks = {}
        key = t0 // 128
        if key not in self._htoks:
            self._htoks[key] = Tok("h%d" % key)
        return self._htoks[key]

    def toks_h(self, t0, n):
        return [self.tok_h(t, 128) for t in range(t0, t0 + n, 128)]

    def stage_out(self):
        P = self.P; nc = P.nc
        m = P.mark()
        SEQ = self.L - N_META
        ht_r = Ring([(P.sb([128, DC, 128], "htout"), Tok()) for _ in range(2)])
        ps_r = self.psring([0, 1])
        o_r = Ring([(P.sb([128, D], "osb"), Tok()) for _ in range(2)])
        for s in range(self.NS):
            for j in range(self.NT):
                t0 = j * 128
                lo = max(t0, N_META); hi = min(t0 + 128, self.L)
                if hi <= lo:
                    continue
                b = s * self.NT + j
                ht, htok = ht_r.next()
                P.q_a.dma(ht[:], self.hT[:, :, b * 128:(b + 1) * 128].rearrange("c p t -> p c t"), outs=[htok],
                          ins=[self.tok_h(b * 128, 128)])
                ot, otok = o_r.next()
                for g in range(4):
                    ps, ptok = ps_r.next()
                    for jj in range(4):
                        c = g * 4 + jj
                        P.pe.op(lambda: nc.tensor.transpose(ps[:, jj * 128:(jj + 1) * 128], ht[:, c, :], self.ident[:]),
                                outs=[ptok], ins=[htok, self.t_ident])
                    P.act.op(lambda: nc.scalar.copy(out=ot[:, g * 512:(g + 1) * 512], in_=ps[:]),
                             outs=[otok], ins=[ptok])
                r0 = s * SEQ + lo - N_META
                P.q_a.dma(self.out[r0:r0 + (hi - lo), :], ot[lo - t0:hi - t0, :], ins=[otok])
        P.release(m)

    def rms_scale(self, src, stok, T, gain_cols, gtok, ps_sum, pstok, sq, sqtok, rstd, rtok, nchunks=DC, dim=D):
        P = self.P; nc = P.nc
        for c in range(nchunks):
            P.act.op(lambda: nc.scalar.activation(out=sq[:, :T], in_=src[:, c, :T], func=AF.Square),
                     outs=[sqtok], ins=[stok])
            P.pe.op(lambda: nc.tensor.matmul(ps_sum[:, :T], self.ones[:], sq[:, :T], start=(c == 0), stop=(c == nchunks - 1)),
                    outs=[pstok], ins=[sqtok, self.t_ones])
        P.act.op(lambda: nc.scalar.activation(out=rstd[:, :T], in_=ps_sum[:, :T], func=AF.Sqrt,
                                              bias=self.epsb[:], scale=1.0 / dim),
                 outs=[rtok], ins=[pstok, self.t_eps])
        P.dve.op(lambda: nc.vector.reciprocal(out=rstd[:, :T], in_=rstd[:, :T]), outs=[rtok], ins=[rtok])
        for c in range(nchunks):
            P.dve.op(lambda: nc.vector.scalar_tensor_tensor(out=src[:, c, :T], in0=src[:, c, :T],
                                                            scalar=gain_cols(c), in1=rstd[:, :T],
                                                            op0=ALU.mult, op1=ALU.mult),
                     outs=[stok], ins=[stok, rtok, gtok])

    def ffn(self, l, which):
        P = self.P; nc = P.nc
        NTOK = self.NTOK
        TT = 768; XS = 256
        HF = FC // 2
        wg = self.w["w%s_gate" % which][l]
        wu = self.w["w%s_up" % which][l]
        wd = self.w["w%s_down" % which][l]
        gain, gtok = self.gains["norm_ffn%s" % which]
        m = P.mark()
        xs, xstok = P.sb([128, DC, XS], "ffxs"), Tok("ffxs")
        xb, xbtok = P.sbh([128, DC, TT], "ffxb"), Tok("ffxb")
        hf = P.sbh([128, FC, TT], "ffh")
        hftoks = [Tok("ffh%d" % f) for f in range(FC)]
        wst_r = Ring([(P.sb([128, DC, 128], "wst"), Tok()) for _ in range(3)])
        wgb_r = Ring([(P.sbh([128, DC, 128], "wgb"), Tok()) for _ in range(2)])
        wub_r = Ring([(P.sbh([128, DC, 128], "wub"), Tok()) for _ in range(2)])
        wdst_r = Ring([(P.sb([128, HF, 128], "wdst"), Tok()) for _ in range(2)])
        wdb_r = Ring([(P.sbh([128, HF, 128], "wdb"), Tok()) for _ in range(4)])
        rstd, rtok = P.sb([128, XS], "ffrstd"), Tok()
        sq_r = Ring([(P.sb([128, XS], "ffsq"), Tok()) for _ in range(2)])
        sg_r = Ring([(P.sb([128, TT // 2], "ffsg"), Tok()) for _ in range(2)])
        hres_r = Ring([(P.sb([128, TT], "ffres"), Tok()) for _ in range(2)])
        psg = [P.psb[0], P.psb[1]]; psu = [P.psb[2], P.psb[3]]; psd = [P.psb[4], P.psb[5]]
        pss, psstok = P.psb[6]
        wgv = wg.rearrange("(c p) f -> p c f", p=128)
        wuv = wu.rearrange("(c p) f -> p c f", p=128)
        wdv = wd.rearrange("(c p) d -> p c d", p=128)
        with nc.allow_low_precision("bf16 matmul operands, fp32 accumulation"):
            for t0 in range(0, NTOK, TT):
                T = min(TT, NTOK - t0)
                HW = T // 2
                halves = [(0, HW), (HW, T - HW)]
                htoks = self.toks_h(t0, T)
                for p0 in range(0, T, XS):
                    P.q_a.dma(xs[:, :, :], self.hT[:, :, t0 + p0:t0 + p0 + XS].rearrange("c p t -> p c t"), outs=[xstok], ins=htoks)
                    for c in range(DC):
                        sq, sqtok = sq_r.next()
                        P.act.op(lambda: nc.scalar.activation(out=sq[:, :], in_=xs[:, c, :], func=AF.Square), outs=[sqtok], ins=[xstok])
                        P.pe.op(lambda: nc.tensor.matmul(pss[:, :XS], self.ones[:], sq[:, :], start=(c == 0), stop=(c == DC - 1)),
                                outs=[psstok], ins=[sqtok, self.t_ones])
                    P.act.op(lambda: nc.scalar.activation(out=rstd[:, :], in_=pss[:, :XS], func=AF.Sqrt, bias=self.epsb[:], scale=1.0 / D),
                             outs=[rtok], ins=[psstok, self.t_eps])
                    P.dve.op(lambda: nc.vector.reciprocal(out=rstd[:, :], in_=rstd[:, :]), outs=[rtok], ins=[rtok])
                    for c in range(DC):
                        P.dve.op(lambda: nc.vector.scalar_tensor_tensor(out=xb[:, c, p0:p0 + XS], in0=xs[:, c, :], scalar=gain[:, l, c:c + 1],
                                                                        in1=rstd[:, :], op0=ALU.mult, op1=ALU.mult),
                                 outs=[xbtok], ins=[xstok, rtok, gtok])
                def load_gu(f):
                    gs, gst = wst_r.next()
                    P.q_w.dma(gs[:], wgv[:, :, f * 128:(f + 1) * 128], outs=[gst])
                    gb, gbt = wgb_r.next()
                    P.dve.op(lambda: nc.vector.tensor_copy(out=gb[:], in_=gs[:]), outs=[gbt], ins=[gst])
                    us, ust = wst_r.next()
                    P.q_w.dma(us[:], wuv[:, :, f * 128:(f + 1) * 128], outs=[ust])
                    ub, ubt = wub_r.next()
                    P.dve.op(lambda: nc.vector.tensor_copy(out=ub[:], in_=us[:]), outs=[ubt], ins=[ust])
                    return gb, gbt, ub, ubt
                nxt = load_gu(0)
                for f in range(FC):
                    g_, gt_, u_, ut_ = nxt
                    if f + 1 < FC:
                        nxt = load_gu(f + 1)
                    for hi, (h0, hw) in enumerate(halves):
                        ps, pt = psg[hi]
                        for c in range(DC):
                            P.pe.op(lambda: nc.tensor.matmul(ps[:, :hw], g_[:, c, :], xb[:, c, h0:h0 + hw], start=(c == 0), stop=(c == DC - 1)),
                                    outs=[pt], ins=[gt_, xbtok])
                    for hi, (h0, hw) in enumerate(halves):
                        ps, pt = psu[hi]
                        for c in range(DC):
                            P.pe.op(lambda: nc.tensor.matmul(ps[:, :hw], u_[:, c, :], xb[:, c, h0:h0 + hw], start=(c == 0), stop=(c == DC - 1)),
                                    outs=[pt], ins=[ut_, xbtok])
                    for hi, (h0, hw) in enumerate(halves):
                        sg, sgt = sg_r.next()
                        P.act.op(lambda: nc.scalar.activation(out=sg[:, :hw], in_=psg[hi][0][:, :hw], func=AF.Silu), outs=[sgt], ins=[psg[hi][1]])
                        P.dve.op(lambda: nc.vector.tensor_tensor(out=hf[:, f, h0:h0 + hw], in0=psu[hi][0][:, :hw], in1=sg[:, :hw], op=ALU.mult),
                                 outs=[hftoks[f]], ins=[psu[hi][1], sgt])
                def load_d(i):
                    dc, hh = divmod(i, 2)
                    ds_, dst_ = wdst_r.next()
                    P.q_w.dma(ds_[:], wdv[:, hh * HF:(hh + 1) * HF, dc * 128:(dc + 1) * 128], outs=[dst_])
                    db, dbt = wdb_r.next()
                    P.act.op(lambda: nc.scalar.copy(out=db[:], in_=ds_[:]), outs=[dbt], ins=[dst_])
                    return db, dbt
                pend = [load_d(0), load_d(1)]
                for dc in range(DC):
                    hres, hrt = hres_r.next()
                    P.q_a.dma(hres[:, :T], self.hT[dc, :, t0:t0 + T], outs=[hrt], ins=htoks)
                    slabs = []
                    for hh in range(2):
                        slabs.append(pend.pop(0))
                        i_next = dc * 2 + hh + 2
                        if i_next < 2 * DC:
                            pend.append(load_d(i_next))
                    for hi, (h0, hw) in enumerate(halves):
                        ps, pt = psd[hi]
                        for hh in range(2):
                            d_, dt_ = slabs[hh]
                            for ff in range(HF):
                                f = hh * HF + ff
                                P.pe.op(lambda: nc.tensor.matmul(ps[:, :hw], d_[:, ff, :], hf[:, f, h0:h0 + hw], start=(f == 0), stop=(f == FC - 1)),
                                        outs=[pt], ins=[dt_, hftoks[f]])
                        P.dve.op(lambda: nc.vector.scalar_tensor_tensor(out=hres[:, h0:h0 + hw], in0=ps[:, :hw], scalar=0.5, in1=hres[:, h0:h0 + hw],
                                                                        op0=ALU.mult, op1=ALU.add),
                                 outs=[hrt], ins=[pt, hrt])
                    P.q_a.dma(self.hT[dc, :, t0:t0 + T], hres[:, :T], ins=[hrt], outs=htoks)
        P.release(m)

    def ffn_fp32(self, l, which):
        P = self.P; nc = P.nc
        NTOK = self.NTOK
        TT = 512
        HF = FC // 2
        wg = self.w["w%s_gate" % which][l]
        wu = self.w["w%s_up" % which][l]
        wd = self.w["w%s_down" % which][l]
        gain, gtok = self.gains["norm_ffn%s" % which]
        m = P.mark()
        xt, xtok = P.sb([128, DC, TT], "ffx"), Tok("ffx")
        hf = P.sb([128, FC, TT], "ffh")
        hftoks = [Tok("ffh%d" % f) for f in range(FC)]
        wg_r = Ring([(P.sb([128, DC, 128], "wg"), Tok()) for _ in range(2)])
        wu_r = Ring([(P.sb([128, DC, 128], "wu"), Tok()) for _ in range(2)])
        wd_r = Ring([(P.sb([128, HF, 128], "wd"), Tok()) for _ in range(3)])
        psg_r = self.psring([0, 1]); psu_r = self.psring([2, 3]); psd_r = self.psring([4, 5])
        pss, psstok = P.psb[6]
        rstd, rtok = P.sb([128, TT], "ffrstd"), Tok()
        sg_r = Ring([(P.sb([128, TT], "ffsg"), Tok()) for _ in range(2)])
        hres_r = Ring([(P.sb([128, TT], "ffres"), Tok()) for _ in range(2)])
        wgv = wg.rearrange("(c p) f -> p c f", p=128)
        wuv = wu.rearrange("(c p) f -> p c f", p=128)
        wdv = wd.rearrange("(c p) d -> p c d", p=128)
        for t0 in range(0, NTOK, TT):
            T = min(TT, NTOK - t0)
            htoks = self.toks_h(t0, T)
            P.q_a.dma(xt[:, :, :T], self.hT[:, :, t0:t0 + T].rearrange("c p t -> p c t"), outs=[xtok], ins=htoks)
            sq, sqtok = sg_r.next()
            self.rms_scale(xt, xtok, T, lambda c: gain[:, l, c:c + 1], gtok, pss, psstok, sq, sqtok, rstd, rtok)
            def load_gu(f):
                g_, gt_ = wg_r.next(); u_, ut_ = wu_r.next()
                P.q_w.dma(g_[:], wgv[:, :, f * 128:(f + 1) * 128], outs=[gt_])
                P.q_w.dma(u_[:], wuv[:, :, f * 128:(f + 1) * 128], outs=[ut_])
                return g_, gt_, u_, ut_
            nxt = load_gu(0)
            for f in range(FC):
                g_, gt_, u_, ut_ = nxt
                if f + 1 < FC:
                    nxt = load_gu(f + 1)
                psg, psgt = psg_r.next(); psu, psut = psu_r.next()
                for c in range(DC):
                    P.pe.op(lambda: nc.tensor.matmul(psg[:, :T], g_[:, c, :], xt[:, c, :T], start=(c == 0), stop=(c == DC - 1)),
                            outs=[psgt], ins=[gt_, xtok])
                for c in range(DC):
                    P.pe.op(lambda: nc.tensor.matmul(psu[:, :T], u_[:, c, :], xt[:, c, :T], start=(c == 0), stop=(c == DC - 1)),
                            outs=[psut], ins=[ut_, xtok])
                sg, sgt = sg_r.next()
                P.act.op(lambda: nc.scalar.activation(out=sg[:, :T], in_=psg[:, :T], func=AF.Silu), outs=[sgt], ins=[psgt])
                P.dve.op(lambda: nc.vector.tensor_tensor(out=hf[:, f, :T], in0=psu[:, :T], in1=sg[:, :T], op=ALU.mult),
                         outs=[hftoks[f]], ins=[psut, sgt])
            def load_d(i):
                dc, hh = divmod(i, 2)
                d_, dt_ = wd_r.next()
                P.q_w.dma(d_[:], wdv[:, hh * HF:(hh + 1) * HF, dc * 128:(dc + 1) * 128], outs=[dt_])
                return d_, dt_
            pend = [load_d(0), load_d(1)]
            for dc in range(DC):
                hres, hrt = hres_r.next()
                P.q_a.dma(hres[:, :T], self.hT[dc, :, t0:t0 + T], outs=[hrt], ins=htoks)
                psd, psdt = psd_r.next()
                for hh in range(2):
                    d_, dt_ = pend.pop(0)
                    i_next = dc * 2 + hh + 2
                    if i_next < 2 * DC:
                        pend.append(load_d(i_next))
                    for ff in range(HF):
                        f = hh * HF + ff
                        P.pe.op(lambda: nc.tensor.matmul(psd[:, :T], d_[:, ff, :], hf[:, f, :T], start=(f == 0), stop=(f == FC - 1)),
                                outs=[psdt], ins=[dt_, hftoks[f]])
                P.dve.op(lambda: nc.vector.scalar_tensor_tensor(out=hres[:, :T], in0=psd[:, :T], scalar=0.5, in1=hres[:, :T],
                                                                op0=ALU.mult, op1=ALU.add),
                         outs=[hrt], ins=[psdt, hrt])
                P.q_a.dma(self.hT[dc, :, t0:t0 + T], hres[:, :T], ins=[hrt], outs=htoks)
        P.release(m)


    def bc_rows(self, dram_ap_row_tensor, offset, n, parts=128):
        return bass.AP(dram_ap_row_tensor.tensor, offset, [[0, parts], [1, n]])

    def even_mixer(self, l):
        import os
        st = os.environ.get("DBG_STAGES", "inproj,attn,s5,out").split(",")
        if "inproj" in st:
            self.even_inproj(l)
        if "attn" in st:
            self.even_attn(l)
        if "s5" in st:
            self.even_s5(l)
        if "out" in st:
            self.even_out(l)

    def headnorm_rope(self, buf, btok, g_rep, gtok, cos, sin, cstok, tmp, ttok, ss, sstok, t4, t4tok):
        P = self.P; nc = P.nc
        P.dve.op(lambda: nc.vector.tensor_tensor(out=tmp[:], in0=buf[:], in1=buf[:], op=ALU.mult), outs=[ttok], ins=[btok])
        P.dve.op(lambda: nc.vector.tensor_reduce(out=ss[:], in_=tmp[:], axis=AX.X, op=ALU.add), outs=[sstok], ins=[ttok])
        P.act.op(lambda: nc.scalar.activation(out=ss[:], in_=ss[:], func=AF.Sqrt, bias=self.epsb[:], scale=1.0 / 192),
                 outs=[sstok], ins=[sstok, self.t_eps])
        P.dve.op(lambda: nc.vector.reciprocal(out=ss[:], in_=ss[:]), outs=[sstok], ins=[sstok])
        P.dve.op(lambda: nc.vector.tensor_tensor(out=buf[:], in0=buf[:], in1=ss[:].unsqueeze(2).to_broadcast([128, 8, 192]), op=ALU.mult),
                 outs=[btok], ins=[btok, sstok])
        P.dve.op(lambda: nc.vector.tensor_tensor(out=buf[:], in0=buf[:], in1=g_rep[:].unsqueeze(1).to_broadcast([128, 8, 192]), op=ALU.mult),
                 outs=[btok], ins=[btok, gtok])
        x1 = buf[:, :, 128:160]; x2 = buf[:, :, 160:192]
        cb = cos.unsqueeze(1).to_broadcast([128, 8, 32]); sb_ = sin.unsqueeze(1).to_broadcast([128, 8, 32])
        P.dve.op(lambda: nc.vector.tensor_tensor(out=t4[:, 0], in0=x1, in1=cb, op=ALU.mult), outs=[t4tok], ins=[btok, cstok])
        P.dve.op(lambda: nc.vector.tensor_tensor(out=t4[:, 1], in0=x2, in1=sb_, op=ALU.mult), outs=[t4tok], ins=[btok, cstok])
        P.dve.op(lambda: nc.vector.tensor_tensor(out=t4[:, 2], in0=x1, in1=sb_, op=ALU.mult), outs=[t4tok], ins=[btok, cstok])
        P.dve.op(lambda: nc.vector.tensor_tensor(out=t4[:, 3], in0=x2, in1=cb, op=ALU.mult), outs=[t4tok], ins=[btok, cstok])
        P.dve.op(lambda: nc.vector.tensor_tensor(out=x1, in0=t4[:, 0], in1=t4[:, 1], op=ALU.subtract), outs=[btok], ins=[t4tok])
        P.dve.op(lambda: nc.vector.tensor_tensor(out=x2, in0=t4[:, 2], in1=t4[:, 3], op=ALU.add), outs=[btok], ins=[t4tok])

    def even_inproj(self, l):
        P = self.P; nc = P.nc; i = l // 2
        NS, LP, NT = self.NS, self.LP, self.NT
        m = P.mark()
        TT = 512
        Wv = self.w["ev_w_in"][i].rearrange("(c p) f -> p c f", p=128)
        gain, gtok = self.gains["norm_mix"]
        wuq = P.sb([128, 4, 1536], "wuq"); t_wuq = Tok()
        P.q_w.dma(wuq[:], self.w["mla_w_uq"][i].rearrange("(c p) f -> p c f", p=128), outs=[t_wuq])
        wukv = P.sb([128, 2, 2048], "wukv"); t_wukv = Tok()
        P.q_w.dma(wukv[:], self.w["mla_w_ukv"][i].rearrange("(c p) f -> p c f", p=128), outs=[t_wukv])
        wkr = P.sb([128, DC, 64], "wkr"); t_wkr = Tok()
        P.q_w.dma(wkr[:], Wv[:, :, 1280:1344], outs=[t_wkr])
        gcq = P.sb([128, 4], "gcq"); gckv = P.sb([128, 2], "gckv"); t_g = Tok()
        P.q_a.dma(gcq[:], self.w["mla_g_cq"][i].rearrange("(c p) -> p c", p=128), outs=[t_g], allow_slow_non_contiguous=True)
        P.q_a.dma(gckv[:], self.w["mla_g_ckv"][i].rearrange("(c p) -> p c", p=128), outs=[t_g], allow_slow_non_contiguous=True)
        gq = P.sb([128, 192], "gq"); gk = P.sb([128, 192], "gk"); t_gq = Tok()
        P.q_a.dma(gq[:], self.bc_rows(self.w["mla_g_q"], i * 192, 192), outs=[t_gq])
        P.q_a.dma(gk[:], self.bc_rows(self.w["mla_g_k"], i * 192, 192), outs=[t_gq])
        cosT = P.sb([128, NT, 32], "cos"); sinT = P.sb([128, NT, 32], "sin"); t_cs = Tok()
        P.q_a.dma(cosT[:], self.c_rope_cos.rearrange("(j p) f -> p j f", p=128), outs=[t_cs])
        P.q_a.dma(sinT[:], self.c_rope_sin.rearrange("(j p) f -> p j f", p=128), outs=[t_cs])
        xt, xtok = P.sb([128, DC, TT], "evx"), Tok()
        cq, cqtok = P.sb([128, 4, TT], "cq"), Tok()
        ckv, ckvtok = P.sb([128, 2, TT], "ckv"), Tok()
        win_r = Ring([(P.sb([128, DC, 128], "win"), Tok()) for _ in range(2)])
        ub_r = Ring([(P.sb([128, TT], "ub"), Tok()) for _ in range(2)])
        rstd, rtok = P.sb([128, TT], "rstd"), Tok()
        sq, sqtok = P.sb([128, TT], "sq"), Tok()
        q_sb, qtok = P.sb([128, 8, 192], "q_sb"), Tok()
        k_sb, ktok = P.sb([128, 8, 192], "k_sb"), Tok()
        v_sb, vtok = P.sb([128, 8, 128], "v_sb"), Tok()
        tmp, ttok = P.sb([128, 8, 192], "tmp"), Tok()
        ss, sstok = P.sb([128, 8], "ss"), Tok()
        t4, t4tok = P.sb([128, 4, 8, 32], "t4"), Tok()
        qTn, qTntok = P.sb([128, 8, 128], "qTn"), Tok()
        qTr, qTrtok = P.sb([64, 8, 128], "qTr"), Tok()
        kTn, kTntok = P.sb([128, 8, 128], "kTn"), Tok()
        kTr, kTrtok = P.sb([64, 8, 128], "kTr"), Tok()
        ps01 = self.psring([0, 1])
        pss, psstok = P.psb[7]
        for s in range(NS):
            for t0 in range(0, LP, TT):
                T = min(TT, LP - t0)
                tok0 = s * LP + t0
                htoks = self.toks_h(tok0, T)
                P.q_a.dma(xt[:, :, :T], self.hT[:, :, tok0:tok0 + T].rearrange("c p t -> p c t"), outs=[xtok], ins=htoks)
                self.rms_scale(xt, xtok, T, lambda c: gain[:, l, c:c + 1], gtok, pss, psstok, sq, sqtok, rstd, rtok)
                def load_w(j):
                    w_, wt_ = win_r.next()
                    P.q_w.dma(w_[:], Wv[:, :, j * 128:(j + 1) * 128], outs=[wt_])
                    return w_, wt_
                nxt = load_w(0)
                for j in range(10):
                    w_, wt_ = nxt
                    if j + 1 < 10:
                        nxt = load_w(j + 1)
                    ps, ptok = ps01.next()
                    for c in range(DC):
                        P.pe.op(lambda: nc.tensor.matmul(ps[:, :T], w_[:, c, :], xt[:, c, :T], start=(c == 0), stop=(c == DC - 1)),
                                outs=[ptok], ins=[wt_, xtok])
                    if j < 4:
                        ub, ubtok = ub_r.next()
                        P.act.op(lambda: nc.scalar.copy(out=ub[:, :T], in_=ps[:, :T]), outs=[ubtok], ins=[ptok])
                        P.q_a.dma(self.uT[j, :, tok0:tok0 + T], ub[:, :T], ins=[ubtok])
                    elif j < 8:
                        P.act.op(lambda: nc.scalar.copy(out=cq[:, j - 4, :T], in_=ps[:, :T]), outs=[cqtok], ins=[ptok])
                    else:
                        P.act.op(lambda: nc.scalar.copy(out=ckv[:, j - 8, :T], in_=ps[:, :T]), outs=[ckvtok], ins=[ptok])
                self.rms_scale(cq, cqtok, T, lambda c: gcq[:, c:c + 1], t_g, pss, psstok, sq, sqtok, rstd, rtok, nchunks=4, dim=512)
                self.rms_scale(ckv, ckvtok, T, lambda c: gckv[:, c:c + 1], t_g, pss, psstok, sq, sqtok, rstd, rtok, nchunks=2, dim=256)
                import os
                LV = int(os.environ.get("DBG_INPROJ", "9"))
                for tb in range(T // 128 if LV >= 2 else 0):
                    tsl = slice(tb * 128, (tb + 1) * 128)
                    jblk = (t0 // 128) + tb
                    tokb = tok0 + tb * 128
                    qflat = q_sb[:].rearrange("p h d -> p (h d)")
                    SUB = os.environ.get("DBG_SUB", "q,kv,kr").split(",")
                    for n in range(3 if "q" in SUB else 0):
                        ps, ptok = P.psb[2 + n]
                        for c in range(4):
                            P.pe.op(lambda: nc.tensor.matmul(ps[:, :], cq[:, c, tsl], wuq[:, c, n * 512:(n + 1) * 512], start=(c == 0), stop=(c == 3)),
                                    outs=[ptok], ins=[cqtok, t_wuq])
                        P.act.op(lambda: nc.scalar.copy(out=qflat[:, n * 512:(n + 1) * 512], in_=ps[:, :]), outs=[qtok], ins=[ptok])
                    for n4 in range(4 if "kv" in SUB else 0):
                        ps, ptok = P.psb[5 + (n4 % 2)]
                        for c in range(2):
                            P.pe.op(lambda: nc.tensor.matmul(ps[:, :], ckv[:, c, tsl], wukv[:, c, n4 * 512:(n4 + 1) * 512], start=(c == 0), stop=(c == 1)),
                                    outs=[ptok], ins=[ckvtok, t_wukv])
                        psv = ps[:, :].rearrange("p (h d) -> p h d", h=2)
                        P.act.op(lambda: nc.scalar.copy(out=k_sb[:, 2 * n4:2 * n4 + 2, 0:128], in_=psv[:, :, 0:128]), outs=[ktok], ins=[ptok])
                        P.dve.op(lambda: nc.vector.tensor_copy(out=v_sb[:, 2 * n4:2 * n4 + 2, :], in_=psv[:, :, 128:256]), outs=[vtok], ins=[ptok, ktok])
                    ps, ptok = ps01.next()
                    for c in range(DC if "kr" in SUB else 0):
                        P.pe.op(lambda: nc.tensor.matmul(ps[:, 0:64], xt[:, c, tsl], wkr[:, c, :], start=(c == 0), stop=(c == DC - 1)),
                                outs=[ptok], ins=[xtok, t_wkr])
                    if "kr" in SUB:
                      P.act.op(lambda: nc.scalar.copy(out=k_sb[:, :, 128:192], in_=ps[:, 0:64].unsqueeze(1).to_broadcast([128, 8, 64])),
                             outs=[ktok], ins=[ptok])
                    if LV >= 3:
                        P.q_a.dma(self.Vd[tokb:tokb + 128, :], v_sb[:].rearrange("p h d -> p (h d)"), ins=[vtok])
                    if LV < 4:
                        continue
                    self.headnorm_rope(q_sb, qtok, gq, t_gq, cosT[:, jblk, :], sinT[:, jblk, :], t_cs, tmp, ttok, ss, sstok, t4, t4tok)
                    self.headnorm_rope(k_sb, ktok, gk, t_gq, cosT[:, jblk, :], sinT[:, jblk, :], t_cs, tmp, ttok, ss, sstok, t4, t4tok)
                    if LV < 5:
                        continue
                    for (src, stok_, dn, dntok, dr, drtok, dst) in ((q_sb, qtok, qTn, qTntok, qTr, qTrtok, self.QT),
                                                                  (k_sb, ktok, kTn, kTntok, kTr, kTrtok, self.KT)):
                        for hg in range(2):
                            ps, ptok = ps01.next()
                            for hl in range(4):
                                h = hg * 4 + hl
                                P.pe.op(lambda: nc.tensor.transpose(ps[:, hl * 128:(hl + 1) * 128], src[:, h, 0:128], self.ident[:]),
                                        outs=[ptok], ins=[stok_, self.t_ident])
                            P.act.op(lambda: nc.scalar.copy(out=dn[:, hg * 4:(hg + 1) * 4, :], in_=ps[:, :].rearrange("p (h t) -> p h t", h=4)),
                                     outs=[dntok], ins=[ptok])
                            ps, ptok = ps01.next()
                            for hl in range(4):
                                h = hg * 4 + hl
                                P.pe.op(lambda: nc.tensor.transpose(ps[0:64, hl * 128:(hl + 1) * 128], src[:, h, 128:192], self.ident[:]),
                                        outs=[ptok], ins=[stok_, self.t_ident])
                            P.dve.op(lambda: nc.vector.tensor_copy(out=dr[:, hg * 4:(hg + 1) * 4, :], in_=ps[0:64, :].rearrange("p (h t) -> p h t", h=4)),
                                     outs=[drtok], ins=[ptok])
                        P.q_a.dma(dst[:, 0:128, tokb:tokb + 128].rearrange("h p t -> p h t"), dn[:], ins=[dntok])
                        P.q_a.dma(dst[:, 128:192, tokb:tokb + 128].rearrange("h p t -> p h t"), dr[:], ins=[drtok])
        P.release(m)

    def even_attn(self, l):
        P = self.P; nc = P.nc
        NS, LP, NT = self.NS, self.LP, self.NT
        m = P.mark()
        maskT = P.sb([128, 128], "maskT"); t_mask = Tok()
        P.q_a.dma(maskT[:], self.c_maskT[:, :], outs=[t_mask])
        onesb = P.sbh([128, 128], "onesb"); t_ob = Tok()
        P.dve.op(lambda: nc.vector.tensor_copy(out=onesb[:], in_=self.ones[:]), outs=[t_ob], ins=[self.t_ones])
        sets = Ring([dict(qn=P.sb([128, LP]), qr=P.sb([64, LP]), kn=P.sb([128, LP]), kr=P.sb([64, LP]), v=P.sb([128, NT, 128]),
                          vb=P.sbh([128, NT, 128]), tok=Tok(), vtok=Tok())
                     for _ in range(2)])
        pt_r = Ring([(P.sbh([128, 512], "pt"), Tok()) for _ in range(3)])
        rden, rdtok = P.sb([128, 512], "rden"), Tok()
        yo_r = Ring([(P.sb([128, 512], "yo"), Tok()) for _ in range(2)])
        ps_s = self.psring([0, 1, 2])
        ps_o = self.psring([3, 4]); ps_d = self.psring([5, 6])
        scale = 192.0 ** -0.5
        with nc.allow_low_precision("bf16 P@V and softmax denominators (fp32 scores, fp32 accumulation)"):
            for s in range(NS):
                for h in range(8):
                    S = sets.next(); stok = S["tok"]; vtok = S["vtok"]
                    c0s = s * LP
                    P.q_a.dma(S["qn"][:], self.QT[h, 0:128, c0s:c0s + LP], outs=[stok])
                    P.q_a.dma(S["qr"][:], self.QT[h, 128:192, c0s:c0s + LP], outs=[stok])
                    P.q_a.dma(S["kn"][:], self.KT[h, 0:128, c0s:c0s + LP], outs=[stok])
                    P.q_a.dma(S["kr"][:], self.KT[h, 128:192, c0s:c0s + LP], outs=[stok])
                    P.q_a.dma(S["v"][:], self.Vd[c0s:c0s + LP, h * 128:(h + 1) * 128].rearrange("(b p) d -> p b d", p=128), outs=[stok])
                    P.act.op(lambda: nc.scalar.copy(out=S["vb"][:], in_=S["v"][:]), outs=[vtok], ins=[stok])
                    for q0 in range(0, LP, 512):
                        Tq = min(512, LP - q0)
                        nkb = (q0 + Tq) // 128
                        po, potok = ps_o.next(); pd, pdtok = ps_d.next()
                        def scores(kb):
                            c0 = max(q0, kb * 128); off = c0 - q0; w = q0 + Tq - c0
                            ps, pstok = ps_s.next()
                            ksl = slice(kb * 128, (kb + 1) * 128)
                            P.pe.op(lambda: nc.tensor.matmul(ps[:, :w], S["kn"][:, ksl], S["qn"][:, c0:c0 + w], start=True, stop=False),
                                    outs=[pstok], ins=[stok])
                            P.pe.op(lambda: nc.tensor.matmul(ps[:, :w], S["kr"][:, ksl], S["qr"][:, c0:c0 + w], start=False, stop=True),
                                    outs=[pstok], ins=[stok])
                            pt, pttok = pt_r.next()
                            P.act.op(lambda: nc.scalar.activation(out=pt[:, :w], in_=ps[:, :w], func=AF.Exp, scale=scale), outs=[pttok], ins=[pstok])
                            if kb * 128 >= q0:
                                P.dve.op(lambda: nc.vector.tensor_tensor(out=pt[:, 0:128], in0=pt[:, 0:128], in1=maskT[:], op=ALU.mult),
                                         outs=[pttok], ins=[pttok, t_mask])
                            return pt, pttok, off, w
                        nxt = scores(0)
                        for kb in range(nkb):
                            pt, pttok, off, w = nxt
                            if kb + 1 < nkb:
                                nxt = scores(kb + 1)
                            P.pe.op(lambda: nc.tensor.matmul(po[:, off:off + w], S["vb"][:, kb, :], pt[:, :w], start=(kb == 0), stop=(kb == nkb - 1)),
                                    outs=[potok], ins=[vtok, pttok])
                            P.pe.op(lambda: nc.tensor.matmul(pd[:, off:off + w], onesb[:], pt[:, :w], start=(kb == 0), stop=(kb == nkb - 1)),
                                    outs=[pdtok], ins=[t_ob, pttok])
                        P.dve.op(lambda: nc.vector.reciprocal(out=rden[:, :Tq], in_=pd[:, :Tq]), outs=[rdtok], ins=[pdtok])
                        yo, yotok = yo_r.next()
                        P.dve.op(lambda: nc.vector.tensor_tensor(out=yo[:, :Tq], in0=po[:, :Tq], in1=rden[:, :Tq], op=ALU.mult),
                                 outs=[yotok], ins=[potok, rdtok])
                        P.q_a.dma(self.yE[4 + h, :, c0s + q0:c0s + q0 + Tq], yo[:, :Tq], ins=[yotok])
        P.release(m)

    def even_s5(self, l):
        P = self.P; nc = P.nc; i = l // 2
        NS, LP, NT = self.NS, self.LP, self.NT
        m = P.mark()
        PI = math.pi
        dve = P.dve; act = P.act; pe = P.pe
        V = nc.vector
        tk = Tok("s5setup")
        def sbt(shape, name):
            return P.sb(shape, name)
        ps0, ps0tok = P.psb[0]
        raw = sbt([16, 3, 128], "s5raw")
        P.q_a.dma(raw[:, 0, :], self.w["s5_a_re"][i].rearrange("(gp gl) p -> gp (gl p)", gl=2), outs=[tk])
        P.q_a.dma(raw[:, 1, :], self.w["s5_a_im"][i].rearrange("(gp gl) p -> gp (gl p)", gl=2), outs=[tk])
        ldt = sbt([16, 2], "ldt")
        P.q_a.dma(ldt[:], self.w["s5_log_dt"][i].rearrange("(gp gl) -> gp gl", gl=2), outs=[tk])
        dve.op(lambda: V.tensor_copy(out=raw[:, 2, :].rearrange("g (a b) -> g a b", a=2), in_=ldt[:].unsqueeze(2).to_broadcast([16, 2, 64])),
               outs=[tk], ins=[tk])
        ar = sbt([128, 16], "ar"); ai = sbt([128, 16], "ai"); dt = sbt([128, 16], "dt")
        for k, dst in enumerate((ar, ai, dt)):
            pe.op(lambda: nc.tensor.transpose(ps0[:, k * 16:(k + 1) * 16], raw[:, k, :], self.ident[0:16, 0:16]), outs=[ps0tok], ins=[tk, self.t_ident])
        act.op(lambda: nc.scalar.copy(out=ar[:], in_=ps0[:, 0:16]), outs=[tk], ins=[ps0tok])
        act.op(lambda: nc.scalar.copy(out=ai[:], in_=ps0[:, 16:32]), outs=[tk], ins=[ps0tok])
        act.op(lambda: nc.scalar.activation(out=dt[:], in_=ps0[:, 32:48], func=AF.Exp), outs=[tk], ins=[ps0tok])
        def T16(name):
            return sbt([128, 16], name)
        def tt(out, a, b, op):
            dve.op(lambda: V.tensor_tensor(out=out, in0=a, in1=b, op=op), outs=[tk], ins=[tk])
        def ts(out, a, s1, op0, s2=None, op1=None):
            if op1 is None:
                dve.op(lambda: V.tensor_scalar(out, a, s1, None, op0), outs=[tk], ins=[tk])
            else:
                dve.op(lambda: V.tensor_scalar(out, a, s1, s2, op0, op1), outs=[tk], ins=[tk])
        mag = T16("mag"); ang = T16("ang"); sn = T16("sn"); cs = T16("cs"); tmpa = T16("tmpa"); tmpb = T16("tmpb")
        twopi = T16("twopi")
        dve.op(lambda: V.memset(twopi[:], 2 * PI), outs=[tk], ins=[tk])
        tt(tmpa[:], dt[:], ar[:], ALU.mult)
        act.op(lambda: nc.scalar.activation(out=mag[:], in_=tmpa[:], func=AF.Exp), outs=[tk], ins=[tk])
        tt(ang[:], dt[:], ai[:], ALU.mult)
        for (dst, shift) in ((sn, PI), (cs, PI + PI / 2)):
            ts(tmpa[:], ang[:], shift, ALU.add)
            for mult_ in (16.0, 8.0, 4.0, 2.0, 1.0):
                ts(tmpb[:], tmpa[:], mult_ * 2 * PI, ALU.is_ge, -mult_ * 2 * PI, ALU.mult)
                tt(tmpa[:], tmpa[:], tmpb[:], ALU.add)
            ts(tmpa[:], tmpa[:], -PI, ALU.add, 3.1415925, ALU.min)
            ts(tmpa[:], tmpa[:], -3.1415925, ALU.max)
            act.op(lambda: nc.scalar.activation(out=dst[:], in_=tmpa[:], func=AF.Sin), outs=[tk], ins=[tk])
        lr = T16("lr"); li = T16("li"); lr1 = T16("lr1"); den = T16("den"); cr = T16("cr"); ci = T16("ci")
        tt(lr[:], mag[:], cs[:], ALU.mult); tt(li[:], mag[:], sn[:], ALU.mult)
        ts(lr1[:], lr[:], -1.0, ALU.add)
        tt(den[:], ar[:], ar[:], ALU.mult); tt(tmpa[:], ai[:], ai[:], ALU.mult); tt(den[:], den[:], tmpa[:], ALU.add)
        dve.op(lambda: V.reciprocal(out=den[:], in_=den[:]), outs=[tk], ins=[tk])
        tt(cr[:], lr1[:], ar[:], ALU.mult); tt(tmpa[:], li[:], ai[:], ALU.mult); tt(cr[:], cr[:], tmpa[:], ALU.add); tt(cr[:], cr[:], den[:], ALU.mult)
        tt(ci[:], li[:], ar[:], ALU.mult); tt(tmpa[:], lr1[:], ai[:], ALU.mult); tt(ci[:], ci[:], tmpa[:], ALU.subtract); tt(ci[:], ci[:], den[:], ALU.mult)
        br = sbt([128, 16, 16], "br"); bi = sbt([128, 16, 16], "bi")
        P.q_a.dma(br[:], self.w["s5_b_re"][i].rearrange("(gp gl) p c -> (gl p) gp c", gl=2), outs=[tk], ins=[tk])
        P.q_a.dma(bi[:], self.w["s5_b_im"][i].rearrange("(gp gl) p c -> (gl p) gp c", gl=2), outs=[tk], ins=[tk])
        t3a = sbt([128, 16, 16], "t3a"); t3b = sbt([128, 16, 16], "t3b")
        Bblk = [sbt([128, 16, 32], "Bblk_r"), sbt([128, 16, 32], "Bblk_i")]
        crb = cr[:].unsqueeze(2).to_broadcast([128, 16, 16]); cib = ci[:].unsqueeze(2).to_broadcast([128, 16, 16])
        for k in range(2):
            dve.op(lambda: V.memset(Bblk[k][:], 0.0), outs=[tk], ins=[tk])
        for k, (x0, x1, op) in enumerate(((br, bi, ALU.subtract), (bi, br, ALU.add))):
            tt(t3a[:], x0[:], crb, ALU.mult); tt(t3b[:], x1[:], cib, ALU.mult); tt(t3a[:], t3a[:], t3b[:], op)
            dve.op(lambda: V.tensor_copy(out=Bblk[k][0:64, :, 0:16], in_=t3a[0:64]), outs=[tk], ins=[tk])
            dve.op(lambda: V.tensor_copy(out=Bblk[k][64:128, :, 16:32], in_=t3a[64:128]), outs=[tk], ins=[tk])
        WbT = [sbt([32, 16, 128], "WbT_r"), sbt([32, 16, 128], "WbT_i")]
        for k in range(2):
            for g4 in range(4):
                for gq in range(4):
                    gp = g4 * 4 + gq
                    pe.op(lambda: nc.tensor.transpose(ps0[0:32, gq * 128:(gq + 1) * 128], Bblk[k][:, gp, :], self.ident[:]), outs=[ps0tok], ins=[tk, self.t_ident])
                act.op(lambda: nc.scalar.copy(out=WbT[k][:, g4 * 4:(g4 + 1) * 4, :], in_=ps0[0:32, :].rearrange("p (g s) -> p g s", g=4)), outs=[tk], ins=[ps0tok])
        Cblk = [sbt([128, 16, 32], "Cblk_r"), sbt([128, 16, 32], "Cblk_i")]
        craw = sbt([128, 128], "craw")
        for k, nm in enumerate(("s5_c_re", "s5_c_im")):
            dve.op(lambda: V.memset(Cblk[k][:], 0.0), outs=[tk], ins=[tk])
            cv = self.w[nm][i].rearrange("g c p -> (g c) p")
            for q4 in range(4):
                P.q_a.dma(craw[:, 0:64], cv[q4 * 128:(q4 + 1) * 128, :], outs=[tk], ins=[tk])
                P.q_a.dma(craw[:, 64:128], cv[q4 * 128:(q4 + 1) * 128, :], outs=[tk], ins=[tk])
                pe.op(lambda: nc.tensor.transpose(ps0[:, 0:128], craw[:], self.ident[:]), outs=[ps0tok], ins=[tk, self.t_ident])
                psv = ps0[:, 0:128].rearrange("p (gq gl c) -> p gq gl c", gq=4, gl=2)
                sc = 1.0 if k == 0 else -1.0
                act.op(lambda: nc.scalar.mul(out=Cblk[k][0:64, q4 * 4:(q4 + 1) * 4, 0:16], in_=psv[0:64, :, 0, :], mul=sc), outs=[tk], ins=[ps0tok])
                act.op(lambda: nc.scalar.mul(out=Cblk[k][64:128, q4 * 4:(q4 + 1) * 4, 16:32], in_=psv[64:128, :, 1, :], mul=sc), outs=[tk], ins=[ps0tok])
        dsk = sbt([32, 16], "dsk")
        P.q_a.dma(dsk[:], self.w["s5_d"][i].rearrange("(gp r) -> r gp", r=32), outs=[tk], ins=[tk], allow_slow_non_contiguous=True)
        NK = max(1, (LP - 1).bit_length())
        wr = sbt([128, NK, 16], "wr"); wi = sbt([128, NK, 16], "wi")
        dve.op(lambda: V.tensor_copy(out=wr[:, 0, :], in_=cs[:]), outs=[tk], ins=[tk])
        ts(wi[:, 0, :], sn[:], -1.0, ALU.mult)
        for k in range(1, NK):
            tt(tmpa[:], wr[:, k - 1, :], wr[:, k - 1, :], ALU.mult); tt(tmpb[:], wi[:, k - 1, :], wi[:, k - 1, :], ALU.mult)
            tt(wr[:, k, :], tmpa[:], tmpb[:], ALU.subtract)
            tt(tmpa[:], wr[:, k - 1, :], wi[:, k - 1, :], ALU.mult)
            ts(wi[:, k, :], tmpa[:], 2.0, ALU.mult)
        Er = sbt([128, LP], "Er"); Ei = sbt([128, LP], "Ei"); tmpE = sbt([128, LP], "tmpE"); tE = Tok("E")
        u_r = Ring([(sbt([32, LP], "u_gp"), Tok()) for _ in range(2)])
        vr = sbt([128, LP], "vr"); vi = sbt([128, LP], "vi"); tv = Tok("v")
        zr = sbt([128, LP], "zr"); zi = sbt([128, LP], "zi"); tz = Tok("z")
        ta = sbt([128, 512], "ta"); tb_ = sbt([128, 512], "tb"); tta = Tok(); ttb = Tok()
        y_r = Ring([(sbt([32, LP], "y_gp"), Tok()) for _ in range(2)])
        ps_b = self.psring([1, 2, 3, 4]); ps_y = self.psring([5, 6])
        for gp in range(16):
            dve.op(lambda: V.memset(Er[:, 0:1], 1.0), outs=[tE], ins=[tk])
            dve.op(lambda: V.memset(Ei[:, 0:1], 0.0), outs=[tE], ins=[tk])
            n = 1; k = 0
            while n < LP:
                cnt = min(n, LP - n)
                wrk = wr[:, k, gp:gp + 1]; wik = wi[:, k, gp:gp + 1]
                dve.op(lambda: V.tensor_scalar(tmpE[:, 0:cnt], Ei[:, 0:cnt], wik, None, ALU.mult), outs=[tE], ins=[tE, tk])
                dve.op(lambda: V.scalar_tensor_tensor(out=Er[:, n:n + cnt], in0=Er[:, 0:cnt], scalar=wrk, in1=tmpE[:, 0:cnt], op0=ALU.mult, op1=ALU.subtract), outs=[tE], ins=[tE, tk])
                dve.op(lambda: V.tensor_scalar(tmpE[:, 0:cnt], Ei[:, 0:cnt], wrk, None, ALU.mult), outs=[tE], ins=[tE, tk])
                dve.op(lambda: V.scalar_tensor_tensor(out=Ei[:, n:n + cnt], in0=Er[:, 0:cnt], scalar=wik, in1=tmpE[:, 0:cnt], op0=ALU.mult, op1=ALU.add), outs=[tE], ins=[tE, tk])
                n += cnt; k += 1
            for s in range(NS):
                c0s = s * LP
                u, utok = u_r.next()
                P.q_a.dma(u[:], self.uT[gp // 4, (gp % 4) * 32:(gp % 4) * 32 + 32, c0s:c0s + LP], outs=[utok])
                for q0 in range(0, LP, 512):
                    Tq = min(512, LP - q0); sl = slice(q0, q0 + Tq)
                    pbr, pbrt = ps_b.next(); pbi, pbit = ps_b.next()
                    pe.op(lambda: nc.tensor.matmul(pbr[:, :Tq], WbT[0][:, gp, :], u[:, sl], start=True, stop=True), outs=[pbrt], ins=[tk, utok])
                    pe.op(lambda: nc.tensor.matmul(pbi[:, :Tq], WbT[1][:, gp, :], u[:, sl], start=True, stop=True), outs=[pbit], ins=[tk, utok])
                    dve.op(lambda: V.tensor_tensor(out=ta[:, :Tq], in0=pbr[:, :Tq], in1=Er[:, sl], op=ALU.mult), outs=[tta], ins=[pbrt, tE])
                    dve.op(lambda: V.tensor_tensor(out=tb_[:, :Tq], in0=pbi[:, :Tq], in1=Ei[:, sl], op=ALU.mult), outs=[ttb], ins=[pbit, tE])
                    dve.op(lambda: V.tensor_tensor(out=vr[:, sl], in0=ta[:, :Tq], in1=tb_[:, :Tq], op=ALU.subtract), outs=[tv], ins=[tta, ttb])
                    dve.op(lambda: V.tensor_tensor(out=ta[:, :Tq], in0=pbi[:, :Tq], in1=Er[:, sl], op=ALU.mult), outs=[tta], ins=[pbit, tE])
                    dve.op(lambda: V.tensor_tensor(out=tb_[:, :Tq], in0=pbr[:, :Tq], in1=Ei[:, sl], op=ALU.mult), outs=[ttb], ins=[pbrt, tE])
                    dve.op(lambda: V.tensor_tensor(out=vi[:, sl], in0=ta[:, :Tq], in1=tb_[:, :Tq], op=ALU.add), outs=[tv], ins=[tta, ttb])
                rho = mag[:, gp:gp + 1].to_broadcast([128, LP])
                dve.op(lambda: V.tensor_tensor_scan(out=zr[:], data0=rho, data1=vr[:], initial=0.0, op0=ALU.mult, op1=ALU.add), outs=[tz], ins=[tv, tk])
                dve.op(lambda: V.tensor_tensor_scan(out=zi[:], data0=rho, data1=vi[:], initial=0.0, op0=ALU.mult, op1=ALU.add), outs=[tz], ins=[tv, tk])
                dve.op(lambda: V.tensor_tensor(out=vr[:], in0=Er[:], in1=zr[:], op=ALU.mult), outs=[tv], ins=[tz, tE])
                dve.op(lambda: V.tensor_tensor(out=tmpE[:], in0=Ei[:], in1=zi[:], op=ALU.mult), outs=[tv], ins=[tz, tE])
                dve.op(lambda: V.tensor_tensor(out=vr[:], in0=vr[:], in1=tmpE[:], op=ALU.add), outs=[tv], ins=[tv])
                dve.op(lambda: V.tensor_tensor(out=vi[:], in0=Er[:], in1=zi[:], op=ALU.mult), outs=[tv], ins=[tz, tE])
                dve.op(lambda: V.tensor_tensor(out=tmpE[:], in0=Ei[:], in1=zr[:], op=ALU.mult), outs=[tv], ins=[tz, tE])
                dve.op(lambda: V.tensor_tensor(out=vi[:], in0=vi[:], in1=tmpE[:], op=ALU.subtract), outs=[tv], ins=[tv])
                y, ytok = y_r.next()
                for q0 in range(0, LP, 512):
                    Tq = min(512, LP - q0); sl = slice(q0, q0 + Tq)
                    py, pyt = ps_y.next()
                    pe.op(lambda: nc.tensor.matmul(py[0:32, :Tq], Cblk[0][:, gp, :], vr[:, sl], start=True, stop=False), outs=[pyt], ins=[tk, tv])
                    pe.op(lambda: nc.tensor.matmul(py[0:32, :Tq], Cblk[1][:, gp, :], vi[:, sl], start=False, stop=True), outs=[pyt], ins=[tk, tv])
                    dve.op(lambda: V.scalar_tensor_tensor(out=y[:, sl], in0=u[:, sl], scalar=dsk[:, gp:gp + 1], in1=py[0:32, :Tq], op0=ALU.mult, op1=ALU.add),
                           outs=[ytok], ins=[pyt, utok, tk])
                P.q_a.dma(self.yE[gp // 4, (gp % 4) * 32:(gp % 4) * 32 + 32, c0s:c0s + LP], y[:], ins=[ytok])
        P.release(m)

    def out_proj(self, Wd, NCH, ymix, ytoks, T, tok0, htoks, wst_r, w_r, psd_r, hres_r, ymb, ymbtok):
        P = self.P; nc = P.nc
        Wv = Wd.rearrange("(c p) d -> p c d", p=128)
        hc = NCH // 2
        P.dve.op(lambda: nc.vector.tensor_copy(out=ymb[:, 0:hc, :T], in_=ymix[:, 0:hc, :T]), outs=[ymbtok], ins=ytoks)
        P.act.op(lambda: nc.scalar.copy(out=ymb[:, hc:NCH, :T], in_=ymix[:, hc:NCH, :T]), outs=[ymbtok], ins=ytoks + [ymbtok])
        def load_d(dc):
            ds_, dst_ = wst_r.next()
            P.q_w.dma(ds_[:, :NCH, :], Wv[:, :, dc * 128:(dc + 1) * 128], outs=[dst_])
            d_, dt_ = w_r.next()
            if dc % 2 == 0:
                P.act.op(lambda: nc.scalar.copy(out=d_[:, :NCH, :], in_=ds_[:, :NCH, :]), outs=[dt_], ins=[dst_])
            else:
                P.dve.op(lambda: nc.vector.tensor_copy(out=d_[:, :NCH, :], in_=ds_[:, :NCH, :]), outs=[dt_], ins=[dst_])
            return d_, dt_
        with nc.allow_low_precision("bf16 matmul operands, fp32 accumulation"):
            nxt = load_d(0)
            for dc in range(DC):
                d_, dt_ = nxt
                if dc + 1 < DC:
                    nxt = load_d(dc + 1)
                hres, hrt = hres_r.next()
                P.q_a.dma(hres[:, :T], self.hT[dc, :, tok0:tok0 + T], outs=[hrt], ins=htoks)
                psd, psdt = psd_r.next()
                for c in range(NCH):
                    P.pe.op(lambda: nc.tensor.matmul(psd[:, :T], d_[:, c, :], ymb[:, c, :T], start=(c == 0), stop=(c == NCH - 1)),
                            outs=[psdt], ins=[dt_, ymbtok])
                P.dve.op(lambda: nc.vector.tensor_tensor(out=hres[:, :T], in0=psd[:, :T], in1=hres[:, :T], op=ALU.add),
                         outs=[hrt], ins=[psdt, hrt])
                P.q_a.dma(self.hT[dc, :, tok0:tok0 + T], hres[:, :T], ins=[hrt], outs=htoks)

    def even_out(self, l):
        P = self.P; nc = P.nc; i = l // 2
        NS, LP, NT, NTOK = self.NS, self.LP, self.NT, self.NTOK
        m = P.mark()
        TT = 512
        V = nc.vector
        wglu = P.sb([128, 4, 512], "wglu"); t_wglu = Tok()
        P.q_w.dma(wglu[:], self.w["s5_w_glu"][i].rearrange("(c p) f -> p c f", p=128), outs=[t_wglu])
        ya, yatok = P.sb([128, 4, TT], "ya"), Tok()
        yg, ygtok = P.sb([128, 4, TT], "yg"), Tok()
        ymix = P.sb([128, 12, TT], "ymix"); ymtok = Tok(); ybtok = Tok()
        sg_r = Ring([(P.sb([128, TT], "sg"), Tok()) for _ in range(2)])
        wst_r = Ring([(P.sb([128, 12, 128], "wost"), Tok()) for _ in range(2)])
        w_r = Ring([(P.sbh([128, 12, 128], "wo"), Tok()) for _ in range(2)])
        ymb = P.sbh([128, 12, TT], "ymb"); ymbtok = Tok()
        hres_r = Ring([(P.sb([128, TT], "hres"), Tok()) for _ in range(2)])
        psg_r = self.psring([0, 1]); psd_r = self.psring([2, 3])
        C0 = math.sqrt(2.0 / math.pi)
        for t0 in range(0, NTOK, TT):
            T = min(TT, NTOK - t0)
            htoks = self.toks_h(t0, T)
            P.q_a.dma(ya[:, :, :T], self.yE[0:4, :, t0:t0 + T].rearrange("c p t -> p c t"), outs=[yatok])
            P.q_a.dma(ymix[:, 4:12, :T], self.yE[4:12, :, t0:t0 + T].rearrange("c p t -> p c t"), outs=[ybtok])
            P.dve.op(lambda: V.tensor_tensor(out=yg[:, :, :T], in0=ya[:, :, :T], in1=ya[:, :, :T], op=ALU.mult), outs=[ygtok], ins=[yatok])
            P.dve.op(lambda: V.tensor_scalar(yg[:, :, :T], yg[:, :, :T], 0.044715, 1.0, ALU.mult, ALU.add), outs=[ygtok], ins=[ygtok])
            P.dve.op(lambda: V.tensor_tensor(out=yg[:, :, :T], in0=yg[:, :, :T], in1=ya[:, :, :T], op=ALU.mult), outs=[ygtok], ins=[ygtok, yatok])
            P.act.op(lambda: nc.scalar.activation(out=yg[:, :, :T], in_=yg[:, :, :T], func=AF.Tanh, scale=C0), outs=[ygtok], ins=[ygtok])
            P.dve.op(lambda: V.tensor_scalar(yg[:, :, :T], yg[:, :, :T], 1.0, 0.5, ALU.add, ALU.mult), outs=[ygtok], ins=[ygtok])
            P.dve.op(lambda: V.tensor_tensor(out=yg[:, :, :T], in0=yg[:, :, :T], in1=ya[:, :, :T], op=ALU.mult), outs=[ygtok], ins=[ygtok, yatok])
            for jc in range(4):
                ps, ptok = psg_r.next()
                for c in range(4):
                    P.pe.op(lambda: nc.tensor.matmul(ps[:, :T], wglu[:, c, jc * 128:(jc + 1) * 128], yg[:, c, :T], start=(c == 0), stop=(c == 3)),
                            outs=[ptok], ins=[t_wglu, ygtok])
                sg, sgt = sg_r.next()
                P.act.op(lambda: nc.scalar.activation(out=sg[:, :T], in_=ps[:, :T], func=AF.Sigmoid), outs=[sgt], ins=[ptok])
                P.dve.op(lambda: V.tensor_tensor(out=ymix[:, jc, :T], in0=yg[:, jc, :T], in1=sg[:, :T], op=ALU.mult), outs=[ymtok], ins=[ygtok, sgt])
            self.out_proj(self.w["ev_w_out"][i], 12, ymix, [ymtok, ybtok], T, t0, htoks, wst_r, w_r, psd_r, hres_r, ymb, ymbtok)
        P.release(m)


    def odd_mixer(self, l):
        import os
        st = os.environ.get("DBG_STAGES", "inproj,attn,gdn,out").split(",")
        if "inproj" in st:
            self.odd_inproj(l)
        if "attn" in st:
            self.odd_sb(l)
        if "gdn" in st:
            self.odd_gdn(l)
        if "out" in st:
            self.odd_out(l)

    def odd_inproj(self, l):
        P = self.P; nc = P.nc; i = l // 2
        NS, LP, NT = self.NS, self.LP, self.NT
        m = P.mark()
        TT = 512
        Wv = self.w["od_w_in"][i].rearrange("(c p) f -> p c f", p=128)
        gain, gtok = self.gains["norm_mix"]
        xt, xtok = P.sb([128, DC, TT], "odx"), Tok()
        xb, xbtok = P.sbh([128, DC, TT], "odxb"), Tok()
        wst_r = Ring([(P.sb([128, DC, 128], "wst"), Tok()) for _ in range(3)])
        win_r = Ring([(P.sbh([128, DC, 128], "win"), Tok()) for _ in range(3)])
        wvst, wvstt = P.sb([128, DC, 512], "wsvst"), Tok()
        wv, wvt = P.sbh([128, DC, 512], "wsv"), Tok()
        wab = P.sb([128, DC, 16], "wab"); t_wab = Tok()
        P.q_w.dma(wab[:], Wv[:, :, 6144:6160], outs=[t_wab])
        ob_r = Ring([(P.sb([128, TT], "ob"), Tok()) for _ in range(3)])
        rstd, rtok = P.sb([128, TT], "rstd"), Tok()
        sq, sqtok = P.sb([128, TT], "sq"), Tok()
        ps01 = self.psring([0, 1, 2]); psv_r = self.psring([3, 4])
        pss, psstok = P.psb[7]
        plan = []
        for h in range(8):
            plan.append((self.sbQ, h, h * 128))
        for h in range(8):
            plan.append((self.sbK, h, 1024 + h * 128))
        for c in range(24):
            plan.append((self.gT, c, 3072 + c * 128))
        for h in range(8):
            plan.append((self.gzT, h, 6160 + h * 128))
        with nc.allow_low_precision("bf16 matmul operands, fp32 accumulation"):
            for s in range(NS):
                for t0 in range(0, LP, TT):
                    T = min(TT, LP - t0)
                    tok0 = s * LP + t0
                    htoks = self.toks_h(tok0, T)
                    P.q_a.dma(xt[:, :, :T], self.hT[:, :, tok0:tok0 + T].rearrange("c p t -> p c t"), outs=[xtok], ins=htoks)
                    self.rms_scale(xt, xtok, T, lambda c: gain[:, l, c:c + 1], gtok, pss, psstok, sq, sqtok, rstd, rtok)
                    P.dve.op(lambda: nc.vector.tensor_copy(out=xb[:, 0:8, :T], in_=xt[:, 0:8, :T]), outs=[xbtok], ins=[xtok])
                    P.act.op(lambda: nc.scalar.copy(out=xb[:, 8:16, :T], in_=xt[:, 8:16, :T]), outs=[xbtok], ins=[xtok])
                    def load_w(j):
                        ws_, wst_ = wst_r.next()
                        c0 = plan[j][2]
                        P.q_w.dma(ws_[:], Wv[:, :, c0:c0 + 128], outs=[wst_])
                        w_, wt_ = win_r.next()
                        if j % 2 == 0:
                            P.dve.op(lambda: nc.vector.tensor_copy(out=w_[:], in_=ws_[:]), outs=[wt_], ins=[wst_])
                        else:
                            P.act.op(lambda: nc.scalar.copy(out=w_[:], in_=ws_[:]), outs=[wt_], ins=[wst_])
                        return w_, wt_
                    pend = [load_w(0), load_w(1)]
                    for j in range(len(plan)):
                        w_, wt_ = pend.pop(0)
                        if j + 2 < len(plan):
                            pend.append(load_w(j + 2))
                        ps, ptok = ps01.next()
                        for c in range(DC):
                            P.pe.op(lambda: nc.tensor.matmul(ps[:, :T], w_[:, c, :], xb[:, c, :T], start=(c == 0), stop=(c == DC - 1)),
                                    outs=[ptok], ins=[wt_, xbtok])
                        ob, obtok = ob_r.next()
                        if j % 2 == 0:
                            P.act.op(lambda: nc.scalar.copy(out=ob[:, :T], in_=ps[:, :T]), outs=[obtok], ins=[ptok])
                        else:
                            P.dve.op(lambda: nc.vector.tensor_copy(out=ob[:, :T], in_=ps[:, :T]), outs=[obtok], ins=[ptok])
                        P.q_a.dma(plan[j][0][plan[j][1], :, tok0:tok0 + T], ob[:, :T], ins=[obtok])
                    ps, ptok = ps01.next()
                    for c in range(DC):
                        P.pe.op(lambda: nc.tensor.matmul(ps[0:16, :T], wab[:, c, :], xt[:, c, :T], start=(c == 0), stop=(c == DC - 1)),
                                outs=[ptok], ins=[t_wab, xtok])
                    ob, obtok = ob_r.next()
                    P.act.op(lambda: nc.scalar.copy(out=ob[0:16, :T], in_=ps[0:16, :T]), outs=[obtok], ins=[ptok])
                    P.q_a.dma(self.gabT[:, tok0:tok0 + T], ob[0:16, :T], ins=[obtok])
                    for n in range(2):
                        P.q_w.dma(wvst[:], Wv[:, :, 2048 + n * 512:2048 + (n + 1) * 512], outs=[wvstt])
                        P.act.op(lambda: nc.scalar.copy(out=wv[:, 0:8, :], in_=wvst[:, 0:8, :]), outs=[wvt], ins=[wvstt])
                        P.dve.op(lambda: nc.vector.tensor_copy(out=wv[:, 8:16, :], in_=wvst[:, 8:16, :]), outs=[wvt], ins=[wvstt, wvt])
                        for tb in range(T // 128):
                            ps, ptok = psv_r.next()
                            for c in range(DC):
                                P.pe.op(lambda: nc.tensor.matmul(ps[:, :], xb[:, c, tb * 128:(tb + 1) * 128], wv[:, c, :], start=(c == 0), stop=(c == DC - 1)),
                                        outs=[ptok], ins=[wvt, xbtok])
                            ob, obtok = ob_r.next()
                            P.act.op(lambda: nc.scalar.copy(out=ob[:, :], in_=ps[:, :]), outs=[obtok], ins=[ptok])
                            P.q_a.dma(self.sbV[tok0 + tb * 128:tok0 + (tb + 1) * 128, n * 512:(n + 1) * 512], ob[:, :], ins=[obtok])
        P.release(m)

    def load_const(self, dram_ap, shape, name):
        t = self.P.sb(shape, name); tok = Tok(name)
        self.P.q_a.dma(t[:], dram_ap, outs=[tok])
        return t, tok

    def odd_sb(self, l):
        P = self.P; nc = P.nc
        NS, LP, NT = self.NS, self.LP, self.NT
        m = P.mark()
        V = nc.vector
        maskS, t_ms = self.load_const(self.c_maskS[:, :], [128, 128], "maskS")
        Uincl, t_ui = self.load_const(self.c_maskL[:, :], [128, 128], "Uincl")
        zerob = P.sbh([128, 128], "zerob"); t_z = Tok()
        P.dve.op(lambda: V.memset(zerob[:], 0.0), outs=[t_z])
        sets = Ring([dict(q=P.sb([128, LP]), k=P.sb([128, LP]), nk=P.sb([128, LP]), v=P.sb([128, NT, 128]), vb=P.sbh([128, NT, 128]),
                          qb=P.sbh([128, 512]), tok=Tok(), vtok=Tok()) for _ in range(2)])
        ez_r = Ring([(P.sb([128, 512], "ez"), Tok()) for _ in range(2)])
        sp_r = Ring([(P.sb([128, 512], "sp"), Tok()) for _ in range(2)])
        wt_r = Ring([(P.sbh([128, 512], "wt"), Tok()) for _ in range(2)])
        A, Atok = P.sb([128, 512], "A"), Tok()
        yo_r = Ring([(P.sb([128, 512], "yo"), Tok()) for _ in range(2)])
        ps_z = self.psring([0, 1]); ps_e = self.psring([2, 3]); ps_o = self.psring([4, 5])
        scale = 128.0 ** -0.5
        with nc.allow_low_precision("bf16 weights@V (fp32 scores and log-sums, fp32 accumulation)"):
            for s in range(NS):
                for h in range(8):
                    S = sets.next(); stok = S["tok"]; vtok = S["vtok"]
                    c0s = s * LP
                    P.q_a.dma(S["q"][:], self.sbQ[h, :, c0s:c0s + LP], outs=[stok])
                    P.q_a.dma(S["k"][:], self.sbK[h, :, c0s:c0s + LP], outs=[stok])
                    P.q_a.dma(S["v"][:], self.sbV[c0s:c0s + LP, h * 128:(h + 1) * 128].rearrange("(b p) d -> p b d", p=128), outs=[stok])
                    P.act.op(lambda: nc.scalar.mul(out=S["q"][:], in_=S["q"][:], mul=scale), outs=[stok], ins=[stok])
                    P.act.op(lambda: nc.scalar.mul(out=S["nk"][:], in_=S["k"][:], mul=-1.0), outs=[stok], ins=[stok])
                    P.dve.op(lambda: V.tensor_copy(out=S["vb"][:], in_=S["v"][:]), outs=[vtok], ins=[stok])
                    P.dve.op(lambda: V.memset(S["qb"][:], 0.0), outs=[vtok], ins=[vtok])
                    for q0 in range(0, LP, 512):
                        Tq = min(512, LP - q0)
                        nkb = (q0 + Tq) // 128
                        po, potok = ps_o.next()
                        P.pe.op(lambda: nc.tensor.matmul(po[:, :Tq], zerob[:], S["qb"][:, :Tq], start=True, stop=False),
                                outs=[potok], ins=[t_z, vtok])
                        P.dve.op(lambda: V.memset(A[:, :Tq], 0.0), outs=[Atok])
                        kbs = list(range(nkb - 1, -1, -1))
                        def geom(kb):
                            c0 = max(q0, kb * 128)
                            return c0, c0 - q0, q0 + Tq - c0, (kb * 128 >= q0), slice(kb * 128, (kb + 1) * 128)
                        def stage1(kb):
                            c0, off, w, diag, ksl = geom(kb)
                            pz, pztok = ps_z.next()
                            P.pe.op(lambda: nc.tensor.matmul(pz[:, :w], S["k"][:, ksl], S["q"][:, c0:c0 + w], start=True, stop=True),
                                    outs=[pztok], ins=[stok])
                            ez, eztok = ez_r.next()
                            P.act.op(lambda: nc.scalar.activation(out=ez[:, :w], in_=pz[:, :w], func=AF.Exp), outs=[eztok], ins=[pztok])
                            sp, sptok = sp_r.next()
                            P.act.op(lambda: nc.scalar.activation(out=sp[:, :w], in_=ez[:, :w], func=AF.Ln, bias=self.ones[:, 0:1]),
                                     outs=[sptok], ins=[eztok, self.t_ones])
                            if diag:
                                P.dve.op(lambda: V.tensor_tensor(out=sp[:, 0:128], in0=sp[:, 0:128], in1=maskS[:], op=ALU.mult),
                                         outs=[sptok], ins=[sptok, t_ms])
                            return sp, sptok
                        def stage2(kb, sp, sptok, first):
                            c0, off, w, diag, ksl = geom(kb)
                            pe_, petok = ps_e.next()
                            P.pe.op(lambda: nc.tensor.matmul(pe_[:, :w], Uincl[:], sp[:, :w], start=True, stop=False), outs=[petok], ins=[t_ui, sptok])
                            if not first:
                                P.pe.op(lambda: nc.tensor.matmul(pe_[:, :w], self.ones[:], A[:, off:off + w], start=False, stop=False),
                                        outs=[petok], ins=[self.t_ones, Atok])
                            P.pe.op(lambda: nc.tensor.matmul(pe_[:, :w], S["nk"][:, ksl], S["q"][:, c0:c0 + w], start=False, stop=True),
                                    outs=[petok], ins=[stok])
                            wt, wttok = wt_r.next()
                            P.act.op(lambda: nc.scalar.activation(out=wt[:, :w], in_=pe_[:, :w], func=AF.Exp, scale=-1.0), outs=[wttok], ins=[petok])
                            if diag:
                                P.dve.op(lambda: V.tensor_tensor(out=wt[:, 0:128], in0=wt[:, 0:128], in1=maskS[:], op=ALU.mult),
                                         outs=[wttok], ins=[wttok, t_ms])
                            if kb > 0:
                                P.dve.op(lambda: V.tensor_tensor(out=A[:, off:off + w], in0=A[:, off:off + w], in1=sp[:, :w], op=ALU.add),
                                         outs=[Atok], ins=[Atok, sptok])
                            return wt, wttok
                        def stage3(kb, wt, wttok):
                            c0, off, w, diag, ksl = geom(kb)
                            P.pe.op(lambda: nc.tensor.matmul(po[:, off:off + w], S["vb"][:, kb, :], wt[:, :w], start=False, stop=(kb == 0)),
                                    outs=[potok], ins=[vtok, wttok])
                        s1 = stage1(kbs[0])
                        pend3 = None
                        for idx, kb in enumerate(kbs):
                            cur1 = s1
                            if idx + 1 < len(kbs):
                                s1 = stage1(kbs[idx + 1])
                            w2 = stage2(kb, cur1[0], cur1[1], idx == 0)
                            if pend3 is not None:
                                stage3(*pend3)
                            pend3 = (kb, w2[0], w2[1])
                        stage3(*pend3)
                        yo, yotok = yo_r.next()
                        P.act.op(lambda: nc.scalar.copy(out=yo[:, :Tq], in_=po[:, :Tq]), outs=[yotok], ins=[potok])
                        P.q_a.dma(self.yO[h, :, c0s + q0:c0s + q0 + Tq], yo[:, :Tq], ins=[yotok])
        P.release(m)

    def odd_out(self, l):
        P = self.P; nc = P.nc; i = l // 2
        NTOK = self.NTOK
        m = P.mark()
        TT = 512
        ymix = P.sb([128, 16, TT], "ymix"); ytok = Tok()
        wst_r = Ring([(P.sb([128, 16, 128], "wost"), Tok()) for _ in range(2)])
        w_r = Ring([(P.sbh([128, 16, 128], "wo"), Tok()) for _ in range(2)])
        ymb = P.sbh([128, 16, TT], "ymb"); ymbtok = Tok()
        hres_r = Ring([(P.sb([128, TT], "hres"), Tok()) for _ in range(2)])
        psd_r = self.psring([2, 3])
        for t0 in range(0, NTOK, TT):
            T = min(TT, NTOK - t0)
            htoks = self.toks_h(t0, T)
            P.q_a.dma(ymix[:, :, :T], self.yO[:, :, t0:t0 + T].rearrange("c p t -> p c t"), outs=[ytok])
            self.out_proj(self.w["od_w_out"][i], 16, ymix, [ytok], T, t0, htoks, wst_r, w_r, psd_r, hres_r, ymb, ymbtok)
        P.release(m)
    def odd_gdn(self, l):
        P = self.P; nc = P.nc; i = l // 2
        NS, LP, NT = self.NS, self.LP, self.NT
        m = P.mark()
        V = nc.vector
        dve = P.dve; act = P.act; pe = P.pe
        maskI, t_mi = self.load_const(self.c_maskT[:, :], [128, 128], "maskI")
        maskSt, t_mst = self.load_const(self.c_maskS[:, :], [128, 128], "maskSt")
        sel, t_sel = self.load_const(self.c_sel[:, :], [8, 1024], "sel")
        tk = Tok("gdnsetup")
        cw = P.sb([128, 24, 4], "cw")
        for j in range(4):
            P.q_a.dma(cw[:, :, j], self.w["gdn_conv"][i, j].rearrange("(c p) -> p c", p=128), outs=[tk], allow_slow_non_contiguous=True)
        nA = P.sb([8, 1], "nA"); dtb = P.sb([8, 1], "dtb"); gout = P.sb([128, 1], "gout")
        P.q_a.dma(nA[:], self.w["gdn_a_log"][i].rearrange("(p o) -> p o", o=1), outs=[tk])
        P.q_a.dma(dtb[:], self.w["gdn_dt_bias"][i].rearrange("(p o) -> p o", o=1), outs=[tk])
        P.q_a.dma(gout[:], self.w["gdn_g_out"][i].rearrange("(p o) -> p o", o=1), outs=[tk])
        act.op(lambda: nc.scalar.activation(out=nA[:], in_=nA[:], func=AF.Exp), outs=[tk], ins=[tk])
        dve.op(lambda: V.tensor_scalar(nA[:], nA[:], -1.0, None, ALU.mult), outs=[tk], ins=[tk])
        def big(name):
            return P.sb([128, LP], name)
        g_ = big("g"); be = big("beta"); Gc = big("Gc"); tg = Tok("g")
        cols = P.sb([128, NT, 16], "cols"); tcols = Tok("cols")
        qT = big("qT"); kT = big("kT"); vT = big("vT"); qdT = big("qdT"); cv = big("cv")
        tq = Tok("q"); tkk = Tok("k"); tv = Tok("v"); tqd = Tok("qd"); tcv = Tok("cv")
        Grow = big("Grow"); eG = big("eG"); Brow = big("Brow"); tGr = Tok("Grow")
        k_tok = P.sb([128, NT, 128], "k_tok"); v_tok = P.sb([128, NT, 128], "v_tok"); tkv = [Tok("kvtok%d" % c) for c in range(NT)]
        u_tok = P.sb([128, NT, 128], "u_tok"); wT = P.sb([128, NT, 128], "wT"); attnT = P.sb([128, NT, 128], "attnT"); kd_tok = P.sb([128, NT, 128], "kd_tok")
        tpre = [Tok("pre%d" % c) for c in range(NT)]
        zT = big("zT"); tz = Tok("z")
        KI = 4
        slots = []
        for k_ in range(KI):
            R = {}
            for nm in ("dT", "dI", "aa", "Xu", "Xw"):
                R[nm] = (P.sb([128, 128], nm), Tok())
            R["sc"] = (P.sb([128, 4], "sc"), Tok())
            R["Mp"] = [P.sb([128, 128], "Mp") for _ in range(7)]; R["tM"] = [Tok() for _ in range(7)]
            R["Np"] = [P.sb([128, 128], "Np") for _ in range(6)]; R["tN"] = [Tok() for _ in range(6)]
            R["Y"] = [(P.sb([128, 128], "Y"), Tok()) for _ in range(2)]
            R["bA"] = P.psb[2 * k_]; R["bB"] = P.psb[2 * k_ + 1]
            slots.append(R)
        Sst = P.sb([128, 128], "S"); tS = Tok("S")
        vn = P.sb([128, 128], "vn"); tvn = Tok()
        rs = P.sb([128, 512], "rs"); trs = Tok()
        b0, tb0 = P.psb[0]; b1, tb1 = P.psb[1]; b2, tb2 = P.psb[2]; b3, tb3 = P.psb[3]
        b4, tb4 = P.psb[4]; b5, tb5 = P.psb[5]; b6, tb6 = P.psb[6]; b7, tb7 = P.psb[7]
        chunks = [(q0, min(512, LP - q0)) for q0 in range(0, LP, 512)]
        for s in range(NS):
            c0s = s * LP
            P.q_a.dma(g_[0:8, :], self.gabT[0:8, c0s:c0s + LP], outs=[tg])
            P.q_a.dma(be[0:8, :], self.gabT[8:16, c0s:c0s + LP], outs=[tg])
            dve.op(lambda: V.tensor_scalar(g_[0:8, :], g_[0:8, :], dtb[:, 0:1], None, ALU.add), outs=[tg], ins=[tg, tk])
            act.op(lambda: nc.scalar.activation(out=g_[0:8, :], in_=g_[0:8, :], func=AF.Exp), outs=[tg], ins=[tg])
            act.op(lambda: nc.scalar.activation(out=g_[0:8, :], in_=g_[0:8, :], func=AF.Ln, bias=self.ones[0:8, 0:1]), outs=[tg], ins=[tg, self.t_ones])
            dve.op(lambda: V.tensor_scalar(g_[0:8, :], g_[0:8, :], nA[:, 0:1], None, ALU.mult), outs=[tg], ins=[tg, tk])
            act.op(lambda: nc.scalar.activation(out=be[0:8, :], in_=be[0:8, :], func=AF.Sigmoid), outs=[tg], ins=[tg])
            for c in range(NT):
                csl = slice(c * 128, (c + 1) * 128)
                dve.op(lambda: V.tensor_tensor_scan(out=Gc[0:8, csl], data0=self.ones[0:8, 0:128], data1=g_[0:8, csl], initial=0.0,
                                                    op0=ALU.mult, op1=ALU.add), outs=[tg], ins=[tg, self.t_ones])
            for c in range(NT):
                csl = slice(c * 128, (c + 1) * 128)
                pe.op(lambda: nc.tensor.transpose(b0[:, c * 16:c * 16 + 8], Gc[0:8, csl], self.ident[0:8, 0:8]), outs=[tb0], ins=[tg, self.t_ident])
                pe.op(lambda: nc.tensor.transpose(b0[:, c * 16 + 8:c * 16 + 16], be[0:8, csl], self.ident[0:8, 0:8]), outs=[tb0], ins=[tg, self.t_ident])
            act.op(lambda: nc.scalar.copy(out=cols[:], in_=b0[:, 0:NT * 16].rearrange("p (c k) -> p c k", k=16)), outs=[tcols], ins=[tb0])
            for h in range(8):
                for part, (x, tx) in enumerate(((qT, tq), (kT, tkk), (vT, tv))):
                    ch = part * 8 + h
                    P.q_a.dma(x[:], self.gT[ch, :, c0s:c0s + LP], outs=[tx])
                    dve.op(lambda: V.tensor_scalar(cv[:], x[:], cw[:, ch, 3:4], None, ALU.mult), outs=[tcv], ins=[tx, tk])
                    for sh in (1, 2, 3):
                        dve.op(lambda: V.scalar_tensor_tensor(out=cv[:, sh:LP], in0=x[:, 0:LP - sh], scalar=cw[:, ch, 3 - sh:4 - sh], in1=cv[:, sh:LP],
                                                              op0=ALU.mult, op1=ALU.add), outs=[tcv], ins=[tcv, tx, tk])
                    act.op(lambda: nc.scalar.activation(out=x[:], in_=cv[:], func=AF.Silu), outs=[tx], ins=[tcv])
                P.q_a.dma(zT[:], self.gzT[h, :, c0s:c0s + LP], outs=[tz])
                for (x, tx, extra) in ((qT, tq, 128.0 ** -0.5), (kT, tkk, 1.0)):
                    act.op(lambda: nc.scalar.activation(out=cv[:], in_=x[:], func=AF.Square), outs=[tcv], ins=[tx])
                    for (q0, Tq) in chunks:
                        sl = slice(q0, q0 + Tq)
                        pe.op(lambda: nc.tensor.matmul(b0[:, :Tq], self.ones[:], cv[:, sl], start=True, stop=True), outs=[tb0], ins=[tcv, self.t_ones])
                        act.op(lambda: nc.scalar.activation(out=rs[:, :Tq], in_=b0[:, :Tq], func=AF.Sqrt, bias=self.epsb[:], scale=1.0), outs=[trs], ins=[tb0, self.t_eps])
                        dve.op(lambda: V.reciprocal(out=rs[:, :Tq], in_=rs[:, :Tq]), outs=[trs], ins=[trs])
                        dve.op(lambda: V.scalar_tensor_tensor(out=x[:, sl], in0=x[:, sl], scalar=extra, in1=rs[:, :Tq], op0=ALU.mult, op1=ALU.mult),
                               outs=[tx], ins=[tx, trs])
                for (q0, Tq) in chunks:
                    sl = slice(q0, q0 + Tq)
                    pe.op(lambda: nc.tensor.matmul(b0[:, :Tq], sel[:, h * 128:(h + 1) * 128], Gc[0:8, sl], start=True, stop=True), outs=[tb0], ins=[t_sel, tg])
                    act.op(lambda: nc.scalar.copy(out=Grow[:, sl], in_=b0[:, :Tq]), outs=[tGr], ins=[tb0])
                    act.op(lambda: nc.scalar.activation(out=eG[:, sl], in_=b0[:, :Tq], func=AF.Exp), outs=[tGr], ins=[tb0])
                    pe.op(lambda: nc.tensor.matmul(b0[:, :Tq], sel[:, h * 128:(h + 1) * 128], be[0:8, sl], start=True, stop=True), outs=[tb0], ins=[t_sel, tg])
                    act.op(lambda: nc.scalar.copy(out=Brow[:, sl], in_=b0[:, :Tq]), outs=[tGr], ins=[tb0])
                dve.op(lambda: V.tensor_tensor(out=qdT[:], in0=qT[:], in1=eG[:], op=ALU.mult), outs=[tqd], ins=[tq, tGr])
                def chunk_gen(c, R):
                    csl = slice(c * 128, (c + 1) * 128)
                    cend = c * 128 + 127
                    Gcol = cols[:, c, h:h + 1]; bcol = cols[:, c, 8 + h:9 + h]
                    bA, tA = R["bA"]; bB, tB = R["bB"]
                    dT, tdT = R["dT"]; dI, tdI = R["dI"]; aa, taa = R["aa"]; Xu, tXu = R["Xu"]; Xw, tXw = R["Xw"]; sc, tsc = R["sc"]
                    Np, tN = R["Np"], R["tN"]; Mp, tM = R["Mp"], R["tM"]
                    pe.op(lambda: nc.tensor.transpose(bB[:, 0:128], kT[:, csl], self.ident[:]), outs=[tB], ins=[tkk, self.t_ident])
                    pe.op(lambda: nc.tensor.transpose(bB[:, 128:256], vT[:, csl], self.ident[:]), outs=[tB], ins=[tv, self.t_ident])
                    dve.op(lambda: V.tensor_scalar(dT[:], Grow[:, csl], Gcol, 0.0, ALU.subtract, ALU.min), outs=[tdT], ins=[tGr, tcols])
                    pe.op(lambda: nc.tensor.matmul(bA[:, 0:128], kT[:, csl], kT[:, csl], start=True, stop=True), outs=[tA], ins=[tkk])
                    pe.op(lambda: nc.tensor.matmul(bA[:, 128:256], kT[:, csl], qT[:, csl], start=True, stop=True), outs=[tA], ins=[tkk, tq])
                    yield
                    act.op(lambda: nc.scalar.copy(out=k_tok[:, c, :], in_=bB[:, 0:128]), outs=[tkv[c]], ins=[tB])
                    act.op(lambda: nc.scalar.copy(out=v_tok[:, c, :], in_=bB[:, 128:256]), outs=[tkv[c]], ins=[tB])
                    act.op(lambda: nc.scalar.activation(out=dT[:], in_=dT[:], func=AF.Exp), outs=[tdT], ins=[tdT])
                    act.op(lambda: nc.scalar.activation(out=sc[:, 0:1], in_=Gcol, func=AF.Exp), outs=[tsc], ins=[tcols])
                    act.op(lambda: nc.scalar.activation(out=sc[:, 1:2], in_=Gcol, func=AF.Exp, scale=-1.0, bias=Grow[:, cend:cend + 1]), outs=[tsc], ins=[tcols, tGr])
                    yield
                    dve.op(lambda: V.tensor_tensor(out=dI[:], in0=dT[:], in1=maskI[:], op=ALU.mult), outs=[tdI], ins=[tdT, t_mi])
                    dve.op(lambda: V.tensor_tensor(out=attnT[:, c, :], in0=bA[:, 128:256], in1=dI[:], op=ALU.mult), outs=[tpre[c]], ins=[tA, tdI])
                    dve.op(lambda: V.tensor_tensor(out=aa[:], in0=dT[:], in1=maskSt[:], op=ALU.mult), outs=[taa], ins=[tdT, t_mst])
                    dve.op(lambda: V.tensor_tensor(out=aa[:], in0=aa[:], in1=Brow[:, csl], op=ALU.mult), outs=[taa], ins=[taa, tGr])
                    dve.op(lambda: V.tensor_tensor(out=Np[0][:], in0=bA[:, 0:128], in1=aa[:], op=ALU.mult), outs=[tN[0]], ins=[tA, taa])
                    yield
                    pe.op(lambda: nc.tensor.transpose(bB[:, 256:384], Np[0][:], self.ident[:]), outs=[tB], ins=[tN[0], self.t_ident])
                    Yi = 0
                    Y, tY = R["Y"][Yi]
                    dve.op(lambda: V.tensor_tensor(out=Y[:], in0=self.ident[:], in1=Np[0][:], op=ALU.subtract), outs=[tY], ins=[tN[0], self.t_ident])
                    dve.op(lambda: V.tensor_scalar(Xu[:], v_tok[:, c, :], bcol, None, ALU.mult), outs=[tXu], ins=[tkv[c], tcols])
                    dve.op(lambda: V.tensor_scalar(Xw[:], k_tok[:, c, :], bcol, sc[:, 0:1], ALU.mult, ALU.mult), outs=[tXw], ins=[tkv[c], tcols, tsc])
                    dve.op(lambda: V.tensor_scalar(kd_tok[:, c, :], k_tok[:, c, :], sc[:, 1:2], None, ALU.mult), outs=[tpre[c]], ins=[tkv[c], tsc])
                    yield
                    act.op(lambda: nc.scalar.copy(out=Mp[0][:], in_=bB[:, 256:384]), outs=[tM[0]], ins=[tB])
                    yield
                    for k in range(1, 7):
                        pe.op(lambda: nc.tensor.matmul(bB[:, 256:384], Np[k - 1][:], Mp[k - 1][:], start=True, stop=True), outs=[tB], ins=[tN[k - 1], tM[k - 1]])
                        if k < 6:
                            pe.op(lambda: nc.tensor.matmul(bA[:, 256:384], Mp[k - 1][:], Np[k - 1][:], start=True, stop=True), outs=[tA], ins=[tN[k - 1], tM[k - 1]])
                        yield
                        act.op(lambda: nc.scalar.copy(out=Mp[k][:], in_=bB[:, 256:384]), outs=[tM[k]], ins=[tB])
                        if k < 6:
                            dve.op(lambda: V.tensor_copy(out=Np[k][:], in_=bA[:, 256:384]), outs=[tN[k]], ins=[tA])
                        yield
                        pe.op(lambda: nc.tensor.matmul(bA[:, 384:512], Mp[k][:], Y[:], start=True, stop=True), outs=[tA], ins=[tM[k], tY])
                        yield
                        Yi ^= 1
                        Y2, tY2 = R["Y"][Yi]
                        dve.op(lambda: V.tensor_tensor(out=Y2[:], in0=bA[:, 384:512], in1=Y[:], op=ALU.add), outs=[tY2], ins=[tA, tY])
                        Y, tY = Y2, tY2
                    yield
                    pe.op(lambda: nc.tensor.matmul(bB[:, 384:512], Y[:], Xu[:], start=True, stop=True), outs=[tB], ins=[tY, tXu])
                    pe.op(lambda: nc.tensor.matmul(bB[:, 0:128], Xw[:], Y[:], start=True, stop=True), outs=[tB], ins=[tY, tXw])
                    yield
                    act.op(lambda: nc.scalar.copy(out=u_tok[:, c, :], in_=bB[:, 384:512]), outs=[tpre[c]], ins=[tB])
                    act.op(lambda: nc.scalar.copy(out=wT[:, c, :], in_=bB[:, 0:128]), outs=[tpre[c]], ins=[tB])
                for c0 in range(0, NT, KI):
                    gens = [chunk_gen(c, slots[c - c0]) for c in range(c0, min(NT, c0 + KI))]
                    while gens:
                        for g in list(gens):
                            try:
                                next(g)
                            except StopIteration:
                                gens.remove(g)
                dve.op(lambda: V.memset(Sst[:], 0.0), outs=[tS])
                for c in range(NT):
                    csl = slice(c * 128, (c + 1) * 128)
                    cend = c * 128 + 127
                    pe.op(lambda: nc.tensor.matmul(b6[:, 0:128], wT[:, c, :], Sst[:], start=True, stop=True), outs=[tb6], ins=[tpre[c], tS])
                    dve.op(lambda: V.tensor_tensor(out=vn[:], in0=u_tok[:, c, :], in1=b6[:, 0:128], op=ALU.subtract), outs=[tvn], ins=[tpre[c], tb6])
                    pe.op(lambda: nc.tensor.matmul(b7[:, 0:128], Sst[:], qdT[:, csl], start=True, stop=False), outs=[tb7], ins=[tS, tqd])
                    pe.op(lambda: nc.tensor.matmul(b7[:, 0:128], vn[:], attnT[:, c, :], start=False, stop=True), outs=[tb7], ins=[tvn, tpre[c]])
                    act.op(lambda: nc.scalar.copy(out=cv[:, csl], in_=b7[:, 0:128]), outs=[tcv], ins=[tb7])
                    pe.op(lambda: nc.tensor.matmul(b6[:, 128:256], kd_tok[:, c, :], vn[:], start=True, stop=True), outs=[tb6], ins=[tpre[c], tvn])
                    dve.op(lambda: V.scalar_tensor_tensor(out=Sst[:], in0=Sst[:], scalar=eG[:, cend:cend + 1], in1=b6[:, 128:256], op0=ALU.mult, op1=ALU.add),
                           outs=[tS], ins=[tS, tb6, tGr])
                oT = cv
                act.op(lambda: nc.scalar.activation(out=zT[:], in_=zT[:], func=AF.Silu), outs=[tz], ins=[tz])
                for (q0, Tq) in chunks:
                    sl = slice(q0, q0 + Tq)
                    act.op(lambda: nc.scalar.activation(out=qdT[:, sl], in_=oT[:, sl], func=AF.Square), outs=[tqd], ins=[tcv])
                    pe.op(lambda: nc.tensor.matmul(b0[:, :Tq], self.ones[:], qdT[:, sl], start=True, stop=True), outs=[tb0], ins=[tqd, self.t_ones])
                    act.op(lambda: nc.scalar.activation(out=rs[:, :Tq], in_=b0[:, :Tq], func=AF.Sqrt, bias=self.epsb[:], scale=1.0 / 128), outs=[trs], ins=[tb0, self.t_eps])
                    dve.op(lambda: V.reciprocal(out=rs[:, :Tq], in_=rs[:, :Tq]), outs=[trs], ins=[trs])
                    dve.op(lambda: V.scalar_tensor_tensor(out=oT[:, sl], in0=oT[:, sl], scalar=gout[:, 0:1], in1=rs[:, :Tq], op0=ALU.mult, op1=ALU.mult),
                           outs=[tcv], ins=[tcv, trs, tk])
                    dve.op(lambda: V.tensor_tensor(out=oT[:, sl], in0=oT[:, sl], in1=zT[:, sl], op=ALU.mult), outs=[tcv], ins=[tcv, tz])
                P.q_a.dma(self.yO[8 + h, :, c0s:c0s + LP], oT[:], ins=[tcv])
        P.release(m)


def host_consts(LP=None, mix=True):
    c = {"ident": np.eye(128, dtype=np.float32)}
    if mix and LP is not None:
        half = 32
        inv = (10000.0 ** (-np.arange(half, dtype=np.float32) / half)).astype(np.float32)
        ang = np.arange(LP, dtype=np.float32)[:, None] * inv[None, :]
        c["rope_cos"] = np.cos(ang).astype(np.float32)
        c["rope_sin"] = np.sin(ang).astype(np.float32)
        k = np.arange(128)[:, None]; q = np.arange(128)[None, :]
        c["maskT"] = (q >= k).astype(np.float32)
        c["maskS"] = (q > k).astype(np.float32)
        c["maskL"] = (q <= k).astype(np.float32)
        sel = np.zeros((8, 8, 128), np.float32)
        for h in range(8):
            sel[h, h, :] = 1.0
        c["sel"] = sel.reshape(8, 1024)
    return c


def make_xin(x, meta, LP):
    NS, SEQ, _ = x.shape
    xin = np.zeros((NS, LP, D), np.float32)
    xin[:, :N_META] = meta[None]
    xin[:, N_META:N_META + SEQ] = x
    return xin.reshape(NS * LP, D)


_WNAMES = ["norm_ffn1", "norm_mix", "norm_ffn2", "w1_gate", "w1_up", "w2_gate", "w2_up", "w1_down", "w2_down"]


def kernel(**inputs):
    x = np.asarray(inputs["x"])
    B, SEQ, _ = x.shape
    ncores = 8
    NS = B // ncores
    L = SEQ + N_META
    NT = (L + 127) // 128
    K = Kern(NS, NT, depth=2, seq_real=L)
    consts = host_consts(NT * 128)
    in_maps = []
    for c in range(ncores):
        m = {"xin": make_xin(x[c * NS:(c + 1) * NS], np.asarray(inputs["meta_tokens"]), NT * 128)}
        m.update(consts)
        for nm in K.w:
            m[nm] = np.ascontiguousarray(np.asarray(inputs[nm]))
        in_maps.append(m)
    res = run_bass_kernel_spmd(K.P.nc, in_maps, core_ids=list(range(ncores)))
    outs = [r["out"].reshape(NS, SEQ, D) for r in res.results]
    return np.concatenate(outs, axis=0).astype(np.float32)
```

```python
import math
import numpy as np
import concourse.bass as bass
import concourse.mybir as mybir
from concourse.bass_utils import run_bass_kernel_spmd

F32 = mybir.dt.float32
BF16 = mybir.dt.bfloat16
AF = mybir.ActivationFunctionType
ALU = mybir.AluOpType
AX = mybir.AxisListType

D = 2048
DC = 16
DFF = 5632
FC = 44
N_META = 16
EPS = 1e-6
ARENA_WORDS = 53200


class Tok:
    __slots__ = ("w", "r", "name")

    def __init__(self, name=""):
        self.w = None
        self.r = {}
        self.name = name


class Eng:
    def __init__(self, P, eng, name, is_pe=False):
        self.P = P
        self.eng = eng
        self.name = name
        self.sem = P.nc.alloc_semaphore("s_" + name)
        self.key = "E" + name
        self.cnt = 0
        self.seen = {}
        self.is_pe = is_pe
        self.nwait = 0

    def wait_ev(self, ev):
        if ev is None:
            return
        key, sem, val = ev
        if self.is_pe and key == self.key:
            return
        if key == self.key and val <= self.cnt_done_known():
            pass
        if self.seen.get(key, 0) >= val:
            return
        self.eng.wait_ge(sem, val)
        self.nwait += 1
        self.seen[key] = val

    def cnt_done_known(self):
        return self.seen.get(self.key, 0)

    def deps(self, outs, ins):
        for t in ins:
            self.wait_ev(t.w)
        for t in outs:
            self.wait_ev(t.w)
            for k, (sem, val) in t.r.items():
                self.wait_ev((k, sem, val))

    def op(self, fn, outs=(), ins=()):
        self.deps(outs, ins)
        inst = fn()
        self.cnt += 1
        inst.then_inc(self.sem, 1)
        ev = (self.key, self.sem, self.cnt)
        for t in ins:
            t.r[self.key] = (self.sem, self.cnt)
        for t in outs:
            t.w = ev
            t.r = {}
        return inst


class DmaQ:
    def __init__(self, P, eng, name, nsem):
        self.P = P
        self.eng = eng
        self.name = name
        self.sems = [P.nc.alloc_semaphore("d_%s%d" % (name, i)) for i in range(nsem)]
        self.vals = [0] * nsem
        self.i = 0
        self.seen = {}
        self.nwait = 0

    def wait_ev(self, ev):
        if ev is None:
            return
        key, sem, val = ev
        if self.seen.get(key, 0) >= val:
            return
        self.eng.wait_ge(sem, val)
        self.nwait += 1
        self.seen[key] = val

    def dma(self, out, in_, outs=(), ins=(), **kw):
        for t in ins:
            self.wait_ev(t.w)
        for t in outs:
            self.wait_ev(t.w)
            for k, (sem, val) in t.r.items():
                self.wait_ev((k, sem, val))
        i = self.i
        self.i = (self.i + 1) % len(self.sems)
        key = "D%s%d" % (self.name, i)
        if self.vals[i] > 0:
            self.wait_ev((key, self.sems[i], self.vals[i]))
        inst = self.eng.dma_start(out=out, in_=in_, **kw)
        self.vals[i] += 16
        inst.then_inc(self.sems[i], 16)
        ev = (key, self.sems[i], self.vals[i])
        for t in ins:
            t.r[key] = (self.sems[i], self.vals[i])
        for t in outs:
            t.w = ev
            t.r = {}
        return ev

    def drain(self):
        for i, s in enumerate(self.sems):
            if self.vals[i] > 0:
                self.wait_ev(("D%s%d" % (self.name, i), s, self.vals[i]))


class Prog:
    def __init__(self):
        self.nc = bass.Bass("TRN2", target_bir_lowering=False)
        nc = self.nc
        self.pe = Eng(self, nc.tensor, "pe", is_pe=True)
        self.act = Eng(self, nc.scalar, "act")
        self.dve = Eng(self, nc.vector, "dve")
        self.pool = Eng(self, nc.gpsimd, "pool")
        self.q_w = DmaQ(self, nc.sync, "w", 12)
        self.q_a = DmaQ(self, nc.gpsimd, "a", 8)
        self._n = 0
        self.AW = ARENA_WORDS
        self.arena = nc.alloc_sbuf_tensor("arena", [128, self.AW], F32)
        self.off = 0
        self.psb = [(nc.alloc_psum_tensor("psb%d" % i, [128, 512], F32), Tok("psb%d" % i)) for i in range(8)]

    def sb(self, shape, name=None):
        n = 1
        for d in shape[1:]:
            n *= d
        assert self.off + n <= self.AW, ("SBUF arena overflow", name, self.off, n)
        ap = self.arena[0:shape[0], self.off:self.off + n]
        self.off += n
        if len(shape) == 3:
            ap = ap.rearrange("p (a b) -> p a b", a=shape[1])
        elif len(shape) == 4:
            ap = ap.rearrange("p (a b c) -> p a b c", a=shape[1], b=shape[2])
        return ap

    def sbh(self, shape, name=None):
        n = 1
        for d in shape[1:]:
            n *= d
        import os
        if os.environ.get("DBG_NOBF"):
            return self.sb(shape, name)
        assert n % 2 == 0
        w = n // 2
        assert self.off + w <= self.AW, ("SBUF arena overflow", name, self.off, w)
        ap = self.arena[0:shape[0], self.off:self.off + w].bitcast(BF16)
        self.off += w
        if len(shape) == 3:
            ap = ap.rearrange("p (a b) -> p a b", a=shape[1])
        return ap

    def mark(self):
        return self.off

    def release(self, m):
        self.barrier()
        self.off = m

    def barrier(self):
        evs = []
        for e in (self.pe, self.act, self.dve, self.pool):
            if e.cnt > 0:
                evs.append((e.key, e.sem, e.cnt))
        for q in (self.q_w, self.q_a):
            for i, sm in enumerate(q.sems):
                if q.vals[i] > 0:
                    evs.append(("D%s%d" % (q.name, i), sm, q.vals[i]))
        for e in (self.pe, self.act, self.dve, self.pool, self.q_w, self.q_a):
            for ev in evs:
                e.wait_ev(ev)

    def dram(self, name, shape, kind="Internal", dtype=F32):
        return self.nc.dram_tensor(name, list(shape), dtype, kind=kind).ap()


class Ring:
    def __init__(self, bufs):
        self.bufs = bufs
        self.i = 0

    def next(self):
        b = self.bufs[self.i]
        self.i = (self.i + 1) % len(self.bufs)
        return b


class Kern:
    LAYERS = None

    def __init__(self, NS, NT, depth=2, do=("ffn1", "mix", "ffn2"), seq_real=None):
        self.NS = NS
        self.NT = NT
        self.LP = NT * 128
        self.NTOK = NS * self.LP
        self.depth = depth
        self.do = do
        self.L = seq_real if seq_real is not None else (self.LP - 112)
        self.P = Prog()
        self.build()

    def build(self):
        P = self.P
        nc = P.nc
        NS, NT, LP, NTOK = self.NS, self.NT, self.LP, self.NTOK
        SEQ = self.L - N_META
        dep = self.depth
        self.xin = P.dram("xin", [NS * LP, D], "ExternalInput")
        self.ident_d = P.dram("ident", [128, 128], "ExternalInput")
        self.w = {}
        def ext(name, shape):
            self.w[name] = P.dram(name, shape, "ExternalInput")
        ext("norm_ffn1", [dep, D]); ext("norm_mix", [dep, D]); ext("norm_ffn2", [dep, D])
        for nm in ("w1_gate", "w1_up", "w2_gate", "w2_up"):
            ext(nm, [dep, D, DFF])
        for nm in ("w1_down", "w2_down"):
            ext(nm, [dep, DFF, D])
        ne, no = (dep + 1) // 2, dep // 2
        if "mix" in self.do:
            ext("ev_w_in", [ne, D, 1344]); ext("s5_log_dt", [ne, 32]); ext("s5_a_re", [ne, 32, 64]); ext("s5_a_im", [ne, 32, 64])
            ext("s5_b_re", [ne, 32, 64, 16]); ext("s5_b_im", [ne, 32, 64, 16]); ext("s5_c_re", [ne, 32, 16, 64]); ext("s5_c_im", [ne, 32, 16, 64])
            ext("s5_d", [ne, 512]); ext("s5_w_glu", [ne, 512, 512]); ext("mla_g_cq", [ne, 512]); ext("mla_g_ckv", [ne, 256])
            ext("mla_w_uq", [ne, 512, 1536]); ext("mla_w_ukv", [ne, 256, 2048]); ext("mla_g_q", [ne, 192]); ext("mla_g_k", [ne, 192])
            ext("ev_w_out", [ne, 1536, D])
            if no > 0:
                ext("od_w_in", [no, D, 7184]); ext("gdn_conv", [no, 4, 3072]); ext("gdn_a_log", [no, 8]); ext("gdn_dt_bias", [no, 8])
                ext("gdn_g_out", [no, 128]); ext("od_w_out", [no, 2048, D])
            self.c_rope_cos = P.dram("rope_cos", [LP, 32], "ExternalInput")
            self.c_rope_sin = P.dram("rope_sin", [LP, 32], "ExternalInput")
            self.c_maskT = P.dram("maskT", [128, 128], "ExternalInput")
            self.uT = P.dram("uT", [4, 128, NTOK])
            self.QT = P.dram("QT", [8, 192, NTOK])
            self.KT = P.dram("KT", [8, 192, NTOK])
            self.Vd = P.dram("Vd", [NTOK, 1024])
            self.yE = P.dram("yE", [12, 128, NTOK])
            if no > 0:
                self.c_maskS = P.dram("maskS", [128, 128], "ExternalInput")
                self.c_maskL = P.dram("maskL", [128, 128], "ExternalInput")
                self.c_sel = P.dram("sel", [8, 1024], "ExternalInput")
                self.sbQ = P.dram("sbQ", [8, 128, NTOK]); self.sbK = P.dram("sbK", [8, 128, NTOK]); self.sbV = P.dram("sbV", [NTOK, 1024])
                self.gT = P.dram("gT", [24, 128, NTOK]); self.gzT = P.dram("gzT", [8, 128, NTOK]); self.gabT = P.dram("gabT", [16, NTOK])
                self.yO = P.dram("yO", [16, 128, NTOK])
        self.out = P.dram("out", [NS * SEQ, D], "ExternalOutput")
        self.hT = P.dram("hT", [DC, 128, NTOK])
        self.ident = P.sb([128, 128], "ident"); self.t_ident = Tok("ident")
        P.q_a.dma(self.ident[:], self.ident_d[:, :], outs=[self.t_ident])
        self.ones = P.sb([128, 128], "ones"); self.t_ones = Tok("ones")
        P.dve.op(lambda: nc.vector.memset(self.ones[:], 1.0), outs=[self.t_ones])
        self.epsb = P.sb([128, 1], "epsb"); self.t_eps = Tok("eps")
        P.dve.op(lambda: nc.vector.memset(self.epsb[:], EPS), outs=[self.t_eps])
        self.gains = {}
        for nm in ("norm_ffn1", "norm_mix", "norm_ffn2"):
            g = P.sb([128, dep, DC], nm); t = Tok(nm)
            for l in range(dep):
                P.q_a.dma(g[:, l, :], self.w[nm][l].rearrange("(c p) -> p c", p=128), outs=[t],
                          allow_slow_non_contiguous=True)
            self.gains[nm] = (g, t)

        self.stage_in()
        for l in (self.LAYERS if self.LAYERS is not None else range(dep)):
            if "ffn1" in self.do:
                self.ffn(l, "1")
            if "mix" in self.do:
                if l % 2 == 0:
                    self.even_mixer(l)
                else:
                    self.odd_mixer(l)
            if "ffn2" in self.do:
                self.ffn(l, "2")
        self.stage_out()
        P.q_a.drain()
        P.q_w.drain()

    def psring(self, idx):
        return Ring([self.P.psb[i] for i in idx])

    def stage_in(self):
        P = self.P; nc = P.nc
        m = P.mark()
        NB = self.NTOK // 128
        xin_r = Ring([(P.sb([128, D], "xin"), Tok()) for _ in range(2)])
        ps_r = self.psring([0, 1])
        ht_r = Ring([(P.sb([128, DC, 128], "htin"), Tok()) for _ in range(2)])
        for b in range(NB):
            xt, xtok = xin_r.next()
            P.q_a.dma(xt[:], self.xin[b * 128:(b + 1) * 128, :], outs=[xtok])
            ht, htok = ht_r.next()
            for g in range(4):
                ps, ptok = ps_r.next()
                for j in range(4):
                    c = g * 4 + j
                    P.pe.op(lambda: nc.tensor.transpose(ps[:, j * 128:(j + 1) * 128],
                                                        xt[:, c * 128:(c + 1) * 128], self.ident[:]),
                            outs=[ptok], ins=[xtok, self.t_ident])
                P.act.op(lambda: nc.scalar.copy(out=ht[:, g * 4:(g + 1) * 4, :],
                                                in_=ps[:].rearrange("p (c t) -> p c t", c=4)),
                         outs=[htok], ins=[ptok])
            P.q_a.dma(self.hT[:, :, b * 128:(b + 1) * 128].rearrange("c p t -> p c t"), ht[:], ins=[htok],
                      outs=[self.tok_h(b * 128, 128)])
        P.release(m)

    def tok_h(self, t0, n):
        if not hasattr(self, "_htoks"):
            self._htoks = {}
        key = t0 // 128
        if key not in self._htoks:
            self._htoks[key] = Tok("h%d" % key)
        return self._htoks[key]

    def toks_h(self, t0, n):
        return [self.tok_h(t, 128) for t in range(t0, t0 + n, 128)]

    def stage_out(self):
        P = self.P; nc = P.nc
        m = P.mark()
        SEQ = self.L - N_META
        ht_r = Ring([(P.sb([128, DC, 128], "htout"), Tok()) for _ in range(2)])
        ps_r = self.psring([0, 1])
        o_r = Ring([(P.sb([128, D], "osb"), Tok()) for _ in range(2)])
        for s in range(self.NS):
            for j in range(self.NT):
                t0 = j * 128
                lo = max(t0, N_META); hi = min(t0 + 128, self.L)
                if hi <= lo:
                    continue
                b = s * self.NT + j
                ht, htok = ht_r.next()
                P.q_a.dma(ht[:], self.hT[:, :, b * 128:(b + 1) * 128].rearrange("c p t -> p c t"), outs=[htok],
                          ins=[self.tok_h(b * 128, 128)])
                ot, otok = o_r.next()
                for g in range(4):
                    ps, ptok = ps_r.next()
                    for jj in range(4):
                        c = g * 4 + jj
                        P.pe.op(lambda: nc.tensor.transpose(ps[:, jj * 128:(jj + 1) * 128], ht[:, c, :], self.ident[:]),
                                outs=[ptok], ins=[htok, self.t_ident])
                    P.act.op(lambda: nc.scalar.copy(out=ot[:, g * 512:(g + 1) * 512], in_=ps[:]),
                             outs=[otok], ins=[ptok])
                r0 = s * SEQ + lo - N_META
                P.q_a.dma(self.out[r0:r0 + (hi - lo), :], ot[lo - t0:hi - t0, :], ins=[otok])
        P.release(m)

    def rms_scale(self, src, stok, T, gain_cols, gtok, ps_sum, pstok, sq, sqtok, rstd, rtok, nchunks=DC, dim=D):
        P = self.P; nc = P.nc
        for c in range(nchunks):
            P.act.op(lambda: nc.scalar.activation(out=sq[:, :T], in_=src[:, c, :T], func=AF.Square),
                     outs=[sqtok], ins=[stok])
            P.pe.op(lambda: nc.tensor.matmul(ps_sum[:, :T], self.ones[:], sq[:, :T], start=(c == 0), stop=(c == nchunks - 1)),
                    outs=[pstok], ins=[sqtok, self.t_ones])
        P.act.op(lambda: nc.scalar.activation(out=rstd[:, :T], in_=ps_sum[:, :T], func=AF.Sqrt,
                                              bias=self.epsb[:], scale=1.0 / dim),
                 outs=[rtok], ins=[pstok, self.t_eps])
        P.dve.op(lambda: nc.vector.reciprocal(out=rstd[:, :T], in_=rstd[:, :T]), outs=[rtok], ins=[rtok])
        for c in range(nchunks):
            P.dve.op(lambda: nc.vector.scalar_tensor_tensor(out=src[:, c, :T], in0=src[:, c, :T],
                                                            scalar=gain_cols(c), in1=rstd[:, :T],
                                                            op0=ALU.mult, op1=ALU.mult),
                     outs=[stok], ins=[stok, rtok, gtok])

    def ffn(self, l, which):
        P = self.P; nc = P.nc
        NTOK = self.NTOK
        TT = 768; XS = 256
        HF = FC // 2
        wg = self.w["w%s_gate" % which][l]
        wu = self.w["w%s_up" % which][l]
        wd = self.w["w%s_down" % which][l]
        gain, gtok = self.gains["norm_ffn%s" % which]
        m = P.mark()
        xs, xstok = P.sb([128, DC, XS], "ffxs"), Tok("ffxs")
        xb, xbtok = P.sbh([128, DC, TT], "ffxb"), Tok("ffxb")
        hf = P.sbh([128, FC, TT], "ffh")
        hftoks = [Tok("ffh%d" % f) for f in range(FC)]
        wst_r = Ring([(P.sb([128, DC, 128], "wst"), Tok()) for _ in range(3)])
        wgb_r = Ring([(P.sbh([128, DC, 128], "wgb"), Tok()) for _ in range(2)])
        wub_r = Ring([(P.sbh([128, DC, 128], "wub"), Tok()) for _ in range(2)])
        wdst_r = Ring([(P.sb([128, HF, 128], "wdst"), Tok()) for _ in range(2)])
        wdb_r = Ring([(P.sbh([128, HF, 128], "wdb"), Tok()) for _ in range(4)])
        rstd, rtok = P.sb([128, XS], "ffrstd"), Tok()
        sq_r = Ring([(P.sb([128, XS], "ffsq"), Tok()) for _ in range(2)])
        sg_r = Ring([(P.sb([128, TT // 2], "ffsg"), Tok()) for _ in range(2)])
        hres_r = Ring([(P.sb([128, TT], "ffres"), Tok()) for _ in range(2)])
        psg = [P.psb[0], P.psb[1]]; psu = [P.psb[2], P.psb[3]]; psd = [P.psb[4], P.psb[5]]
        pss, psstok = P.psb[6]
        wgv = wg.rearrange("(c p) f -> p c f", p=128)
        wuv = wu.rearrange("(c p) f -> p c f", p=128)
        wdv = wd.rearrange("(c p) d -> p c d", p=128)
        with nc.allow_low_precision("bf16 matmul operands, fp32 accumulation"):
            for t0 in range(0, NTOK, TT):
                T = min(TT, NTOK - t0)
                HW = T // 2
                halves = [(0, HW), (HW, T - HW)]
                htoks = self.toks_h(t0, T)
                for p0 in range(0, T, XS):
                    P.q_a.dma(xs[:, :, :], self.hT[:, :, t0 + p0:t0 + p0 + XS].rearrange("c p t -> p c t"), outs=[xstok], ins=htoks)
                    for c in range(DC):
                        sq, sqtok = sq_r.next()
                        P.act.op(lambda: nc.scalar.activation(out=sq[:, :], in_=xs[:, c, :], func=AF.Square), outs=[sqtok], ins=[xstok])
                        P.pe.op(lambda: nc.tensor.matmul(pss[:, :XS], self.ones[:], sq[:, :], start=(c == 0), stop=(c == DC - 1)),
                                outs=[psstok], ins=[sqtok, self.t_ones])
                    P.act.op(lambda: nc.scalar.activation(out=rstd[:, :], in_=pss[:, :XS], func=AF.Sqrt, bias=self.epsb[:], scale=1.0 / D),
                             outs=[rtok], ins=[psstok, self.t_eps])
                    P.dve.op(lambda: nc.vector.reciprocal(out=rstd[:, :], in_=rstd[:, :]), outs=[rtok], ins=[rtok])
                    for c in range(DC):
                        P.dve.op(lambda: nc.vector.scalar_tensor_tensor(out=xb[:, c, p0:p0 + XS], in0=xs[:, c, :], scalar=gain[:, l, c:c + 1],
                                                                        in1=rstd[:, :], op0=ALU.mult, op1=ALU.mult),
                                 outs=[xbtok], ins=[xstok, rtok, gtok])
                def load_gu(f):
                    gs, gst = wst_r.next()
                    P.q_w.dma(gs[:], wgv[:, :, f * 128:(f + 1) * 128], outs=[gst])
                    gb, gbt = wgb_r.next()
                    P.dve.op(lambda: nc.vector.tensor_copy(out=gb[:], in_=gs[:]), outs=[gbt], ins=[gst])
                    us, ust = wst_r.next()
                    P.q_w.dma(us[:], wuv[:, :, f * 128:(f + 1) * 128], outs=[ust])
                    ub, ubt = wub_r.next()
                    P.dve.op(lambda: nc.vector.tensor_copy(out=ub[:], in_=us[:]), outs=[ubt], ins=[ust])
                    return gb, gbt, ub, ubt
                nxt = load_gu(0)
                for f in range(FC):
                    g_, gt_, u_, ut_ = nxt
                    if f + 1 < FC:
                        nxt = load_gu(f + 1)
                    for hi, (h0, hw) in enumerate(halves):
                        ps, pt = psg[hi]
                        for c in range(DC):
                            P.pe.op(lambda: nc.tensor.matmul(ps[:, :hw], g_[:, c, :], xb[:, c, h0:h0 + hw], start=(c == 0), stop=(c == DC - 1)),
                                    outs=[pt], ins=[gt_, xbtok])
                    for hi, (h0, hw) in enumerate(halves):
                        ps, pt = psu[hi]
                        for c in range(DC):
                            P.pe.op(lambda: nc.tensor.matmul(ps[:, :hw], u_[:, c, :], xb[:, c, h0:h0 + hw], start=(c == 0), stop=(c == DC - 1)),
                                    outs=[pt], ins=[ut_, xbtok])
                    for hi, (h0, hw) in enumerate(halves):
                        sg, sgt = sg_r.next()
                        P.act.op(lambda: nc.scalar.activation(out=sg[:, :hw], in_=psg[hi][0][:, :hw], func=AF.Silu), outs=[sgt], ins=[psg[hi][1]])
                        P.dve.op(lambda: nc.vector.tensor_tensor(out=hf[:, f, h0:h0 + hw], in0=psu[hi][0][:, :hw], in1=sg[:, :hw], op=ALU.mult),
                                 outs=[hftoks[f]], ins=[psu[hi][1], sgt])
                def load_d(i):
                    dc, hh = divmod(i, 2)
                    ds_, dst_ = wdst_r.next()
                    P.q_w.dma(ds_[:], wdv[:, hh * HF:(hh + 1) * HF, dc * 128:(dc + 1) * 128], outs=[dst_])
                    db, dbt = wdb_r.next()
                    P.act.op(lambda: nc.scalar.copy(out=db[:], in_=ds_[:]), outs=[dbt], ins=[dst_])
                    return db, dbt
                pend = [load_d(0), load_d(1)]
                for dc in range(DC):
                    hres, hrt = hres_r.next()
                    P.q_a.dma(hres[:, :T], self.hT[dc, :, t0:t0 + T], outs=[hrt], ins=htoks)
                    slabs = []
                    for hh in range(2):
                        slabs.append(pend.pop(0))
                        i_next = dc * 2 + hh + 2
                        if i_next < 2 * DC:
                            pend.append(load_d(i_next))
                    for hi, (h0, hw) in enumerate(halves):
                        ps, pt = psd[hi]
                        for hh in range(2):
                            d_, dt_ = slabs[hh]
                            for ff in range(HF):
                                f = hh * HF + ff
                                P.pe.op(lambda: nc.tensor.matmul(ps[:, :hw], d_[:, ff, :], hf[:, f, h0:h0 + hw], start=(f == 0), stop=(f == FC - 1)),
                                        outs=[pt], ins=[dt_, hftoks[f]])
                        P.dve.op(lambda: nc.vector.scalar_tensor_tensor(out=hres[:, h0:h0 + hw], in0=ps[:, :hw], scalar=0.5, in1=hres[:, h0:h0 + hw],
                                                                        op0=ALU.mult, op1=ALU.add),
                                 outs=[hrt], ins=[pt, hrt])
                    P.q_a.dma(self.hT[dc, :, t0:t0 + T], hres[:, :T], ins=[hrt], outs=htoks)
        P.release(m)

    def ffn_fp32(self, l, which):
        P = self.P; nc = P.nc
        NTOK = self.NTOK
        TT = 512
        HF = FC // 2
        wg = self.w["w%s_gate" % which][l]
        wu = self.w["w%s_up" % which][l]
        wd = self.w["w%s_down" % which][l]
        gain, gtok = self.gains["norm_ffn%s" % which]
        m = P.mark()
        xt, xtok = P.sb([128, DC, TT], "ffx"), Tok("ffx")
        hf = P.sb([128, FC, TT], "ffh")
        hftoks = [Tok("ffh%d" % f) for f in range(FC)]
        wg_r = Ring([(P.sb([128, DC, 128], "wg"), Tok()) for _ in range(2)])
        wu_r = Ring([(P.sb([128, DC, 128], "wu"), Tok()) for _ in range(2)])
        wd_r = Ring([(P.sb([128, HF, 128], "wd"), Tok()) for _ in range(3)])
        psg_r = self.psring([0, 1]); psu_r = self.psring([2, 3]); psd_r = self.psring([4, 5])
        pss, psstok = P.psb[6]
        rstd, rtok = P.sb([128, TT], "ffrstd"), Tok()
        sg_r = Ring([(P.sb([128, TT], "ffsg"), Tok()) for _ in range(2)])
        hres_r = Ring([(P.sb([128, TT], "ffres"), Tok()) for _ in range(2)])
        wgv = wg.rearrange("(c p) f -> p c f", p=128)
        wuv = wu.rearrange("(c p) f -> p c f", p=128)
        wdv = wd.rearrange("(c p) d -> p c d", p=128)
        for t0 in range(0, NTOK, TT):
            T = min(TT, NTOK - t0)
            htoks = self.toks_h(t0, T)
            P.q_a.dma(xt[:, :, :T], self.hT[:, :, t0:t0 + T].rearrange("c p t -> p c t"), outs=[xtok], ins=htoks)
            sq, sqtok = sg_r.next()
            self.rms_scale(xt, xtok, T, lambda c: gain[:, l, c:c + 1], gtok, pss, psstok, sq, sqtok, rstd, rtok)
            def load_gu(f):
                g_, gt_ = wg_r.next(); u_, ut_ = wu_r.next()
                P.q_w.dma(g_[:], wgv[:, :, f * 128:(f + 1) * 128], outs=[gt_])
                P.q_w.dma(u_[:], wuv[:, :, f * 128:(f + 1) * 128], outs=[ut_])
                return g_, gt_, u_, ut_
            nxt = load_gu(0)
            for f in range(FC):
                g_, gt_, u_, ut_ = nxt
                if f + 1 < FC:
                    nxt = load_gu(f + 1)
                psg, psgt = psg_r.next(); psu, psut = psu_r.next()
                for c in range(DC):
                    P.pe.op(lambda: nc.tensor.matmul(psg[:, :T], g_[:, c, :], xt[:, c, :T], start=(c == 0), stop=(c == DC - 1)),
                            outs=[psgt], ins=[gt_, xtok])
                for c in range(DC):
                    P.pe.op(lambda: nc.tensor.matmul(psu[:, :T], u_[:, c, :], xt[:, c, :T], start=(c == 0), stop=(c == DC - 1)),
                            outs=[psut], ins=[ut_, xtok])
                sg, sgt = sg_r.next()
                P.act.op(lambda: nc.scalar.activation(out=sg[:, :T], in_=psg[:, :T], func=AF.Silu), outs=[sgt], ins=[psgt])
                P.dve.op(lambda: nc.vector.tensor_tensor(out=hf[:, f, :T], in0=psu[:, :T], in1=sg[:, :T], op=ALU.mult),
                         outs=[hftoks[f]], ins=[psut, sgt])
            def load_d(i):
                dc, hh = divmod(i, 2)
                d_, dt_ = wd_r.next()
                P.q_w.dma(d_[:], wdv[:, hh * HF:(hh + 1) * HF, dc * 128:(dc + 1) * 128], outs=[dt_])
                return d_, dt_
            pend = [load_d(0), load_d(1)]
            for dc in range(DC):
                hres, hrt = hres_r.next()
                P.q_a.dma(hres[:, :T], self.hT[dc, :, t0:t0 + T], outs=[hrt], ins=htoks)
                psd, psdt = psd_r.next()
                for hh in range(2):
                    d_, dt_ = pend.pop(0)
                    i_next = dc * 2 + hh + 2
                    if i_next < 2 * DC:
                        pend.append(load_d(i_next))
                    for ff in range(HF):
                        f = hh * HF + ff
                        P.pe.op(lambda: nc.tensor.matmul(psd[:, :T], d_[:, ff, :], hf[:, f, :T], start=(f == 0), stop=(f == FC - 1)),
                                outs=[psdt], ins=[dt_, hftoks[f]])
                P.dve.op(lambda: nc.vector.scalar_tensor_tensor(out=hres[:, :T], in0=psd[:, :T], scalar=0.5, in1=hres[:, :T],
                                                                op0=ALU.mult, op1=ALU.add),
                         outs=[hrt], ins=[psdt, hrt])
                P.q_a.dma(self.hT[dc, :, t0:t0 + T], hres[:, :T], ins=[hrt], outs=htoks)
        P.release(m)


    def bc_rows(self, dram_ap_row_tensor, offset, n, parts=128):
        return bass.AP(dram_ap_row_tensor.tensor, offset, [[0, parts], [1, n]])

    def even_mixer(self, l):
        import os
        st = os.environ.get("DBG_STAGES", "inproj,attn,s5,out").split(",")
        if "inproj" in st:
            self.even_inproj(l)
        if "attn" in st:
            self.even_attn(l)
        if "s5" in st:
            self.even_s5(l)
        if "out" in st:
            self.even_out(l)

    def headnorm_rope(self, buf, btok, g_rep, gtok, cos, sin, cstok, tmp, ttok, ss, sstok, t4, t4tok):
        P = self.P; nc = P.nc
        P.dve.op(lambda: nc.vector.tensor_tensor(out=tmp[:], in0=buf[:], in1=buf[:], op=ALU.mult), outs=[ttok], ins=[btok])
        P.dve.op(lambda: nc.vector.tensor_reduce(out=ss[:], in_=tmp[:], axis=AX.X, op=ALU.add), outs=[sstok], ins=[ttok])
        P.act.op(lambda: nc.scalar.activation(out=ss[:], in_=ss[:], func=AF.Sqrt, bias=self.epsb[:], scale=1.0 / 192),
                 outs=[sstok], ins=[sstok, self.t_eps])
        P.dve.op(lambda: nc.vector.reciprocal(out=ss[:], in_=ss[:]), outs=[sstok], ins=[sstok])
        P.dve.op(lambda: nc.vector.tensor_tensor(out=buf[:], in0=buf[:], in1=ss[:].unsqueeze(2).to_broadcast([128, 8, 192]), op=ALU.mult),
                 outs=[btok], ins=[btok, sstok])
        P.dve.op(lambda: nc.vector.tensor_tensor(out=buf[:], in0=buf[:], in1=g_rep[:].unsqueeze(1).to_broadcast([128, 8, 192]), op=ALU.mult),
                 outs=[btok], ins=[btok, gtok])
        x1 = buf[:, :, 128:160]; x2 = buf[:, :, 160:192]
        cb = cos.unsqueeze(1).to_broadcast([128, 8, 32]); sb_ = sin.unsqueeze(1).to_broadcast([128, 8, 32])
        P.dve.op(lambda: nc.vector.tensor_tensor(out=t4[:, 0], in0=x1, in1=cb, op=ALU.mult), outs=[t4tok], ins=[btok, cstok])
        P.dve.op(lambda: nc.vector.tensor_tensor(out=t4[:, 1], in0=x2, in1=sb_, op=ALU.mult), outs=[t4tok], ins=[btok, cstok])
        P.dve.op(lambda: nc.vector.tensor_tensor(out=t4[:, 2], in0=x1, in1=sb_, op=ALU.mult), outs=[t4tok], ins=[btok, cstok])
        P.dve.op(lambda: nc.vector.tensor_tensor(out=t4[:, 3], in0=x2, in1=cb, op=ALU.mult), outs=[t4tok], ins=[btok, cstok])
        P.dve.op(lambda: nc.vector.tensor_tensor(out=x1, in0=t4[:, 0], in1=t4[:, 1], op=ALU.subtract), outs=[btok], ins=[t4tok])
        P.dve.op(lambda: nc.vector.tensor_tensor(out=x2, in0=t4[:, 2], in1=t4[:, 3], op=ALU.add), outs=[btok], ins=[t4tok])

    def even_inproj(self, l):
        P = self.P; nc = P.nc; i = l // 2
        NS, LP, NT = self.NS, self.LP, self.NT
        m = P.mark()
        TT = 512
        Wv = self.w["ev_w_in"][i].rearrange("(c p) f -> p c f", p=128)
        gain, gtok = self.gains["norm_mix"]
        wuq = P.sb([128, 4, 1536], "wuq"); t_wuq = Tok()
        P.q_w.dma(wuq[:], self.w["mla_w_uq"][i].rearrange("(c p) f -> p c f", p=128), outs=[t_wuq])
        wukv = P.sb([128, 2, 2048], "wukv"); t_wukv = Tok()
        P.q_w.dma(wukv[:], self.w["mla_w_ukv"][i].rearrange("(c p) f -> p c f", p=128), outs=[t_wukv])
        wkr = P.sb([128, DC, 64], "wkr"); t_wkr = Tok()
        P.q_w.dma(wkr[:], Wv[:, :, 1280:1344], outs=[t_wkr])
        gcq = P.sb([128, 4], "gcq"); gckv = P.sb([128, 2], "gckv"); t_g = Tok()
        P.q_a.dma(gcq[:], self.w["mla_g_cq"][i].rearrange("(c p) -> p c", p=128), outs=[t_g], allow_slow_non_contiguous=True)
        P.q_a.dma(gckv[:], self.w["mla_g_ckv"][i].rearrange("(c p) -> p c", p=128), outs=[t_g], allow_slow_non_contiguous=True)
        gq = P.sb([128, 192], "gq"); gk = P.sb([128, 192], "gk"); t_gq = Tok()
        P.q_a.dma(gq[:], self.bc_rows(self.w["mla_g_q"], i * 192, 192), outs=[t_gq])
        P.q_a.dma(gk[:], self.bc_rows(self.w["mla_g_k"], i * 192, 192), outs=[t_gq])
        cosT = P.sb([128, NT, 32], "cos"); sinT = P.sb([128, NT, 32], "sin"); t_cs = Tok()
        P.q_a.dma(cosT[:], self.c_rope_cos.rearrange("(j p) f -> p j f", p=128), outs=[t_cs])
        P.q_a.dma(sinT[:], self.c_rope_sin.rearrange("(j p) f -> p j f", p=128), outs=[t_cs])
        xt, xtok = P.sb([128, DC, TT], "evx"), Tok()
        cq, cqtok = P.sb([128, 4, TT], "cq"), Tok()
        ckv, ckvtok = P.sb([128, 2, TT], "ckv"), Tok()
        wst_r = Ring([(P.sb([128, DC, 128], "winst"), Tok()) for _ in range(2)])
        win_r = Ring([(P.sbh([128, DC, 128], "win"), Tok()) for _ in range(2)])
        xb, xbtok = P.sbh([128, DC, TT], "evxb"), Tok()
        ub_r = Ring([(P.sb([128, TT], "ub"), Tok()) for _ in range(2)])
        rstd, rtok = P.sb([128, TT], "rstd"), Tok()
        sq, sqtok = P.sb([128, TT], "sq"), Tok()
        q_sb, qtok = P.sb([128, 8, 192], "q_sb"), Tok()
        k_sb, ktok = P.sb([128, 8, 192], "k_sb"), Tok()
        v_sb, vtok = P.sb([128, 8, 128], "v_sb"), Tok()
        tmp, ttok = P.sb([128, 8, 192], "tmp"), Tok()
        ss, sstok = P.sb([128, 8], "ss"), Tok()
        t4, t4tok = P.sb([128, 4, 8, 32], "t4"), Tok()
        qTn, qTntok = P.sb([128, 8, 128], "qTn"), Tok()
        qTr, qTrtok = P.sb([64, 8, 128], "qTr"), Tok()
        kTn, kTntok = P.sb([128, 8, 128], "kTn"), Tok()
        kTr, kTrtok = P.sb([64, 8, 128], "kTr"), Tok()
        ps01 = self.psring([0, 1])
        pss, psstok = P.psb[7]
        for s in range(NS):
            for t0 in range(0, LP, TT):
                T = min(TT, LP - t0)
                tok0 = s * LP + t0
                htoks = self.toks_h(tok0, T)
                P.q_a.dma(xt[:, :, :T], self.hT[:, :, tok0:tok0 + T].rearrange("c p t -> p c t"), outs=[xtok], ins=htoks)
                self.rms_scale(xt, xtok, T, lambda c: gain[:, l, c:c + 1], gtok, pss, psstok, sq, sqtok, rstd, rtok)
                P.dve.op(lambda: nc.vector.tensor_copy(out=xb[:, 0:8, :T], in_=xt[:, 0:8, :T]), outs=[xbtok], ins=[xtok])
                P.act.op(lambda: nc.scalar.copy(out=xb[:, 8:16, :T], in_=xt[:, 8:16, :T]), outs=[xbtok], ins=[xtok, xbtok])
                def load_w(j):
                    ws_, wst_ = wst_r.next()
                    P.q_w.dma(ws_[:], Wv[:, :, j * 128:(j + 1) * 128], outs=[wst_])
                    w_, wt_ = win_r.next()
                    if j % 2 == 0:
                        P.dve.op(lambda: nc.vector.tensor_copy(out=w_[:], in_=ws_[:]), outs=[wt_], ins=[wst_])
                    else:
                        P.act.op(lambda: nc.scalar.copy(out=w_[:], in_=ws_[:]), outs=[wt_], ins=[wst_])
                    return w_, wt_
                nxt = load_w(0)
                for j in range(10):
                    w_, wt_ = nxt
                    if j + 1 < 10:
                        nxt = load_w(j + 1)
                    ps, ptok = ps01.next()
                    for c in range(DC):
                        with nc.allow_low_precision("bf16 matmul operands, fp32 accumulation"):
                            P.pe.op(lambda: nc.tensor.matmul(ps[:, :T], w_[:, c, :], xb[:, c, :T], start=(c == 0), stop=(c == DC - 1)),
                                    outs=[ptok], ins=[wt_, xbtok])
                    if j < 4:
                        ub, ubtok = ub_r.next()
                        P.act.op(lambda: nc.scalar.copy(out=ub[:, :T], in_=ps[:, :T]), outs=[ubtok], ins=[ptok])
                        P.q_a.dma(self.uT[j, :, tok0:tok0 + T], ub[:, :T], ins=[ubtok])
                    elif j < 8:
                        P.act.op(lambda: nc.scalar.copy(out=cq[:, j - 4, :T], in_=ps[:, :T]), outs=[cqtok], ins=[ptok])
                    else:
                        P.act.op(lambda: nc.scalar.copy(out=ckv[:, j - 8, :T], in_=ps[:, :T]), outs=[ckvtok], ins=[ptok])
                self.rms_scale(cq, cqtok, T, lambda c: gcq[:, c:c + 1], t_g, pss, psstok, sq, sqtok, rstd, rtok, nchunks=4, dim=512)
                self.rms_scale(ckv, ckvtok, T, lambda c: gckv[:, c:c + 1], t_g, pss, psstok, sq, sqtok, rstd, rtok, nchunks=2, dim=256)
                import os
                LV = int(os.environ.get("DBG_INPROJ", "9"))
                for tb in range(T // 128 if LV >= 2 else 0):
                    tsl = slice(tb * 128, (tb + 1) * 128)
                    jblk = (t0 // 128) + tb
                    tokb = tok0 + tb * 128
                    qflat = q_sb[:].rearrange("p h d -> p (h d)")
                    SUB = os.environ.get("DBG_SUB", "q,kv,kr").split(",")
                    for n in range(3 if "q" in SUB else 0):
                        ps, ptok = P.psb[2 + n]
                        for c in range(4):
                            P.pe.op(lambda: nc.tensor.matmul(ps[:, :], cq[:, c, tsl], wuq[:, c, n * 512:(n + 1) * 512], start=(c == 0), stop=(c == 3)),
                                    outs=[ptok], ins=[cqtok, t_wuq])
                        P.act.op(lambda: nc.scalar.copy(out=qflat[:, n * 512:(n + 1) * 512], in_=ps[:, :]), outs=[qtok], ins=[ptok])
                    for n4 in range(4 if "kv" in SUB else 0):
                        ps, ptok = P.psb[5 + (n4 % 2)]
                        for c in range(2):
                            P.pe.op(lambda: nc.tensor.matmul(ps[:, :], ckv[:, c, tsl], wukv[:, c, n4 * 512:(n4 + 1) * 512], start=(c == 0), stop=(c == 1)),
                                    outs=[ptok], ins=[ckvtok, t_wukv])
                        psv = ps[:, :].rearrange("p (h d) -> p h d", h=2)
                        P.act.op(lambda: nc.scalar.copy(out=k_sb[:, 2 * n4:2 * n4 + 2, 0:128], in_=psv[:, :, 0:128]), outs=[ktok], ins=[ptok])
                        P.dve.op(lambda: nc.vector.tensor_copy(out=v_sb[:, 2 * n4:2 * n4 + 2, :], in_=psv[:, :, 128:256]), outs=[vtok], ins=[ptok, ktok])
                    ps, ptok = ps01.next()
                    for c in range(DC if "kr" in SUB else 0):
                        P.pe.op(lambda: nc.tensor.matmul(ps[:, 0:64], xt[:, c, tsl], wkr[:, c, :], start=(c == 0), stop=(c == DC - 1)),
                                outs=[ptok], ins=[xtok, t_wkr])
                    if "kr" in SUB:
                      P.act.op(lambda: nc.scalar.copy(out=k_sb[:, :, 128:192], in_=ps[:, 0:64].unsqueeze(1).to_broadcast([128, 8, 64])),
                             outs=[ktok], ins=[ptok])
                    if LV >= 3:
                        P.q_a.dma(self.Vd[tokb:tokb + 128, :], v_sb[:].rearrange("p h d -> p (h d)"), ins=[vtok])
                    if LV < 4:
                        continue
                    self.headnorm_rope(q_sb, qtok, gq, t_gq, cosT[:, jblk, :], sinT[:, jblk, :], t_cs, tmp, ttok, ss, sstok, t4, t4tok)
                    self.headnorm_rope(k_sb, ktok, gk, t_gq, cosT[:, jblk, :], sinT[:, jblk, :], t_cs, tmp, ttok, ss, sstok, t4, t4tok)
                    if LV < 5:
                        continue
                    for (src, stok_, dn, dntok, dr, drtok, dst) in ((q_sb, qtok, qTn, qTntok, qTr, qTrtok, self.QT),
                                                                  (k_sb, ktok, kTn, kTntok, kTr, kTrtok, self.KT)):
                        for hg in range(2):
                            ps, ptok = ps01.next()
                            for hl in range(4):
                                h = hg * 4 + hl
                                P.pe.op(lambda: nc.tensor.transpose(ps[:, hl * 128:(hl + 1) * 128], src[:, h, 0:128], self.ident[:]),
                                        outs=[ptok], ins=[stok_, self.t_ident])
                            P.act.op(lambda: nc.scalar.copy(out=dn[:, hg * 4:(hg + 1) * 4, :], in_=ps[:, :].rearrange("p (h t) -> p h t", h=4)),
                                     outs=[dntok], ins=[ptok])
                            ps, ptok = ps01.next()
                            for hl in range(4):
                                h = hg * 4 + hl
                                P.pe.op(lambda: nc.tensor.transpose(ps[0:64, hl * 128:(hl + 1) * 128], src[:, h, 128:192], self.ident[:]),
                                        outs=[ptok], ins=[stok_, self.t_ident])
                            P.dve.op(lambda: nc.vector.tensor_copy(out=dr[:, hg * 4:(hg + 1) * 4, :], in_=ps[0:64, :].rearrange("p (h t) -> p h t", h=4)),
                                     outs=[drtok], ins=[ptok])
                        P.q_a.dma(dst[:, 0:128, tokb:tokb + 128].rearrange("h p t -> p h t"), dn[:], ins=[dntok])
                        P.q_a.dma(dst[:, 128:192, tokb:tokb + 128].rearrange("h p t -> p h t"), dr[:], ins=[drtok])
        P.release(m)

    def even_attn(self, l):
        P = self.P; nc = P.nc
        NS, LP, NT = self.NS, self.LP, self.NT
        m = P.mark()
        maskT = P.sb([128, 128], "maskT"); t_mask = Tok()
        P.q_a.dma(maskT[:], self.c_maskT[:, :], outs=[t_mask])
        onesb = P.sbh([128, 128], "onesb"); t_ob = Tok()
        P.dve.op(lambda: nc.vector.tensor_copy(out=onesb[:], in_=self.ones[:]), outs=[t_ob], ins=[self.t_ones])
        sets = Ring([dict(qn=P.sb([128, LP]), qr=P.sb([64, LP]), kn=P.sb([128, LP]), kr=P.sb([64, LP]), v=P.sb([128, NT, 128]),
                          vb=P.sbh([128, NT, 128]), tok=Tok(), vtok=Tok())
                     for _ in range(2)])
        pt_r = Ring([(P.sbh([128, 512], "pt"), Tok()) for _ in range(3)])
        rden, rdtok = P.sb([128, 512], "rden"), Tok()
        yo_r = Ring([(P.sb([128, 512], "yo"), Tok()) for _ in range(2)])
        ps_s = self.psring([0, 1, 2])
        ps_o = self.psring([3, 4]); ps_d = self.psring([5, 6])
        scale = 192.0 ** -0.5
        with nc.allow_low_precision("bf16 P@V and softmax denominators (fp32 scores, fp32 accumulation)"):
            for s in range(NS):
                for h in range(8):
                    S = sets.next(); stok = S["tok"]; vtok = S["vtok"]
                    c0s = s * LP
                    P.q_a.dma(S["qn"][:], self.QT[h, 0:128, c0s:c0s + LP], outs=[stok])
                    P.q_a.dma(S["qr"][:], self.QT[h, 128:192, c0s:c0s + LP], outs=[stok])
                    P.q_a.dma(S["kn"][:], self.KT[h, 0:128, c0s:c0s + LP], outs=[stok])
                    P.q_a.dma(S["kr"][:], self.KT[h, 128:192, c0s:c0s + LP], outs=[stok])
                    P.q_a.dma(S["v"][:], self.Vd[c0s:c0s + LP, h * 128:(h + 1) * 128].rearrange("(b p) d -> p b d", p=128), outs=[stok])
                    P.act.op(lambda: nc.scalar.copy(out=S["vb"][:], in_=S["v"][:]), outs=[vtok], ins=[stok])
                    for q0 in range(0, LP, 512):
                        Tq = min(512, LP - q0)
                        nkb = (q0 + Tq) // 128
                        po, potok = ps_o.next(); pd, pdtok = ps_d.next()
                        def scores(kb):
                            c0 = max(q0, kb * 128); off = c0 - q0; w = q0 + Tq - c0
                            ps, pstok = ps_s.next()
                            ksl = slice(kb * 128, (kb + 1) * 128)
                            P.pe.op(lambda: nc.tensor.matmul(ps[:, :w], S["kn"][:, ksl], S["qn"][:, c0:c0 + w], start=True, stop=False),
                                    outs=[pstok], ins=[stok])
                            P.pe.op(lambda: nc.tensor.matmul(ps[:, :w], S["kr"][:, ksl], S["qr"][:, c0:c0 + w], start=False, stop=True),
                                    outs=[pstok], ins=[stok])
                            pt, pttok = pt_r.next()
                            P.act.op(lambda: nc.scalar.activation(out=pt[:, :w], in_=ps[:, :w], func=AF.Exp, scale=scale), outs=[pttok], ins=[pstok])
                            if kb * 128 >= q0:
                                P.dve.op(lambda: nc.vector.tensor_tensor(out=pt[:, 0:128], in0=pt[:, 0:128], in1=maskT[:], op=ALU.mult),
                                         outs=[pttok], ins=[pttok, t_mask])
                            return pt, pttok, off, w
                        nxt = scores(0)
                        for kb in range(nkb):
                            pt, pttok, off, w = nxt
                            if kb + 1 < nkb:
                                nxt = scores(kb + 1)
                            P.pe.op(lambda: nc.tensor.matmul(po[:, off:off + w], S["vb"][:, kb, :], pt[:, :w], start=(kb == 0), stop=(kb == nkb - 1)),
                                    outs=[potok], ins=[vtok, pttok])
                            P.pe.op(lambda: nc.tensor.matmul(pd[:, off:off + w], onesb[:], pt[:, :w], start=(kb == 0), stop=(kb == nkb - 1)),
                                    outs=[pdtok], ins=[t_ob, pttok])
                        P.dve.op(lambda: nc.vector.reciprocal(out=rden[:, :Tq], in_=pd[:, :Tq]), outs=[rdtok], ins=[pdtok])
                        yo, yotok = yo_r.next()
                        P.dve.op(lambda: nc.vector.tensor_tensor(out=yo[:, :Tq], in0=po[:, :Tq], in1=rden[:, :Tq], op=ALU.mult),
                                 outs=[yotok], ins=[potok, rdtok])
                        P.q_a.dma(self.yE[4 + h, :, c0s + q0:c0s + q0 + Tq], yo[:, :Tq], ins=[yotok])
        P.release(m)

    def even_s5(self, l):
        P = self.P; nc = P.nc; i = l // 2
        NS, LP, NT = self.NS, self.LP, self.NT
        m = P.mark()
        PI = math.pi
        dve = P.dve; act = P.act; pe = P.pe
        V = nc.vector
        tk = Tok("s5setup")
        def sbt(shape, name):
            return P.sb(shape, name)
        ps0, ps0tok = P.psb[0]
        raw = sbt([16, 3, 128], "s5raw")
        P.q_a.dma(raw[:, 0, :], self.w["s5_a_re"][i].rearrange("(gp gl) p -> gp (gl p)", gl=2), outs=[tk])
        P.q_a.dma(raw[:, 1, :], self.w["s5_a_im"][i].rearrange("(gp gl) p -> gp (gl p)", gl=2), outs=[tk])
        ldt = sbt([16, 2], "ldt")
        P.q_a.dma(ldt[:], self.w["s5_log_dt"][i].rearrange("(gp gl) -> gp gl", gl=2), outs=[tk])
        dve.op(lambda: V.tensor_copy(out=raw[:, 2, :].rearrange("g (a b) -> g a b", a=2), in_=ldt[:].unsqueeze(2).to_broadcast([16, 2, 64])),
               outs=[tk], ins=[tk])
        ar = sbt([128, 16], "ar"); ai = sbt([128, 16], "ai"); dt = sbt([128, 16], "dt")
        for k, dst in enumerate((ar, ai, dt)):
            pe.op(lambda: nc.tensor.transpose(ps0[:, k * 16:(k + 1) * 16], raw[:, k, :], self.ident[0:16, 0:16]), outs=[ps0tok], ins=[tk, self.t_ident])
        act.op(lambda: nc.scalar.copy(out=ar[:], in_=ps0[:, 0:16]), outs=[tk], ins=[ps0tok])
        act.op(lambda: nc.scalar.copy(out=ai[:], in_=ps0[:, 16:32]), outs=[tk], ins=[ps0tok])
        act.op(lambda: nc.scalar.activation(out=dt[:], in_=ps0[:, 32:48], func=AF.Exp), outs=[tk], ins=[ps0tok])
        def T16(name):
            return sbt([128, 16], name)
        def tt(out, a, b, op):
            dve.op(lambda: V.tensor_tensor(out=out, in0=a, in1=b, op=op), outs=[tk], ins=[tk])
        def ts(out, a, s1, op0, s2=None, op1=None):
            if op1 is None:
                dve.op(lambda: V.tensor_scalar(out, a, s1, None, op0), outs=[tk], ins=[tk])
            else:
                dve.op(lambda: V.tensor_scalar(out, a, s1, s2, op0, op1), outs=[tk], ins=[tk])
        mag = T16("mag"); ang = T16("ang"); sn = T16("sn"); cs = T16("cs"); tmpa = T16("tmpa"); tmpb = T16("tmpb")
        twopi = T16("twopi")
        dve.op(lambda: V.memset(twopi[:], 2 * PI), outs=[tk], ins=[tk])
        tt(tmpa[:], dt[:], ar[:], ALU.mult)
        act.op(lambda: nc.scalar.activation(out=mag[:], in_=tmpa[:], func=AF.Exp), outs=[tk], ins=[tk])
        tt(ang[:], dt[:], ai[:], ALU.mult)
        for (dst, shift) in ((sn, PI), (cs, PI + PI / 2)):
            ts(tmpa[:], ang[:], shift, ALU.add)
            for mult_ in (16.0, 8.0, 4.0, 2.0, 1.0):
                ts(tmpb[:], tmpa[:], mult_ * 2 * PI, ALU.is_ge, -mult_ * 2 * PI, ALU.mult)
                tt(tmpa[:], tmpa[:], tmpb[:], ALU.add)
            ts(tmpa[:], tmpa[:], -PI, ALU.add, 3.1415925, ALU.min)
            ts(tmpa[:], tmpa[:], -3.1415925, ALU.max)
            act.op(lambda: nc.scalar.activation(out=dst[:], in_=tmpa[:], func=AF.Sin), outs=[tk], ins=[tk])
        lr = T16("lr"); li = T16("li"); lr1 = T16("lr1"); den = T16("den"); cr = T16("cr"); ci = T16("ci")
        tt(lr[:], mag[:], cs[:], ALU.mult); tt(li[:], mag[:], sn[:], ALU.mult)
        ts(lr1[:], lr[:], -1.0, ALU.add)
        tt(den[:], ar[:], ar[:], ALU.mult); tt(tmpa[:], ai[:], ai[:], ALU.mult); tt(den[:], den[:], tmpa[:], ALU.add)
        dve.op(lambda: V.reciprocal(out=den[:], in_=den[:]), outs=[tk], ins=[tk])
        tt(cr[:], lr1[:], ar[:], ALU.mult); tt(tmpa[:], li[:], ai[:], ALU.mult); tt(cr[:], cr[:], tmpa[:], ALU.add); tt(cr[:], cr[:], den[:], ALU.mult)
        tt(ci[:], li[:], ar[:], ALU.mult); tt(tmpa[:], lr1[:], ai[:], ALU.mult); tt(ci[:], ci[:], tmpa[:], ALU.subtract); tt(ci[:], ci[:], den[:], ALU.mult)
        br = sbt([128, 16, 16], "br"); bi = sbt([128, 16, 16], "bi")
        P.q_a.dma(br[:], self.w["s5_b_re"][i].rearrange("(gp gl) p c -> (gl p) gp c", gl=2), outs=[tk], ins=[tk])
        P.q_a.dma(bi[:], self.w["s5_b_im"][i].rearrange("(gp gl) p c -> (gl p) gp c", gl=2), outs=[tk], ins=[tk])
        t3a = sbt([128, 16, 16], "t3a"); t3b = sbt([128, 16, 16], "t3b")
        Bblk = [sbt([128, 16, 32], "Bblk_r"), sbt([128, 16, 32], "Bblk_i")]
        crb = cr[:].unsqueeze(2).to_broadcast([128, 16, 16]); cib = ci[:].unsqueeze(2).to_broadcast([128, 16, 16])
        for k in range(2):
            dve.op(lambda: V.memset(Bblk[k][:], 0.0), outs=[tk], ins=[tk])
        for k, (x0, x1, op) in enumerate(((br, bi, ALU.subtract), (bi, br, ALU.add))):
            tt(t3a[:], x0[:], crb, ALU.mult); tt(t3b[:], x1[:], cib, ALU.mult); tt(t3a[:], t3a[:], t3b[:], op)
            dve.op(lambda: V.tensor_copy(out=Bblk[k][0:64, :, 0:16], in_=t3a[0:64]), outs=[tk], ins=[tk])
            dve.op(lambda: V.tensor_copy(out=Bblk[k][64:128, :, 16:32], in_=t3a[64:128]), outs=[tk], ins=[tk])
        WbT = [sbt([32, 16, 128], "WbT_r"), sbt([32, 16, 128], "WbT_i")]
        for k in range(2):
            for g4 in range(4):
                for gq in range(4):
                    gp = g4 * 4 + gq
                    pe.op(lambda: nc.tensor.transpose(ps0[0:32, gq * 128:(gq + 1) * 128], Bblk[k][:, gp, :], self.ident[:]), outs=[ps0tok], ins=[tk, self.t_ident])
                act.op(lambda: nc.scalar.copy(out=WbT[k][:, g4 * 4:(g4 + 1) * 4, :], in_=ps0[0:32, :].rearrange("p (g s) -> p g s", g=4)), outs=[tk], ins=[ps0tok])
        Cblk = [sbt([128, 16, 32], "Cblk_r"), sbt([128, 16, 32], "Cblk_i")]
        craw = sbt([128, 128], "craw")
        for k, nm in enumerate(("s5_c_re", "s5_c_im")):
            dve.op(lambda: V.memset(Cblk[k][:], 0.0), outs=[tk], ins=[tk])
            cv = self.w[nm][i].rearrange("g c p -> (g c) p")
            for q4 in range(4):
                P.q_a.dma(craw[:, 0:64], cv[q4 * 128:(q4 + 1) * 128, :], outs=[tk], ins=[tk])
                P.q_a.dma(craw[:, 64:128], cv[q4 * 128:(q4 + 1) * 128, :], outs=[tk], ins=[tk])
                pe.op(lambda: nc.tensor.transpose(ps0[:, 0:128], craw[:], self.ident[:]), outs=[ps0tok], ins=[tk, self.t_ident])
                psv = ps0[:, 0:128].rearrange("p (gq gl c) -> p gq gl c", gq=4, gl=2)
                sc = 1.0 if k == 0 else -1.0
                act.op(lambda: nc.scalar.mul(out=Cblk[k][0:64, q4 * 4:(q4 + 1) * 4, 0:16], in_=psv[0:64, :, 0, :], mul=sc), outs=[tk], ins=[ps0tok])
                act.op(lambda: nc.scalar.mul(out=Cblk[k][64:128, q4 * 4:(q4 + 1) * 4, 16:32], in_=psv[64:128, :, 1, :], mul=sc), outs=[tk], ins=[ps0tok])
        dsk = sbt([32, 16], "dsk")
        P.q_a.dma(dsk[:], self.w["s5_d"][i].rearrange("(gp r) -> r gp", r=32), outs=[tk], ins=[tk], allow_slow_non_contiguous=True)
        NK = max(1, (LP - 1).bit_length())
        wr = sbt([128, NK, 16], "wr"); wi = sbt([128, NK, 16], "wi")
        dve.op(lambda: V.tensor_copy(out=wr[:, 0, :], in_=cs[:]), outs=[tk], ins=[tk])
        ts(wi[:, 0, :], sn[:], -1.0, ALU.mult)
        for k in range(1, NK):
            tt(tmpa[:], wr[:, k - 1, :], wr[:, k - 1, :], ALU.mult); tt(tmpb[:], wi[:, k - 1, :], wi[:, k - 1, :], ALU.mult)
            tt(wr[:, k, :], tmpa[:], tmpb[:], ALU.subtract)
            tt(tmpa[:], wr[:, k - 1, :], wi[:, k - 1, :], ALU.mult)
            ts(wi[:, k, :], tmpa[:], 2.0, ALU.mult)
        Er = sbt([128, LP], "Er"); Ei = sbt([128, LP], "Ei"); tmpE = sbt([128, LP], "tmpE"); tE = Tok("E")
        u_r = Ring([(sbt([32, LP], "u_gp"), Tok()) for _ in range(2)])
        vr = sbt([128, LP], "vr"); vi = sbt([128, LP], "vi"); tv = Tok("v")
        zr = sbt([128, LP], "zr"); zi = sbt([128, LP], "zi"); tz = Tok("z")
        ta = sbt([128, 512], "ta"); tb_ = sbt([128, 512], "tb"); tta = Tok(); ttb = Tok()
        y_r = Ring([(sbt([32, LP], "y_gp"), Tok()) for _ in range(2)])
        ps_b = self.psring([1, 2, 3, 4]); ps_y = self.psring([5, 6])
        for gp in range(16):
            dve.op(lambda: V.memset(Er[:, 0:1], 1.0), outs=[tE], ins=[tk])
            dve.op(lambda: V.memset(Ei[:, 0:1], 0.0), outs=[tE], ins=[tk])
            n = 1; k = 0
            while n < LP:
                cnt = min(n, LP - n)
                wrk = wr[:, k, gp:gp + 1]; wik = wi[:, k, gp:gp + 1]
                dve.op(lambda: V.tensor_scalar(tmpE[:, 0:cnt], Ei[:, 0:cnt], wik, None, ALU.mult), outs=[tE], ins=[tE, tk])
                dve.op(lambda: V.scalar_tensor_tensor(out=Er[:, n:n + cnt], in0=Er[:, 0:cnt], scalar=wrk, in1=tmpE[:, 0:cnt], op0=ALU.mult, op1=ALU.subtract), outs=[tE], ins=[tE, tk])
                dve.op(lambda: V.tensor_scalar(tmpE[:, 0:cnt], Ei[:, 0:cnt], wrk, None, ALU.mult), outs=[tE], ins=[tE, tk])
                dve.op(lambda: V.scalar_tensor_tensor(out=Ei[:, n:n + cnt], in0=Er[:, 0:cnt], scalar=wik, in1=tmpE[:, 0:cnt], op0=ALU.mult, op1=ALU.add), outs=[tE], ins=[tE, tk])
                n += cnt; k += 1
            for s in range(NS):
                c0s = s * LP
                u, utok = u_r.next()
                P.q_a.dma(u[:], self.uT[gp // 4, (gp % 4) * 32:(gp % 4) * 32 + 32, c0s:c0s + LP], outs=[utok])
                for q0 in range(0, LP, 512):
                    Tq = min(512, LP - q0); sl = slice(q0, q0 + Tq)
                    pbr, pbrt = ps_b.next(); pbi, pbit = ps_b.next()
                    pe.op(lambda: nc.tensor.matmul(pbr[:, :Tq], WbT[0][:, gp, :], u[:, sl], start=True, stop=True), outs=[pbrt], ins=[tk, utok])
                    pe.op(lambda: nc.tensor.matmul(pbi[:, :Tq], WbT[1][:, gp, :], u[:, sl], start=True, stop=True), outs=[pbit], ins=[tk, utok])
                    dve.op(lambda: V.tensor_tensor(out=ta[:, :Tq], in0=pbr[:, :Tq], in1=Er[:, sl], op=ALU.mult), outs=[tta], ins=[pbrt, tE])
                    dve.op(lambda: V.tensor_tensor(out=tb_[:, :Tq], in0=pbi[:, :Tq], in1=Ei[:, sl], op=ALU.mult), outs=[ttb], ins=[pbit, tE])
                    dve.op(lambda: V.tensor_tensor(out=vr[:, sl], in0=ta[:, :Tq], in1=tb_[:, :Tq], op=ALU.subtract), outs=[tv], ins=[tta, ttb])
                    dve.op(lambda: V.tensor_tensor(out=ta[:, :Tq], in0=pbi[:, :Tq], in1=Er[:, sl], op=ALU.mult), outs=[tta], ins=[pbit, tE])
                    dve.op(lambda: V.tensor_tensor(out=tb_[:, :Tq], in0=pbr[:, :Tq], in1=Ei[:, sl], op=ALU.mult), outs=[ttb], ins=[pbrt, tE])
                    dve.op(lambda: V.tensor_tensor(out=vi[:, sl], in0=ta[:, :Tq], in1=tb_[:, :Tq], op=ALU.add), outs=[tv], ins=[tta, ttb])
                rho = mag[:, gp:gp + 1].to_broadcast([128, LP])
                dve.op(lambda: V.tensor_tensor_scan(out=zr[:], data0=rho, data1=vr[:], initial=0.0, op0=ALU.mult, op1=ALU.add), outs=[tz], ins=[tv, tk])
                dve.op(lambda: V.tensor_tensor_scan(out=zi[:], data0=rho, data1=vi[:], initial=0.0, op0=ALU.mult, op1=ALU.add), outs=[tz], ins=[tv, tk])
                dve.op(lambda: V.tensor_tensor(out=vr[:], in0=Er[:], in1=zr[:], op=ALU.mult), outs=[tv], ins=[tz, tE])
                dve.op(lambda: V.tensor_tensor(out=tmpE[:], in0=Ei[:], in1=zi[:], op=ALU.mult), outs=[tv], ins=[tz, tE])
                dve.op(lambda: V.tensor_tensor(out=vr[:], in0=vr[:], in1=tmpE[:], op=ALU.add), outs=[tv], ins=[tv])
                dve.op(lambda: V.tensor_tensor(out=vi[:], in0=Er[:], in1=zi[:], op=ALU.mult), outs=[tv], ins=[tz, tE])
                dve.op(lambda: V.tensor_tensor(out=tmpE[:], in0=Ei[:], in1=zr[:], op=ALU.mult), outs=[tv], ins=[tz, tE])
                dve.op(lambda: V.tensor_tensor(out=vi[:], in0=vi[:], in1=tmpE[:], op=ALU.subtract), outs=[tv], ins=[tv])
                y, ytok = y_r.next()
                for q0 in range(0, LP, 512):
                    Tq = min(512, LP - q0); sl = slice(q0, q0 + Tq)
                    py, pyt = ps_y.next()
                    pe.op(lambda: nc.tensor.matmul(py[0:32, :Tq], Cblk[0][:, gp, :], vr[:, sl], start=True, stop=False), outs=[pyt], ins=[tk, tv])
                    pe.op(lambda: nc.tensor.matmul(py[0:32, :Tq], Cblk[1][:, gp, :], vi[:, sl], start=False, stop=True), outs=[pyt], ins=[tk, tv])
                    dve.op(lambda: V.scalar_tensor_tensor(out=y[:, sl], in0=u[:, sl], scalar=dsk[:, gp:gp + 1], in1=py[0:32, :Tq], op0=ALU.mult, op1=ALU.add),
                           outs=[ytok], ins=[pyt, utok, tk])
                P.q_a.dma(self.yE[gp // 4, (gp % 4) * 32:(gp % 4) * 32 + 32, c0s:c0s + LP], y[:], ins=[ytok])
        P.release(m)

    def out_proj(self, Wd, NCH, ymix, ytoks, T, tok0, htoks, wst_r, w_r, psd_r, hres_r, ymb, ymbtok):
        P = self.P; nc = P.nc
        Wv = Wd.rearrange("(c p) d -> p c d", p=128)
        hc = NCH // 2
        P.dve.op(lambda: nc.vector.tensor_copy(out=ymb[:, 0:hc, :T], in_=ymix[:, 0:hc, :T]), outs=[ymbtok], ins=ytoks)
        P.act.op(lambda: nc.scalar.copy(out=ymb[:, hc:NCH, :T], in_=ymix[:, hc:NCH, :T]), outs=[ymbtok], ins=ytoks + [ymbtok])
        def load_d(dc):
            ds_, dst_ = wst_r.next()
            P.q_w.dma(ds_[:, :NCH, :], Wv[:, :, dc * 128:(dc + 1) * 128], outs=[dst_])
            d_, dt_ = w_r.next()
            if dc % 2 == 0:
                P.act.op(lambda: nc.scalar.copy(out=d_[:, :NCH, :], in_=ds_[:, :NCH, :]), outs=[dt_], ins=[dst_])
            else:
                P.dve.op(lambda: nc.vector.tensor_copy(out=d_[:, :NCH, :], in_=ds_[:, :NCH, :]), outs=[dt_], ins=[dst_])
            return d_, dt_
        with nc.allow_low_precision("bf16 matmul operands, fp32 accumulation"):
            nxt = load_d(0)
            for dc in range(DC):
                d_, dt_ = nxt
                if dc + 1 < DC:
                    nxt = load_d(dc + 1)
                hres, hrt = hres_r.next()
                P.q_a.dma(hres[:, :T], self.hT[dc, :, tok0:tok0 + T], outs=[hrt], ins=htoks)
                psd, psdt = psd_r.next()
                for c in range(NCH):
                    P.pe.op(lambda: nc.tensor.matmul(psd[:, :T], d_[:, c, :], ymb[:, c, :T], start=(c == 0), stop=(c == NCH - 1)),
                            outs=[psdt], ins=[dt_, ymbtok])
                P.dve.op(lambda: nc.vector.tensor_tensor(out=hres[:, :T], in0=psd[:, :T], in1=hres[:, :T], op=ALU.add),
                         outs=[hrt], ins=[psdt, hrt])
                P.q_a.dma(self.hT[dc, :, tok0:tok0 + T], hres[:, :T], ins=[hrt], outs=htoks)

    def even_out(self, l):
        P = self.P; nc = P.nc; i = l // 2
        NS, LP, NT, NTOK = self.NS, self.LP, self.NT, self.NTOK
        m = P.mark()
        TT = 512
        V = nc.vector
        wglu = P.sb([128, 4, 512], "wglu"); t_wglu = Tok()
        P.q_w.dma(wglu[:], self.w["s5_w_glu"][i].rearrange("(c p) f -> p c f", p=128), outs=[t_wglu])
        ya, yatok = P.sb([128, 4, TT], "ya"), Tok()
        yg, ygtok = P.sb([128, 4, TT], "yg"), Tok()
        ymix = P.sb([128, 12, TT], "ymix"); ymtok = Tok(); ybtok = Tok()
        sg_r = Ring([(P.sb([128, TT], "sg"), Tok()) for _ in range(2)])
        wst_r = Ring([(P.sb([128, 12, 128], "wost"), Tok()) for _ in range(2)])
        w_r = Ring([(P.sbh([128, 12, 128], "wo"), Tok()) for _ in range(2)])
        ymb = P.sbh([128, 12, TT], "ymb"); ymbtok = Tok()
        hres_r = Ring([(P.sb([128, TT], "hres"), Tok()) for _ in range(2)])
        psg_r = self.psring([0, 1]); psd_r = self.psring([2, 3])
        C0 = math.sqrt(2.0 / math.pi)
        for t0 in range(0, NTOK, TT):
            T = min(TT, NTOK - t0)
            htoks = self.toks_h(t0, T)
            P.q_a.dma(ya[:, :, :T], self.yE[0:4, :, t0:t0 + T].rearrange("c p t -> p c t"), outs=[yatok])
            P.q_a.dma(ymix[:, 4:12, :T], self.yE[4:12, :, t0:t0 + T].rearrange("c p t -> p c t"), outs=[ybtok])
            P.dve.op(lambda: V.tensor_tensor(out=yg[:, :, :T], in0=ya[:, :, :T], in1=ya[:, :, :T], op=ALU.mult), outs=[ygtok], ins=[yatok])
            P.dve.op(lambda: V.tensor_scalar(yg[:, :, :T], yg[:, :, :T], 0.044715, 1.0, ALU.mult, ALU.add), outs=[ygtok], ins=[ygtok])
            P.dve.op(lambda: V.tensor_tensor(out=yg[:, :, :T], in0=yg[:, :, :T], in1=ya[:, :, :T], op=ALU.mult), outs=[ygtok], ins=[ygtok, yatok])
            P.act.op(lambda: nc.scalar.activation(out=yg[:, :, :T], in_=yg[:, :, :T], func=AF.Tanh, scale=C0), outs=[ygtok], ins=[ygtok])
            P.dve.op(lambda: V.tensor_scalar(yg[:, :, :T], yg[:, :, :T], 1.0, 0.5, ALU.add, ALU.mult), outs=[ygtok], ins=[ygtok])
            P.dve.op(lambda: V.tensor_tensor(out=yg[:, :, :T], in0=yg[:, :, :T], in1=ya[:, :, :T], op=ALU.mult), outs=[ygtok], ins=[ygtok, yatok])
            for jc in range(4):
                ps, ptok = psg_r.next()
                for c in range(4):
                    P.pe.op(lambda: nc.tensor.matmul(ps[:, :T], wglu[:, c, jc * 128:(jc + 1) * 128], yg[:, c, :T], start=(c == 0), stop=(c == 3)),
                            outs=[ptok], ins=[t_wglu, ygtok])
                sg, sgt = sg_r.next()
                P.act.op(lambda: nc.scalar.activation(out=sg[:, :T], in_=ps[:, :T], func=AF.Sigmoid), outs=[sgt], ins=[ptok])
                P.dve.op(lambda: V.tensor_tensor(out=ymix[:, jc, :T], in0=yg[:, jc, :T], in1=sg[:, :T], op=ALU.mult), outs=[ymtok], ins=[ygtok, sgt])
            self.out_proj(self.w["ev_w_out"][i], 12, ymix, [ymtok, ybtok], T, t0, htoks, wst_r, w_r, psd_r, hres_r, ymb, ymbtok)
        P.release(m)


    def odd_mixer(self, l):
        import os
        st = os.environ.get("DBG_STAGES", "inproj,attn,gdn,out").split(",")
        if "inproj" in st:
            self.odd_inproj(l)
        if "attn" in st:
            self.odd_sb(l)
        if "gdn" in st:
            self.odd_gdn(l)
        if "out" in st:
            self.odd_out(l)

    def odd_inproj(self, l):
        P = self.P; nc = P.nc; i = l // 2
        NS, LP, NT = self.NS, self.LP, self.NT
        m = P.mark()
        TT = 512
        Wv = self.w["od_w_in"][i].rearrange("(c p) f -> p c f", p=128)
        gain, gtok = self.gains["norm_mix"]
        xt, xtok = P.sb([128, DC, TT], "odx"), Tok()
        xb, xbtok = P.sbh([128, DC, TT], "odxb"), Tok()
        wst_r = Ring([(P.sb([128, DC, 128], "wst"), Tok()) for _ in range(3)])
        win_r = Ring([(P.sbh([128, DC, 128], "win"), Tok()) for _ in range(3)])
        wvst, wvstt = P.sb([128, DC, 512], "wsvst"), Tok()
        wv, wvt = P.sbh([128, DC, 512], "wsv"), Tok()
        wab = P.sb([128, DC, 16], "wab"); t_wab = Tok()
        P.q_w.dma(wab[:], Wv[:, :, 6144:6160], outs=[t_wab])
        ob_r = Ring([(P.sb([128, TT], "ob"), Tok()) for _ in range(3)])
        rstd, rtok = P.sb([128, TT], "rstd"), Tok()
        sq, sqtok = P.sb([128, TT], "sq"), Tok()
        ps01 = self.psring([0, 1, 2]); psv_r = self.psring([3, 4])
        pss, psstok = P.psb[7]
        plan = []
        for h in range(8):
            plan.append((self.sbQ, h, h * 128))
        for h in range(8):
            plan.append((self.sbK, h, 1024 + h * 128))
        for c in range(24):
            plan.append((self.gT, c, 3072 + c * 128))
        for h in range(8):
            plan.append((self.gzT, h, 6160 + h * 128))
        with nc.allow_low_precision("bf16 matmul operands, fp32 accumulation"):
            for s in range(NS):
                for t0 in range(0, LP, TT):
                    T = min(TT, LP - t0)
                    tok0 = s * LP + t0
                    htoks = self.toks_h(tok0, T)
                    P.q_a.dma(xt[:, :, :T], self.hT[:, :, tok0:tok0 + T].rearrange("c p t -> p c t"), outs=[xtok], ins=htoks)
                    self.rms_scale(xt, xtok, T, lambda c: gain[:, l, c:c + 1], gtok, pss, psstok, sq, sqtok, rstd, rtok)
                    P.dve.op(lambda: nc.vector.tensor_copy(out=xb[:, 0:8, :T], in_=xt[:, 0:8, :T]), outs=[xbtok], ins=[xtok])
                    P.act.op(lambda: nc.scalar.copy(out=xb[:, 8:16, :T], in_=xt[:, 8:16, :T]), outs=[xbtok], ins=[xtok])
                    def load_w(j):
                        ws_, wst_ = wst_r.next()
                        c0 = plan[j][2]
                        P.q_w.dma(ws_[:], Wv[:, :, c0:c0 + 128], outs=[wst_])
                        w_, wt_ = win_r.next()
                        if j % 2 == 0:
                            P.dve.op(lambda: nc.vector.tensor_copy(out=w_[:], in_=ws_[:]), outs=[wt_], ins=[wst_])
                        else:
                            P.act.op(lambda: nc.scalar.copy(out=w_[:], in_=ws_[:]), outs=[wt_], ins=[wst_])
                        return w_, wt_
                    pend = [load_w(0), load_w(1)]
                    for j in range(len(plan)):
                        w_, wt_ = pend.pop(0)
                        if j + 2 < len(plan):
                            pend.append(load_w(j + 2))
                        ps, ptok = ps01.next()
                        for c in range(DC):
                            P.pe.op(lambda: nc.tensor.matmul(ps[:, :T], w_[:, c, :], xb[:, c, :T], start=(c == 0), stop=(c == DC - 1)),
                                    outs=[ptok], ins=[wt_, xbtok])
                        ob, obtok = ob_r.next()
                        if j % 2 == 0:
                            P.act.op(lambda: nc.scalar.copy(out=ob[:, :T], in_=ps[:, :T]), outs=[obtok], ins=[ptok])
                        else:
                            P.dve.op(lambda: nc.vector.tensor_copy(out=ob[:, :T], in_=ps[:, :T]), outs=[obtok], ins=[ptok])
                        P.q_a.dma(plan[j][0][plan[j][1], :, tok0:tok0 + T], ob[:, :T], ins=[obtok])
                    ps, ptok = ps01.next()
                    for c in range(DC):
                        P.pe.op(lambda: nc.tensor.matmul(ps[0:16, :T], wab[:, c, :], xt[:, c, :T], start=(c == 0), stop=(c == DC - 1)),
                                outs=[ptok], ins=[t_wab, xtok])
                    ob, obtok = ob_r.next()
                    P.act.op(lambda: nc.scalar.copy(out=ob[0:16, :T], in_=ps[0:16, :T]), outs=[obtok], ins=[ptok])
                    P.q_a.dma(self.gabT[:, tok0:tok0 + T], ob[0:16, :T], ins=[obtok])
                    for n in range(2):
                        P.q_w.dma(wvst[:], Wv[:, :, 2048 + n * 512:2048 + (n + 1) * 512], outs=[wvstt])
                        P.act.op(lambda: nc.scalar.copy(out=wv[:, 0:8, :], in_=wvst[:, 0:8, :]), outs=[wvt], ins=[wvstt])
                        P.dve.op(lambda: nc.vector.tensor_copy(out=wv[:, 8:16, :], in_=wvst[:, 8:16, :]), outs=[wvt], ins=[wvstt, wvt])
                        for tb in range(T // 128):
                            ps, ptok = psv_r.next()
                            for c in range(DC):
                                P.pe.op(lambda: nc.tensor.matmul(ps[:, :], xb[:, c, tb * 128:(tb + 1) * 128], wv[:, c, :], start=(c == 0), stop=(c == DC - 1)),
                                        outs=[ptok], ins=[wvt, xbtok])
                            ob, obtok = ob_r.next()
                            P.act.op(lambda: nc.scalar.copy(out=ob[:, :], in_=ps[:, :]), outs=[obtok], ins=[ptok])
                            P.q_a.dma(self.sbV[tok0 + tb * 128:tok0 + (tb + 1) * 128, n * 512:(n + 1) * 512], ob[:, :], ins=[obtok])
        P.release(m)

    def load_const(self, dram_ap, shape, name):
        t = self.P.sb(shape, name); tok = Tok(name)
        self.P.q_a.dma(t[:], dram_ap, outs=[tok])
        return t, tok

    def odd_sb(self, l):
        P = self.P; nc = P.nc
        NS, LP, NT = self.NS, self.LP, self.NT
        m = P.mark()
        V = nc.vector
        maskS, t_ms = self.load_const(self.c_maskS[:, :], [128, 128], "maskS")
        Uincl, t_ui = self.load_const(self.c_maskL[:, :], [128, 128], "Uincl")
        zerob = P.sbh([128, 128], "zerob"); t_z = Tok()
        P.dve.op(lambda: V.memset(zerob[:], 0.0), outs=[t_z])
        sets = Ring([dict(q=P.sb([128, LP]), k=P.sb([128, LP]), nk=P.sb([128, LP]), v=P.sb([128, NT, 128]), vb=P.sbh([128, NT, 128]),
                          qb=P.sbh([128, 512]), tok=Tok(), vtok=Tok()) for _ in range(2)])
        ez_r = Ring([(P.sb([128, 512], "ez"), Tok()) for _ in range(2)])
        sp_r = Ring([(P.sb([128, 512], "sp"), Tok()) for _ in range(2)])
        wt_r = Ring([(P.sbh([128, 512], "wt"), Tok()) for _ in range(2)])
        A, Atok = P.sb([128, 512], "A"), Tok()
        yo_r = Ring([(P.sb([128, 512], "yo"), Tok()) for _ in range(2)])
        ps_z = self.psring([0, 1]); ps_e = self.psring([2, 3]); ps_o = self.psring([4, 5])
        scale = 128.0 ** -0.5
        with nc.allow_low_precision("bf16 weights@V (fp32 scores and log-sums, fp32 accumulation)"):
            for s in range(NS):
                for h in range(8):
                    S = sets.next(); stok = S["tok"]; vtok = S["vtok"]
                    c0s = s * LP
                    P.q_a.dma(S["q"][:], self.sbQ[h, :, c0s:c0s + LP], outs=[stok])
                    P.q_a.dma(S["k"][:], self.sbK[h, :, c0s:c0s + LP], outs=[stok])
                    P.q_a.dma(S["v"][:], self.sbV[c0s:c0s + LP, h * 128:(h + 1) * 128].rearrange("(b p) d -> p b d", p=128), outs=[stok])
                    P.act.op(lambda: nc.scalar.mul(out=S["q"][:], in_=S["q"][:], mul=scale), outs=[stok], ins=[stok])
                    P.act.op(lambda: nc.scalar.mul(out=S["nk"][:], in_=S["k"][:], mul=-1.0), outs=[stok], ins=[stok])
                    P.dve.op(lambda: V.tensor_copy(out=S["vb"][:], in_=S["v"][:]), outs=[vtok], ins=[stok])
                    P.dve.op(lambda: V.memset(S["qb"][:], 0.0), outs=[vtok], ins=[vtok])
                    for q0 in range(0, LP, 512):
                        Tq = min(512, LP - q0)
                        nkb = (q0 + Tq) // 128
                        po, potok = ps_o.next()
                        P.pe.op(lambda: nc.tensor.matmul(po[:, :Tq], zerob[:], S["qb"][:, :Tq], start=True, stop=False),
                                outs=[potok], ins=[t_z, vtok])
                        P.dve.op(lambda: V.memset(A[:, :Tq], 0.0), outs=[Atok])
                        kbs = list(range(nkb - 1, -1, -1))
                        def geom(kb):
                            c0 = max(q0, kb * 128)
                            return c0, c0 - q0, q0 + Tq - c0, (kb * 128 >= q0), slice(kb * 128, (kb + 1) * 128)
                        def stage1(kb):
                            c0, off, w, diag, ksl = geom(kb)
                            pz, pztok = ps_z.next()
                            P.pe.op(lambda: nc.tensor.matmul(pz[:, :w], S["k"][:, ksl], S["q"][:, c0:c0 + w], start=True, stop=True),
                                    outs=[pztok], ins=[stok])
                            ez, eztok = ez_r.next()
                            P.act.op(lambda: nc.scalar.activation(out=ez[:, :w], in_=pz[:, :w], func=AF.Exp), outs=[eztok], ins=[pztok])
                            sp, sptok = sp_r.next()
                            P.act.op(lambda: nc.scalar.activation(out=sp[:, :w], in_=ez[:, :w], func=AF.Ln, bias=self.ones[:, 0:1]),
                                     outs=[sptok], ins=[eztok, self.t_ones])
                            if diag:
                                P.dve.op(lambda: V.tensor_tensor(out=sp[:, 0:128], in0=sp[:, 0:128], in1=maskS[:], op=ALU.mult),
                                         outs=[sptok], ins=[sptok, t_ms])
                            return sp, sptok
                        def stage2(kb, sp, sptok, first):
                            c0, off, w, diag, ksl = geom(kb)
                            pe_, petok = ps_e.next()
                            P.pe.op(lambda: nc.tensor.matmul(pe_[:, :w], Uincl[:], sp[:, :w], start=True, stop=False), outs=[petok], ins=[t_ui, sptok])
                            if not first:
                                P.pe.op(lambda: nc.tensor.matmul(pe_[:, :w], self.ones[:], A[:, off:off + w], start=False, stop=False),
                                        outs=[petok], ins=[self.t_ones, Atok])
                            P.pe.op(lambda: nc.tensor.matmul(pe_[:, :w], S["nk"][:, ksl], S["q"][:, c0:c0 + w], start=False, stop=True),
                                    outs=[petok], ins=[stok])
                            wt, wttok = wt_r.next()
                            P.act.op(lambda: nc.scalar.activation(out=wt[:, :w], in_=pe_[:, :w], func=AF.Exp, scale=-1.0), outs=[wttok], ins=[petok])
                            if diag:
                                P.dve.op(lambda: V.tensor_tensor(out=wt[:, 0:128], in0=wt[:, 0:128], in1=maskS[:], op=ALU.mult),
                                         outs=[wttok], ins=[wttok, t_ms])
                            if kb > 0:
                                P.dve.op(lambda: V.tensor_tensor(out=A[:, off:off + w], in0=A[:, off:off + w], in1=sp[:, :w], op=ALU.add),
                                         outs=[Atok], ins=[Atok, sptok])
                            return wt, wttok
                        def stage3(kb, wt, wttok):
                            c0, off, w, diag, ksl = geom(kb)
                            P.pe.op(lambda: nc.tensor.matmul(po[:, off:off + w], S["vb"][:, kb, :], wt[:, :w], start=False, stop=(kb == 0)),
                                    outs=[potok], ins=[vtok, wttok])
                        s1 = stage1(kbs[0])
                        pend3 = None
                        for idx, kb in enumerate(kbs):
                            cur1 = s1
                            if idx + 1 < len(kbs):
                                s1 = stage1(kbs[idx + 1])
                            w2 = stage2(kb, cur1[0], cur1[1], idx == 0)
                            if pend3 is not None:
                                stage3(*pend3)
                            pend3 = (kb, w2[0], w2[1])
                        stage3(*pend3)
                        yo, yotok = yo_r.next()
                        P.act.op(lambda: nc.scalar.copy(out=yo[:, :Tq], in_=po[:, :Tq]), outs=[yotok], ins=[potok])
                        P.q_a.dma(self.yO[h, :, c0s + q0:c0s + q0 + Tq], yo[:, :Tq], ins=[yotok])
        P.release(m)

    def odd_out(self, l):
        P = self.P; nc = P.nc; i = l // 2
        NTOK = self.NTOK
        m = P.mark()
        TT = 512
        ymix = P.sb([128, 16, TT], "ymix"); ytok = Tok()
        wst_r = Ring([(P.sb([128, 16, 128], "wost"), Tok()) for _ in range(2)])
        w_r = Ring([(P.sbh([128, 16, 128], "wo"), Tok()) for _ in range(2)])
        ymb = P.sbh([128, 16, TT], "ymb"); ymbtok = Tok()
        hres_r = Ring([(P.sb([128, TT], "hres"), Tok()) for _ in range(2)])
        psd_r = self.psring([2, 3])
        for t0 in range(0, NTOK, TT):
            T = min(TT, NTOK - t0)
            htoks = self.toks_h(t0, T)
            P.q_a.dma(ymix[:, :, :T], self.yO[:, :, t0:t0 + T].rearrange("c p t -> p c t"), outs=[ytok])
            self.out_proj(self.w["od_w_out"][i], 16, ymix, [ytok], T, t0, htoks, wst_r, w_r, psd_r, hres_r, ymb, ymbtok)
        P.release(m)
    def odd_gdn(self, l):
        P = self.P; nc = P.nc; i = l // 2
        NS, LP, NT = self.NS, self.LP, self.NT
        m = P.mark()
        V = nc.vector
        dve = P.dve; act = P.act; pe = P.pe
        maskI, t_mi = self.load_const(self.c_maskT[:, :], [128, 128], "maskI")
        maskSt, t_mst = self.load_const(self.c_maskS[:, :], [128, 128], "maskSt")
        sel, t_sel = self.load_const(self.c_sel[:, :], [8, 1024], "sel")
        tk = Tok("gdnsetup")
        cw = P.sb([128, 24, 4], "cw")
        for j in range(4):
            P.q_a.dma(cw[:, :, j], self.w["gdn_conv"][i, j].rearrange("(c p) -> p c", p=128), outs=[tk], allow_slow_non_contiguous=True)
        nA = P.sb([8, 1], "nA"); dtb = P.sb([8, 1], "dtb"); gout = P.sb([128, 1], "gout")
        P.q_a.dma(nA[:], self.w["gdn_a_log"][i].rearrange("(p o) -> p o", o=1), outs=[tk])
        P.q_a.dma(dtb[:], self.w["gdn_dt_bias"][i].rearrange("(p o) -> p o", o=1), outs=[tk])
        P.q_a.dma(gout[:], self.w["gdn_g_out"][i].rearrange("(p o) -> p o", o=1), outs=[tk])
        act.op(lambda: nc.scalar.activation(out=nA[:], in_=nA[:], func=AF.Exp), outs=[tk], ins=[tk])
        dve.op(lambda: V.tensor_scalar(nA[:], nA[:], -1.0, None, ALU.mult), outs=[tk], ins=[tk])
        def big(name):
            return P.sb([128, LP], name)
        g_ = big("g"); be = big("beta"); Gc = big("Gc"); tg = Tok("g")
        cols = P.sb([128, NT, 16], "cols"); tcols = Tok("cols")
        qT = big("qT"); kT = big("kT"); vT = big("vT"); qdT = big("qdT"); cv = big("cv")
        tq = Tok("q"); tkk = Tok("k"); tv = Tok("v"); tqd = Tok("qd"); tcv = Tok("cv")
        Grow = big("Grow"); eG = big("eG"); Brow = big("Brow"); tGr = Tok("Grow")
        k_tok = P.sb([128, NT, 128], "k_tok"); v_tok = P.sb([128, NT, 128], "v_tok"); tkv = [Tok("kvtok%d" % c) for c in range(NT)]
        u_tok = P.sb([128, NT, 128], "u_tok"); wT = P.sb([128, NT, 128], "wT"); attnT = P.sb([128, NT, 128], "attnT"); kd_tok = P.sb([128, NT, 128], "kd_tok")
        tpre = [Tok("pre%d" % c) for c in range(NT)]
        zT = big("zT"); tz = Tok("z")
        KI = 4
        slots = []
        for k_ in range(KI):
            R = {}
            for nm in ("dT", "dI", "aa", "Xu", "Xw"):
                R[nm] = (P.sb([128, 128], nm), Tok())
            R["sc"] = (P.sb([128, 4], "sc"), Tok())
            R["Mp"] = [P.sb([128, 128], "Mp") for _ in range(7)]; R["tM"] = [Tok() for _ in range(7)]
            R["Np"] = [P.sb([128, 128], "Np") for _ in range(6)]; R["tN"] = [Tok() for _ in range(6)]
            R["Y"] = [(P.sb([128, 128], "Y"), Tok()) for _ in range(2)]
            R["bA"] = P.psb[2 * k_]; R["bB"] = P.psb[2 * k_ + 1]
            slots.append(R)
        Sst = P.sb([128, 128], "S"); tS = Tok("S")
        vn = P.sb([128, 128], "vn"); tvn = Tok()
        rs = P.sb([128, 512], "rs"); trs = Tok()
        b0, tb0 = P.psb[0]; b1, tb1 = P.psb[1]; b2, tb2 = P.psb[2]; b3, tb3 = P.psb[3]
        b4, tb4 = P.psb[4]; b5, tb5 = P.psb[5]; b6, tb6 = P.psb[6]; b7, tb7 = P.psb[7]
        chunks = [(q0, min(512, LP - q0)) for q0 in range(0, LP, 512)]
        for s in range(NS):
            c0s = s * LP
            P.q_a.dma(g_[0:8, :], self.gabT[0:8, c0s:c0s + LP], outs=[tg])
            P.q_a.dma(be[0:8, :], self.gabT[8:16, c0s:c0s + LP], outs=[tg])
            dve.op(lambda: V.tensor_scalar(g_[0:8, :], g_[0:8, :], dtb[:, 0:1], None, ALU.add), outs=[tg], ins=[tg, tk])
            act.op(lambda: nc.scalar.activation(out=g_[0:8, :], in_=g_[0:8, :], func=AF.Exp), outs=[tg], ins=[tg])
            act.op(lambda: nc.scalar.activation(out=g_[0:8, :], in_=g_[0:8, :], func=AF.Ln, bias=self.ones[0:8, 0:1]), outs=[tg], ins=[tg, self.t_ones])
            dve.op(lambda: V.tensor_scalar(g_[0:8, :], g_[0:8, :], nA[:, 0:1], None, ALU.mult), outs=[tg], ins=[tg, tk])
            act.op(lambda: nc.scalar.activation(out=be[0:8, :], in_=be[0:8, :], func=AF.Sigmoid), outs=[tg], ins=[tg])
            for c in range(NT):
                csl = slice(c * 128, (c + 1) * 128)
                dve.op(lambda: V.tensor_tensor_scan(out=Gc[0:8, csl], data0=self.ones[0:8, 0:128], data1=g_[0:8, csl], initial=0.0,
                                                    op0=ALU.mult, op1=ALU.add), outs=[tg], ins=[tg, self.t_ones])
            for c in range(NT):
                csl = slice(c * 128, (c + 1) * 128)
                pe.op(lambda: nc.tensor.transpose(b0[:, c * 16:c * 16 + 8], Gc[0:8, csl], self.ident[0:8, 0:8]), outs=[tb0], ins=[tg, self.t_ident])
                pe.op(lambda: nc.tensor.transpose(b0[:, c * 16 + 8:c * 16 + 16], be[0:8, csl], self.ident[0:8, 0:8]), outs=[tb0], ins=[tg, self.t_ident])
            act.op(lambda: nc.scalar.copy(out=cols[:], in_=b0[:, 0:NT * 16].rearrange("p (c k) -> p c k", k=16)), outs=[tcols], ins=[tb0])
            for h in range(8):
                for part, (x, tx) in enumerate(((qT, tq), (kT, tkk), (vT, tv))):
                    ch = part * 8 + h
                    P.q_a.dma(x[:], self.gT[ch, :, c0s:c0s + LP], outs=[tx])
                    dve.op(lambda: V.tensor_scalar(cv[:], x[:], cw[:, ch, 3:4], None, ALU.mult), outs=[tcv], ins=[tx, tk])
                    for sh in (1, 2, 3):
                        dve.op(lambda: V.scalar_tensor_tensor(out=cv[:, sh:LP], in0=x[:, 0:LP - sh], scalar=cw[:, ch, 3 - sh:4 - sh], in1=cv[:, sh:LP],
                                                              op0=ALU.mult, op1=ALU.add), outs=[tcv], ins=[tcv, tx, tk])
                    act.op(lambda: nc.scalar.activation(out=x[:], in_=cv[:], func=AF.Silu), outs=[tx], ins=[tcv])
                P.q_a.dma(zT[:], self.gzT[h, :, c0s:c0s + LP], outs=[tz])
                for (x, tx, extra) in ((qT, tq, 128.0 ** -0.5), (kT, tkk, 1.0)):
                    act.op(lambda: nc.scalar.activation(out=cv[:], in_=x[:], func=AF.Square), outs=[tcv], ins=[tx])
                    for (q0, Tq) in chunks:
                        sl = slice(q0, q0 + Tq)
                        pe.op(lambda: nc.tensor.matmul(b0[:, :Tq], self.ones[:], cv[:, sl], start=True, stop=True), outs=[tb0], ins=[tcv, self.t_ones])
                        act.op(lambda: nc.scalar.activation(out=rs[:, :Tq], in_=b0[:, :Tq], func=AF.Sqrt, bias=self.epsb[:], scale=1.0), outs=[trs], ins=[tb0, self.t_eps])
                        dve.op(lambda: V.reciprocal(out=rs[:, :Tq], in_=rs[:, :Tq]), outs=[trs], ins=[trs])
                        dve.op(lambda: V.scalar_tensor_tensor(out=x[:, sl], in0=x[:, sl], scalar=extra, in1=rs[:, :Tq], op0=ALU.mult, op1=ALU.mult),
                               outs=[tx], ins=[tx, trs])
                for (q0, Tq) in chunks:
                    sl = slice(q0, q0 + Tq)
                    pe.op(lambda: nc.tensor.matmul(b0[:, :Tq], sel[:, h * 128:(h + 1) * 128], Gc[0:8, sl], start=True, stop=True), outs=[tb0], ins=[t_sel, tg])
                    act.op(lambda: nc.scalar.copy(out=Grow[:, sl], in_=b0[:, :Tq]), outs=[tGr], ins=[tb0])
                    act.op(lambda: nc.scalar.activation(out=eG[:, sl], in_=b0[:, :Tq], func=AF.Exp), outs=[tGr], ins=[tb0])
                    pe.op(lambda: nc.tensor.matmul(b0[:, :Tq], sel[:, h * 128:(h + 1) * 128], be[0:8, sl], start=True, stop=True), outs=[tb0], ins=[t_sel, tg])
                    act.op(lambda: nc.scalar.copy(out=Brow[:, sl], in_=b0[:, :Tq]), outs=[tGr], ins=[tb0])
                dve.op(lambda: V.tensor_tensor(out=qdT[:], in0=qT[:], in1=eG[:], op=ALU.mult), outs=[tqd], ins=[tq, tGr])
                def chunk_gen(c, R):
                    csl = slice(c * 128, (c + 1) * 128)
                    cend = c * 128 + 127
                    Gcol = cols[:, c, h:h + 1]; bcol = cols[:, c, 8 + h:9 + h]
                    bA, tA = R["bA"]; bB, tB = R["bB"]
                    dT, tdT = R["dT"]; dI, tdI = R["dI"]; aa, taa = R["aa"]; Xu, tXu = R["Xu"]; Xw, tXw = R["Xw"]; sc, tsc = R["sc"]
                    Np, tN = R["Np"], R["tN"]; Mp, tM = R["Mp"], R["tM"]
                    pe.op(lambda: nc.tensor.transpose(bB[:, 0:128], kT[:, csl], self.ident[:]), outs=[tB], ins=[tkk, self.t_ident])
                    pe.op(lambda: nc.tensor.transpose(bB[:, 128:256], vT[:, csl], self.ident[:]), outs=[tB], ins=[tv, self.t_ident])
                    dve.op(lambda: V.tensor_scalar(dT[:], Grow[:, csl], Gcol, 0.0, ALU.subtract, ALU.min), outs=[tdT], ins=[tGr, tcols])
                    pe.op(lambda: nc.tensor.matmul(bA[:, 0:128], kT[:, csl], kT[:, csl], start=True, stop=True), outs=[tA], ins=[tkk])
                    pe.op(lambda: nc.tensor.matmul(bA[:, 128:256], kT[:, csl], qT[:, csl], start=True, stop=True), outs=[tA], ins=[tkk, tq])
                    yield
                    act.op(lambda: nc.scalar.copy(out=k_tok[:, c, :], in_=bB[:, 0:128]), outs=[tkv[c]], ins=[tB])
                    act.op(lambda: nc.scalar.copy(out=v_tok[:, c, :], in_=bB[:, 128:256]), outs=[tkv[c]], ins=[tB])
                    act.op(lambda: nc.scalar.activation(out=dT[:], in_=dT[:], func=AF.Exp), outs=[tdT], ins=[tdT])
                    act.op(lambda: nc.scalar.activation(out=sc[:, 0:1], in_=Gcol, func=AF.Exp), outs=[tsc], ins=[tcols])
                    act.op(lambda: nc.scalar.activation(out=sc[:, 1:2], in_=Gcol, func=AF.Exp, scale=-1.0, bias=Grow[:, cend:cend + 1]), outs=[tsc], ins=[tcols, tGr])
                    yield
                    dve.op(lambda: V.tensor_tensor(out=dI[:], in0=dT[:], in1=maskI[:], op=ALU.mult), outs=[tdI], ins=[tdT, t_mi])
                    dve.op(lambda: V.tensor_tensor(out=attnT[:, c, :], in0=bA[:, 128:256], in1=dI[:], op=ALU.mult), outs=[tpre[c]], ins=[tA, tdI])
                    dve.op(lambda: V.tensor_tensor(out=aa[:], in0=dT[:], in1=maskSt[:], op=ALU.mult), outs=[taa], ins=[tdT, t_mst])
                    dve.op(lambda: V.tensor_tensor(out=aa[:], in0=aa[:], in1=Brow[:, csl], op=ALU.mult), outs=[taa], ins=[taa, tGr])
                    dve.op(lambda: V.tensor_tensor(out=Np[0][:], in0=bA[:, 0:128], in1=aa[:], op=ALU.mult), outs=[tN[0]], ins=[tA, taa])
                    yield
                    pe.op(lambda: nc.tensor.transpose(bB[:, 256:384], Np[0][:], self.ident[:]), outs=[tB], ins=[tN[0], self.t_ident])
                    Yi = 0
                    Y, tY = R["Y"][Yi]
                    dve.op(lambda: V.tensor_tensor(out=Y[:], in0=self.ident[:], in1=Np[0][:], op=ALU.subtract), outs=[tY], ins=[tN[0], self.t_ident])
                    dve.op(lambda: V.tensor_scalar(Xu[:], v_tok[:, c, :], bcol, None, ALU.mult), outs=[tXu], ins=[tkv[c], tcols])
                    dve.op(lambda: V.tensor_scalar(Xw[:], k_tok[:, c, :], bcol, sc[:, 0:1], ALU.mult, ALU.mult), outs=[tXw], ins=[tkv[c], tcols, tsc])
                    dve.op(lambda: V.tensor_scalar(kd_tok[:, c, :], k_tok[:, c, :], sc[:, 1:2], None, ALU.mult), outs=[tpre[c]], ins=[tkv[c], tsc])
                    yield
                    act.op(lambda: nc.scalar.copy(out=Mp[0][:], in_=bB[:, 256:384]), outs=[tM[0]], ins=[tB])
                    yield
                    for k in range(1, 7):
                        pe.op(lambda: nc.tensor.matmul(bB[:, 256:384], Np[k - 1][:], Mp[k - 1][:], start=True, stop=True), outs=[tB], ins=[tN[k - 1], tM[k - 1]])
                        if k < 6:
                            pe.op(lambda: nc.tensor.matmul(bA[:, 256:384], Mp[k - 1][:], Np[k - 1][:], start=True, stop=True), outs=[tA], ins=[tN[k - 1], tM[k - 1]])
                        yield
                        act.op(lambda: nc.scalar.copy(out=Mp[k][:], in_=bB[:, 256:384]), outs=[tM[k]], ins=[tB])
                        if k < 6:
                            dve.op(lambda: V.tensor_copy(out=Np[k][:], in_=bA[:, 256:384]), outs=[tN[k]], ins=[tA])
                        yield
                        pe.op(lambda: nc.tensor.matmul(bA[:, 384:512], Mp[k][:], Y[:], start=True, stop=True), outs=[tA], ins=[tM[k], tY])
                        yield
                        Yi ^= 1
                        Y2, tY2 = R["Y"][Yi]
                        dve.op(lambda: V.tensor_tensor(out=Y2[:], in0=bA[:, 384:512], in1=Y[:], op=ALU.add), outs=[tY2], ins=[tA, tY])
                        Y, tY = Y2, tY2
                    yield
                    pe.op(lambda: nc.tensor.matmul(bB[:, 384:512], Y[:], Xu[:], start=True, stop=True), outs=[tB], ins=[tY, tXu])
                    pe.op(lambda: nc.tensor.matmul(bB[:, 0:128], Xw[:], Y[:], start=True, stop=True), outs=[tB], ins=[tY, tXw])
                    yield
                    act.op(lambda: nc.scalar.copy(out=u_tok[:, c, :], in_=bB[:, 384:512]), outs=[tpre[c]], ins=[tB])
                    act.op(lambda: nc.scalar.copy(out=wT[:, c, :], in_=bB[:, 0:128]), outs=[tpre[c]], ins=[tB])
                for c0 in range(0, NT, KI):
                    gens = [chunk_gen(c, slots[c - c0]) for c in range(c0, min(NT, c0 + KI))]
                    while gens:
                        for g in list(gens):
                            try:
                                next(g)
                            except StopIteration:
                                gens.remove(g)
                dve.op(lambda: V.memset(Sst[:], 0.0), outs=[tS])
                for c in range(NT):
                    csl = slice(c * 128, (c + 1) * 128)
                    cend = c * 128 + 127
                    pe.op(lambda: nc.tensor.matmul(b6[:, 0:128], wT[:, c, :], Sst[:], start=True, stop=True), outs=[tb6], ins=[tpre[c], tS])
                    dve.op(lambda: V.tensor_tensor(out=vn[:], in0=u_tok[:, c, :], in1=b6[:, 0:128], op=ALU.subtract), outs=[tvn], ins=[tpre[c], tb6])
                    pe.op(lambda: nc.tensor.matmul(b7[:, 0:128], Sst[:], qdT[:, csl], start=True, stop=False), outs=[tb7], ins=[tS, tqd])
                    pe.op(lambda: nc.tensor.matmul(b7[:, 0:128], vn[:], attnT[:, c, :], start=False, stop=True), outs=[tb7], ins=[tvn, tpre[c]])
                    act.op(lambda: nc.scalar.copy(out=cv[:, csl], in_=b7[:, 0:128]), outs=[tcv], ins=[tb7])
                    pe.op(lambda: nc.tensor.matmul(b6[:, 128:256], kd_tok[:, c, :], vn[:], start=True, stop=True), outs=[tb6], ins=[tpre[c], tvn])
                    dve.op(lambda: V.scalar_tensor_tensor(out=Sst[:], in0=Sst[:], scalar=eG[:, cend:cend + 1], in1=b6[:, 128:256], op0=ALU.mult, op1=ALU.add),
                           outs=[tS], ins=[tS, tb6, tGr])
                oT = cv
                act.op(lambda: nc.scalar.activation(out=zT[:], in_=zT[:], func=AF.Silu), outs=[tz], ins=[tz])
                for (q0, Tq) in chunks:
                    sl = slice(q0, q0 + Tq)
                    act.op(lambda: nc.scalar.activation(out=qdT[:, sl], in_=oT[:, sl], func=AF.Square), outs=[tqd], ins=[tcv])
                    pe.op(lambda: nc.tensor.matmul(b0[:, :Tq], self.ones[:], qdT[:, sl], start=True, stop=True), outs=[tb0], ins=[tqd, self.t_ones])
                    act.op(lambda: nc.scalar.activation(out=rs[:, :Tq], in_=b0[:, :Tq], func=AF.Sqrt, bias=self.epsb[:], scale=1.0 / 128), outs=[trs], ins=[tb0, self.t_eps])
                    dve.op(lambda: V.reciprocal(out=rs[:, :Tq], in_=rs[:, :Tq]), outs=[trs], ins=[trs])
                    dve.op(lambda: V.scalar_tensor_tensor(out=oT[:, sl], in0=oT[:, sl], scalar=gout[:, 0:1], in1=rs[:, :Tq], op0=ALU.mult, op1=ALU.mult),
                           outs=[tcv], ins=[tcv, trs, tk])
                    dve.op(lambda: V.tensor_tensor(out=oT[:, sl], in0=oT[:, sl], in1=zT[:, sl], op=ALU.mult), outs=[tcv], ins=[tcv, tz])
                P.q_a.dma(self.yO[8 + h, :, c0s:c0s + LP], oT[:], ins=[tcv])
        P.release(m)


def host_consts(LP=None, mix=True):
    c = {"ident": np.eye(128, dtype=np.float32)}
    if mix and LP is not None:
        half = 32
        inv = (10000.0 ** (-np.arange(half, dtype=np.float32) / half)).astype(np.float32)
        ang = np.arange(LP, dtype=np.float32)[:, None] * inv[None, :]
        c["rope_cos"] = np.cos(ang).astype(np.float32)
        c["rope_sin"] = np.sin(ang).astype(np.float32)
        k = np.arange(128)[:, None]; q = np.arange(128)[None, :]
        c["maskT"] = (q >= k).astype(np.float32)
        c["maskS"] = (q > k).astype(np.float32)
        c["maskL"] = (q <= k).astype(np.float32)
        sel = np.zeros((8, 8, 128), np.float32)
        for h in range(8):
            sel[h, h, :] = 1.0
        c["sel"] = sel.reshape(8, 1024)
    return c


def make_xin(x, meta, LP):
    NS, SEQ, _ = x.shape
    xin = np.zeros((NS, LP, D), np.float32)
    xin[:, :N_META] = meta[None]
    xin[:, N_META:N_META + SEQ] = x
    return xin.reshape(NS * LP, D)


_WNAMES = ["norm_ffn1", "norm_mix", "norm_ffn2", "w1_gate", "w1_up", "w2_gate", "w2_up", "w1_down", "w2_down"]


def kernel(**inputs):
    x = np.asarray(inputs["x"])
    B, SEQ, _ = x.shape
    ncores = 8
    NS = B // ncores
    L = SEQ + N_META
    NT = (L + 127) // 128
    K = Kern(NS, NT, depth=2, seq_real=L)
    consts = host_consts(NT * 128)
    in_maps = []
    for c in range(ncores):
        m = {"xin": make_xin(x[c * NS:(c + 1) * NS], np.asarray(inputs["meta_tokens"]), NT * 128)}
        m.update(consts)
        for nm in K.w:
            m[nm] = np.ascontiguousarray(np.asarray(inputs[nm]))
        in_maps.append(m)
    res = run_bass_kernel_spmd(K.P.nc, in_maps, core_ids=list(range(ncores)))
    outs = [r["out"].reshape(NS, SEQ, D) for r in res.results]
    return np.concatenate(outs, axis=0).astype(np.float32)
```
